# Optimizing a Trainium2 kernel written in Bass

```python
import math
import jax, jax.numpy as jnp
from jax import lax
import numpy as np

D_MODEL = 1024
BATCH = 2
SEQ = 16384
DEPTH = 1
DEC_BATCH = 32
DEC_SEQ = 2048
PAST_LEN = 128

ATT_HEADS = 8
ATT_HEAD_DIM = 64
ATT_WIDTH = ATT_HEADS * ATT_HEAD_DIM
DILATED_PATTERNS = ((128, 1), (512, 4), (2048, 16))
ATT_BLOCK = 64
NEG_INF = -1e30
DN_HEADS = 4
DN_HEAD_DIM = 128
DN_WIDTH = DN_HEADS * DN_HEAD_DIM
DN_CONV = 5
DN_CHUNK = 64
MIX_WIDTH = ATT_WIDTH + DN_WIDTH
IN_COLS = 3 * ATT_WIDTH + 4 * DN_WIDTH + 4 * DN_HEADS
FFN_DIM = 2816
FFN_CONV = 3
ROPE_THETA = 10000.0
NORM_EPS = 1e-6

kernel_name = 'hymba_dilated_swa_bi_gated_deltanet_convglu'


def _rms_norm(x, gain):
    xf = x.astype(jnp.float32)
    y = xf * lax.rsqrt(jnp.mean(xf * xf, axis=-1, keepdims=True) + NORM_EPS)
    return (y * gain.astype(jnp.float32)).astype(x.dtype)


def _l2_norm(x):
    return x * lax.rsqrt(jnp.sum(x * x, axis=-1, keepdims=True) + NORM_EPS)


def _dwconv_centered(x, w):
    k_width = w.shape[0]
    pad = k_width // 2
    length = x.shape[1]
    xp = jnp.pad(x, ((0, 0), (pad, pad), (0, 0)))
    out = xp[:, 0:length] * w[0]
    for j in range(1, k_width):
        out = out + xp[:, j:j + length] * w[j]
    return out


def _rotary(x):
    length, dh = x.shape[1], x.shape[-1]
    half = dh // 2
    inv_freq = 1.0 / (ROPE_THETA ** (jnp.arange(half, dtype=jnp.float32) * 2.0 / dh))
    ang = jnp.arange(length, dtype=jnp.float32)[:, None] * inv_freq[None, :]
    cos = jnp.cos(ang)[None, :, None, :]
    sin = jnp.sin(ang)[None, :, None, :]
    x1, x2 = x[..., :half], x[..., half:]
    return jnp.concatenate([x1 * cos - x2 * sin, x2 * cos + x1 * sin], axis=-1)


def _band_attention(q, k, v, radius):
    n, length, h, dh = q.shape
    blk = ATT_BLOCK
    nn = -(-radius // blk)
    nb = -(-length // blk)
    lp = nb * blk
    qp = jnp.pad(q, ((0, 0), (0, lp - length), (0, 0), (0, 0))).reshape(n, nb, blk, h, dh)
    pad_kv = ((0, 0), (nn * blk, lp - length + nn * blk), (0, 0), (0, 0))
    kp = jnp.pad(k, pad_kv).reshape(n, nb + 2 * nn, blk, h, dh)
    vp = jnp.pad(v, pad_kv).reshape(n, nb + 2 * nn, blk, h, dh)
    kw = jnp.concatenate([kp[:, j:j + nb] for j in range(2 * nn + 1)], axis=2)
    vw = jnp.concatenate([vp[:, j:j + nb] for j in range(2 * nn + 1)], axis=2)
    qpos = jnp.arange(nb)[:, None] * blk + jnp.arange(blk)[None, :]
    kpos = jnp.arange(nb)[:, None] * blk - nn * blk + jnp.arange((2 * nn + 1) * blk)[None, :]
    valid = ((jnp.abs(qpos[:, :, None] - kpos[:, None, :]) <= radius)
             & (kpos[:, None, :] >= 0) & (kpos[:, None, :] < length))
    s = jnp.einsum('nbqhd,nbkhd->nbhqk', qp, kw) * (1.0 / math.sqrt(dh))
    s = jnp.where(valid[None, :, None, :, :], s, NEG_INF)
    m = jnp.max(s, axis=-1)
    p = jnp.exp(s - m[..., None])
    den = jnp.sum(p, axis=-1)
    num = jnp.einsum('nbhqk,nbkhd->nbqhd', p, vw).reshape(n, lp, h, dh)[:, :length]
    m = m.transpose(0, 1, 3, 2).reshape(n, lp, h)[:, :length]
    den = den.transpose(0, 1, 3, 2).reshape(n, lp, h)[:, :length]
    return num, m, den


def _dilated_window(q, k, v, window, dilation):
    b, length, h, dh = q.shape
    ld = length // dilation

    def to_sub(t):
        return t.reshape(b, ld, dilation, h, dh).transpose(0, 2, 1, 3, 4).reshape(b * dilation, ld, h, dh)

    num, m, den = _band_attention(to_sub(q), to_sub(k), to_sub(v), window // (2 * dilation))
    num = num.reshape(b, dilation, ld, h, dh).transpose(0, 2, 1, 3, 4).reshape(b, length, h, dh)
    m = m.reshape(b, dilation, ld, h).transpose(0, 2, 1, 3).reshape(b, length, h)
    den = den.reshape(b, dilation, ld, h).transpose(0, 2, 1, 3).reshape(b, length, h)
    return num, m, den


def _dilated_attention(q, k, v):
    parts = [_dilated_window(q, k, v, w, d) for (w, d) in DILATED_PATTERNS]
    m_all = parts[0][1]
    for part in parts[1:]:
        m_all = jnp.maximum(m_all, part[1])
    scale0 = jnp.exp(parts[0][1] - m_all)
    num = parts[0][0] * scale0[..., None]
    den = parts[0][2] * scale0
    for part in parts[1:]:
        sc = jnp.exp(part[1] - m_all)
        num = num + part[0] * sc[..., None]
        den = den + part[2] * sc
    return num / den[..., None]


def _gated_delta_rule(q, k, v, g, beta):
    b, length, h, dk = q.shape
    dv = v.shape[-1]
    c = DN_CHUNK
    nc = length // c

    def chunks(t):
        return t.reshape(b, nc, c, h, -1).transpose(0, 1, 3, 2, 4)

    q, k, v = chunks(q), chunks(k), chunks(v)
    g = g.reshape(b, nc, c, h).transpose(0, 1, 3, 2)
    beta = beta.reshape(b, nc, c, h).transpose(0, 1, 3, 2)
    gc = jnp.cumsum(g, axis=-1)
    tri = jnp.tril(jnp.ones((c, c), dtype=bool))
    strict = jnp.tril(jnp.ones((c, c), dtype=bool), -1)
    decay = jnp.exp(jnp.where(tri, gc[..., :, None] - gc[..., None, :], -jnp.inf))
    kb = k * beta[..., None]
    lmat = jnp.where(strict, jnp.einsum('bnhid,bnhjd->bnhij', kb, k) * decay, 0.0)
    amat = lmat + jnp.eye(c, dtype=lmat.dtype)
    rhs = jnp.concatenate([v * beta[..., None], kb * jnp.exp(gc)[..., None]], axis=-1)
    sol = lax.linalg.triangular_solve(amat, rhs, left_side=True, lower=True, unit_diagonal=True)
    u, w = sol[..., :dv], sol[..., dv:]
    qk = jnp.einsum('bnhid,bnhjd->bnhij', q, k) * decay
    q_dec = q * jnp.exp(gc)[..., None]
    g_last = gc[..., -1]
    k_dec = k * jnp.exp(g_last[..., None] - gc)[..., None]
    xs = tuple(t.swapaxes(0, 1) for t in (u, w, qk, q_dec, k_dec, g_last))

    def step(state, inp):
        u_c, w_c, qk_c, qd_c, kd_c, gl_c = inp
        v_new = u_c - jnp.einsum('bhcd,bhde->bhce', w_c, state)
        o_c = jnp.einsum('bhcd,bhde->bhce', qd_c, state) + jnp.einsum('bhij,bhje->bhie', qk_c, v_new)
        state = state * jnp.exp(gl_c)[..., None, None] + jnp.einsum('bhcd,bhce->bhde', kd_c, v_new)
        return state, o_c

    s0 = jnp.zeros((b, h, dk, dv), jnp.float32)
    _, o = lax.scan(step, s0, xs)
    return o.transpose(1, 0, 3, 2, 4).reshape(b, length, h, dv)


def _layer(x, norm1, w_in, att_q_norm, att_k_norm, dn_conv_w, dn_a_log, dn_dt_bias, dn_out_norm,
           w_out, norm2, w_up, ffn_conv_w, ffn_conv_b, w_down):
    b, length, _ = x.shape
    f32 = jnp.float32
    n = _rms_norm(x, norm1)
    proj = n @ w_in
    o1 = ATT_WIDTH
    o3 = 3 * ATT_WIDTH
    o4 = o3 + 3 * DN_WIDTH
    o5 = o4 + DN_WIDTH
    o6 = o5 + 2 * DN_HEADS
    att_q, att_k, att_v, dn_qkv, dn_g, dn_b, dn_a = jnp.split(proj, [o1, 2 * o1, o3, o4, o5, o6], axis=-1)

    hs = (b, length, ATT_HEADS, ATT_HEAD_DIM)
    aq = _rotary(_rms_norm(att_q.reshape(hs), att_q_norm).astype(f32))
    ak = _rotary(_rms_norm(att_k.reshape(hs), att_k_norm).astype(f32))
    av = att_v.reshape(hs).astype(f32)
    att_out = _dilated_attention(aq, ak, av).reshape(b, length, ATT_WIDTH).astype(x.dtype)

    qkv = jax.nn.silu(_dwconv_centered(dn_qkv, dn_conv_w).astype(f32))
    dq, dk, dv = jnp.split(qkv, [DN_WIDTH, 2 * DN_WIDTH], axis=-1)
    ds = (b, length, DN_HEADS, DN_HEAD_DIM)
    dq = _l2_norm(dq.reshape(ds)) * (DN_HEAD_DIM ** -0.5)
    dk = _l2_norm(dk.reshape(ds))
    dv = dv.reshape(ds)
    beta = jax.nn.sigmoid(dn_b.astype(f32)).reshape(b, length, 2, DN_HEADS)
    log_decay = -jnp.exp(dn_a_log.astype(f32)) * jax.nn.softplus(
        dn_a.astype(f32).reshape(b, length, 2, DN_HEADS) + dn_dt_bias.astype(f32))
    o_fwd = _gated_delta_rule(dq, dk, dv, log_decay[:, :, 0], beta[:, :, 0])
    flip = lambda t: jnp.flip(t, axis=1)
    o_bwd = flip(_gated_delta_rule(flip(dq), flip(dk), flip(dv), flip(log_decay[:, :, 1]), flip(beta[:, :, 1])))
    o_dn = _rms_norm(o_fwd + o_bwd, dn_out_norm) * jax.nn.silu(dn_g.astype(f32)).reshape(ds)
    dn_out = o_dn.reshape(b, length, DN_WIDTH).astype(x.dtype)

    h = x + jnp.concatenate([att_out, dn_out], axis=-1) @ w_out

    u = _dwconv_centered(_rms_norm(h, norm2) @ w_up, ffn_conv_w) + ffn_conv_b
    gate, up = jnp.split(u, 2, axis=-1)
    return h + (jax.nn.silu(gate) * up) @ w_down


def setup_inputs(seed: int = 0) -> dict:
    key = jax.random.key(seed)
    ks = jax.random.split(key, 20)
    f32 = jnp.float32

    def nrm(k, shape, scale):
        return jax.random.normal(k, shape, f32) * scale

    x_prompt = nrm(ks[0], (BATCH, SEQ, D_MODEL), 1.0)
    x_sample = nrm(ks[1], (DEC_BATCH, DEC_SEQ, D_MODEL), 1.0)
    norm1 = 1.0 + nrm(ks[2], (DEPTH, D_MODEL), 0.02)
    w_in = nrm(ks[3], (DEPTH, D_MODEL, IN_COLS), D_MODEL ** -0.5)
    att_q_norm = 1.0 + nrm(ks[4], (DEPTH, ATT_HEAD_DIM), 0.02)
    att_k_norm = 1.0 + nrm(ks[5], (DEPTH, ATT_HEAD_DIM), 0.02)
    dn_conv_w = nrm(ks[6], (DEPTH, DN_CONV, 3 * DN_WIDTH), DN_CONV ** -0.5)
    dn_a_log = jnp.log(jax.random.uniform(ks[7], (DEPTH, 2, DN_HEADS), f32, 1.0, 16.0))
    dt = jnp.exp(jax.random.uniform(ks[8], (DEPTH, 2, DN_HEADS), f32, math.log(1e-3), math.log(1e-1)))
    dn_dt_bias = dt + jnp.log(-jnp.expm1(-dt))
    dn_out_norm = 1.0 + nrm(ks[9], (DEPTH, DN_HEAD_DIM), 0.02)
    w_out = nrm(ks[10], (DEPTH, MIX_WIDTH, D_MODEL), MIX_WIDTH ** -0.5)
    norm2 = 1.0 + nrm(ks[11], (DEPTH, D_MODEL), 0.02)
    w_up = nrm(ks[12], (DEPTH, D_MODEL, 2 * FFN_DIM), D_MODEL ** -0.5)
    ffn_conv_w = nrm(ks[13], (DEPTH, FFN_CONV, 2 * FFN_DIM), FFN_CONV ** -0.5)
    ffn_conv_b = nrm(ks[14], (DEPTH, 2 * FFN_DIM), 0.02)
    w_down = nrm(ks[15], (DEPTH, FFN_DIM, D_MODEL), FFN_DIM ** -0.5)
    return {'x_prompt': x_prompt, 'x_sample': x_sample, 'norm1': norm1, 'w_in': w_in,
            'att_q_norm': att_q_norm, 'att_k_norm': att_k_norm, 'dn_conv_w': dn_conv_w,
            'dn_a_log': dn_a_log, 'dn_dt_bias': dn_dt_bias, 'dn_out_norm': dn_out_norm,
            'w_out': w_out, 'norm2': norm2, 'w_up': w_up, 'ffn_conv_w': ffn_conv_w,
            'ffn_conv_b': ffn_conv_b, 'w_down': w_down}


def reference(x_prompt, x_sample, norm1, w_in, att_q_norm, att_k_norm, dn_conv_w, dn_a_log,
              dn_dt_bias, dn_out_norm, w_out, norm2, w_up, ffn_conv_w, ffn_conv_b, w_down):
    def trunk(x):
        for l in range(DEPTH):
            x = _layer(x, norm1[l], w_in[l], att_q_norm[l], att_k_norm[l], dn_conv_w[l], dn_a_log[l],
                       dn_dt_bias[l], dn_out_norm[l], w_out[l], norm2[l], w_up[l], ffn_conv_w[l],
                       ffn_conv_b[l], w_down[l])
        return x

    y_prompt = trunk(x_prompt)
    y_sample = trunk(x_sample)
    return (y_prompt, y_sample)
```

```python
import numpy as np
import concourse.bass as bass
import concourse.mybir as mybir
from concourse.bass_utils import run_bass_kernel_spmd

F32 = mybir.dt.float32
BF16 = mybir.dt.bfloat16
I32 = mybir.dt.int32
AF = mybir.ActivationFunctionType
ALU = mybir.AluOpType
AX = mybir.AxisListType

D = 1024
FFN = 2816
NFC = FFN // 128
SEG = 2048
NCORES = 8
EPS = 1e-6


class _Op:
    __slots__ = ("eng", "fn", "deps", "sig", "is_dma", "dsem", "dval", "need_sig", "idx")


class Prog:
    ENGS = ("sp", "act", "dve", "pool", "pe")

    def __init__(self, nc, n_dma_sems=8):
        self.nc = nc
        self.ops = []
        self.last_w = {}
        self.readers = {}
        self.n_dma_sems = n_dma_sems
        self.dma_rr = {"sp": 0, "pool": 0, "act": 0}
        self.dma_last = {}
        self.dma_cnt = {}
        self.extra = {}
        self.ns = None
        self.shared = set()
        self.last_op = {}

    def _k(self, k):
        if self.ns is None or k in self.shared or (isinstance(k, tuple) and k[0] in self.shared):
            return k
        return (self.ns, k)

    def op(self, eng, fn, r=(), w=(), dma=False):
        if self.ns is not None:
            r = [self._k(k) for k in r]
            w = [self._k(k) for k in w]
        o = _Op()
        o.eng = eng; o.fn = fn; o.is_dma = dma; o.need_sig = dma; o.sig = None
        o.idx = len(self.ops)
        deps = list(self.extra.pop(eng, []))
        for k in r:
            p = self.last_w.get(k)
            if p is not None:
                deps.append(p)
        for k in w:
            p = self.last_w.get(k)
            if p is not None:
                deps.append(p)
            for q in self.readers.get(k, ()):
                deps.append(q)
        if dma:
            j = self.dma_rr[eng]
            self.dma_rr[eng] = (j + 1) % self.n_dma_sems
            key = (eng, j)
            prev = self.dma_last.get(key)
            if prev is not None:
                deps.append(prev)
            self.dma_last[key] = o
            self.dma_cnt[key] = self.dma_cnt.get(key, 0) + 1
            o.dsem = key
            o.dval = 16 * self.dma_cnt[key]
        dd = []
        seen = set()
        for p in deps:
            if p is o or id(p) in seen:
                continue
            if eng == "pe" and p.eng == "pe" and not p.is_dma:
                continue
            seen.add(id(p))
            dd.append(p)
            p.need_sig = True
        o.deps = dd
        for k in w:
            self.last_w[k] = o
            self.readers[k] = []
        for k in r:
            self.readers.setdefault(k, []).append(o)
        self.ops.append(o)
        if not dma:
            self.last_op[eng] = o
        return o

    def barrier(self):
        markers = [o for o in self.last_op.values()] + [o for o in self.dma_last.values()]
        for e in self.ENGS:
            self.extra[e] = list(markers) + self.extra.get(e, [])

    def emit(self, final_wait_eng="sp"):
        nc = self.nc
        cnt = {e: 0 for e in self.ENGS}
        for o in self.ops:
            if o.is_dma:
                o.sig = (("dma",) + o.dsem, o.dval)
            elif o.need_sig:
                cnt[o.eng] += 1
                o.sig = (("eng", o.eng), cnt[o.eng])
        sem_keys = [("eng", e) for e in self.ENGS] + [("dma", q, j) for q in ("sp", "pool", "act") for j in range(self.n_dma_sems)]
        per_eng = {e: [o for o in self.ops if o.eng == e] for e in self.ENGS}
        finals = {}
        for o in self.ops:
            if o.is_dma:
                finals[o.sig[0]] = max(finals.get(o.sig[0], 0), o.sig[1])
        import contextlib
        with contextlib.ExitStack() as st:
            sems = {}
            for k in sem_keys:
                sems[k] = st.enter_context(nc.semaphore("s_" + "_".join(str(x) for x in k)))
            block = st.enter_context(nc.Block())

            def replay(e, eobj):
                known = {}
                for o in per_eng[e]:
                    need = {}
                    for p in o.deps:
                        sk, v = p.sig
                        if v > need.get(sk, 0):
                            need[sk] = v
                    for sk, v in need.items():
                        if known.get(sk, 0) < v:
                            eobj.wait_ge(sems[sk], v)
                            known[sk] = v
                    ins = o.fn(eobj)
                    if o.sig is not None:
                        ins.then_inc(sems[o.sig[0]], 16 if o.is_dma else 1)
                if e == final_wait_eng:
                    for sk, v in finals.items():
                        if known.get(sk, 0) < v:
                            eobj.wait_ge(sems[sk], v)

            @block.sync
            def _(e):
                replay("sp", e)

            @block.scalar
            def _(e):
                replay("act", e)

            @block.vector
            def _(e):
                replay("dve", e)

            @block.gpsimd
            def _(e):
                replay("pool", e)

            @block.tensor
            def _(e):
                replay("pe", e)


import contextlib

IN_COLS = 3600
KPAD = 1024
DQPAD = 2
TWO_PI = float(2 * np.pi)


def _sl(start, count, step):
    return slice(start, start + (count - 1) * step + 1, step)


def build_program(NSEG=8, passes=("1", "2", "P", "3", "4", "5", "6"), dbg=False):
    nc = bass.Bass("TRN2", target_bir_lowering=False)
    NTOK = NSEG * SEG
    NT = NTOK // 128
    NU = NTOK // 512
    P = Prog(nc)

    def din(name, shape, dt=F32):
        return nc.dram_tensor(name, list(shape), dt, kind="ExternalInput").ap()

    def dout(name, shape, dt=F32):
        return nc.dram_tensor(name, list(shape), dt, kind="ExternalOutput").ap()

    def dscr(name, shape, dt=F32):
        return nc.dram_tensor(name, list(shape), dt, kind=("ExternalOutput" if dbg else "Internal")).ap()

    xs = din("xs", [NTOK, D])
    ident_d = din("ident", [128, 128])
    w_in_d = din("w_in", [D, IN_COLS])
    w_out_d = din("w_out", [D, D])
    w_up_d = din("w_up", [D, 2 * FFN])
    w_down_d = din("w_down", [FFN, D])
    norm1_d = din("norm1", [128, 8])
    norm2_d = din("norm2", [128, 8])
    qkg_d = din("qkg", [128, 1024])
    invf_d = din("invf", [128, 64])
    phase_d = din("phase", [128, 64])
    pos_d = din("pos", [128, NT])
    alog_d = din("alog", [128, 8])
    dtb_d = din("dtb", [128, 8])
    dncw_d = din("dncw", [128, 5, 12])
    dnog_d = din("dnog", [128, 512])
    cw_d = din("ffn_cw", [128, 3, 44])
    cb_d = din("ffn_cb", [128, 44])
    halo_sc_d = din("halo_sc", [2, NU])
    lv_d = din("lv", [128, 2 * NSEG])
    amask_d = din("amask", [128, 256])
    dnc_d = din("dnc", [128, 11, 128])
    dnsc_d = din("dnsc", [128, 4 * NT])
    ys = dout("ys", [NTOK, D])
    KW = NTOK + 2 * KPAD
    QT_d = dscr("QT_s", [4, 128, KW], BF16)
    KT_d = dscr("KT_s", [4, 128, KW], BF16)
    V_d = dscr("V_s", [KW, 512], BF16)
    DQ_d = dscr("DQ_s", [1536, NTOK + 2 * DQPAD], BF16)
    DN_d = dscr("DN_s", [1536, NTOK], BF16)
    G_d = dscr("G_s", [NTOK, 512], F32)
    GB_d = dscr("GB_s", [NTOK, 16], F32)
    AT_d = dscr("AT_s", [4, 128, NTOK], BF16)
    OD_d = [dscr("OF_s", [NTOK, 512], F32), dscr("OB_s", [NTOK, 512], F32)]
    H_d = dscr("H_s", [NTOK, D], F32)

    def MM(out, lhsT, rhs, start, stop, r, w):
        P.op("pe", lambda e, o=out, l=lhsT, rr=rhs, s=start, t=stop: e.matmul(o, lhsT=l, rhs=rr, start=s, stop=t), r=r, w=w)

    def TR(out, in_, idn, r, w):
        P.op("pe", lambda e, o=out, i=in_, d=idn: e.transpose(out=o, in_=i, identity=d), r=r, w=w)

    def ACTV(out, in_, func, r, w, **kw):
        P.op("act", lambda e, o=out, i=in_, f=func, kw=kw: e.activation(out=o, in_=i, func=f, **kw), r=r, w=w)

    def TT(eng, out, in0, in1, op, r, w):
        P.op(eng, lambda e, o=out, a=in0, b=in1, p=op: e.tensor_tensor(out=o, in0=a, in1=b, op=p), r=r, w=w)

    def TS(eng, out, in0, s1, s2, op0, op1, r, w):
        if s2 is None:
            P.op(eng, lambda e, o=out, a=in0, x=s1, p0=op0: e.tensor_scalar(out=o, in0=a, scalar1=x, scalar2=None, op0=p0), r=r, w=w)
        else:
            P.op(eng, lambda e, o=out, a=in0, x=s1, y=s2, p0=op0, p1=op1: e.tensor_scalar(out=o, in0=a, scalar1=x, scalar2=y, op0=p0, op1=p1), r=r, w=w)

    def STT(eng, out, in0, scalar, in1, op0, op1, r, w):
        P.op(eng, lambda e, o=out, a=in0, sc=scalar, b=in1, p0=op0, p1=op1: e.scalar_tensor_tensor(out=o, in0=a, scalar=sc, in1=b, op0=p0, op1=p1), r=r, w=w)

    def CP(eng, out, in_, r, w):
        if eng == "act":
            P.op("act", lambda e, o=out, i=in_: e.copy(out=o, in_=i), r=r, w=w)
        else:
            P.op(eng, lambda e, o=out, i=in_: e.tensor_copy(out=o, in_=i), r=r, w=w)

    def MSET(eng, ap, val, w):
        P.op(eng, lambda e, a=ap, v=val: e.memset(a, v), w=w)

    def DMA(q, out, in_, r=(), w=()):
        P.op(q, lambda e, o=out, i=in_: e.dma_start(out=o, in_=i), r=r, w=w, dma=True)

    def RED(eng, out, in_, r, w):
        P.op(eng, lambda e, o=out, i=in_: e.tensor_reduce(out=o, in_=i, axis=AX.X, op=ALU.add), r=r, w=w)

    def RECIP(out, in_, r, w):
        P.op("dve", lambda e, o=out, i=in_: e.reciprocal(out=o, in_=i), r=r, w=w)

    with contextlib.ExitStack() as es0:
        def sb0(name, shape, dt=F32):
            return es0.enter_context(nc.sbuf_tensor(name, list(shape), dt))
        ident_f = sb0("ident_f", [128, 128])
        ident = sb0("ident_b", [128, 128], BF16)
        epsT = sb0("epsT", [128, 1])
        oneT = sb0("oneT", [128, 1])
        DMA("sp", ident_f[:], ident_d[:, :], w=["ident_f"])
        CP("dve", ident[:], ident_f[:], r=["ident_f"], w=["ident"])
        MSET("dve", epsT[:], EPS, w=["eps"])
        MSET("dve", oneT[:], 1.0, w=["one"])

        def rmsnorm_T(es_sb, src_ap, npart, msT, hnb, sq, pTt, pkey, dst_fn, rkeys, keyp, scale_ap=None, scale_key=None):
            ACTV(sq[0:npart, :], src_ap, AF.Square, r=rkeys, w=["sq", keyp + "ms"], scale=1.0 / 32, accum_out=msT[0:npart, :])
            ACTV(msT[0:npart, :], msT[0:npart, :], AF.Ln, r=[keyp + "ms", "eps"], w=[keyp + "ms"], bias=epsT[0:npart, :])
            ACTV(msT[0:npart, :], msT[0:npart, :], AF.Exp, r=[keyp + "ms"], w=[keyp + "ms"], scale=-0.5)
            if scale_ap is not None:
                TT("dve", msT[0:npart, :], msT[0:npart, :], scale_ap, ALU.mult, r=[keyp + "ms", scale_key], w=[keyp + "ms"])
            TS("dve", hnb[0:npart, :], src_ap, msT[0:npart, 0:1], None, ALU.mult, None, r=rkeys + [keyp + "ms"], w=[keyp + "hn"])
            for kc in range(8):
                TR(pTt[:, kc, 0:npart], hnb[0:npart, kc * 128:(kc + 1) * 128], ident[0:npart, 0:npart],
                   r=[keyp + "hn", "ident"], w=[pkey])
            dst_fn()

        def pass1():
            with contextlib.ExitStack() as es:
                def sb(name, shape, dt=F32):
                    return es.enter_context(nc.sbuf_tensor("p1_" + name, list(shape), dt))

                def ps(name, shape, dt=F32):
                    return es.enter_context(nc.psum_tensor("p1_" + name, list(shape), dt))
                win = sb("win", [128, 8, IN_COLS], BF16)
                g1 = sb("g1", [128, 8])
                qkg = sb("qkg", [128, 1024])
                invf = sb("invf", [128, 64])
                phase = sb("phase", [128, 64])
                post = sb("post", [128, NT])
                nexpA = sb("nexpA", [128, 8])
                dtb = sb("dtb", [128, 8])
                zb = sb("zb", [128, 2, 512], BF16)
                zf = sb("zf", [128, 12, DQPAD], BF16)
                DMA("sp", g1[:], norm1_d[:, :], w=["g1"])
                DMA("sp", qkg[:], qkg_d[:, :], w=["qkg"])
                DMA("sp", invf[:], invf_d[:, :], w=["invf"])
                DMA("sp", phase[:], phase_d[:, :], w=["phase"])
                DMA("sp", post[:], pos_d[:, :], w=["post"])
                DMA("sp", nexpA[:], alog_d[:, :], w=["nexpA"])
                DMA("sp", dtb[:], dtb_d[:, :], w=["dtb"])
                ACTV(nexpA[:], nexpA[:], AF.Exp, r=["nexpA"], w=["nexpA"])
                TS("dve", nexpA[:], nexpA[:], -1.0, None, ALU.mult, None, r=["nexpA"], w=["nexpA"])
                MSET("pool", zb[:], 0.0, w=["zb"])
                MSET("pool", zf[:], 0.0, w=["zf"])
                for j in range(4):
                    DMA("sp", KT_d[j, :, 0:KPAD], zb[:, 0:2, :].rearrange("p a b -> p (a b)"), r=["zb"], w=[("KT_d", "padL")])
                    DMA("sp", KT_d[j, :, KPAD + NTOK:KW], zb[:, 0:2, :].rearrange("p a b -> p (a b)"), r=["zb"], w=[("KT_d", "padR")])
                for q4 in range(4):
                    DMA("sp", V_d[256 * q4:256 * (q4 + 1), :].rearrange("(t p) c -> p t c", p=128), zb[:], r=["zb"], w=[("V_d", "padL")])
                    DMA("sp", V_d[KPAD + NTOK + 256 * q4:KPAD + NTOK + 256 * (q4 + 1), :].rearrange("(t p) c -> p t c", p=128), zb[:],
                        r=["zb"], w=[("V_d", "padR")])
                dqv = DQ_d.rearrange("(f p) w -> p f w", p=128)
                DMA("sp", dqv[:, :, 0:DQPAD], zf[:], r=["zf"], w=[("DQ_d", "padL")])
                DMA("sp", dqv[:, :, DQPAD + NTOK:DQPAD + NTOK + DQPAD], zf[:], r=["zf"], w=[("DQ_d", "padR")])
                for kc in range(8):
                    DMA("pool", win[:, kc, :], w_in_d[kc * 128:(kc + 1) * 128, :], w=[("win", kc)])
                    TS("dve", win[:, kc, :], win[:, kc, :], g1[:, kc:kc + 1], None, ALU.mult, None, r=["g1", ("win", kc)], w=[("win", kc)])
                xt = [sb(f"xt{i}", [128, 4, D]) for i in range(2)]
                nT = sb("nT", [128, 8, 512], BF16)
                hn = sb("hn", [128, D], BF16)
                sq = sb("sq", [128, D])
                ms = sb("ms", [128, 1])
                dqs = sb("dqs", [128, 12, 512], BF16)
                vb = sb("vb", [128, 4, 512], BF16)
                gs = sb("gs", [128, 4, 512])
                gbt = sb("gbt", [128, 4, 16])
                zt = sb("zt", [128, 8])
                bt8 = sb("bt8", [128, 8])
                qn = sb("qn", [128, 1024])
                t1 = sb("t1", [128, 512]); t2 = sb("t2", [128, 512]); t3 = sb("t3", [128, 512]); t4 = sb("t4", [128, 512])
                qrs = [sb(f"qr{i}", [128, 1024], BF16) for i in range(2)]
                qkTs = [sb(f"qkT{i}", [128, 8, 512], BF16) for i in range(2)]
                ssq = sb("ssq", [128, 16])
                ang16 = sb("ang16", [128, 16, 64]); kfi16 = sb("kfi16", [128, 16, 64], I32); kff16 = sb("kff16", [128, 16, 64])
                cs16 = sb("cs16", [128, 16, 64])
                deferred = []
                pTx = ps("pTx", [128, 8, 128], BF16)
                pF = ps("pF", [128, 512]); pQ = ps("pQ", [128, 512]); pK = ps("pK", [128, 512]); pV = ps("pV", [128, 512])
                pG = ps("pG", [128, 512]); pB = ps("pB", [128, 512]); pTq = ps("pTq", [128, 8, 128], BF16)
                allwin = [("win", kc) for kc in range(8)]

                for u in range(NU):
                    b = u % 2
                    t0 = u * 512
                    DMA("sp", xt[b][:], xs[t0:t0 + 512, :].rearrange("(t p) d -> p t d", p=128), w=[("xt", b)])
                    for t in range(4):
                        def dst(t=t):
                            CP("act", nT[:, :, 128 * t:128 * (t + 1)], pTx[:, :, :], r=[], w=["pTx", "nT"])
                        rmsnorm_T(None, xt[b][:, t, :], 128, ms, hn, sq, pTx, "pTx", dst, [("xt", b)], "p1")
                    for fc in range(12):
                        c0 = 1536 + fc * 128
                        for kc in range(8):
                            MM(pF[:, :], win[:, kc, c0:c0 + 128], nT[:, kc, :], kc == 0, kc == 7, r=[("win", kc), "nT"], w=["pF"])
                        CP("act", dqs[:, fc, :], pF[:, :], r=[], w=["pF", "dqs"])
                    DMA("sp", dqv[:, :, DQPAD + t0:DQPAD + t0 + 512], dqs[:], r=["dqs"], w=[("DQ_d", u)])
                    for t in range(4):
                        ti = u * 4 + t
                        lw = nT
                        for (pp, key, c0, n) in ((pQ, "pQ", 0, 512), (pK, "pK", 512, 512), (pV, "pV", 1024, 512),
                                                 (pG, "pG", 3072, 512), (pB, "pB", 3584, 16)):
                            for kc in range(8):
                                MM(pp[:, 0:n], nT[:, kc, 128 * t:128 * (t + 1)], win[:, kc, c0:c0 + n], kc == 0, kc == 7,
                                   r=[("win", kc), "nT"], w=[key])
                        while deferred:
                            deferred.pop(0)()
                        CP("act", vb[:, t, :], pV[:, :], r=[], w=["pV", "vb"])
                        CP("act", gs[:, t, :], pG[:, :], r=[], w=["pG", "gs"])
                        ACTV(bt8[:], pB[:, 0:8], AF.Exp, r=[], w=["pB", "bt8"], scale=-1.0)
                        TS("dve", bt8[:], bt8[:], 1.0, None, ALU.add, None, r=["bt8"], w=["bt8"])
                        RECIP(gbt[:, t, 8:16], bt8[:], r=["bt8"], w=["gbt"])
                        TT("dve", zt[:], pB[:, 8:16], dtb[:], ALU.add, r=["dtb"], w=["pB", "zt"])
                        ACTV(zt[:], zt[:], AF.Exp, r=["zt"], w=["zt"])
                        ACTV(zt[:], zt[:], AF.Ln, r=["zt", "one"], w=["zt"], bias=oneT[:])
                        TT("dve", gbt[:, t, 0:8], zt[:], nexpA[:], ALU.mult, r=["zt", "nexpA"], w=["gbt"])
                        ACTV(sq[:, 0:512], pQ[:, :], AF.Square, r=[], w=["pQ", "sq"])
                        ACTV(sq[:, 512:1024], pK[:, :], AF.Square, r=[], w=["pK", "sq"])
                        RED("dve", ssq[:], sq[:].rearrange("p (h d) -> p h d", d=64), r=["sq"], w=["ssq"])
                        ACTV(ssq[:], ssq[:], AF.Ln, r=["ssq", "eps"], w=["ssq"], scale=1.0 / 64, bias=epsT[:])
                        ACTV(ssq[:], ssq[:], AF.Exp, r=["ssq"], w=["ssq"], scale=-0.5)
                        TT("dve", qn[:, 0:512].rearrange("p (h d) -> p h d", d=64), pQ[:, :].rearrange("p (h d) -> p h d", d=64),
                           ssq[:, 0:8].unsqueeze(2).to_broadcast([128, 8, 64]), ALU.mult, r=["ssq"], w=["pQ", "qn"])
                        TT("dve", qn[:, 512:1024].rearrange("p (h d) -> p h d", d=64), pK[:, :].rearrange("p (h d) -> p h d", d=64),
                           ssq[:, 8:16].unsqueeze(2).to_broadcast([128, 8, 64]), ALU.mult, r=["ssq"], w=["pK", "qn"])
                        TT("pool", qn[:], qn[:], qkg[:], ALU.mult, r=["qn", "qkg"], w=["qn"])
                        if ti % 16 == 0:
                            for tt in range(16):
                                STT("dve", ang16[:, tt, :], invf[:], post[:, ti + tt:ti + tt + 1], phase[:], ALU.mult, ALU.add,
                                    r=["invf", "post", "phase"], w=["ang16"])
                            TS("dve", kfi16[:], ang16[:], 1.0 / TWO_PI, None, ALU.mult, None, r=["ang16"], w=["kfi16"])
                            CP("dve", kff16[:], kfi16[:], r=["kfi16"], w=["kff16"])
                            STT("dve", ang16[:], kff16[:], -TWO_PI, ang16[:], ALU.mult, ALU.add, r=["kff16", "ang16"], w=["ang16"])
                            TS("dve", ang16[:], ang16[:], -float(np.pi), float(np.pi), ALU.max, ALU.min, r=["ang16"], w=["ang16"])
                            ACTV(cs16[:], ang16[:], AF.Sin, r=["ang16"], w=["cs16"])
                        qr = qrs[ti % 2]
                        qrk = ("qr", ti % 2)
                        cs = cs16[:, ti % 16, :]
                        qv = qn[:].rearrange("p (h d) -> p h d", d=64)
                        qrv = qr[:].rearrange("p (h d) -> p h d", d=64)
                        sinb = cs[:, 0:32].unsqueeze(1).to_broadcast([128, 16, 32])
                        cosb = cs[:, 32:64].unsqueeze(1).to_broadcast([128, 16, 32])
                        v3 = lambda tt: tt[:].rearrange("p (h d) -> p h d", d=32)
                        TT("dve", v3(t1), qv[:, :, 0:32], cosb, ALU.mult, r=["qn", "cs16"], w=["t1"])
                        TT("dve", v3(t2), qv[:, :, 32:64], sinb, ALU.mult, r=["qn", "cs16"], w=["t2"])
                        TT("dve", qrv[:, :, 0:32], v3(t1), v3(t2), ALU.subtract, r=["t1", "t2"], w=[qrk])
                        TT("pool", v3(t3), qv[:, :, 32:64], cosb, ALU.mult, r=["qn", "cs16"], w=["t3"])
                        TT("pool", v3(t4), qv[:, :, 0:32], sinb, ALU.mult, r=["qn", "cs16"], w=["t4"])
                        TT("pool", qrv[:, :, 32:64], v3(t3), v3(t4), ALU.add, r=["t3", "t4"], w=[qrk])

                        def tr_stage(qr=qr, qrk=qrk, t=t, qkT=qkTs[u % 2], qkk=("qkT", u % 2)):
                            for j in range(8):
                                TR(pTq[:, j, :], qr[:, j * 128:(j + 1) * 128], ident[:, :], r=[qrk, "ident"], w=["pTq"])
                            CP("act", qkT[:, :, 128 * t:128 * (t + 1)], pTq[:, :, :], r=[], w=["pTq", qkk])
                        deferred.append(tr_stage)
                    while deferred:
                        deferred.pop(0)()
                    qkT = qkTs[u % 2]
                    DMA("sp", V_d[KPAD + t0:KPAD + t0 + 512, :].rearrange("(t p) c -> p t c", p=128), vb[:], r=["vb"], w=[("V_d", u)])
                    DMA("sp", G_d[t0:t0 + 512, :].rearrange("(t p) c -> p t c", p=128), gs[:], r=["gs"], w=[("G_d", u)])
                    DMA("sp", GB_d[t0:t0 + 512, :].rearrange("(t p) c -> p t c", p=128), gbt[:], r=["gbt"], w=[("GB_d", u)])
                    DMA("sp", QT_d[:, :, KPAD + t0:KPAD + t0 + 512].rearrange("j p w -> p j w"), qkT[:, 0:4, :], r=[("qkT", u % 2)], w=[("QT_d", u)])
                    DMA("sp", KT_d[:, :, KPAD + t0:KPAD + t0 + 512].rearrange("j p w -> p j w"), qkT[:, 4:8, :], r=[("qkT", u % 2)], w=[("KT_d", u)])
            P.barrier()

        def pass2():
            with contextlib.ExitStack() as es:
                def sb(name, shape, dt=F32):
                    return es.enter_context(nc.sbuf_tensor("p2_" + name, list(shape), dt))

                def ps(name, shape, dt=F32):
                    return es.enter_context(nc.psum_tensor("p2_" + name, list(shape), dt))
                lv = sb("lv", [128, 2 * NSEG])
                am_f = sb("am_f", [128, 256])
                am = sb("am", [128, 256], BF16)
                amL = sb("amL", [128, 256], BF16); amR = sb("amR", [128, 256], BF16); amLR = sb("amLR", [128, 256], BF16)
                ones_b = sb("ones_b", [128, 128])
                DMA("sp", lv[:], lv_d[:, :], w=["lv"])
                DMA("sp", am_f[:], amask_d[:, :], w=["am_f"])
                CP("dve", am[:], am_f[:], r=["am_f"], w=["am"])
                MSET("dve", ones_b[:], 1.0, w=["ones_b"])
                QTs = [sb(f"QTs{i}", [128, 4, SEG], BF16) for i in range(1)]
                KTw = [sb(f"KTw{i}", [128, 4, 2 * SEG], BF16) for i in range(1)]
                acc = [sb(f"acc{e}", [128, 4, SEG]) for e in range(2)]
                attT = sb("attT", [128, 4, SEG], BF16)
                NVB = 4
                vraw = [sb(f"vraw{i}", [128, 512], BF16) for i in range(NVB)]
                vt2 = [sb(f"vt2_{i}", [128, 4, 2, 128], BF16) for i in range(NVB)]
                NPT = 4
                pt = [sb(f"pt{i}", [128, 256], BF16) for i in range(NPT)]
                rden = sb("rden", [128, 512])
                pS = [ps(f"pS{i}", [128, 512]) for i in range(3)]
                pA = [ps(f"pA{i}", [128, 512]) for i in range(3)]
                pBc = ps("pBc", [128, 512])
                for i in range(NVB):
                    MSET("pool", vt2[i][:], 0.0, w=[("vt2", i)])
                    MSET("pool", vt2[i][:, :, 0, 64:65], 1.0, w=[("vt2", i)])
                    MSET("pool", vt2[i][:, :, 1, 32:33], 1.0, w=[("vt2", i)])
                cnt = {"v": 0, "s": 0, "a": 0, "pt": 0, "m": 0}
                import collections
                pending = collections.deque()
                SKEW = 2
                for s in range(NSEG):
                    b = 0
                    base = KPAD + s * SEG
                    DMA("sp", QTs[b][:], QT_d[:, :, base:base + SEG].rearrange("j p w -> p j w"),
                        r=[("QT_d", u) for u in range(4 * s, 4 * s + 4)], w=[("QTs", b)])
                    ulo = max(0, 4 * s - 2); uhi = min(NU, 4 * s + 6)
                    DMA("sp", KTw[b][:], KT_d[:, :, base - 1024:base + SEG + 1024].rearrange("j p w -> p j w"),
                        r=[("KT_d", u) for u in range(ulo, uhi)] + [("KT_d", "padL"), ("KT_d", "padR")], w=[("KTw", b)])
                    TS("pool", amL[:, 0:128], am[:, 0:128], lv[:, 2 * s:2 * s + 1], None, ALU.mult, None, r=["am", "lv"], w=["amL"])
                    CP("pool", amL[:, 128:256], am[:, 128:256], r=["am"], w=["amL"])
                    CP("pool", amR[:, 0:128], am[:, 0:128], r=["am"], w=["amR"])
                    TS("pool", amR[:, 128:256], am[:, 128:256], lv[:, 2 * s + 1:2 * s + 2], None, ALU.mult, None, r=["am", "lv"], w=["amR"])
                    CP("pool", amLR[:, 0:128], amL[:, 0:128], r=["amL"], w=["amLR"])
                    CP("pool", amLR[:, 128:256], amR[:, 128:256], r=["amR"], w=["amLR"])
                    vkeys = [("V_d", u) for u in range(ulo, uhi)] + [("V_d", "padL"), ("V_d", "padR")]
                    for (d, first) in ((1, True), (4, False), (16, False)):
                        nq = SEG // d
                        nqb = nq // 128
                        for c in range(d):
                            vslot = {}

                            def load_v(k):
                                i = cnt["v"] % NVB
                                cnt["v"] += 1
                                row0 = base + c + d * (128 * k - 64)
                                DMA("sp", vraw[i][:], V_d[_sl(row0, 128, d), :], r=vkeys, w=[("vraw", i)])
                                vr = vraw[i][:].rearrange("p (j e x) -> p j e x", e=2, x=64)
                                CP("pool", vt2[i][:, :, 0, 0:64], vr[:, :, 0, :], r=[("vraw", i)], w=[("vt2", i)])
                                CP("pool", vt2[i][:, :, 1, 64:128], vr[:, :, 1, :], r=[("vraw", i)], w=[("vt2", i)])
                                vslot[k] = i
                            load_v(0)
                            for qb in range(nqb):
                                load_v(qb + 1)
                                if nqb == 1:
                                    msk, mkey = amLR, "amLR"
                                elif qb == 0:
                                    msk, mkey = amL, "amL"
                                elif qb == nqb - 1:
                                    msk, mkey = amR, "amR"
                                else:
                                    msk, mkey = am, "am"
                                qcol0 = c + d * 128 * qb
                                qsl = _sl(qcol0, 128, d)
                                for hp in range(4):
                                    for e in range(2):
                                        pb = 64 * e
                                        iS = cnt["s"] % 3; cnt["s"] += 1
                                        for half in range(2):
                                            k = qb + half
                                            kc0 = 1024 + c + d * (128 * k - 64)
                                            MM(pS[iS][:, 128 * half:128 * (half + 1)], KTw[b][pb:pb + 64, hp, _sl(kc0, 128, d)],
                                               QTs[b][pb:pb + 64, hp, qsl], True, True, r=[("KTw", b), ("QTs", b)], w=[("pS", iS)])
                                        ip = cnt["pt"] % NPT; cnt["pt"] += 1
                                        ACTV(pt[ip][:], pS[iS][:, 0:256], AF.Exp, r=[], w=[("pS", iS), ("pt", ip)], scale=0.125)
                                        meng = "pool" if (cnt["m"] % 2 == 0) else "dve"
                                        cnt["m"] += 1
                                        TT(meng, pt[ip][:], pt[ip][:], msk[:], ALU.mult, r=[("pt", ip), mkey], w=[("pt", ip)])
                                        def pv_stage(ip=ip, e=e, hp=hp, qsl=qsl, first=first, v0=vslot[qb], v1=vslot[qb + 1]):
                                            iA = cnt["a"] % 3; cnt["a"] += 1
                                            M = 65 if e == 0 else 128
                                            for half, vi in ((0, v0), (1, v1)):
                                                MM(pA[iA][0:M, 0:128], vt2[vi][:, hp, e, 0:M], pt[ip][:, 128 * half:128 * (half + 1)],
                                                   half == 0, half == 1, r=[("vt2", vi), ("pt", ip)], w=[("pA", iA)])
                                            lo, hi = (0, 65) if e == 0 else (0, 128)
                                            dst = acc[e][lo:hi, hp, qsl]
                                            if first:
                                                CP("dve", dst, pA[iA][lo:hi, 0:128], r=[], w=[("pA", iA), ("acc", e, hp)])
                                            else:
                                                TT("dve", dst, dst, pA[iA][lo:hi, 0:128], ALU.add, r=[], w=[("pA", iA), ("acc", e, hp)])
                                        pending.append(pv_stage)
                                        if len(pending) > SKEW:
                                            pending.popleft()()
                    while pending:
                        pending.popleft()()
                    for hp in range(4):
                        for e in range(2):
                            dp = 64 if e == 0 else 32
                            lo, hi = (0, 64) if e == 0 else (64, 128)
                            for cc in range(SEG // 512):
                                csl = slice(512 * cc, 512 * (cc + 1))
                                MM(pBc[0:hi, :], ones_b[dp:dp + 1, 0:hi], acc[e][dp:dp + 1, hp, csl], True, True,
                                   r=["ones_b", ("acc", e, hp)], w=["pBc"])
                                RECIP(rden[lo:hi, 0:512], pBc[lo:hi, :], r=[], w=["pBc", "rden"])
                                TT("dve", attT[lo:hi, hp, csl], acc[e][lo:hi, hp, csl], rden[lo:hi, 0:512], ALU.mult,
                                   r=[("acc", e, hp), "rden"], w=["attT"])
                    DMA("sp", AT_d[:, :, s * SEG:(s + 1) * SEG].rearrange("j p w -> p j w"), attT[:], r=["attT"], w=[("AT_d", s)])
            P.barrier()


        def pass_prep():
            with contextlib.ExitStack() as es:
                def sb(name, shape, dt=F32):
                    return es.enter_context(nc.sbuf_tensor("pp_" + name, list(shape), dt))

                def ps(name, shape, dt=F32):
                    return es.enter_context(nc.psum_tensor("pp_" + name, list(shape), dt))
                dncw = sb("dncw", [128, 5, 12])
                dnsc = sb("dnsc", [128, 4 * NT])
                lnq = sb("lnq", [128, 1])
                dg = sb("dg", [128, 60, 128], BF16)
                ones_b = sb("ones_b", [128, 128], BF16)
                DMA("sp", dncw[:], dncw_d[:, :, :], w=["dncw"])
                DMA("sp", dnsc[:], dnsc_d[:, :], w=["dnsc"])
                MSET("dve", lnq[:], float(-0.5 * np.log(128.0)), w=["lnq"])
                MSET("dve", ones_b[:], 1.0, w=["ones_b"])
                for j in range(5):
                    for fc in range(12):
                        TS("dve", dg[:, j * 12 + fc, :], ident[:, :], dncw[:, j, fc:fc + 1], None, ALU.mult, None,
                           r=["ident", "dncw"], w=["dg"])
                dqb = [sb(f"dqb{i}", [128, 12, 516], BF16) for i in range(2)]
                qk32 = sb("qk32", [128, 8, 512])
                sqb = sb("sqb", [128, 8, 512], BF16)
                rn = sb("rn", [128, 8, 512])
                oqkv = [sb(f"oqkv{i}", [128, 12, 512], BF16) for i in range(2)]
                pC = [ps(f"pC{i}", [128, 512]) for i in range(4)]
                pNn = [ps(f"pN{i}", [128, 512]) for i in range(4)]
                dqv = DQ_d.rearrange("(f p) w -> p f w", p=128)
                dnv = DN_d.rearrange("(f p) w -> p f w", p=128)

                def load(u):
                    t0 = u * 512
                    deps = [("DQ_d", uu) for uu in range(max(0, u - 1), min(NU, u + 2))] + [("DQ_d", "padL"), ("DQ_d", "padR")]
                    DMA("sp", dqb[u % 2][:], dqv[:, :, DQPAD + t0 - 2:DQPAD + t0 + 514], r=deps, w=[("dqb", u % 2)])
                load(0)
                for u in range(NU):
                    b = u % 2
                    t0 = u * 512
                    if u + 1 < NU:
                        load(u + 1)
                    ti = u * 4
                    if t0 % SEG == 0:
                        TS("dve", dqb[b][:, :, 0:2], dqb[b][:, :, 0:2], dnsc[:, 4 * ti:4 * ti + 1], None, ALU.mult, None,
                           r=[("dqb", b), "dnsc"], w=[("dqb", b)])
                    if (t0 + 512) % SEG == 0:
                        TS("dve", dqb[b][:, :, 514:516], dqb[b][:, :, 514:516], dnsc[:, 4 * (ti + 3) + 1:4 * (ti + 3) + 2], None,
                           ALU.mult, None, r=[("dqb", b), "dnsc"], w=[("dqb", b)])
                    for fc in range(12):
                        i = fc % 4
                        for j in range(5):
                            MM(pC[i][:, :], dg[:, j * 12 + fc, :], dqb[b][:, fc, j:j + 512], j == 0, j == 4, r=["dg", ("dqb", b)], w=[("ppC", i)])
                        if fc < 8:
                            ACTV(qk32[:, fc, :], pC[i][:, :], AF.Silu, r=[], w=[("ppC", i), ("qk32", fc)])
                        else:
                            ACTV(oqkv[b][:, fc, :], pC[i][:, :], AF.Silu, r=[], w=[("ppC", i), ("oqkv", b)])
                    for fc in range(8):
                        ACTV(sqb[:, fc, :], qk32[:, fc, :], AF.Square, r=[("qk32", fc)], w=[("sqb", fc)])
                    for fc in range(8):
                        i = fc % 4
                        MM(pNn[i][:, :], ones_b[:, :], sqb[:, fc, :], True, True, r=["ones_b", ("sqb", fc)], w=[("ppN", i)])
                        ACTV(rn[:, fc, :], pNn[i][:, :], AF.Ln, r=["eps"], w=[("ppN", i), ("rn", fc)], bias=epsT[:])
                    for fc in range(8):
                        if fc < 4:
                            ACTV(rn[:, fc, :], rn[:, fc, :], AF.Exp, r=[("rn", fc), "lnq"], w=[("rn", fc)], scale=-0.5, bias=lnq[:])
                        else:
                            ACTV(rn[:, fc, :], rn[:, fc, :], AF.Exp, r=[("rn", fc)], w=[("rn", fc)], scale=-0.5)
                        TT("dve", oqkv[b][:, fc, :], qk32[:, fc, :], rn[:, fc, :], ALU.mult, r=[("qk32", fc), ("rn", fc)], w=[("oqkv", b)])
                    DMA("sp", dnv[:, :, t0:t0 + 512], oqkv[b][:], r=[("oqkv", b)], w=[("DN_d", u)])
            P.barrier()

        def pass_dn_gen(dr, es):
            NEGC = 11
            if True:
                def sb(name, shape, dt=F32):
                    return es.enter_context(nc.sbuf_tensor(f"p3{dr}_" + name, list(shape), dt))

                def ps(name, shape, dt=F32):
                    return es.enter_context(nc.psum_tensor(f"p3{dr}_" + name, list(shape), dt))
                dnc = sb("dnc", [128, NEGC, 128])
                dnsc = sb("dnsc", [128, 4 * NT])
                DMA("sp", dnc[:], dnc_d[:, :, :], w=["dnc"])
                DMA("sp", dnsc[:], dnsc_d[:, :], w=["dnsc"])
                U = dnc[:, 0 + dr, :]
                ONES = dnc[:, 2, :]
                MLOW = dnc[:, 3 + dr, :]
                MQ = dnc[:, 5 + dr, :]
                IDF = dnc[:, 7, :]
                BD = dnc[:, 10, :]
                qkvb = [sb(f"qkvb{i}", [128, 12, 128], BF16) for i in range(2)]
                gb = [sb(f"gb{i}", [128, 16]) for i in range(2)]
                gst = sb("gst", [128, 16])
                egc = sb("egc", [128, 4]); be = sb("be", [128, 4]); edk = sb("edk", [128, 4]); egl = sb("egl", [128, 8])
                Ug = sb("Ug", [128, 4, 128])
                Nm = sb("Nm", [128, 4, 128])
                DL = sb("DL", [128, 4, 128]); DQm = sb("DQm", [128, 4, 128])
                egcr = sb("egcr", [128, 4, 128])
                Lb = [sb(f"Lb{i}", [128, 4, 128], BF16) for i in range(2)]
                Mb = [sb(f"Mb{i}", [128, 4, 128], BF16) for i in range(2)]
                Z = sb("Z", [128, 4, 128], BF16)
                kkb = sb("kkb", [128, 4, 128])
                qkTm = sb("qkTm", [128, 4, 128], BF16)
                kbe = sb("kbe", [128, 4, 128], BF16); kdec = sb("kdec", [128, 4, 128], BF16); vbt = sb("vbt", [128, 4, 128], BF16)
                um = sb("um", [128, 4, 128]); wTm = sb("wTm", [128, 4, 128], BF16); qdTm = sb("qdTm", [128, 4, 128], BF16)
                vn = sb("vn", [128, 4, 128], BF16)
                ot = [sb(f"ot{i}", [128, 4, 128]) for i in range(2)]
                S = sb("S", [128, 4, 128])
                Sb = sb("Sb", [128, 4, 128], BF16)
                g = [ps(f"g{i}", [128, 4, 128]) for i in range(3)]
                tbb = ps("tbb", [128, 8, 128], BF16)
                tb = [tbb[:, 0:4, :], tbb[:, 4:8, :]]
                gi = [0]
                tbi = [0]

                def nextg():
                    i = gi[0] % 3
                    gi[0] += 1
                    return g[i], ("p3g", i)

                def nexttb():
                    i = tbi[0] % 2
                    tbi[0] += 1
                    return tb[i], "p3tb"
                bc = lambda ap4: ap4.unsqueeze(2).to_broadcast([128, 4, 128])
                identb4 = ident[:, :].unsqueeze(1).to_broadcast([128, 4, 128])
                MSET("dve", S[:], 0.0, w=["S"])
                MSET("dve", Sb[:], 0.0, w=["Sb"])
                order = list(range(NT)) if dr == 0 else list(range(NT - 1, -1, -1))
                dnv = DN_d.rearrange("(f p) w -> p f w", p=128)

                def load(ti, b):
                    t0 = ti * 128
                    u = ti // 4
                    DMA("sp", qkvb[b][:], dnv[:, :, t0:t0 + 128], r=[("DN_d", u)], w=[("qkvb", b)])
                    DMA("sp", gb[b][:], GB_d[t0:t0 + 128, :], r=[("GB_d", u)], w=[("gb", b)])
                load(order[0], 0)
                for n_, ti in enumerate(order):
                    b = n_ % 2
                    t0 = ti * 128
                    if n_ + 1 < NT:
                        load(order[n_ + 1], 1 - b)
                    g4 = gb[b][:, 4 * dr:4 * dr + 4]
                    b4 = gb[b][:, 8 + 4 * dr:12 + 4 * dr]
                    qT = lambda h: qkvb[b][:, h, :]
                    kT = lambda h: qkvb[b][:, 4 + h, :]
                    vT = lambda h: qkvb[b][:, 8 + h, :]
                    pG3, kG = nextg()
                    pG = pG3[:, :, :].rearrange("p a b -> p (a b)")
                    MM(pG[:, 0:4], U, g4, True, True, r=["dnc", ("gb", b)], w=[kG])
                    MM(pG[:, 4:8], BD, g4, True, True, r=["dnc", ("gb", b)], w=[kG])
                    MM(pG[:, 8:12], dnc[:, 8, :], g4, True, True, r=["dnc", ("gb", b)], w=[kG])
                    MM(pG[:, 12:16], dnc[:, 9, :], g4, True, True, r=["dnc", ("gb", b)], w=[kG])
                    CP("dve", gst[:], pG[:, 0:16], r=[], w=[kG, "gst"])
                    ACTV(egc[:], gst[:, 0:4], AF.Exp, r=["gst"], w=["egc"])
                    TT("dve", be[:], egc[:], b4, ALU.mult, r=["egc", ("gb", b)], w=["be"])
                    TT("dve", edk[:], gst[:, 4:8], gst[:, 0:4], ALU.subtract, r=["gst"], w=["edk"])
                    ACTV(edk[:], edk[:], AF.Exp, r=["edk"], w=["edk"])
                    ACTV(egl[:], gst[:, 8:16], AF.Exp, r=["gst"], w=["egl"])
                    TT("dve", Ug[:], U.unsqueeze(1).to_broadcast([128, 4, 128]), bc(g4), ALU.mult, r=["dnc", ("gb", b)], w=["Ug"])
                    yield
                    pR, kR = nextg()
                    MM(pR[:, :, :], ONES, Ug[:, :, :], True, True, r=["dnc", "Ug"], w=[kR])
                    TT("dve", Nm[:], pR[:, :, :], bc(gst[:, 0:4]), ALU.subtract, r=["gst"], w=[kR, "Nm"])
                    ACTV(egcr[:], pR[:, :, :], AF.Exp, r=[], w=[kR, "egcr"])
                    mlb = MLOW.unsqueeze(1).to_broadcast([128, 4, 128])
                    mqb = MQ.unsqueeze(1).to_broadcast([128, 4, 128])
                    STT("dve", DL[:], Nm[:], -1.0, mlb, ALU.mult, ALU.add, r=["Nm", "dnc"], w=["DL"])
                    TT("pool", DQm[:], Nm[:], mqb, ALU.add, r=["Nm", "dnc"], w=["DQm"])
                    ACTV(DL[:], DL[:], AF.Exp, r=["DL"], w=["DL"])
                    ACTV(DQm[:], DQm[:], AF.Exp, r=["DQm"], w=["DQm"])
                    yield
                    pK_, kK = nextg()
                    for h in range(4):
                        MM(pK_[:, h, :], kT(h), kT(h), True, True, r=[("qkvb", b)], w=[kK])
                    TT("dve", kkb[:], pK_[:, :, :], bc(b4), ALU.mult, r=[("gb", b)], w=[kK, "kkb"])
                    TT("dve", Lb[0][:], kkb[:], DL[:], ALU.mult, r=["kkb", "DL"], w=[("Lb", 0)])
                    yield
                    pQ_, kQ = nextg()
                    for h in range(4):
                        MM(pQ_[:, h, :], kT(h), qT(h), True, True, r=[("qkvb", b)], w=[kQ])
                    TT("dve", qkTm[:], pQ_[:, :, :], DQm[:], ALU.mult, r=["DQm"], w=[kQ, "qkTm"])
                    pM_, kM = nexttb()
                    for h in range(4):
                        TR(pM_[:, h, :], Lb[0][:, h, :], ident[:, :], r=[("Lb", 0), "ident"], w=[kM])
                    CP("act", Mb[0][:], pM_[:, :, :], r=[], w=[kM, ("Mb", 0)])
                    yield
                    STT("dve", Z[:], Mb[0][:], -1.0, identb4, ALU.mult, ALU.add, r=[("Mb", 0), "ident"], w=["Z"])
                    cur = 0
                    for lvl in range(5):
                        nxt = 1 - cur
                        pP_, kP = nextg()
                        for h in range(4):
                            MM(pP_[:, h, :], Mb[cur][:, h, :], Lb[cur][:, h, :], True, True, r=[("Mb", cur), ("Lb", cur)], w=[kP])
                        CP("act", Lb[nxt][:], pP_[:, :, :], r=[], w=[kP, ("Lb", nxt)])
                        yield
                        if lvl < 4:
                            pM2, kM2 = nextg()
                            for h in range(4):
                                MM(pM2[:, h, :], Lb[cur][:, h, :], Mb[cur][:, h, :], True, True, r=[("Mb", cur), ("Lb", cur)], w=[kM2])
                            CP("dve", Mb[nxt][:], pM2[:, :, :], r=[], w=[kM2, ("Mb", nxt)])
                        pZ_, kZ = nextg()
                        for h in range(4):
                            MM(pZ_[:, h, :], Lb[nxt][:, h, :], Z[:, h, :], True, True, r=[("Lb", nxt), "Z"], w=[kZ])
                        TT("dve", Z[:], Z[:], pZ_[:, :, :], ALU.add, r=["Z"], w=[kZ, "Z"])
                        yield
                        cur = nxt
                    pT1, kT1 = nexttb()
                    for h in range(4):
                        TR(pT1[:, h, :], kT(h), ident[:, :], r=[("qkvb", b), "ident"], w=[kT1])
                    TT("dve", kbe[:], pT1[:, :, :], bc(be[:]), ALU.mult, r=["be"], w=[kT1, "kbe"])
                    TT("dve", kdec[:], pT1[:, :, :], bc(edk[:]), ALU.mult, r=["edk"], w=[kT1, "kdec"])
                    pT2, kT2 = nexttb()
                    for h in range(4):
                        TR(pT2[:, h, :], vT(h), ident[:, :], r=[("qkvb", b), "ident"], w=[kT2])
                    TT("dve", vbt[:], pT2[:, :, :], bc(b4), ALU.mult, r=[("gb", b)], w=[kT2, "vbt"])
                    yield
                    pU_, kU = nextg()
                    for h in range(4):
                        MM(pU_[:, h, :], Z[:, h, :], vbt[:, h, :], True, True, r=["Z", "vbt"], w=[kU])
                    CP("act", um[:], pU_[:, :, :], r=[], w=[kU, "um"])
                    pW_, kW = nextg()
                    for h in range(4):
                        MM(pW_[:, h, :], kbe[:, h, :], Z[:, h, :], True, True, r=["Z", "kbe"], w=[kW])
                    CP("act", wTm[:], pW_[:, :, :], r=[], w=[kW, "wTm"])
                    TT("pool", qdTm[:], qkvb[b][:, 0:4, :], egcr[:], ALU.mult, r=[("qkvb", b), "egcr"], w=["qdTm"])
                    yield
                    carry = None
                    if dr == 0 and ti % 16 == 0 and ti > 0:
                        carry = dnsc[:, 4 * ti + 2:4 * ti + 3]
                    if dr == 1 and ti % 16 == 15 and ti < NT - 1:
                        carry = dnsc[:, 4 * ti + 3:4 * ti + 4]
                    if carry is not None:
                        TS("dve", S[:], S[:], carry, None, ALU.mult, None, r=["S", "dnsc"], w=["S"])
                        CP("act", Sb[:], S[:], r=["S"], w=["Sb"])
                    io = n_ % 2
                    for c in ((0, 1) if dr == 0 else (1, 0)):
                        cs = slice(64 * c, 64 * c + 64)
                        pVN, kVN = nextg()
                        for h in range(4):
                            MM(pVN[cs, h, :], wTm[:, h, cs], Sb[:, h, :], True, True, r=["wTm", "Sb"], w=[kVN])
                        TT("dve", vn[cs, :, :], um[cs, :, :], pVN[cs, :, :], ALU.subtract, r=["um"], w=[kVN, "vn"])
                        yield
                        pO, kO = nextg()
                        for h in range(4):
                            MM(pO[cs, h, :], qdTm[:, h, cs], Sb[:, h, :], True, False, r=["qdTm", "Sb"], w=[kO])
                            MM(pO[cs, h, :], qkTm[cs, h, cs], vn[cs, h, :], False, True, r=["qkTm", "vn"], w=[kO])
                        pSn, kSn = nextg()
                        for h in range(4):
                            MM(pSn[:, h, :], kdec[cs, h, :], vn[cs, h, :], True, True, r=["kdec", "vn"], w=[kSn])
                        for h in range(4):
                            STT("dve", S[:, h, :], S[:, h, :], egl[:, 4 * c + h:4 * c + h + 1], pSn[:, h, :], ALU.mult, ALU.add,
                                r=["S", "egl"], w=[kSn, "S"])
                        CP("act", Sb[:], S[:], r=["S"], w=["Sb"])
                        CP("act", ot[io][cs, :, :], pO[cs, :, :], r=[], w=[kO, ("ot", io)])
                        yield
                    DMA("sp", OD_d[dr][t0:t0 + 128, :], ot[io][:].rearrange("p h d -> p (h d)"), r=[("ot", io)], w=[("OD_d", dr, ti)])
                    yield

        def pass_dn_both(dirs):
            P.shared = {"eps", "one", "ident", "GB_d", "DQ_d", "OD_d", "DN_d"}
            with contextlib.ExitStack() as es:
                gens = [(dr, pass_dn_gen(dr, es)) for dr in dirs]
                while gens:
                    for item in list(gens):
                        P.ns = ("dn", item[0])
                        try:
                            next(item[1])
                        except StopIteration:
                            gens.remove(item)
                P.ns = None
            P.barrier()

        def pass5():
            with contextlib.ExitStack() as es:
                def sb(name, shape, dt=F32):
                    return es.enter_context(nc.sbuf_tensor("p5_" + name, list(shape), dt))

                def ps(name, shape, dt=F32):
                    return es.enter_context(nc.psum_tensor("p5_" + name, list(shape), dt))
                wout = sb("wout", [128, 8, D], BF16)
                dnog = sb("dnog", [128, 512])
                DMA("sp", dnog[:], dnog_d[:, :], w=["dnog"])
                for kc in range(8):
                    DMA("pool", wout[:, kc, :], w_out_d[kc * 128:(kc + 1) * 128, :], w=[("wout", kc)])
                NB = 2
                of_ = [sb(f"of{i}", [128, 512]) for i in range(NB)]
                ob_ = [sb(f"ob{i}", [128, 512]) for i in range(NB)]
                gt = [sb(f"gt{i}", [128, 512]) for i in range(NB)]
                xt = [sb(f"xt{i}", [128, D]) for i in range(NB)]
                at = [sb(f"at{i}", [128, 4, 128], BF16) for i in range(NB)]
                osum = sb("osum", [128, 512]); sq = sb("sq", [128, 512]); ss = sb("ss", [128, 4])
                gg = sb("gg", [128, 512]); on = sb("on", [128, 512]); odn = sb("odn", [128, 512], BF16)
                eg = sb("eg", [128, 512])
                dnT = sb("dnT", [128, 4, 128], BF16)
                ht = [sb(f"ht{i}", [128, D]) for i in range(2)]
                pT = ps("pT", [128, 4, 128], BF16)
                pH = [ps(f"pH{i}", [128, 512]) for i in range(2)]
                for ti in range(NT):
                    b = ti % NB
                    t0 = ti * 128
                    u = ti // 4
                    DMA("sp", of_[b][:], OD_d[0][t0:t0 + 128, :], r=[("OD_d", 0, ti)], w=[("of", b)])
                    DMA("sp", ob_[b][:], OD_d[1][t0:t0 + 128, :], r=[("OD_d", 1, ti)], w=[("ob", b)])
                    DMA("sp", gt[b][:], G_d[t0:t0 + 128, :], r=[("G_d", u)], w=[("gt", b)])
                    DMA("sp", xt[b][:], xs[t0:t0 + 128, :], w=[("xt5", b)])
                    DMA("sp", at[b][:], AT_d[:, :, t0:t0 + 128].rearrange("j p w -> p j w"), r=[("AT_d", ti // 16)], w=[("at", b)])
                    TT("dve", osum[:], of_[b][:], ob_[b][:], ALU.add, r=[("of", b), ("ob", b)], w=["osum"])
                    ACTV(sq[:], osum[:], AF.Square, r=["osum"], w=["sq5"])
                    RED("dve", ss[:], sq[:].rearrange("p (h d) -> p h d", d=128), r=["sq5"], w=["ss5"])
                    ACTV(ss[:], ss[:], AF.Ln, r=["ss5", "eps"], w=["ss5"], scale=1.0 / 128, bias=epsT[:])
                    ACTV(ss[:], ss[:], AF.Exp, r=["ss5"], w=["ss5"], scale=-0.5)
                    ACTV(eg[:], gt[b][:], AF.Exp, r=[("gt", b)], w=["eg"], scale=-1.0)
                    TS("dve", eg[:], eg[:], 1.0, None, ALU.add, None, r=["eg"], w=["eg"])
                    TT("pool", gg[:], gt[b][:], dnog[:], ALU.mult, r=[("gt", b), "dnog"], w=["gg"])
                    RECIP(eg[:], eg[:], r=["eg"], w=["eg"])
                    TT("dve", gg[:], gg[:], eg[:], ALU.mult, r=["gg", "eg"], w=["gg"])
                    TT("dve", on[:].rearrange("p (h d) -> p h d", d=128), osum[:].rearrange("p (h d) -> p h d", d=128),
                       ss[:].unsqueeze(2).to_broadcast([128, 4, 128]), ALU.mult, r=["osum", "ss5"], w=["on"])
                    TT("dve", odn[:], on[:], gg[:], ALU.mult, r=["on", "gg"], w=["odn"])
                    for j in range(4):
                        TR(pT[:, j, :], odn[:, j * 128:(j + 1) * 128], ident[:, :], r=["odn", "ident"], w=["p5T"])
                    CP("act", dnT[:], pT[:, :, :], r=[], w=["p5T", "dnT"])
                    ih = ti % 2
                    for nh in range(2):
                        csl = slice(512 * nh, 512 * (nh + 1))
                        for j in range(4):
                            MM(pH[nh][:, :], at[b][:, j, :], wout[:, j, csl], j == 0, False, r=[("at", b), ("wout", j)], w=[("p5H", nh)])
                        for j in range(4):
                            MM(pH[nh][:, :], dnT[:, j, :], wout[:, 4 + j, csl], False, j == 3, r=["dnT", ("wout", 4 + j)], w=[("p5H", nh)])
                        TT("dve", ht[ih][:, csl], pH[nh][:, :], xt[b][:, csl], ALU.add, r=[("xt5", b)], w=[("p5H", nh), ("ht5", ih)])
                    DMA("sp", H_d[t0:t0 + 128, :], ht[ih][:], r=[("ht5", ih)], w=[("H_d", u)])
            P.barrier()

        def pass6(hbuf, hkeys_fn):
            with contextlib.ExitStack() as es:
                def sb(name, shape, dt=F32):
                    return es.enter_context(nc.sbuf_tensor("p6_" + name, list(shape), dt))

                def ps(name, shape, dt=F32):
                    return es.enter_context(nc.psum_tensor("p6_" + name, list(shape), dt))
                wup = sb("wup", [128, 8, 2 * FFN], BF16)
                wdn = sb("wdn", [128, NFC, D], BF16)
                g2 = sb("g2", [128, 8])
                cw = sb("cw", [128, 3, 44])
                cb = sb("cb", [128, 44])
                hsc = sb("hsc", [2, NU])
                DMA("sp", g2[:], norm2_d[:, :], w=["g2"])
                DMA("sp", cw[:], cw_d[:, :, :], w=["cw"])
                DMA("sp", cb[:], cb_d[:, :], w=["cb"])
                DMA("sp", hsc[:], halo_sc_d[:, :], w=["hsc"])
                for kc in range(8):
                    DMA("pool", wup[:, kc, :], w_up_d[kc * 128:(kc + 1) * 128, :], w=[("wup", kc)])
                    TS("dve", wup[:, kc, :], wup[:, kc, :], g2[:, kc:kc + 1], None, ALU.mult, None, r=["g2", ("wup", kc)], w=[("wup", kc)])
                for j in range(NFC):
                    DMA("pool", wdn[:, j, :], w_down_d[j * 128:(j + 1) * 128, :], w=[("wdn", j)])
                ht = sb("ht", [128, 4, D])
                hh = sb("hh", [2, D])
                hn = sb("hn", [128, D], BF16)
                hhn = sb("hhn", [2, D], BF16)
                sq = sb("sq", [128, D])
                ms = [sb(f"ms{i}", [128, 1]) for i in range(2)]
                hnT = sb("hnT", [128, 8, 514], BF16)
                actT = sb("actT", [128, NFC, 512], BF16)
                accg = [sb(f"accg{i}", [128, 256]) for i in range(2)]
                accu = [sb(f"accu{i}", [128, 256]) for i in range(2)]
                sg = [sb(f"sg{i}", [128, 256]) for i in range(2)]
                yt = [sb(f"yt{i}", [128, D]) for i in range(2)]
                pT = [ps(f"pT{i}", [128, 8, 128], BF16) for i in range(2)]
                pU = [ps(f"pU{i}", [128, 512]) for i in range(4)]
                pD = [ps(f"pD{i}", [128, 512]) for i in range(2)]
                ucount = [0]
                dcount = [0]
                for u in range(NU):
                    t0 = u * 512
                    hk = hkeys_fn(u)
                    DMA("sp", ht[:], hbuf[t0:t0 + 512, :].rearrange("(t p) d -> p t d", p=128), r=hk, w=["ht"])
                    lo = max(t0 - 1, 0)
                    hi = min(t0 + 512, NTOK - 1)
                    DMA("sp", hh[0:1, :], hbuf[lo:lo + 1, :], r=hkeys_fn(max(u - 1, 0)), w=["hh"])
                    DMA("sp", hh[1:2, :], hbuf[hi:hi + 1, :], r=hkeys_fn(min(u + 1, NU - 1)), w=["hh"])

                    def dst_halo():
                        CP("dve", hnT[:, :, 0:514:513], pT[1][:, :, 0:2], r=[], w=["pT1", "hnT"])
                    rmsnorm_T(None, hh[:, :], 2, ms[1], hhn, sq, pT[1], "pT1", dst_halo, ["hh"], "p6h",
                              scale_ap=hsc[:, u:u + 1], scale_key="hsc")
                    for t in range(4):
                        def dst_main(t=t):
                            CP("act", hnT[:, :, 1 + 128 * t:1 + 128 * (t + 1)], pT[0][:, :, :], r=[], w=["pT0", "hnT"])
                        rmsnorm_T(None, ht[:, t, :], 128, ms[0], hn, sq, pT[0], "pT0", dst_main, ["ht"], "p6m")
                    for j in range(NFC):
                        for wdw in range(2):
                            ig = (ucount[0] * 2) % 4
                            iu = (ucount[0] * 2 + 1) % 4
                            ia = ucount[0] % 2
                            ucount[0] += 1
                            for (ip, fc) in ((ig, j), (iu, NFC + j)):
                                for kc in range(8):
                                    MM(pU[ip][:, 0:258], wup[:, kc, fc * 128:(fc + 1) * 128], hnT[:, kc, 256 * wdw:256 * wdw + 258],
                                       kc == 0, kc == 7, r=[("wup", kc), "hnT"], w=[("pU", ip)])
                            for (ip, fc, acc_, ka) in ((ig, j, accg[ia], ("accg", ia)), (iu, NFC + j, accu[ia], ("accu", ia))):
                                ACTV(acc_[:], pU[ip][:, 1:257], AF.Identity, r=["cw", "cb"], w=[("pU", ip), ka],
                                     scale=cw[:, 1, fc:fc + 1], bias=cb[:, fc:fc + 1])
                                STT("dve", acc_[:], pU[ip][:, 0:256], cw[:, 0, fc:fc + 1], acc_[:], ALU.mult, ALU.add,
                                    r=["cw", ka], w=[("pU", ip), ka])
                                STT("dve", acc_[:], pU[ip][:, 2:258], cw[:, 2, fc:fc + 1], acc_[:], ALU.mult, ALU.add,
                                    r=["cw", ka], w=[("pU", ip), ka])
                            ACTV(sg[ia][:], accg[ia][:], AF.Silu, r=[("accg", ia)], w=[("sg", ia)])
                            TT("pool", actT[:, j, 256 * wdw:256 * (wdw + 1)], sg[ia][:], accu[ia][:], ALU.mult,
                               r=[("sg", ia), ("accu", ia)], w=["actT"])
                    for t in range(4):
                        iy = dcount[0] % 2
                        for nh in range(2):
                            ipd = dcount[0] % 2
                            dcount[0] += 1
                            for j in range(NFC):
                                MM(pD[ipd][:, :], actT[:, j, 128 * t:128 * (t + 1)], wdn[:, j, 512 * nh:512 * (nh + 1)],
                                   j == 0, j == NFC - 1, r=["actT", ("wdn", j)], w=[("pD", ipd)])
                            TT("dve", yt[iy][:, 512 * nh:512 * (nh + 1)], pD[ipd][:, :], ht[:, t, 512 * nh:512 * (nh + 1)], ALU.add,
                               r=["ht"], w=[("pD", ipd), ("yt", iy)])
                        DMA("sp", ys[t0 + 128 * t:t0 + 128 * (t + 1), :], yt[iy][:], r=[("yt", iy)])
            P.barrier()

        if "1" in passes:
            pass1()
        if "2" in passes:
            pass2()
        if "P" in passes:
            pass_prep()
        dirs = [dr for dr, nm in ((0, "3"), (1, "4")) if nm in passes]
        if dirs:
            pass_dn_both(dirs)
        if "5" in passes:
            pass5()
        if "6" in passes:
            if "5" in passes:
                pass6(H_d, lambda u: [("H_d", u)])
            else:
                pass6(xs, lambda u: [])
        P.emit()
    return nc


def _dn_consts():
    NEG = -30000.0
    p = np.arange(128)[:, None]
    f = np.arange(128)[None, :]
    same = (p // 64) == (f // 64)
    c = np.zeros((128, 11, 128), np.float32)
    c[:, 0] = same & (p <= f)
    c[:, 1] = same & (p >= f)
    c[:, 2] = 1.0
    c[:, 3] = np.where(same & (f < p), 0.0, NEG)
    c[:, 4] = np.where(same & (f > p), 0.0, NEG)
    c[:, 5] = np.where(same & (f >= p), 0.0, NEG)
    c[:, 6] = np.where(same & (f <= p), 0.0, NEG)
    c[:, 7] = np.eye(128)
    c[:, 8] = (p // 64 == 0) & (f >= 0)
    c[:, 9] = (p // 64 == 1) & (f >= 0)
    c[:, 10] = same
    return c


def _dn_scales(NSEG, link):
    NT = NSEG * SEG // 128
    sc = np.ones((128, 4 * NT), np.float32)
    for ti in range(NT):
        s = ti // 16
        if ti % 16 == 0:
            sc[:, 4 * ti + 0] = link[s]
            sc[:, 4 * ti + 2] = link[s]
        if ti % 16 == 15:
            sc[:, 4 * ti + 1] = link[s + 1]
            sc[:, 4 * ti + 3] = link[s + 1]
    return sc


def host_consts(NSEG, link, pos0):
    NTOK = NSEG * SEG
    NU = NTOK // 512
    NT = NTOK // 128
    hs = np.ones((2, NU), np.float32)
    for u in range(NU):
        t0 = u * 512
        if t0 % SEG == 0:
            hs[0, u] = link[t0 // SEG]
        if (t0 + 512) % SEG == 0:
            hs[1, u] = link[(t0 + 512) // SEG]
    lv = np.ones((128, 2 * NSEG), np.float32)
    for s in range(NSEG):
        lv[0:64, 2 * s] = link[s]
        lv[64:128, 2 * s + 1] = link[s + 1]
    r_ = np.arange(128)[:, None]
    q_ = np.arange(128)[None, :]
    amask = np.concatenate([(q_ <= r_), (q_ >= r_)], axis=1).astype(np.float32)
    half = 32
    inv_freq = (1.0 / (10000.0 ** (np.arange(half, dtype=np.float32) * 2.0 / 64))).astype(np.float32)
    invf = np.tile(np.concatenate([inv_freq, inv_freq])[None, :], (128, 1)).astype(np.float32)
    phase = np.tile(np.concatenate([np.zeros(32), np.full(32, np.pi / 2)])[None, :], (128, 1)).astype(np.float32)
    pos = np.zeros((128, NT), np.float32)
    for s in range(NSEG):
        for t in range(16):
            pos[:, s * 16 + t] = pos0[s] + t * 128 + np.arange(128)
    return {"ident": np.eye(128, dtype=np.float32), "halo_sc": hs, "lv": lv, "amask": amask, "invf": invf,
            "phase": phase, "pos": pos,
            "dnc": _dn_consts(), "dnsc": _dn_scales(NSEG, link)}


def weight_maps(norm1, w_in, att_q_norm, att_k_norm, dn_conv_w, dn_a_log, dn_dt_bias, dn_out_norm, w_out, norm2,
                w_up, ffn_conv_w, ffn_conv_b, w_down):
    f = lambda a: np.asarray(a, np.float32)[0]
    qg = np.concatenate([np.tile(f(att_q_norm), 8), np.tile(f(att_k_norm), 8)])
    return {
        "w_in": np.ascontiguousarray(f(w_in)),
        "w_out": np.ascontiguousarray(f(w_out)),
        "w_up": np.ascontiguousarray(f(w_up)),
        "w_down": np.ascontiguousarray(f(w_down)),
        "norm1": np.ascontiguousarray(f(norm1).reshape(8, 128).T),
        "norm2": np.ascontiguousarray(f(norm2).reshape(8, 128).T),
        "qkg": np.ascontiguousarray(np.tile(qg[None, :], (128, 1))),
        "alog": np.ascontiguousarray(np.tile(f(dn_a_log).reshape(1, 8), (128, 1))),
        "dtb": np.ascontiguousarray(np.tile(f(dn_dt_bias).reshape(1, 8), (128, 1))),
        "dncw": np.ascontiguousarray(f(dn_conv_w).reshape(5, 12, 128).transpose(2, 0, 1)),
        "dnog": np.ascontiguousarray(np.tile(f(dn_out_norm)[None, :], (128, 4))),
        "ffn_cw": np.ascontiguousarray(f(ffn_conv_w).reshape(3, 44, 128).transpose(2, 0, 1)),
        "ffn_cb": np.ascontiguousarray(f(ffn_conv_b).reshape(44, 128).T),
    }


SAMPLE_SLOTS = [6, 6, 5, 5, 5, 5]
_NC_CACHE = {}


def _core_layout():
    lay = []
    for b in range(2):
        lay.append([("p", b, s) for s in range(8)])
    nxt = 0
    for n in SAMPLE_SLOTS:
        row = []
        for i in range(8):
            if i < n:
                row.append(("s", nxt, 0))
                nxt += 1
            else:
                row.append(None)
        lay.append(row)
    return lay


def kernel(x_prompt, x_sample, norm1, w_in, att_q_norm, att_k_norm, dn_conv_w, dn_a_log, dn_dt_bias,
           dn_out_norm, w_out, norm2, w_up, ffn_conv_w, ffn_conv_b, w_down):
    NSEG = 8
    lay = _core_layout()
    if "nc" not in _NC_CACHE:
        _NC_CACHE["nc"] = build_program(NSEG=NSEG)
    nc = _NC_CACHE["nc"]
    x_prompt = np.asarray(x_prompt, np.float32)
    x_sample = np.asarray(x_sample, np.float32)
    common = weight_maps(norm1, w_in, att_q_norm, att_k_norm, dn_conv_w, dn_a_log, dn_dt_bias, dn_out_norm, w_out,
                         norm2, w_up, ffn_conv_w, ffn_conv_b, w_down)
    in_maps = []
    for c in range(NCORES):
        xs = np.zeros((NSEG * SEG, D), np.float32)
        link = np.zeros(NSEG + 1, np.float32)
        pos0 = np.zeros(NSEG, np.float32)
        for i, ent in enumerate(lay[c]):
            if ent is None:
                continue
            kind, b, s = ent
            if kind == "p":
                xs[i * SEG:(i + 1) * SEG] = x_prompt[b, s * SEG:(s + 1) * SEG]
                pos0[i] = s * SEG
                if s > 0:
                    link[i] = 1.0
            else:
                xs[i * SEG:(i + 1) * SEG] = x_sample[b]
        m = dict(common)
        m.update(host_consts(NSEG, link, pos0))
        m["xs"] = xs
        in_maps.append(m)
    res = run_bass_kernel_spmd(nc, in_maps, core_ids=list(range(NCORES)))
    y_prompt = np.zeros_like(x_prompt)
    y_sample = np.zeros_like(x_sample)
    for c in range(NCORES):
        ys = res.results[c]["ys"]
        for i, ent in enumerate(lay[c]):
            if ent is None:
                continue
            kind, b, s = ent
            if kind == "p":
                y_prompt[b, s * SEG:(s + 1) * SEG] = ys[i * SEG:(i + 1) * SEG]
            else:
                y_sample[b] = ys[i * SEG:(i + 1) * SEG]
    return (y_prompt, y_sample)
```

```python
import numpy as np
import concourse.bass as bass
import concourse.mybir as mybir
from concourse.bass_utils import run_bass_kernel_spmd

F32 = mybir.dt.float32
BF16 = mybir.dt.bfloat16
I32 = mybir.dt.int32
AF = mybir.ActivationFunctionType
ALU = mybir.AluOpType
AX = mybir.AxisListType

D = 1024
FFN = 2816
NFC = FFN // 128
SEG = 2048
NCORES = 8
EPS = 1e-6


class _Op:
    __slots__ = ("eng", "fn", "deps", "sig", "is_dma", "dsem", "dval", "need_sig", "idx")


class Prog:
    ENGS = ("sp", "act", "dve", "pool", "pe")

    def __init__(self, nc, n_dma_sems=8):
        self.nc = nc
        self.ops = []
        self.last_w = {}
        self.readers = {}
        self.n_dma_sems = n_dma_sems
        self.dma_rr = {"sp": 0, "pool": 0, "act": 0}
        self.dma_last = {}
        self.dma_cnt = {}
        self.extra = {}
        self.ns = None
        self.shared = set()
        self.last_op = {}

    def _k(self, k):
        if self.ns is None or k in self.shared or (isinstance(k, tuple) and k[0] in self.shared):
            return k
        return (self.ns, k)

    def op(self, eng, fn, r=(), w=(), dma=False):
        if self.ns is not None:
            r = [self._k(k) for k in r]
            w = [self._k(k) for k in w]
        o = _Op()
        o.eng = eng; o.fn = fn; o.is_dma = dma; o.need_sig = dma; o.sig = None
        o.idx = len(self.ops)
        deps = list(self.extra.pop(eng, []))
        for k in r:
            p = self.last_w.get(k)
            if p is not None:
                deps.append(p)
        for k in w:
            p = self.last_w.get(k)
            if p is not None:
                deps.append(p)
            for q in self.readers.get(k, ()):
                deps.append(q)
        if dma:
            j = self.dma_rr[eng]
            self.dma_rr[eng] = (j + 1) % self.n_dma_sems
            key = (eng, j)
            prev = self.dma_last.get(key)
            if prev is not None:
                deps.append(prev)
            self.dma_last[key] = o
            self.dma_cnt[key] = self.dma_cnt.get(key, 0) + 1
            o.dsem = key
            o.dval = 16 * self.dma_cnt[key]
        dd = []
        seen = set()
        for p in deps:
            if p is o or id(p) in seen:
                continue
            if eng == "pe" and p.eng == "pe" and not p.is_dma:
                continue
            seen.add(id(p))
            dd.append(p)
            p.need_sig = True
        o.deps = dd
        for k in w:
            self.last_w[k] = o
            self.readers[k] = []
        for k in r:
            self.readers.setdefault(k, []).append(o)
        self.ops.append(o)
        if not dma:
            self.last_op[eng] = o
        return o

    def barrier(self):
        markers = [o for o in self.last_op.values()] + [o for o in self.dma_last.values()]
        for e in self.ENGS:
            self.extra[e] = list(markers) + self.extra.get(e, [])

    def emit(self, final_wait_eng="sp"):
        nc = self.nc
        cnt = {e: 0 for e in self.ENGS}
        for o in self.ops:
            if o.is_dma:
                o.sig = (("dma",) + o.dsem, o.dval)
            elif o.need_sig:
                cnt[o.eng] += 1
                o.sig = (("eng", o.eng), cnt[o.eng])
        sem_keys = [("eng", e) for e in self.ENGS] + [("dma", q, j) for q in ("sp", "pool", "act") for j in range(self.n_dma_sems)]
        per_eng = {e: [o for o in self.ops if o.eng == e] for e in self.ENGS}
        finals = {}
        for o in self.ops:
            if o.is_dma:
                finals[o.sig[0]] = max(finals.get(o.sig[0], 0), o.sig[1])
        import contextlib
        with contextlib.ExitStack() as st:
            sems = {}
            for k in sem_keys:
                sems[k] = st.enter_context(nc.semaphore("s_" + "_".join(str(x) for x in k)))
            block = st.enter_context(nc.Block())

            def replay(e, eobj):
                known = {}
                for o in per_eng[e]:
                    need = {}
                    for p in o.deps:
                        sk, v = p.sig
                        if v > need.get(sk, 0):
                            need[sk] = v
                    for sk, v in need.items():
                        if known.get(sk, 0) < v:
                            eobj.wait_ge(sems[sk], v)
                            known[sk] = v
                    ins = o.fn(eobj)
                    if o.sig is not None:
                        ins.then_inc(sems[o.sig[0]], 16 if o.is_dma else 1)
                if e == final_wait_eng:
                    for sk, v in finals.items():
                        if known.get(sk, 0) < v:
                            eobj.wait_ge(sems[sk], v)

            @block.sync
            def _(e):
                replay("sp", e)

            @block.scalar
            def _(e):
                replay("act", e)

            @block.vector
            def _(e):
                replay("dve", e)

            @block.gpsimd
            def _(e):
                replay("pool", e)

            @block.tensor
            def _(e):
                replay("pe", e)


import contextlib

IN_COLS = 3600
KPAD = 1024
DQPAD = 2
TWO_PI = float(2 * np.pi)


def _sl(start, count, step):
    return slice(start, start + (count - 1) * step + 1, step)


def build_program(NSEG=8, passes=("1", "2", "P", "3", "4", "5", "6"), dbg=False):
    nc = bass.Bass("TRN2", target_bir_lowering=False)
    NTOK = NSEG * SEG
    NT = NTOK // 128
    NU = NTOK // 512
    P = Prog(nc)

    def din(name, shape, dt=F32):
        return nc.dram_tensor(name, list(shape), dt, kind="ExternalInput").ap()

    def dout(name, shape, dt=F32):
        return nc.dram_tensor(name, list(shape), dt, kind="ExternalOutput").ap()

    def dscr(name, shape, dt=F32):
        return nc.dram_tensor(name, list(shape), dt, kind=("ExternalOutput" if dbg else "Internal")).ap()

    xs = din("xs", [NTOK, D])
    ident_d = din("ident", [128, 128])
    w_in_d = din("w_in", [D, IN_COLS])
    w_out_d = din("w_out", [D, D])
    w_up_d = din("w_up", [D, 2 * FFN])
    w_down_d = din("w_down", [FFN, D])
    norm1_d = din("norm1", [128, 8])
    norm2_d = din("norm2", [128, 8])
    qkg_d = din("qkg", [128, 1024])
    invf_d = din("invf", [128, 64])
    phase_d = din("phase", [128, 64])
    pos_d = din("pos", [128, NT])
    alog_d = din("alog", [128, 8])
    dtb_d = din("dtb", [128, 8])
    dncw_d = din("dncw", [128, 5, 12])
    dnog_d = din("dnog", [128, 512])
    cw_d = din("ffn_cw", [128, 3, 44])
    cb_d = din("ffn_cb", [128, 44])
    halo_sc_d = din("halo_sc", [2, NU])
    lv_d = din("lv", [128, 2 * NSEG])
    amask_d = din("amask", [128, 256])
    dnc_d = din("dnc", [128, 11, 128])
    dnsc_d = din("dnsc", [128, 4 * NT])
    ys = dout("ys", [NTOK, D])
    KW = NTOK + 2 * KPAD
    QT_d = dscr("QT_s", [4, 128, KW], BF16)
    KT_d = dscr("KT_s", [4, 128, KW], BF16)
    V_d = dscr("V_s", [KW, 512], BF16)
    DQ_d = dscr("DQ_s", [1536, NTOK + 2 * DQPAD], BF16)
    DN_d = dscr("DN_s", [1536, NTOK], BF16)
    G_d = dscr("G_s", [NTOK, 512], F32)
    GB_d = dscr("GB_s", [NTOK, 16], F32)
    AT_d = dscr("AT_s", [4, 128, NTOK], BF16)
    OD_d = [dscr("OF_s", [NTOK, 512], F32), dscr("OB_s", [NTOK, 512], F32)]
    H_d = dscr("H_s", [NTOK, D], F32)

    def MM(out, lhsT, rhs, start, stop, r, w):
        P.op("pe", lambda e, o=out, l=lhsT, rr=rhs, s=start, t=stop: e.matmul(o, lhsT=l, rhs=rr, start=s, stop=t), r=r, w=w)

    def TR(out, in_, idn, r, w):
        P.op("pe", lambda e, o=out, i=in_, d=idn: e.transpose(out=o, in_=i, identity=d), r=r, w=w)

    def ACTV(out, in_, func, r, w, **kw):
        P.op("act", lambda e, o=out, i=in_, f=func, kw=kw: e.activation(out=o, in_=i, func=f, **kw), r=r, w=w)

    def TT(eng, out, in0, in1, op, r, w):
        P.op(eng, lambda e, o=out, a=in0, b=in1, p=op: e.tensor_tensor(out=o, in0=a, in1=b, op=p), r=r, w=w)

    def TS(eng, out, in0, s1, s2, op0, op1, r, w):
        if s2 is None:
            P.op(eng, lambda e, o=out, a=in0, x=s1, p0=op0: e.tensor_scalar(out=o, in0=a, scalar1=x, scalar2=None, op0=p0), r=r, w=w)
        else:
            P.op(eng, lambda e, o=out, a=in0, x=s1, y=s2, p0=op0, p1=op1: e.tensor_scalar(out=o, in0=a, scalar1=x, scalar2=y, op0=p0, op1=p1), r=r, w=w)

    def STT(eng, out, in0, scalar, in1, op0, op1, r, w):
        P.op(eng, lambda e, o=out, a=in0, sc=scalar, b=in1, p0=op0, p1=op1: e.scalar_tensor_tensor(out=o, in0=a, scalar=sc, in1=b, op0=p0, op1=p1), r=r, w=w)

    def CP(eng, out, in_, r, w):
        if eng == "act":
            P.op("act", lambda e, o=out, i=in_: e.copy(out=o, in_=i), r=r, w=w)
        else:
            P.op(eng, lambda e, o=out, i=in_: e.tensor_copy(out=o, in_=i), r=r, w=w)

    def MSET(eng, ap, val, w):
        P.op(eng, lambda e, a=ap, v=val: e.memset(a, v), w=w)

    def DMA(q, out, in_, r=(), w=()):
        P.op(q, lambda e, o=out, i=in_: e.dma_start(out=o, in_=i), r=r, w=w, dma=True)

    def RED(eng, out, in_, r, w):
        P.op(eng, lambda e, o=out, i=in_: e.tensor_reduce(out=o, in_=i, axis=AX.X, op=ALU.add), r=r, w=w)

    def RECIP(out, in_, r, w):
        P.op("dve", lambda e, o=out, i=in_: e.reciprocal(out=o, in_=i), r=r, w=w)

    with contextlib.ExitStack() as es0:
        def sb0(name, shape, dt=F32):
            return es0.enter_context(nc.sbuf_tensor(name, list(shape), dt))
        ident_f = sb0("ident_f", [128, 128])
        ident = sb0("ident_b", [128, 128], BF16)
        epsT = sb0("epsT", [128, 1])
        oneT = sb0("oneT", [128, 1])
        DMA("sp", ident_f[:], ident_d[:, :], w=["ident_f"])
        CP("dve", ident[:], ident_f[:], r=["ident_f"], w=["ident"])
        MSET("dve", epsT[:], EPS, w=["eps"])
        MSET("dve", oneT[:], 1.0, w=["one"])

        def rmsnorm_T(es_sb, src_ap, npart, msT, hnb, sq, pTt, pkey, dst_fn, rkeys, keyp, scale_ap=None, scale_key=None, junk_key="sq",
                      defer=None):
            ACTV(sq[0:npart, :], src_ap, AF.Square, r=rkeys, w=[junk_key, keyp + "ms"], scale=1.0 / 32, accum_out=msT[0:npart, :])
            ACTV(msT[0:npart, :], msT[0:npart, :], AF.Ln, r=[keyp + "ms", "eps"], w=[keyp + "ms"], bias=epsT[0:npart, :])
            ACTV(msT[0:npart, :], msT[0:npart, :], AF.Exp, r=[keyp + "ms"], w=[keyp + "ms"], scale=-0.5)
            if scale_ap is not None:
                TT("dve", msT[0:npart, :], msT[0:npart, :], scale_ap, ALU.mult, r=[keyp + "ms", scale_key], w=[keyp + "ms"])
            TS("dve", hnb[0:npart, :], src_ap, msT[0:npart, 0:1], None, ALU.mult, None, r=rkeys + [keyp + "ms"], w=[keyp + "hn"])
            def second():
                for kc in range(8):
                    TR(pTt[:, kc, 0:npart], hnb[0:npart, kc * 128:(kc + 1) * 128], ident[0:npart, 0:npart],
                       r=[keyp + "hn", "ident"], w=[pkey])
                dst_fn()
            if defer is None:
                second()
            else:
                defer.append(second)

        def pass1():
            with contextlib.ExitStack() as es:
                def sb(name, shape, dt=F32):
                    return es.enter_context(nc.sbuf_tensor("p1_" + name, list(shape), dt))

                def ps(name, shape, dt=F32):
                    return es.enter_context(nc.psum_tensor("p1_" + name, list(shape), dt))
                win = sb("win", [128, 8, IN_COLS], BF16)
                g1 = sb("g1", [128, 8])
                qkg = sb("qkg", [128, 1024])
                invf = sb("invf", [128, 64])
                phase = sb("phase", [128, 64])
                post = sb("post", [128, NT])
                nexpA = sb("nexpA", [128, 8])
                dtb = sb("dtb", [128, 8])
                zb = sb("zb", [128, 2, 512], BF16)
                zf = sb("zf", [128, 12, DQPAD], BF16)
                DMA("sp", g1[:], norm1_d[:, :], w=["g1"])
                DMA("sp", qkg[:], qkg_d[:, :], w=["qkg"])
                DMA("sp", invf[:], invf_d[:, :], w=["invf"])
                DMA("sp", phase[:], phase_d[:, :], w=["phase"])
                DMA("sp", post[:], pos_d[:, :], w=["post"])
                DMA("sp", nexpA[:], alog_d[:, :], w=["nexpA"])
                DMA("sp", dtb[:], dtb_d[:, :], w=["dtb"])
                ACTV(nexpA[:], nexpA[:], AF.Exp, r=["nexpA"], w=["nexpA"])
                TS("dve", nexpA[:], nexpA[:], -1.0, None, ALU.mult, None, r=["nexpA"], w=["nexpA"])
                MSET("pool", zb[:], 0.0, w=["zb"])
                MSET("pool", zf[:], 0.0, w=["zf"])
                for j in range(4):
                    DMA("sp", KT_d[j, :, 0:KPAD], zb[:, 0:2, :].rearrange("p a b -> p (a b)"), r=["zb"], w=[("KT_d", "padL")])
                    DMA("sp", KT_d[j, :, KPAD + NTOK:KW], zb[:, 0:2, :].rearrange("p a b -> p (a b)"), r=["zb"], w=[("KT_d", "padR")])
                for q4 in range(4):
                    DMA("sp", V_d[256 * q4:256 * (q4 + 1), :].rearrange("(t p) c -> p t c", p=128), zb[:], r=["zb"], w=[("V_d", "padL")])
                    DMA("sp", V_d[KPAD + NTOK + 256 * q4:KPAD + NTOK + 256 * (q4 + 1), :].rearrange("(t p) c -> p t c", p=128), zb[:],
                        r=["zb"], w=[("V_d", "padR")])
                dqv = DQ_d.rearrange("(f p) w -> p f w", p=128)
                DMA("sp", dqv[:, :, 0:DQPAD], zf[:], r=["zf"], w=[("DQ_d", "padL")])
                DMA("sp", dqv[:, :, DQPAD + NTOK:DQPAD + NTOK + DQPAD], zf[:], r=["zf"], w=[("DQ_d", "padR")])
                for kc in range(8):
                    DMA("pool", win[:, kc, :], w_in_d[kc * 128:(kc + 1) * 128, :], w=[("win", kc)])
                    TS("dve", win[:, kc, :], win[:, kc, :], g1[:, kc:kc + 1], None, ALU.mult, None, r=["g1", ("win", kc)], w=[("win", kc)])
                xt = [sb(f"xt{i}", [128, 4, D]) for i in range(2)]
                nTs = [sb(f"nT{i}", [128, 8, 512], BF16) for i in range(2)]
                hn = sb("hn", [128, D], BF16)
                sq = sb("sq", [128, D])
                ms = sb("ms", [128, 1])
                dqs = sb("dqs", [128, 12, 512], BF16)
                vb = sb("vb", [128, 4, 512], BF16)
                gs = sb("gs", [128, 4, 512])
                gbt = sb("gbt", [128, 4, 16])
                zt = sb("zt", [128, 8])
                bt8 = sb("bt8", [128, 8])
                qraw = sb("qraw", [128, 1024])
                qn = sb("qn", [128, 1024])
                t1 = sb("t1", [128, 512]); t2 = sb("t2", [128, 512]); t3 = sb("t3", [128, 512]); t4 = sb("t4", [128, 512])
                qrs = [sb(f"qr{i}", [128, 1024], BF16) for i in range(2)]
                qkTs = [sb(f"qkT{i}", [128, 8, 512], BF16) for i in range(2)]
                ssq = sb("ssq", [128, 16])
                ang16 = sb("ang16", [128, 16, 64]); kfi16 = sb("kfi16", [128, 16, 64], I32); kff16 = sb("kff16", [128, 16, 64])
                cs16 = sb("cs16", [128, 16, 64])
                deferred = []
                pTT = ps("pTT", [128, 8, 128], BF16)
                pFs = [ps(f"pF{i}", [128, 512]) for i in range(2)]
                pQ = ps("pQ", [128, 512]); pK = ps("pK", [128, 512]); pV = ps("pV", [128, 512])
                pG = ps("pG", [128, 512]); pB = ps("pB", [128, 512])
                fcnt = [0]

                def norm_piece(u, t):
                    b = u % 2

                    def dst(t=t, b=b):
                        CP("act", nTs[b][:, :, 128 * t:128 * (t + 1)], pTT[:, :, :], r=[], w=["pTT", ("nT", b)])
                    rmsnorm_T(None, xt[b][:, t, :], 128, ms, hn, sq, pTT, "pTT", dst, [("xt", b)], "p1")
                def load_x(u):
                    DMA("sp", xt[u % 2][:], xs[u * 512:u * 512 + 512, :].rearrange("(t p) d -> p t d", p=128), w=[("xt", u % 2)])
                load_x(0)
                for t in range(4):
                    norm_piece(0, t)
                for u in range(NU):
                    b = u % 2
                    if u + 1 < NU:
                        load_x(u + 1)
                    nT = nTs[b]
                    nTk = ("nT", b)
                    t0 = u * 512
                    for fc in range(12):
                        c0 = 1536 + fc * 128
                        pF = pFs[fcnt[0] % 2]; pFk = ("pF", fcnt[0] % 2); fcnt[0] += 1
                        for kc in range(8):
                            MM(pF[:, :], win[:, kc, c0:c0 + 128], nT[:, kc, :], kc == 0, kc == 7, r=[("win", kc), nTk], w=[pFk])
                        CP("act", dqs[:, fc, :], pF[:, :], r=[], w=[pFk, "dqs"])
                    DMA("sp", dqv[:, :, DQPAD + t0:DQPAD + t0 + 512], dqs[:], r=["dqs"], w=[("DQ_d", u)])
                    for t in range(4):
                        ti = u * 4 + t
                        for (pp, key, c0, n) in ((pQ, "pQ", 0, 512), (pK, "pK", 512, 512), (pV, "pV", 1024, 512),
                                                 (pG, "pG", 3072, 512), (pB, "pB", 3584, 16)):
                            for kc in range(8):
                                MM(pp[:, 0:n], nT[:, kc, 128 * t:128 * (t + 1)], win[:, kc, c0:c0 + n], kc == 0, kc == 7,
                                   r=[("win", kc), nTk], w=[key])
                        while deferred:
                            deferred.pop(0)()
                        if u + 1 < NU:
                            norm_piece(u + 1, t)
                        CP("act", qraw[:, 0:512], pQ[:, :], r=[], w=["pQ", "qraw"])
                        CP("dve", qraw[:, 512:1024], pK[:, :], r=[], w=["pK", "qraw"])
                        CP("act", vb[:, t, :], pV[:, :], r=[], w=["pV", "vb"])
                        CP("act", gs[:, t, :], pG[:, :], r=[], w=["pG", "gs"])
                        ACTV(bt8[:], pB[:, 0:8], AF.Exp, r=[], w=["pB", "bt8"], scale=-1.0)
                        TS("dve", bt8[:], bt8[:], 1.0, None, ALU.add, None, r=["bt8"], w=["bt8"])
                        RECIP(gbt[:, t, 8:16], bt8[:], r=["bt8"], w=["gbt"])
                        TT("dve", zt[:], pB[:, 8:16], dtb[:], ALU.add, r=["dtb"], w=["pB", "zt"])
                        ACTV(zt[:], zt[:], AF.Exp, r=["zt"], w=["zt"])
                        ACTV(zt[:], zt[:], AF.Ln, r=["zt", "one"], w=["zt"], bias=oneT[:])
                        TT("dve", gbt[:, t, 0:8], zt[:], nexpA[:], ALU.mult, r=["zt", "nexpA"], w=["gbt"])
                        ACTV(sq[:], qraw[:], AF.Square, r=["qraw"], w=["sq"])
                        RED("dve", ssq[:], sq[:].rearrange("p (h d) -> p h d", d=64), r=["sq"], w=["ssq"])
                        ACTV(ssq[:], ssq[:], AF.Ln, r=["ssq", "eps"], w=["ssq"], scale=1.0 / 64, bias=epsT[:])
                        ACTV(ssq[:], ssq[:], AF.Exp, r=["ssq"], w=["ssq"], scale=-0.5)
                        TT("dve", qn[:].rearrange("p (h d) -> p h d", d=64), qraw[:].rearrange("p (h d) -> p h d", d=64),
                           ssq[:].unsqueeze(2).to_broadcast([128, 16, 64]), ALU.mult, r=["ssq", "qraw"], w=["qn"])
                        TT("dve", qn[:], qn[:], qkg[:], ALU.mult, r=["qn", "qkg"], w=["qn"])
                        if ti % 16 == 0:
                            for tt in range(16):
                                STT("dve", ang16[:, tt, :], invf[:], post[:, ti + tt:ti + tt + 1], phase[:], ALU.mult, ALU.add,
                                    r=["invf", "post", "phase"], w=["ang16"])
                            TS("dve", kfi16[:], ang16[:], 1.0 / TWO_PI, None, ALU.mult, None, r=["ang16"], w=["kfi16"])
                            CP("dve", kff16[:], kfi16[:], r=["kfi16"], w=["kff16"])
                            STT("dve", ang16[:], kff16[:], -TWO_PI, ang16[:], ALU.mult, ALU.add, r=["kff16", "ang16"], w=["ang16"])
                            TS("dve", ang16[:], ang16[:], -float(np.pi), float(np.pi), ALU.max, ALU.min, r=["ang16"], w=["ang16"])
                            ACTV(cs16[:], ang16[:], AF.Sin, r=["ang16"], w=["cs16"])
                        qr = qrs[ti % 2]
                        qrk = ("qr", ti % 2)
                        cs = cs16[:, ti % 16, :]
                        qv = qn[:].rearrange("p (h d) -> p h d", d=64)
                        qrv = qr[:].rearrange("p (h d) -> p h d", d=64)
                        sinb = cs[:, 0:32].unsqueeze(1).to_broadcast([128, 16, 32])
                        cosb = cs[:, 32:64].unsqueeze(1).to_broadcast([128, 16, 32])
                        v3 = lambda tt: tt[:].rearrange("p (h d) -> p h d", d=32)
                        TT("dve", v3(t1), qv[:, :, 0:32], cosb, ALU.mult, r=["qn", "cs16"], w=["t1"])
                        TT("dve", v3(t2), qv[:, :, 32:64], sinb, ALU.mult, r=["qn", "cs16"], w=["t2"])
                        TT("dve", qrv[:, :, 0:32], v3(t1), v3(t2), ALU.subtract, r=["t1", "t2"], w=[qrk])
                        TT("pool", v3(t3), qv[:, :, 32:64], cosb, ALU.mult, r=["qn", "cs16"], w=["t3"])
                        TT("pool", v3(t4), qv[:, :, 0:32], sinb, ALU.mult, r=["qn", "cs16"], w=["t4"])
                        TT("pool", qrv[:, :, 32:64], v3(t3), v3(t4), ALU.add, r=["t3", "t4"], w=[qrk])

                        def tr_stage(qr=qr, qrk=qrk, t=t, qkT=qkTs[u % 2], qkk=("qkT", u % 2)):
                            for j in range(8):
                                TR(pTT[:, j, :], qr[:, j * 128:(j + 1) * 128], ident[:, :], r=[qrk, "ident"], w=["pTT"])
                            CP("act", qkT[:, :, 128 * t:128 * (t + 1)], pTT[:, :, :], r=[], w=["pTT", qkk])
                        deferred.append(tr_stage)
                    while deferred:
                        deferred.pop(0)()
                    qkT = qkTs[u % 2]
                    DMA("sp", V_d[KPAD + t0:KPAD + t0 + 512, :].rearrange("(t p) c -> p t c", p=128), vb[:], r=["vb"], w=[("V_d", u)])
                    DMA("sp", G_d[t0:t0 + 512, :].rearrange("(t p) c -> p t c", p=128), gs[:], r=["gs"], w=[("G_d", u)])
                    DMA("sp", GB_d[t0:t0 + 512, :].rearrange("(t p) c -> p t c", p=128), gbt[:], r=["gbt"], w=[("GB_d", u)])
                    DMA("sp", QT_d[:, :, KPAD + t0:KPAD + t0 + 512].rearrange("j p w -> p j w"), qkT[:, 0:4, :], r=[("qkT", u % 2)], w=[("QT_d", u)])
                    DMA("sp", KT_d[:, :, KPAD + t0:KPAD + t0 + 512].rearrange("j p w -> p j w"), qkT[:, 4:8, :], r=[("qkT", u % 2)], w=[("KT_d", u)])
            P.barrier()

        def pass2():
            with contextlib.ExitStack() as es:
                def sb(name, shape, dt=F32):
                    return es.enter_context(nc.sbuf_tensor("p2_" + name, list(shape), dt))

                def ps(name, shape, dt=F32):
                    return es.enter_context(nc.psum_tensor("p2_" + name, list(shape), dt))
                lv = sb("lv", [128, 2 * NSEG])
                am_f = sb("am_f", [128, 256])
                am = sb("am", [128, 256], BF16)
                amL = sb("amL", [128, 256], BF16); amR = sb("amR", [128, 256], BF16); amLR = sb("amLR", [128, 256], BF16)
                ones_b = sb("ones_b", [128, 128])
                DMA("sp", lv[:], lv_d[:, :], w=["lv"])
                DMA("sp", am_f[:], amask_d[:, :], w=["am_f"])
                CP("dve", am[:], am_f[:], r=["am_f"], w=["am"])
                MSET("dve", ones_b[:], 1.0, w=["ones_b"])
                QTs = [sb(f"QTs{i}", [128, 4, SEG], BF16) for i in range(1)]
                KTw = [sb(f"KTw{i}", [128, 4, 2 * SEG], BF16) for i in range(1)]
                acc = [sb(f"acc{e}", [128, 4, SEG]) for e in range(2)]
                attT = sb("attT", [128, 4, SEG], BF16)
                NVB = 8
                vraw = [sb(f"vraw{i}", [128, 512], BF16) for i in range(NVB)]
                vt2 = [sb(f"vt2_{i}", [128, 4, 2, 128], BF16) for i in range(NVB)]
                NPT = 4
                pt = [sb(f"pt{i}", [128, 256], BF16) for i in range(NPT)]
                rden = sb("rden", [128, SEG])
                pS = [ps(f"pS{i}", [128, 512]) for i in range(3)]
                pA = [ps(f"pA{i}", [128, 512]) for i in range(3)]
                pBc = [ps(f"pBc{i}", [128, 512]) for i in range(2)]
                for i in range(NVB):
                    MSET("pool", vt2[i][:], 0.0, w=[("vt2", i)])
                    MSET("pool", vt2[i][:, :, 0, 64:65], 1.0, w=[("vt2", i)])
                    MSET("pool", vt2[i][:, :, 1, 32:33], 1.0, w=[("vt2", i)])
                cnt = {"v": 0, "s": 0, "a": 0, "pt": 0, "m": 0, "bc": 0}
                import collections
                pending = collections.deque()
                SKEW = 2
                for s in range(NSEG):
                    b = 0
                    base = KPAD + s * SEG
                    DMA("sp", QTs[b][:], QT_d[:, :, base:base + SEG].rearrange("j p w -> p j w"),
                        r=[("QT_d", u) for u in range(4 * s, 4 * s + 4)], w=[("QTs", b)])
                    ulo = max(0, 4 * s - 2); uhi = min(NU, 4 * s + 6)
                    DMA("sp", KTw[b][:], KT_d[:, :, base - 1024:base + SEG + 1024].rearrange("j p w -> p j w"),
                        r=[("KT_d", u) for u in range(ulo, uhi)] + [("KT_d", "padL"), ("KT_d", "padR")], w=[("KTw", b)])
                    TS("pool", amL[:, 0:128], am[:, 0:128], lv[:, 2 * s:2 * s + 1], None, ALU.mult, None, r=["am", "lv"], w=["amL"])
                    CP("pool", amL[:, 128:256], am[:, 128:256], r=["am"], w=["amL"])
                    CP("pool", amR[:, 0:128], am[:, 0:128], r=["am"], w=["amR"])
                    TS("pool", amR[:, 128:256], am[:, 128:256], lv[:, 2 * s + 1:2 * s + 2], None, ALU.mult, None, r=["am", "lv"], w=["amR"])
                    CP("pool", amLR[:, 0:128], amL[:, 0:128], r=["amL"], w=["amLR"])
                    CP("pool", amLR[:, 128:256], amR[:, 128:256], r=["amR"], w=["amLR"])
                    vkeys = [("V_d", u) for u in range(ulo, uhi)] + [("V_d", "padL"), ("V_d", "padR")]
                    groups = []
                    for (d, first) in ((1, True), (4, False), (16, False)):
                        nqb = SEG // d // 128
                        for c in range(d):
                            for qb in range(nqb):
                                groups.append((d, first, c, qb, nqb))
                    needed = []
                    seen = set()
                    for (d, first, c, qb, nqb) in groups:
                        for k in (qb, qb + 1):
                            if (d, c, k) not in seen:
                                seen.add((d, c, k))
                                needed.append((d, c, k))
                    vslot = {}
                    nl = [0]

                    def load_v(d, c, k):
                        i = cnt["v"] % NVB
                        cnt["v"] += 1
                        row0 = base + c + d * (128 * k - 64)
                        DMA("sp", vraw[i][:], V_d[_sl(row0, 128, d), :], r=vkeys, w=[("vraw", i)])
                        vr = vraw[i][:].rearrange("p (j e x) -> p j e x", e=2, x=64)
                        CP("pool", vt2[i][:, :, 0, 0:64], vr[:, :, 0, :], r=[("vraw", i)], w=[("vt2", i)])
                        CP("pool", vt2[i][:, :, 1, 64:128], vr[:, :, 1, :], r=[("vraw", i)], w=[("vt2", i)])
                        vslot[(d, c, k)] = i

                    def ensure_loaded(gi):
                        if gi >= len(groups):
                            return
                        d, first, c, qb, nqb = groups[gi]
                        while not ((d, c, qb) in vslot and (d, c, qb + 1) in vslot):
                            load_v(*needed[nl[0]])
                            nl[0] += 1
                    PF = 2
                    for gi, (d, first, c, qb, nqb) in enumerate(groups):
                        for g2 in range(gi, gi + PF + 1):
                            ensure_loaded(g2)
                        if nqb == 1:
                            msk, mkey = amLR, "amLR"
                        elif qb == 0:
                            msk, mkey = amL, "amL"
                        elif qb == nqb - 1:
                            msk, mkey = amR, "amR"
                        else:
                            msk, mkey = am, "am"
                        qcol0 = c + d * 128 * qb
                        qsl = _sl(qcol0, 128, d)
                        for hp in range(4):
                            for e in range(2):
                                pb = 64 * e
                                iS = cnt["s"] % 3; cnt["s"] += 1
                                for half in range(2):
                                    k = qb + half
                                    kc0 = 1024 + c + d * (128 * k - 64)
                                    MM(pS[iS][:, 128 * half:128 * (half + 1)], KTw[b][pb:pb + 64, hp, _sl(kc0, 128, d)],
                                       QTs[b][pb:pb + 64, hp, qsl], True, True, r=[("KTw", b), ("QTs", b)], w=[("pS", iS)])
                                ip = cnt["pt"] % NPT; cnt["pt"] += 1
                                ACTV(pt[ip][:], pS[iS][:, 0:256], AF.Exp, r=[], w=[("pS", iS), ("pt", ip)], scale=0.125)
                                meng = "pool" if (cnt["m"] % 3 == 0) else "dve"
                                cnt["m"] += 1
                                TT(meng, pt[ip][:], pt[ip][:], msk[:], ALU.mult, r=[("pt", ip), mkey], w=[("pt", ip)])

                                def pv_stage(ip=ip, e=e, hp=hp, qsl=qsl, first=first, v0=vslot[(d, c, qb)], v1=vslot[(d, c, qb + 1)]):
                                    iA = cnt["a"] % 3; cnt["a"] += 1
                                    M = 65 if e == 0 else 128
                                    for half, vi in ((0, v0), (1, v1)):
                                        MM(pA[iA][0:M, 0:128], vt2[vi][:, hp, e, 0:M], pt[ip][:, 128 * half:128 * (half + 1)],
                                           half == 0, half == 1, r=[("vt2", vi), ("pt", ip)], w=[("pA", iA)])
                                    lo, hi = (0, 65) if e == 0 else (0, 128)
                                    dst = acc[e][lo:hi, hp, qsl]
                                    if first:
                                        CP("dve", dst, pA[iA][lo:hi, 0:128], r=[], w=[("pA", iA), ("acc", e, hp)])
                                    else:
                                        TT("dve", dst, dst, pA[iA][lo:hi, 0:128], ALU.add, r=[], w=[("pA", iA), ("acc", e, hp)])
                                pending.append(pv_stage)
                                if len(pending) > SKEW:
                                    pending.popleft()()
                    while pending:
                        pending.popleft()()
                    for hp in range(4):
                        for e in range(2):
                            dp = 64 if e == 0 else 32
                            lo, hi = (0, 64) if e == 0 else (64, 128)
                            ACTV(rden[dp:dp + 1, :], acc[e][dp:dp + 1, hp, :], AF.Ln, r=[("acc", e, hp)], w=["rden"])
                            ACTV(rden[dp:dp + 1, :], rden[dp:dp + 1, :], AF.Exp, r=["rden"], w=["rden"], scale=-1.0)
                            for cc in range(SEG // 512):
                                csl = slice(512 * cc, 512 * (cc + 1))
                                ib = cnt["bc"] % 2; cnt["bc"] += 1
                                MM(pBc[ib][0:hi, :], ones_b[dp:dp + 1, 0:hi], rden[dp:dp + 1, csl], True, True,
                                   r=["ones_b", "rden"], w=[("pBc", ib)])
                                TT("dve", attT[lo:hi, hp, csl], acc[e][lo:hi, hp, csl], pBc[ib][lo:hi, :], ALU.mult,
                                   r=[("acc", e, hp)], w=[("pBc", ib), "attT"])
                    DMA("sp", AT_d[:, :, s * SEG:(s + 1) * SEG].rearrange("j p w -> p j w"), attT[:], r=["attT"], w=[("AT_d", s)])
            P.barrier()


        def pass_prep():
            with contextlib.ExitStack() as es:
                def sb(name, shape, dt=F32):
                    return es.enter_context(nc.sbuf_tensor("pp_" + name, list(shape), dt))

                def ps(name, shape, dt=F32):
                    return es.enter_context(nc.psum_tensor("pp_" + name, list(shape), dt))
                dncw = sb("dncw", [128, 5, 12])
                dnsc = sb("dnsc", [128, 4 * NT])
                lnq = sb("lnq", [128, 1])
                dg = sb("dg", [128, 60, 128], BF16)
                ones_b = sb("ones_b", [128, 128], BF16)
                DMA("sp", dncw[:], dncw_d[:, :, :], w=["dncw"])
                DMA("sp", dnsc[:], dnsc_d[:, :], w=["dnsc"])
                MSET("dve", lnq[:], float(-0.5 * np.log(128.0)), w=["lnq"])
                MSET("dve", ones_b[:], 1.0, w=["ones_b"])
                for j in range(5):
                    for fc in range(12):
                        TS("dve", dg[:, j * 12 + fc, :], ident[:, :], dncw[:, j, fc:fc + 1], None, ALU.mult, None,
                           r=["ident", "dncw"], w=["dg"])
                dqb = [sb(f"dqb{i}", [128, 12, 516], BF16) for i in range(2)]
                qk32 = sb("qk32", [128, 8, 512])
                sqb = sb("sqb", [128, 8, 512], BF16)
                rn = sb("rn", [128, 8, 512])
                oqkv = [sb(f"oqkv{i}", [128, 12, 512], BF16) for i in range(2)]
                pC = [ps(f"pC{i}", [128, 512]) for i in range(4)]
                pNn = [ps(f"pN{i}", [128, 512]) for i in range(4)]
                dqv = DQ_d.rearrange("(f p) w -> p f w", p=128)
                dnv = DN_d.rearrange("(f p) w -> p f w", p=128)

                def load(u):
                    t0 = u * 512
                    deps = [("DQ_d", uu) for uu in range(max(0, u - 1), min(NU, u + 2))] + [("DQ_d", "padL"), ("DQ_d", "padR")]
                    DMA("sp", dqb[u % 2][:], dqv[:, :, DQPAD + t0 - 2:DQPAD + t0 + 514], r=deps, w=[("dqb", u % 2)])
                load(0)
                for u in range(NU):
                    b = u % 2
                    t0 = u * 512
                    if u + 1 < NU:
                        load(u + 1)
                    ti = u * 4
                    if t0 % SEG == 0:
                        TS("dve", dqb[b][:, :, 0:2], dqb[b][:, :, 0:2], dnsc[:, 4 * ti:4 * ti + 1], None, ALU.mult, None,
                           r=[("dqb", b), "dnsc"], w=[("dqb", b)])
                    if (t0 + 512) % SEG == 0:
                        TS("dve", dqb[b][:, :, 514:516], dqb[b][:, :, 514:516], dnsc[:, 4 * (ti + 3) + 1:4 * (ti + 3) + 2], None,
                           ALU.mult, None, r=[("dqb", b), "dnsc"], w=[("dqb", b)])
                    for fc in range(12):
                        i = fc % 4
                        for j in range(5):
                            MM(pC[i][:, :], dg[:, j * 12 + fc, :], dqb[b][:, fc, j:j + 512], j == 0, j == 4, r=["dg", ("dqb", b)], w=[("ppC", i)])
                        if fc < 8:
                            ACTV(qk32[:, fc, :], pC[i][:, :], AF.Silu, r=[], w=[("ppC", i), ("qk32", fc)])
                        else:
                            ACTV(oqkv[b][:, fc, :], pC[i][:, :], AF.Silu, r=[], w=[("ppC", i), ("oqkv", b)])
                    for fc in range(8):
                        ACTV(sqb[:, fc, :], qk32[:, fc, :], AF.Square, r=[("qk32", fc)], w=[("sqb", fc)])
                    for fc in range(8):
                        i = fc % 4
                        MM(pNn[i][:, :], ones_b[:, :], sqb[:, fc, :], True, True, r=["ones_b", ("sqb", fc)], w=[("ppN", i)])
                        ACTV(rn[:, fc, :], pNn[i][:, :], AF.Ln, r=["eps"], w=[("ppN", i), ("rn", fc)], bias=epsT[:])
                    for fc in range(8):
                        if fc < 4:
                            ACTV(rn[:, fc, :], rn[:, fc, :], AF.Exp, r=[("rn", fc), "lnq"], w=[("rn", fc)], scale=-0.5, bias=lnq[:])
                        else:
                            ACTV(rn[:, fc, :], rn[:, fc, :], AF.Exp, r=[("rn", fc)], w=[("rn", fc)], scale=-0.5)
                        TT("dve", oqkv[b][:, fc, :], qk32[:, fc, :], rn[:, fc, :], ALU.mult, r=[("qk32", fc), ("rn", fc)], w=[("oqkv", b)])
                    DMA("sp", dnv[:, :, t0:t0 + 512], oqkv[b][:], r=[("oqkv", b)], w=[("DN_d", u)])
            P.barrier()

        def pass_dn_gen(dr, es):
            NEGC = 11
            if True:
                def sb(name, shape, dt=F32):
                    return es.enter_context(nc.sbuf_tensor(f"p3{dr}_" + name, list(shape), dt))

                def ps(name, shape, dt=F32):
                    return es.enter_context(nc.psum_tensor(f"p3{dr}_" + name, list(shape), dt))
                dnc = sb("dnc", [128, NEGC, 128])
                dnsc = sb("dnsc", [128, 4 * NT])
                DMA("sp", dnc[:], dnc_d[:, :, :], w=["dnc"])
                DMA("sp", dnsc[:], dnsc_d[:, :], w=["dnsc"])
                U = dnc[:, 0 + dr, :]
                ONES = dnc[:, 2, :]
                MLOW = dnc[:, 3 + dr, :]
                MQ = dnc[:, 5 + dr, :]
                IDF = dnc[:, 7, :]
                BD = dnc[:, 10, :]
                qkvb = [sb(f"qkvb{i}", [128, 12, 128], BF16) for i in range(2)]
                gb = [sb(f"gb{i}", [128, 16]) for i in range(2)]
                gst = sb("gst", [128, 16])
                egc = sb("egc", [128, 4]); be = sb("be", [128, 4]); edk = sb("edk", [128, 4]); egl = sb("egl", [128, 8])
                Ug = sb("Ug", [128, 4, 128])
                Nm = sb("Nm", [128, 4, 128])
                DL = sb("DL", [128, 4, 128]); DQm = sb("DQm", [128, 4, 128])
                egcr = sb("egcr", [128, 4, 128])
                Lb = [sb(f"Lb{i}", [128, 4, 128], BF16) for i in range(2)]
                Mb = [sb(f"Mb{i}", [128, 4, 128], BF16) for i in range(2)]
                Z = sb("Z", [128, 4, 128], BF16)
                kkb = sb("kkb", [128, 4, 128])
                qkTm = sb("qkTm", [128, 4, 128], BF16)
                kbe = sb("kbe", [128, 4, 128], BF16); kdec = sb("kdec", [128, 4, 128], BF16); vbt = sb("vbt", [128, 4, 128], BF16)
                um = sb("um", [128, 4, 128]); wTm = sb("wTm", [128, 4, 128], BF16); qdTm = sb("qdTm", [128, 4, 128], BF16)
                vn = sb("vn", [128, 4, 128], BF16)
                ot = [sb(f"ot{i}", [128, 4, 128]) for i in range(2)]
                S = sb("S", [128, 4, 128])
                Sb = sb("Sb", [128, 4, 128], BF16)
                g = [ps(f"g{i}", [128, 4, 128]) for i in range(3)]
                tbb = ps("tbb", [128, 8, 128], BF16)
                tb = [tbb[:, 0:4, :], tbb[:, 4:8, :]]
                gi = [0]
                tbi = [0]

                def nextg():
                    i = gi[0] % 3
                    gi[0] += 1
                    return g[i], ("p3g", i)

                def nexttb():
                    i = tbi[0] % 2
                    tbi[0] += 1
                    return tb[i], "p3tb"
                bc = lambda ap4: ap4.unsqueeze(2).to_broadcast([128, 4, 128])
                identb4 = ident[:, :].unsqueeze(1).to_broadcast([128, 4, 128])
                MSET("dve", S[:], 0.0, w=["S"])
                MSET("dve", Sb[:], 0.0, w=["Sb"])
                order = list(range(NT)) if dr == 0 else list(range(NT - 1, -1, -1))
                dnv = DN_d.rearrange("(f p) w -> p f w", p=128)

                def load(ti, b):
                    t0 = ti * 128
                    u = ti // 4
                    DMA("sp", qkvb[b][:], dnv[:, :, t0:t0 + 128], r=[("DN_d", u)], w=[("qkvb", b)])
                    DMA("sp", gb[b][:], GB_d[t0:t0 + 128, :], r=[("GB_d", u)], w=[("gb", b)])
                load(order[0], 0)
                for n_, ti in enumerate(order):
                    b = n_ % 2
                    t0 = ti * 128
                    if n_ + 1 < NT:
                        load(order[n_ + 1], 1 - b)
                    g4 = gb[b][:, 4 * dr:4 * dr + 4]
                    b4 = gb[b][:, 8 + 4 * dr:12 + 4 * dr]
                    qT = lambda h: qkvb[b][:, h, :]
                    kT = lambda h: qkvb[b][:, 4 + h, :]
                    vT = lambda h: qkvb[b][:, 8 + h, :]
                    pG3, kG = nextg()
                    pG = pG3[:, :, :].rearrange("p a b -> p (a b)")
                    MM(pG[:, 0:4], U, g4, True, True, r=["dnc", ("gb", b)], w=[kG])
                    MM(pG[:, 4:8], BD, g4, True, True, r=["dnc", ("gb", b)], w=[kG])
                    MM(pG[:, 8:12], dnc[:, 8, :], g4, True, True, r=["dnc", ("gb", b)], w=[kG])
                    MM(pG[:, 12:16], dnc[:, 9, :], g4, True, True, r=["dnc", ("gb", b)], w=[kG])
                    CP("dve", gst[:], pG[:, 0:16], r=[], w=[kG, "gst"])
                    ACTV(egc[:], gst[:, 0:4], AF.Exp, r=["gst"], w=["egc"])
                    TT("dve", be[:], egc[:], b4, ALU.mult, r=["egc", ("gb", b)], w=["be"])
                    TT("dve", edk[:], gst[:, 4:8], gst[:, 0:4], ALU.subtract, r=["gst"], w=["edk"])
                    ACTV(edk[:], edk[:], AF.Exp, r=["edk"], w=["edk"])
                    ACTV(egl[:], gst[:, 8:16], AF.Exp, r=["gst"], w=["egl"])
                    TT("dve", Ug[:], U.unsqueeze(1).to_broadcast([128, 4, 128]), bc(g4), ALU.mult, r=["dnc", ("gb", b)], w=["Ug"])
                    yield
                    pR, kR = nextg()
                    MM(pR[:, :, :], ONES, Ug[:, :, :], True, True, r=["dnc", "Ug"], w=[kR])
                    TT("dve", Nm[:], pR[:, :, :], bc(gst[:, 0:4]), ALU.subtract, r=["gst"], w=[kR, "Nm"])
                    ACTV(egcr[:], pR[:, :, :], AF.Exp, r=[], w=[kR, "egcr"])
                    mlb = MLOW.unsqueeze(1).to_broadcast([128, 4, 128])
                    mqb = MQ.unsqueeze(1).to_broadcast([128, 4, 128])
                    STT("dve", DL[:], Nm[:], -1.0, mlb, ALU.mult, ALU.add, r=["Nm", "dnc"], w=["DL"])
                    TT("pool", DQm[:], Nm[:], mqb, ALU.add, r=["Nm", "dnc"], w=["DQm"])
                    ACTV(DL[:], DL[:], AF.Exp, r=["DL"], w=["DL"])
                    ACTV(DQm[:], DQm[:], AF.Exp, r=["DQm"], w=["DQm"])
                    yield
                    pK_, kK = nextg()
                    for h in range(4):
                        MM(pK_[:, h, :], kT(h), kT(h), True, True, r=[("qkvb", b)], w=[kK])
                    TT("dve", kkb[:], pK_[:, :, :], bc(b4), ALU.mult, r=[("gb", b)], w=[kK, "kkb"])
                    TT("dve", Lb[0][:], kkb[:], DL[:], ALU.mult, r=["kkb", "DL"], w=[("Lb", 0)])
                    yield
                    pQ_, kQ = nextg()
                    for h in range(4):
                        MM(pQ_[:, h, :], kT(h), qT(h), True, True, r=[("qkvb", b)], w=[kQ])
                    TT("dve", qkTm[:], pQ_[:, :, :], DQm[:], ALU.mult, r=["DQm"], w=[kQ, "qkTm"])
                    pM_, kM = nexttb()
                    for h in range(4):
                        TR(pM_[:, h, :], Lb[0][:, h, :], ident[:, :], r=[("Lb", 0), "ident"], w=[kM])
                    CP("act", Mb[0][:], pM_[:, :, :], r=[], w=[kM, ("Mb", 0)])
                    yield
                    STT("dve", Z[:], Mb[0][:], -1.0, identb4, ALU.mult, ALU.add, r=[("Mb", 0), "ident"], w=["Z"])
                    cur = 0
                    for lvl in range(5):
                        nxt = 1 - cur
                        pP_, kP = nextg()
                        for h in range(4):
                            MM(pP_[:, h, :], Mb[cur][:, h, :], Lb[cur][:, h, :], True, True, r=[("Mb", cur), ("Lb", cur)], w=[kP])
                        CP("act", Lb[nxt][:], pP_[:, :, :], r=[], w=[kP, ("Lb", nxt)])
                        yield
                        if lvl < 4:
                            pM2, kM2 = nextg()
                            for h in range(4):
                                MM(pM2[:, h, :], Lb[cur][:, h, :], Mb[cur][:, h, :], True, True, r=[("Mb", cur), ("Lb", cur)], w=[kM2])
                            CP("dve", Mb[nxt][:], pM2[:, :, :], r=[], w=[kM2, ("Mb", nxt)])
                        pZ_, kZ = nextg()
                        for h in range(4):
                            MM(pZ_[:, h, :], Lb[nxt][:, h, :], Z[:, h, :], True, True, r=[("Lb", nxt), "Z"], w=[kZ])
                        TT("dve", Z[:], Z[:], pZ_[:, :, :], ALU.add, r=["Z"], w=[kZ, "Z"])
                        yield
                        cur = nxt
                    pT1, kT1 = nexttb()
                    for h in range(4):
                        TR(pT1[:, h, :], kT(h), ident[:, :], r=[("qkvb", b), "ident"], w=[kT1])
                    TT("dve", kbe[:], pT1[:, :, :], bc(be[:]), ALU.mult, r=["be"], w=[kT1, "kbe"])
                    TT("dve", kdec[:], pT1[:, :, :], bc(edk[:]), ALU.mult, r=["edk"], w=[kT1, "kdec"])
                    pT2, kT2 = nexttb()
                    for h in range(4):
                        TR(pT2[:, h, :], vT(h), ident[:, :], r=[("qkvb", b), "ident"], w=[kT2])
                    TT("dve", vbt[:], pT2[:, :, :], bc(b4), ALU.mult, r=[("gb", b)], w=[kT2, "vbt"])
                    yield
                    pU_, kU = nextg()
                    for h in range(4):
                        MM(pU_[:, h, :], Z[:, h, :], vbt[:, h, :], True, True, r=["Z", "vbt"], w=[kU])
                    CP("act", um[:], pU_[:, :, :], r=[], w=[kU, "um"])
                    pW_, kW = nextg()
                    for h in range(4):
                        MM(pW_[:, h, :], kbe[:, h, :], Z[:, h, :], True, True, r=["Z", "kbe"], w=[kW])
                    CP("act", wTm[:], pW_[:, :, :], r=[], w=[kW, "wTm"])
                    TT("pool", qdTm[:], qkvb[b][:, 0:4, :], egcr[:], ALU.mult, r=[("qkvb", b), "egcr"], w=["qdTm"])
                    yield
                    carry = None
                    if dr == 0 and ti % 16 == 0 and ti > 0:
                        carry = dnsc[:, 4 * ti + 2:4 * ti + 3]
                    if dr == 1 and ti % 16 == 15 and ti < NT - 1:
                        carry = dnsc[:, 4 * ti + 3:4 * ti + 4]
                    if carry is not None:
                        TS("dve", S[:], S[:], carry, None, ALU.mult, None, r=["S", "dnsc"], w=["S"])
                        CP("act", Sb[:], S[:], r=["S"], w=["Sb"])
                    io = n_ % 2
                    for c in ((0, 1) if dr == 0 else (1, 0)):
                        cs = slice(64 * c, 64 * c + 64)
                        pVN, kVN = nextg()
                        for h in range(4):
                            MM(pVN[cs, h, :], wTm[:, h, cs], Sb[:, h, :], True, True, r=["wTm", "Sb"], w=[kVN])
                        TT("dve", vn[cs, :, :], um[cs, :, :], pVN[cs, :, :], ALU.subtract, r=["um"], w=[kVN, "vn"])
                        yield
                        pO, kO = nextg()
                        for h in range(4):
                            MM(pO[cs, h, :], qdTm[:, h, cs], Sb[:, h, :], True, False, r=["qdTm", "Sb"], w=[kO])
                            MM(pO[cs, h, :], qkTm[cs, h, cs], vn[cs, h, :], False, True, r=["qkTm", "vn"], w=[kO])
                        pSn, kSn = nextg()
                        for h in range(4):
                            MM(pSn[:, h, :], kdec[cs, h, :], vn[cs, h, :], True, True, r=["kdec", "vn"], w=[kSn])
                        for h in range(4):
                            STT("dve", S[:, h, :], S[:, h, :], egl[:, 4 * c + h:4 * c + h + 1], pSn[:, h, :], ALU.mult, ALU.add,
                                r=["S", "egl"], w=[kSn, "S"])
                        CP("act", Sb[:], S[:], r=["S"], w=["Sb"])
                        CP("act", ot[io][cs, :, :], pO[cs, :, :], r=[], w=[kO, ("ot", io)])
                        yield
                    DMA("sp", OD_d[dr][t0:t0 + 128, :], ot[io][:].rearrange("p h d -> p (h d)"), r=[("ot", io)], w=[("OD_d", dr, ti)])
                    yield

        def pass_dn_both(dirs):
            P.shared = {"eps", "one", "ident", "GB_d", "DQ_d", "OD_d", "DN_d"}
            with contextlib.ExitStack() as es:
                gens = [(dr, pass_dn_gen(dr, es)) for dr in dirs]
                while gens:
                    for item in list(gens):
                        P.ns = ("dn", item[0])
                        try:
                            next(item[1])
                        except StopIteration:
                            gens.remove(item)
                P.ns = None
            P.barrier()

        def pass5():
            with contextlib.ExitStack() as es:
                def sb(name, shape, dt=F32):
                    return es.enter_context(nc.sbuf_tensor("p5_" + name, list(shape), dt))

                def ps(name, shape, dt=F32):
                    return es.enter_context(nc.psum_tensor("p5_" + name, list(shape), dt))
                wout = sb("wout", [128, 8, D], BF16)
                dnog = sb("dnog", [128, 512])
                DMA("sp", dnog[:], dnog_d[:, :], w=["dnog"])
                for kc in range(8):
                    DMA("pool", wout[:, kc, :], w_out_d[kc * 128:(kc + 1) * 128, :], w=[("wout", kc)])
                NB = 2
                of_ = [sb(f"of{i}", [128, 512]) for i in range(NB)]
                ob_ = [sb(f"ob{i}", [128, 512]) for i in range(NB)]
                gt = [sb(f"gt{i}", [128, 512]) for i in range(NB)]
                xt = [sb(f"xt{i}", [128, D]) for i in range(NB)]
                at = [sb(f"at{i}", [128, 4, 128], BF16) for i in range(NB)]
                osum = sb("osum", [128, 512]); sq = sb("sq", [128, 512]); ss = sb("ss", [128, 4])
                gg = sb("gg", [128, 512]); on = sb("on", [128, 512]); odn = sb("odn", [128, 512], BF16)
                eg = sb("eg", [128, 512])
                dnT = sb("dnT", [128, 4, 128], BF16)
                ht = [sb(f"ht{i}", [128, D]) for i in range(2)]
                pT = ps("pT", [128, 4, 128], BF16)
                pH = [ps(f"pH{i}", [128, 512]) for i in range(2)]
                for ti in range(NT):
                    b = ti % NB
                    t0 = ti * 128
                    u = ti // 4
                    DMA("sp", of_[b][:], OD_d[0][t0:t0 + 128, :], r=[("OD_d", 0, ti)], w=[("of", b)])
                    DMA("sp", ob_[b][:], OD_d[1][t0:t0 + 128, :], r=[("OD_d", 1, ti)], w=[("ob", b)])
                    DMA("sp", gt[b][:], G_d[t0:t0 + 128, :], r=[("G_d", u)], w=[("gt", b)])
                    DMA("sp", xt[b][:], xs[t0:t0 + 128, :], w=[("xt5", b)])
                    DMA("sp", at[b][:], AT_d[:, :, t0:t0 + 128].rearrange("j p w -> p j w"), r=[("AT_d", ti // 16)], w=[("at", b)])
                    TT("dve", osum[:], of_[b][:], ob_[b][:], ALU.add, r=[("of", b), ("ob", b)], w=["osum"])
                    ACTV(sq[:], osum[:], AF.Square, r=["osum"], w=["sq5"])
                    RED("dve", ss[:], sq[:].rearrange("p (h d) -> p h d", d=128), r=["sq5"], w=["ss5"])
                    ACTV(ss[:], ss[:], AF.Ln, r=["ss5", "eps"], w=["ss5"], scale=1.0 / 128, bias=epsT[:])
                    ACTV(ss[:], ss[:], AF.Exp, r=["ss5"], w=["ss5"], scale=-0.5)
                    ACTV(eg[:], gt[b][:], AF.Exp, r=[("gt", b)], w=["eg"], scale=-1.0)
                    ACTV(eg[:], eg[:], AF.Ln, r=["eg", "one"], w=["eg"], bias=oneT[:])
                    ACTV(eg[:], eg[:], AF.Exp, r=["eg"], w=["eg"], scale=-1.0)
                    TT("pool", gg[:], gt[b][:], dnog[:], ALU.mult, r=[("gt", b), "dnog"], w=["gg"])
                    TT("dve", gg[:], gg[:], eg[:], ALU.mult, r=["gg", "eg"], w=["gg"])
                    TT("dve", on[:].rearrange("p (h d) -> p h d", d=128), osum[:].rearrange("p (h d) -> p h d", d=128),
                       ss[:].unsqueeze(2).to_broadcast([128, 4, 128]), ALU.mult, r=["osum", "ss5"], w=["on"])
                    TT("dve", odn[:], on[:], gg[:], ALU.mult, r=["on", "gg"], w=["odn"])
                    for j in range(4):
                        TR(pT[:, j, :], odn[:, j * 128:(j + 1) * 128], ident[:, :], r=["odn", "ident"], w=["p5T"])
                    CP("act", dnT[:], pT[:, :, :], r=[], w=["p5T", "dnT"])
                    ih = ti % 2
                    for nh in range(2):
                        csl = slice(512 * nh, 512 * (nh + 1))
                        for j in range(4):
                            MM(pH[nh][:, :], at[b][:, j, :], wout[:, j, csl], j == 0, False, r=[("at", b), ("wout", j)], w=[("p5H", nh)])
                        for j in range(4):
                            MM(pH[nh][:, :], dnT[:, j, :], wout[:, 4 + j, csl], False, j == 3, r=["dnT", ("wout", 4 + j)], w=[("p5H", nh)])
                        TT("dve", ht[ih][:, csl], pH[nh][:, :], xt[b][:, csl], ALU.add, r=[("xt5", b)], w=[("p5H", nh), ("ht5", ih)])
                    DMA("sp", H_d[t0:t0 + 128, :], ht[ih][:], r=[("ht5", ih)], w=[("H_d", u)])
            P.barrier()

        def pass6(hbuf, hkeys_fn):
            with contextlib.ExitStack() as es:
                def sb(name, shape, dt=F32):
                    return es.enter_context(nc.sbuf_tensor("p6_" + name, list(shape), dt))

                def ps(name, shape, dt=F32):
                    return es.enter_context(nc.psum_tensor("p6_" + name, list(shape), dt))
                wup = sb("wup", [128, 8, 2 * FFN], BF16)
                wdn = sb("wdn", [128, NFC, D], BF16)
                g2 = sb("g2", [128, 8])
                cw = sb("cw", [128, 3, 44])
                cb = sb("cb", [128, 44])
                hsc = sb("hsc", [2, NU])
                DMA("sp", g2[:], norm2_d[:, :], w=["g2"])
                DMA("sp", cw[:], cw_d[:, :, :], w=["cw"])
                DMA("sp", cb[:], cb_d[:, :], w=["cb"])
                DMA("sp", hsc[:], halo_sc_d[:, :], w=["hsc"])
                for kc in range(8):
                    DMA("pool", wup[:, kc, :], w_up_d[kc * 128:(kc + 1) * 128, :], w=[("wup", kc)])
                    TS("dve", wup[:, kc, :], wup[:, kc, :], g2[:, kc:kc + 1], None, ALU.mult, None, r=["g2", ("wup", kc)], w=[("wup", kc)])
                for j in range(NFC):
                    DMA("pool", wdn[:, j, :], w_down_d[j * 128:(j + 1) * 128, :], w=[("wdn", j)])
                ht = sb("ht", [128, 4, D])
                hh = sb("hh", [2, D], BF16)
                hn = sb("hn", [128, D], BF16)
                hhn = sb("hhn", [2, D], BF16)
                ms = [sb(f"ms{i}", [128, 1]) for i in range(2)]
                hnTs = [sb(f"hnT{i}", [128, 8, 514], BF16) for i in range(2)]
                actT = sb("actT", [128, NFC, 512], BF16)
                accg = [sb(f"accg{i}", [128, 256]) for i in range(2)]
                accu = [sb(f"accu{i}", [128, 256]) for i in range(2)]
                sg = [sb(f"sg{i}", [128, 256], BF16) for i in range(2)]
                yt = [sb(f"yt{i}", [128, D]) for i in range(2)]
                pT = [ps(f"pT{i}", [128, 8, 128], BF16) for i in range(2)]
                pU = [ps(f"pU{i}", [128, 512]) for i in range(4)]
                pD = [ps(f"pD{i}", [128, 512]) for i in range(2)]
                ucount = [0]
                dcount = [0]

                def load_h(u):
                    t0 = u * 512
                    DMA("sp", ht[:], hbuf[t0:t0 + 512, :].rearrange("(t p) d -> p t d", p=128), r=hkeys_fn(u), w=["ht"])
                    lo = max(t0 - 1, 0)
                    hi = min(t0 + 512, NTOK - 1)
                    DMA("pool", hh[0:1, :], hbuf[lo:lo + 1, :], r=hkeys_fn(max(u - 1, 0)), w=["hh"])
                    DMA("pool", hh[1:2, :], hbuf[hi:hi + 1, :], r=hkeys_fn(min(u + 1, NU - 1)), w=["hh"])

                ndef = []

                def norm_piece(u, piece, defer=None):
                    hnT = hnTs[u % 2]
                    hk = ("hnT", u % 2)
                    if piece == 0:
                        def dst_halo():
                            CP("dve", hnT[:, :, 0:514:513], pT[1][:, :, 0:2], r=[], w=["pT1", hk])
                        rmsnorm_T(None, hh[:, :], 2, ms[1], hhn, hhn, pT[1], "pT1", dst_halo, ["hh"], "p6h",
                                  scale_ap=hsc[:, u:u + 1], scale_key="hsc", junk_key="p6hhn", defer=defer)
                    else:
                        t = piece - 1

                        def dst_main(t=t):
                            CP("act", hnT[:, :, 1 + 128 * t:1 + 128 * (t + 1)], pT[0][:, :, :], r=[], w=["pT0", hk])
                        rmsnorm_T(None, ht[:, t, :], 128, ms[0], hn, hn, pT[0], "pT0", dst_main, ["ht"], "p6m", junk_key="p6mhn", defer=defer)
                load_h(0)
                for piece in range(5):
                    norm_piece(0, piece)
                for u in range(NU):
                    t0 = u * 512
                    hnT = hnTs[u % 2]
                    hk = ("hnT", u % 2)
                    if u + 1 < NU:
                        load_h(u + 1)
                    pair = 0
                    for j in range(NFC):
                        for wdw in range(2):
                            ig = (ucount[0] * 2) % 4
                            iu = (ucount[0] * 2 + 1) % 4
                            ia = ucount[0] % 2
                            ucount[0] += 1
                            for (ip, fc) in ((ig, j), (iu, NFC + j)):
                                for kc in range(8):
                                    MM(pU[ip][:, 0:258], wup[:, kc, fc * 128:(fc + 1) * 128], hnT[:, kc, 256 * wdw:256 * wdw + 258],
                                       kc == 0, kc == 7, r=[("wup", kc), hk], w=[("pU", ip)])
                            if u + 1 < NU and pair in (4, 12, 20, 28, 36):
                                norm_piece(u + 1, (pair - 4) // 8, defer=ndef)
                            if pair in (7, 15, 23, 31, 39):
                                while ndef:
                                    ndef.pop(0)()
                            pair += 1
                            for (ip, fc, acc_, ka) in ((ig, j, accg[ia], ("accg", ia)), (iu, NFC + j, accu[ia], ("accu", ia))):
                                ACTV(acc_[:], pU[ip][:, 1:257], AF.Identity, r=["cw", "cb"], w=[("pU", ip), ka],
                                     scale=cw[:, 1, fc:fc + 1], bias=cb[:, fc:fc + 1])
                                STT("dve", acc_[:], pU[ip][:, 0:256], cw[:, 0, fc:fc + 1], acc_[:], ALU.mult, ALU.add,
                                    r=["cw", ka], w=[("pU", ip), ka])
                                STT("dve", acc_[:], pU[ip][:, 2:258], cw[:, 2, fc:fc + 1], acc_[:], ALU.mult, ALU.add,
                                    r=["cw", ka], w=[("pU", ip), ka])
                            ACTV(sg[ia][:], accg[ia][:], AF.Silu, r=[("accg", ia)], w=[("sg", ia)])
                            TT("pool", actT[:, j, 256 * wdw:256 * (wdw + 1)], sg[ia][:], accu[ia][:], ALU.mult,
                               r=[("sg", ia), ("accu", ia)], w=["actT"])
                    for t in range(4):
                        iy = (u * 4 + t) % 2
                        DMA("sp", yt[iy][:], hbuf[t0 + 128 * t:t0 + 128 * (t + 1), :], r=hkeys_fn(u), w=[("yt", iy)])
                        for nh in range(2):
                            ipd = dcount[0] % 2
                            dcount[0] += 1
                            for j in range(NFC):
                                MM(pD[ipd][:, :], actT[:, j, 128 * t:128 * (t + 1)], wdn[:, j, 512 * nh:512 * (nh + 1)],
                                   j == 0, j == NFC - 1, r=["actT", ("wdn", j)], w=[("pD", ipd)])
                            TT("dve", yt[iy][:, 512 * nh:512 * (nh + 1)], pD[ipd][:, :], yt[iy][:, 512 * nh:512 * (nh + 1)], ALU.add,
                               r=[("yt", iy)], w=[("pD", ipd), ("yt", iy)])
                        DMA("sp", ys[t0 + 128 * t:t0 + 128 * (t + 1), :], yt[iy][:], r=[("yt", iy)], w=[("ys", u, t)])
            P.barrier()

        if "1" in passes:
            pass1()
        if "2" in passes:
            pass2()
        if "P" in passes:
            pass_prep()
        dirs = [dr for dr, nm in ((0, "3"), (1, "4")) if nm in passes]
        if dirs:
            pass_dn_both(dirs)
        if "5" in passes:
            pass5()
        if "6" in passes:
            if "5" in passes:
                pass6(H_d, lambda u: [("H_d", u)])
            else:
                pass6(xs, lambda u: [])
        P.emit()
    return nc


def _dn_consts():
    NEG = -30000.0
    p = np.arange(128)[:, None]
    f = np.arange(128)[None, :]
    same = (p // 64) == (f // 64)
    c = np.zeros((128, 11, 128), np.float32)
    c[:, 0] = same & (p <= f)
    c[:, 1] = same & (p >= f)
    c[:, 2] = 1.0
    c[:, 3] = np.where(same & (f < p), 0.0, NEG)
    c[:, 4] = np.where(same & (f > p), 0.0, NEG)
    c[:, 5] = np.where(same & (f >= p), 0.0, NEG)
    c[:, 6] = np.where(same & (f <= p), 0.0, NEG)
    c[:, 7] = np.eye(128)
    c[:, 8] = (p // 64 == 0) & (f >= 0)
    c[:, 9] = (p // 64 == 1) & (f >= 0)
    c[:, 10] = same
    return c


def _dn_scales(NSEG, link):
    NT = NSEG * SEG // 128
    sc = np.ones((128, 4 * NT), np.float32)
    for ti in range(NT):
        s = ti // 16
        if ti % 16 == 0:
            sc[:, 4 * ti + 0] = link[s]
            sc[:, 4 * ti + 2] = link[s]
        if ti % 16 == 15:
            sc[:, 4 * ti + 1] = link[s + 1]
            sc[:, 4 * ti + 3] = link[s + 1]
    return sc


def host_consts(NSEG, link, pos0):
    NTOK = NSEG * SEG
    NU = NTOK // 512
    NT = NTOK // 128
    hs = np.ones((2, NU), np.float32)
    for u in range(NU):
        t0 = u * 512
        if t0 % SEG == 0:
            hs[0, u] = link[t0 // SEG]
        if (t0 + 512) % SEG == 0:
            hs[1, u] = link[(t0 + 512) // SEG]
    lv = np.ones((128, 2 * NSEG), np.float32)
    for s in range(NSEG):
        lv[0:64, 2 * s] = link[s]
        lv[64:128, 2 * s + 1] = link[s + 1]
    r_ = np.arange(128)[:, None]
    q_ = np.arange(128)[None, :]
    amask = np.concatenate([(q_ <= r_), (q_ >= r_)], axis=1).astype(np.float32)
    half = 32
    inv_freq = (1.0 / (10000.0 ** (np.arange(half, dtype=np.float32) * 2.0 / 64))).astype(np.float32)
    invf = np.tile(np.concatenate([inv_freq, inv_freq])[None, :], (128, 1)).astype(np.float32)
    phase = np.tile(np.concatenate([np.zeros(32), np.full(32, np.pi / 2)])[None, :], (128, 1)).astype(np.float32)
    pos = np.zeros((128, NT), np.float32)
    for s in range(NSEG):
        for t in range(16):
            pos[:, s * 16 + t] = pos0[s] + t * 128 + np.arange(128)
    return {"ident": np.eye(128, dtype=np.float32), "halo_sc": hs, "lv": lv, "amask": amask, "invf": invf,
            "phase": phase, "pos": pos,
            "dnc": _dn_consts(), "dnsc": _dn_scales(NSEG, link)}


def weight_maps(norm1, w_in, att_q_norm, att_k_norm, dn_conv_w, dn_a_log, dn_dt_bias, dn_out_norm, w_out, norm2,
                w_up, ffn_conv_w, ffn_conv_b, w_down):
    f = lambda a: np.asarray(a, np.float32)[0]
    qg = np.concatenate([np.tile(f(att_q_norm), 8), np.tile(f(att_k_norm), 8)])
    return {
        "w_in": np.ascontiguousarray(f(w_in)),
        "w_out": np.ascontiguousarray(f(w_out)),
        "w_up": np.ascontiguousarray(f(w_up)),
        "w_down": np.ascontiguousarray(f(w_down)),
        "norm1": np.ascontiguousarray(f(norm1).reshape(8, 128).T),
        "norm2": np.ascontiguousarray(f(norm2).reshape(8, 128).T),
        "qkg": np.ascontiguousarray(np.tile(qg[None, :], (128, 1))),
        "alog": np.ascontiguousarray(np.tile(f(dn_a_log).reshape(1, 8), (128, 1))),
        "dtb": np.ascontiguousarray(np.tile(f(dn_dt_bias).reshape(1, 8), (128, 1))),
        "dncw": np.ascontiguousarray(f(dn_conv_w).reshape(5, 12, 128).transpose(2, 0, 1)),
        "dnog": np.ascontiguousarray(np.tile(f(dn_out_norm)[None, :], (128, 4))),
        "ffn_cw": np.ascontiguousarray(f(ffn_conv_w).reshape(3, 44, 128).transpose(2, 0, 1)),
        "ffn_cb": np.ascontiguousarray(f(ffn_conv_b).reshape(44, 128).T),
    }


SAMPLE_SLOTS = [6, 6, 5, 5, 5, 5]
_NC_CACHE = {}


def _core_layout():
    lay = []
    for b in range(2):
        lay.append([("p", b, s) for s in range(8)])
    nxt = 0
    for n in SAMPLE_SLOTS:
        row = []
        for i in range(8):
            if i < n:
                row.append(("s", nxt, 0))
                nxt += 1
            else:
                row.append(None)
        lay.append(row)
    return lay


def kernel(x_prompt, x_sample, norm1, w_in, att_q_norm, att_k_norm, dn_conv_w, dn_a_log, dn_dt_bias,
           dn_out_norm, w_out, norm2, w_up, ffn_conv_w, ffn_conv_b, w_down):
    NSEG = 8
    lay = _core_layout()
    if "nc" not in _NC_CACHE:
        _NC_CACHE["nc"] = build_program(NSEG=NSEG)
    nc = _NC_CACHE["nc"]
    x_prompt = np.asarray(x_prompt, np.float32)
    x_sample = np.asarray(x_sample, np.float32)
    common = weight_maps(norm1, w_in, att_q_norm, att_k_norm, dn_conv_w, dn_a_log, dn_dt_bias, dn_out_norm, w_out,
                         norm2, w_up, ffn_conv_w, ffn_conv_b, w_down)
    in_maps = []
    for c in range(NCORES):
        xs = np.zeros((NSEG * SEG, D), np.float32)
        link = np.zeros(NSEG + 1, np.float32)
        pos0 = np.zeros(NSEG, np.float32)
        for i, ent in enumerate(lay[c]):
            if ent is None:
                continue
            kind, b, s = ent
            if kind == "p":
                xs[i * SEG:(i + 1) * SEG] = x_prompt[b, s * SEG:(s + 1) * SEG]
                pos0[i] = s * SEG
                if s > 0:
                    link[i] = 1.0
            else:
                xs[i * SEG:(i + 1) * SEG] = x_sample[b]
        m = dict(common)
        m.update(host_consts(NSEG, link, pos0))
        m["xs"] = xs
        in_maps.append(m)
    res = run_bass_kernel_spmd(nc, in_maps, core_ids=list(range(NCORES)))
    y_prompt = np.zeros_like(x_prompt)
    y_sample = np.zeros_like(x_sample)
    for c in range(NCORES):
        ys = res.results[c]["ys"]
        for i, ent in enumerate(lay[c]):
            if ent is None:
                continue
            kind, b, s = ent
            if kind == "p":
                y_prompt[b, s * SEG:(s + 1) * SEG] = ys[i * SEG:(i + 1) * SEG]
            else:
                y_sample[b] = ys[i * SEG:(i + 1) * SEG]
    return (y_prompt, y_sample)
```

```python
import numpy as np
import concourse.bass as bass
import concourse.mybir as mybir
from concourse.bass_utils import run_bass_kernel_spmd

F32 = mybir.dt.float32
BF16 = mybir.dt.bfloat16
I32 = mybir.dt.int32
AF = mybir.ActivationFunctionType
ALU = mybir.AluOpType
AX = mybir.AxisListType

D = 1024
FFN = 2816
NFC = FFN // 128
SEG = 2048
NCORES = 8
EPS = 1e-6


class _Op:
    __slots__ = ("eng", "fn", "deps", "sig", "is_dma", "dsem", "dval", "need_sig", "idx")


class Prog:
    ENGS = ("sp", "act", "dve", "pool", "pe")

    def __init__(self, nc, n_dma_sems=8):
        self.nc = nc
        self.ops = []
        self.last_w = {}
        self.readers = {}
        self.n_dma_sems = n_dma_sems
        self.dma_rr = {"sp": 0, "pool": 0, "act": 0}
        self.dma_last = {}
        self.dma_cnt = {}
        self.extra = {}
        self.ns = None
        self.shared = set()
        self.last_op = {}

    def _k(self, k):
        if self.ns is None or k in self.shared or (isinstance(k, tuple) and k[0] in self.shared):
            return k
        return (self.ns, k)

    def op(self, eng, fn, r=(), w=(), dma=False):
        if self.ns is not None:
            r = [self._k(k) for k in r]
            w = [self._k(k) for k in w]
        o = _Op()
        o.eng = eng; o.fn = fn; o.is_dma = dma; o.need_sig = dma; o.sig = None
        o.idx = len(self.ops)
        deps = list(self.extra.pop(eng, []))
        for k in r:
            p = self.last_w.get(k)
            if p is not None:
                deps.append(p)
        for k in w:
            p = self.last_w.get(k)
            if p is not None:
                deps.append(p)
            for q in self.readers.get(k, ()):
                deps.append(q)
        if dma:
            j = self.dma_rr[eng]
            self.dma_rr[eng] = (j + 1) % self.n_dma_sems
            key = (eng, j)
            prev = self.dma_last.get(key)
            if prev is not None:
                deps.append(prev)
            self.dma_last[key] = o
            self.dma_cnt[key] = self.dma_cnt.get(key, 0) + 1
            o.dsem = key
            o.dval = 16 * self.dma_cnt[key]
        dd = []
        seen = set()
        for p in deps:
            if p is o or id(p) in seen:
                continue
            if eng == "pe" and p.eng == "pe" and not p.is_dma:
                continue
            seen.add(id(p))
            dd.append(p)
            p.need_sig = True
        o.deps = dd
        for k in w:
            self.last_w[k] = o
            self.readers[k] = []
        for k in r:
            self.readers.setdefault(k, []).append(o)
        self.ops.append(o)
        if not dma:
            self.last_op[eng] = o
        return o

    def barrier(self):
        markers = [o for o in self.last_op.values()] + [o for o in self.dma_last.values()]
        for e in self.ENGS:
            self.extra[e] = list(markers) + self.extra.get(e, [])

    def emit(self, final_wait_eng="sp"):
        nc = self.nc
        cnt = {e: 0 for e in self.ENGS}
        for o in self.ops:
            if o.is_dma:
                o.sig = (("dma",) + o.dsem, o.dval)
            elif o.need_sig:
                cnt[o.eng] += 1
                o.sig = (("eng", o.eng), cnt[o.eng])
        sem_keys = [("eng", e) for e in self.ENGS] + [("dma", q, j) for q in ("sp", "pool", "act") for j in range(self.n_dma_sems)]
        per_eng = {e: [o for o in self.ops if o.eng == e] for e in self.ENGS}
        finals = {}
        for o in self.ops:
            if o.is_dma:
                finals[o.sig[0]] = max(finals.get(o.sig[0], 0), o.sig[1])
        import contextlib
        with contextlib.ExitStack() as st:
            sems = {}
            for k in sem_keys:
                sems[k] = st.enter_context(nc.semaphore("s_" + "_".join(str(x) for x in k)))
            block = st.enter_context(nc.Block())

            def replay(e, eobj):
                known = {}
                for o in per_eng[e]:
                    need = {}
                    for p in o.deps:
                        sk, v = p.sig
                        if v > need.get(sk, 0):
                            need[sk] = v
                    for sk, v in need.items():
                        if known.get(sk, 0) < v:
                            eobj.wait_ge(sems[sk], v)
                            known[sk] = v
                    ins = o.fn(eobj)
                    if o.sig is not None:
                        ins.then_inc(sems[o.sig[0]], 16 if o.is_dma else 1)
                if e == final_wait_eng:
                    for sk, v in finals.items():
                        if known.get(sk, 0) < v:
                            eobj.wait_ge(sems[sk], v)

            @block.sync
            def _(e):
                replay("sp", e)

            @block.scalar
            def _(e):
                replay("act", e)

            @block.vector
            def _(e):
                replay("dve", e)

            @block.gpsimd
            def _(e):
                replay("pool", e)

            @block.tensor
            def _(e):
                replay("pe", e)


import contextlib

IN_COLS = 3600
KPAD = 1024
DQPAD = 2
TWO_PI = float(2 * np.pi)


def _sl(start, count, step):
    return slice(start, start + (count - 1) * step + 1, step)


def build_program(NSEG=8, passes=("1", "2", "P", "3", "4", "5", "6"), dbg=False):
    nc = bass.Bass("TRN2", target_bir_lowering=False)
    NTOK = NSEG * SEG
    NT = NTOK // 128
    NU = NTOK // 512
    P = Prog(nc)

    def din(name, shape, dt=F32):
        return nc.dram_tensor(name, list(shape), dt, kind="ExternalInput").ap()

    def dout(name, shape, dt=F32):
        return nc.dram_tensor(name, list(shape), dt, kind="ExternalOutput").ap()

    def dscr(name, shape, dt=F32):
        return nc.dram_tensor(name, list(shape), dt, kind=("ExternalOutput" if dbg else "Internal")).ap()

    xs = din("xs", [NTOK, D])
    ident_d = din("ident", [128, 128])
    w_in_d = din("w_in", [D, IN_COLS])
    w_out_d = din("w_out", [D, D])
    w_up_d = din("w_up", [D, 2 * FFN])
    w_down_d = din("w_down", [FFN, D])
    norm1_d = din("norm1", [128, 8])
    norm2_d = din("norm2", [128, 8])
    qkg_d = din("qkg", [128, 1024])
    invf_d = din("invf", [128, 64])
    phase_d = din("phase", [128, 64])
    pos_d = din("pos", [128, NT])
    alog_d = din("alog", [128, 8])
    dtb_d = din("dtb", [128, 8])
    dncw_d = din("dncw", [128, 5, 12])
    dnog_d = din("dnog", [128, 512])
    cw_d = din("ffn_cw", [128, 3, 44])
    cb_d = din("ffn_cb", [128, 44])
    halo_sc_d = din("halo_sc", [2, NU])
    lv_d = din("lv", [128, 2 * NSEG])
    amask_d = din("amask", [128, 256])
    dnc_d = din("dnc", [128, 11, 128])
    dnsc_d = din("dnsc", [128, 4 * NT])
    ys = dout("ys", [NTOK, D])
    KW = NTOK + 2 * KPAD
    QT_d = dscr("QT_s", [4, 128, KW], BF16)
    KT_d = dscr("KT_s", [4, 128, KW], BF16)
    V_d = dscr("V_s", [KW, 512], BF16)
    DQ_d = dscr("DQ_s", [1536, NTOK + 2 * DQPAD], BF16)
    DN_d = dscr("DN_s", [1536, NTOK], BF16)
    G_d = dscr("G_s", [NTOK, 512], F32)
    GB_d = dscr("GB_s", [NTOK, 16], F32)
    AT_d = dscr("AT_s", [4, 128, NTOK], BF16)
    OD_d = [dscr("OF_s", [NTOK, 512], F32), dscr("OB_s", [NTOK, 512], F32)]
    H_d = dscr("H_s", [NTOK, D], F32)

    def MM(out, lhsT, rhs, start, stop, r, w):
        P.op("pe", lambda e, o=out, l=lhsT, rr=rhs, s=start, t=stop: e.matmul(o, lhsT=l, rhs=rr, start=s, stop=t), r=r, w=w)

    def TR(out, in_, idn, r, w):
        P.op("pe", lambda e, o=out, i=in_, d=idn: e.transpose(out=o, in_=i, identity=d), r=r, w=w)

    def ACTV(out, in_, func, r, w, **kw):
        P.op("act", lambda e, o=out, i=in_, f=func, kw=kw: e.activation(out=o, in_=i, func=f, **kw), r=r, w=w)

    def TT(eng, out, in0, in1, op, r, w):
        P.op(eng, lambda e, o=out, a=in0, b=in1, p=op: e.tensor_tensor(out=o, in0=a, in1=b, op=p), r=r, w=w)

    def TS(eng, out, in0, s1, s2, op0, op1, r, w):
        if s2 is None:
            P.op(eng, lambda e, o=out, a=in0, x=s1, p0=op0: e.tensor_scalar(out=o, in0=a, scalar1=x, scalar2=None, op0=p0), r=r, w=w)
        else:
            P.op(eng, lambda e, o=out, a=in0, x=s1, y=s2, p0=op0, p1=op1: e.tensor_scalar(out=o, in0=a, scalar1=x, scalar2=y, op0=p0, op1=p1), r=r, w=w)

    def STT(eng, out, in0, scalar, in1, op0, op1, r, w):
        P.op(eng, lambda e, o=out, a=in0, sc=scalar, b=in1, p0=op0, p1=op1: e.scalar_tensor_tensor(out=o, in0=a, scalar=sc, in1=b, op0=p0, op1=p1), r=r, w=w)

    def CP(eng, out, in_, r, w):
        if eng == "act":
            P.op("act", lambda e, o=out, i=in_: e.copy(out=o, in_=i), r=r, w=w)
        else:
            P.op(eng, lambda e, o=out, i=in_: e.tensor_copy(out=o, in_=i), r=r, w=w)

    def MSET(eng, ap, val, w):
        P.op(eng, lambda e, a=ap, v=val: e.memset(a, v), w=w)

    def DMA(q, out, in_, r=(), w=()):
        P.op(q, lambda e, o=out, i=in_: e.dma_start(out=o, in_=i), r=r, w=w, dma=True)

    def RED(eng, out, in_, r, w):
        P.op(eng, lambda e, o=out, i=in_: e.tensor_reduce(out=o, in_=i, axis=AX.X, op=ALU.add), r=r, w=w)

    def RECIP(out, in_, r, w):
        P.op("dve", lambda e, o=out, i=in_: e.reciprocal(out=o, in_=i), r=r, w=w)

    with contextlib.ExitStack() as es0:
        def sb0(name, shape, dt=F32):
            return es0.enter_context(nc.sbuf_tensor(name, list(shape), dt))
        ident_f = sb0("ident_f", [128, 128])
        ident = sb0("ident_b", [128, 128], BF16)
        epsT = sb0("epsT", [128, 1])
        oneT = sb0("oneT", [128, 1])
        DMA("sp", ident_f[:], ident_d[:, :], w=["ident_f"])
        CP("dve", ident[:], ident_f[:], r=["ident_f"], w=["ident"])
        MSET("dve", epsT[:], EPS, w=["eps"])
        MSET("dve", oneT[:], 1.0, w=["one"])

        def rmsnorm_T(es_sb, src_ap, npart, msT, hnb, sq, pTt, pkey, dst_fn, rkeys, keyp, scale_ap=None, scale_key=None, junk_key="sq",
                      defer=None):
            ACTV(sq[0:npart, :], src_ap, AF.Square, r=rkeys, w=[junk_key, keyp + "ms"], scale=1.0 / 32, accum_out=msT[0:npart, :])
            ACTV(msT[0:npart, :], msT[0:npart, :], AF.Ln, r=[keyp + "ms", "eps"], w=[keyp + "ms"], bias=epsT[0:npart, :])
            ACTV(msT[0:npart, :], msT[0:npart, :], AF.Exp, r=[keyp + "ms"], w=[keyp + "ms"], scale=-0.5)
            if scale_ap is not None:
                TT("dve", msT[0:npart, :], msT[0:npart, :], scale_ap, ALU.mult, r=[keyp + "ms", scale_key], w=[keyp + "ms"])
            TS("dve", hnb[0:npart, :], src_ap, msT[0:npart, 0:1], None, ALU.mult, None, r=rkeys + [keyp + "ms"], w=[keyp + "hn"])
            def second():
                for kc in range(8):
                    TR(pTt[:, kc, 0:npart], hnb[0:npart, kc * 128:(kc + 1) * 128], ident[0:npart, 0:npart],
                       r=[keyp + "hn", "ident"], w=[pkey])
                dst_fn()
            if defer is None:
                second()
            else:
                defer.append(second)

        def pass1():
            with contextlib.ExitStack() as es:
                def sb(name, shape, dt=F32):
                    return es.enter_context(nc.sbuf_tensor("p1_" + name, list(shape), dt))

                def ps(name, shape, dt=F32):
                    return es.enter_context(nc.psum_tensor("p1_" + name, list(shape), dt))
                win = sb("win", [128, 8, IN_COLS], BF16)
                g1 = sb("g1", [128, 8])
                qkg = sb("qkg", [128, 1024])
                invf = sb("invf", [128, 64])
                phase = sb("phase", [128, 64])
                post = sb("post", [128, NT])
                nexpA = sb("nexpA", [128, 8])
                dtb = sb("dtb", [128, 8])
                zb = sb("zb", [128, 2, 512], BF16)
                zf = sb("zf", [128, 12, DQPAD], BF16)
                DMA("sp", g1[:], norm1_d[:, :], w=["g1"])
                DMA("sp", qkg[:], qkg_d[:, :], w=["qkg"])
                DMA("sp", invf[:], invf_d[:, :], w=["invf"])
                DMA("sp", phase[:], phase_d[:, :], w=["phase"])
                DMA("sp", post[:], pos_d[:, :], w=["post"])
                DMA("sp", nexpA[:], alog_d[:, :], w=["nexpA"])
                DMA("sp", dtb[:], dtb_d[:, :], w=["dtb"])
                ACTV(nexpA[:], nexpA[:], AF.Exp, r=["nexpA"], w=["nexpA"])
                TS("dve", nexpA[:], nexpA[:], -1.0, None, ALU.mult, None, r=["nexpA"], w=["nexpA"])
                MSET("pool", zb[:], 0.0, w=["zb"])
                MSET("pool", zf[:], 0.0, w=["zf"])
                for j in range(4):
                    DMA("sp", KT_d[j, :, 0:KPAD], zb[:, 0:2, :].rearrange("p a b -> p (a b)"), r=["zb"], w=[("KT_d", "padL")])
                    DMA("sp", KT_d[j, :, KPAD + NTOK:KW], zb[:, 0:2, :].rearrange("p a b -> p (a b)"), r=["zb"], w=[("KT_d", "padR")])
                for q4 in range(4):
                    DMA("sp", V_d[256 * q4:256 * (q4 + 1), :].rearrange("(t p) c -> p t c", p=128), zb[:], r=["zb"], w=[("V_d", "padL")])
                    DMA("sp", V_d[KPAD + NTOK + 256 * q4:KPAD + NTOK + 256 * (q4 + 1), :].rearrange("(t p) c -> p t c", p=128), zb[:],
                        r=["zb"], w=[("V_d", "padR")])
                dqv = DQ_d.rearrange("(f p) w -> p f w", p=128)
                DMA("sp", dqv[:, :, 0:DQPAD], zf[:], r=["zf"], w=[("DQ_d", "padL")])
                DMA("sp", dqv[:, :, DQPAD + NTOK:DQPAD + NTOK + DQPAD], zf[:], r=["zf"], w=[("DQ_d", "padR")])
                for kc in range(8):
                    DMA("pool", win[:, kc, :], w_in_d[kc * 128:(kc + 1) * 128, :], w=[("win", kc)])
                    TS("dve", win[:, kc, :], win[:, kc, :], g1[:, kc:kc + 1], None, ALU.mult, None, r=["g1", ("win", kc)], w=[("win", kc)])
                xt = [sb(f"xt{i}", [128, 4, D]) for i in range(2)]
                nTs = [sb(f"nT{i}", [128, 8, 512], BF16) for i in range(2)]
                hn = sb("hn", [128, D], BF16)
                sq = sb("sq", [128, D])
                ms = sb("ms", [128, 1])
                dqs = sb("dqs", [128, 12, 512], BF16)
                vb = sb("vb", [128, 4, 512], BF16)
                gs = sb("gs", [128, 4, 512])
                gbt = sb("gbt", [128, 4, 16])
                zt = sb("zt", [128, 8])
                bt8 = sb("bt8", [128, 8])
                qraw = sb("qraw", [128, 1024])
                qn = sb("qn", [128, 1024])
                t1 = sb("t1", [128, 512]); t2 = sb("t2", [128, 512]); t3 = sb("t3", [128, 512]); t4 = sb("t4", [128, 512])
                qrs = [sb(f"qr{i}", [128, 1024], BF16) for i in range(2)]
                qkTs = [sb(f"qkT{i}", [128, 8, 512], BF16) for i in range(2)]
                ssq = sb("ssq", [128, 16])
                ang16 = sb("ang16", [128, 16, 64]); kfi16 = sb("kfi16", [128, 16, 64], I32); kff16 = sb("kff16", [128, 16, 64])
                cs16 = sb("cs16", [128, 16, 64])
                deferred = []
                pTT = ps("pTT", [128, 8, 128], BF16)
                pFs = [ps(f"pF{i}", [128, 512]) for i in range(2)]
                pQ = ps("pQ", [128, 512]); pK = ps("pK", [128, 512]); pV = ps("pV", [128, 512])
                pG = ps("pG", [128, 512]); pB = ps("pB", [128, 512])
                fcnt = [0]

                def norm_piece(u, t):
                    b = u % 2

                    def dst(t=t, b=b):
                        CP("act", nTs[b][:, :, 128 * t:128 * (t + 1)], pTT[:, :, :], r=[], w=["pTT", ("nT", b)])
                    rmsnorm_T(None, xt[b][:, t, :], 128, ms, hn, sq, pTT, "pTT", dst, [("xt", b)], "p1")
                def load_x(u):
                    DMA("sp", xt[u % 2][:], xs[u * 512:u * 512 + 512, :].rearrange("(t p) d -> p t d", p=128), w=[("xt", u % 2)])
                load_x(0)
                for t in range(4):
                    norm_piece(0, t)
                for u in range(NU):
                    b = u % 2
                    if u + 1 < NU:
                        load_x(u + 1)
                    nT = nTs[b]
                    nTk = ("nT", b)
                    t0 = u * 512
                    for fc in range(12):
                        c0 = 1536 + fc * 128
                        pF = pFs[fcnt[0] % 2]; pFk = ("pF", fcnt[0] % 2); fcnt[0] += 1
                        for kc in range(8):
                            MM(pF[:, :], win[:, kc, c0:c0 + 128], nT[:, kc, :], kc == 0, kc == 7, r=[("win", kc), nTk], w=[pFk])
                        CP("act", dqs[:, fc, :], pF[:, :], r=[], w=[pFk, "dqs"])
                    DMA("sp", dqv[:, :, DQPAD + t0:DQPAD + t0 + 512], dqs[:], r=["dqs"], w=[("DQ_d", u)])
                    for t in range(4):
                        ti = u * 4 + t
                        for (pp, key, c0, n) in ((pQ, "pQ", 0, 512), (pK, "pK", 512, 512), (pV, "pV", 1024, 512),
                                                 (pG, "pG", 3072, 512), (pB, "pB", 3584, 16)):
                            for kc in range(8):
                                MM(pp[:, 0:n], nT[:, kc, 128 * t:128 * (t + 1)], win[:, kc, c0:c0 + n], kc == 0, kc == 7,
                                   r=[("win", kc), nTk], w=[key])
                        while deferred:
                            deferred.pop(0)()
                        if u + 1 < NU:
                            norm_piece(u + 1, t)
                        CP("act", qraw[:, 0:512], pQ[:, :], r=[], w=["pQ", "qraw"])
                        CP("dve", qraw[:, 512:1024], pK[:, :], r=[], w=["pK", "qraw"])
                        CP("act", vb[:, t, :], pV[:, :], r=[], w=["pV", "vb"])
                        CP("act", gs[:, t, :], pG[:, :], r=[], w=["pG", "gs"])
                        ACTV(bt8[:], pB[:, 0:8], AF.Exp, r=[], w=["pB", "bt8"], scale=-1.0)
                        TS("dve", bt8[:], bt8[:], 1.0, None, ALU.add, None, r=["bt8"], w=["bt8"])
                        RECIP(gbt[:, t, 8:16], bt8[:], r=["bt8"], w=["gbt"])
                        TT("dve", zt[:], pB[:, 8:16], dtb[:], ALU.add, r=["dtb"], w=["pB", "zt"])
                        ACTV(zt[:], zt[:], AF.Exp, r=["zt"], w=["zt"])
                        ACTV(zt[:], zt[:], AF.Ln, r=["zt", "one"], w=["zt"], bias=oneT[:])
                        TT("dve", gbt[:, t, 0:8], zt[:], nexpA[:], ALU.mult, r=["zt", "nexpA"], w=["gbt"])
                        ACTV(sq[:], qraw[:], AF.Square, r=["qraw"], w=["sq"])
                        RED("dve", ssq[:], sq[:].rearrange("p (h d) -> p h d", d=64), r=["sq"], w=["ssq"])
                        ACTV(ssq[:], ssq[:], AF.Ln, r=["ssq", "eps"], w=["ssq"], scale=1.0 / 64, bias=epsT[:])
                        ACTV(ssq[:], ssq[:], AF.Exp, r=["ssq"], w=["ssq"], scale=-0.5)
                        TT("dve", qn[:].rearrange("p (h d) -> p h d", d=64), qraw[:].rearrange("p (h d) -> p h d", d=64),
                           ssq[:].unsqueeze(2).to_broadcast([128, 16, 64]), ALU.mult, r=["ssq", "qraw"], w=["qn"])
                        TT("dve", qn[:], qn[:], qkg[:], ALU.mult, r=["qn", "qkg"], w=["qn"])
                        if ti % 16 == 0:
                            for tt in range(16):
                                STT("dve", ang16[:, tt, :], invf[:], post[:, ti + tt:ti + tt + 1], phase[:], ALU.mult, ALU.add,
                                    r=["invf", "post", "phase"], w=["ang16"])
                            TS("dve", kfi16[:], ang16[:], 1.0 / TWO_PI, None, ALU.mult, None, r=["ang16"], w=["kfi16"])
                            CP("dve", kff16[:], kfi16[:], r=["kfi16"], w=["kff16"])
                            STT("dve", ang16[:], kff16[:], -TWO_PI, ang16[:], ALU.mult, ALU.add, r=["kff16", "ang16"], w=["ang16"])
                            TS("dve", ang16[:], ang16[:], -float(np.pi), float(np.pi), ALU.max, ALU.min, r=["ang16"], w=["ang16"])
                            ACTV(cs16[:], ang16[:], AF.Sin, r=["ang16"], w=["cs16"])
                        qr = qrs[ti % 2]
                        qrk = ("qr", ti % 2)
                        cs = cs16[:, ti % 16, :]
                        qv = qn[:].rearrange("p (h d) -> p h d", d=64)
                        qrv = qr[:].rearrange("p (h d) -> p h d", d=64)
                        sinb = cs[:, 0:32].unsqueeze(1).to_broadcast([128, 16, 32])
                        cosb = cs[:, 32:64].unsqueeze(1).to_broadcast([128, 16, 32])
                        v3 = lambda tt: tt[:].rearrange("p (h d) -> p h d", d=32)
                        TT("dve", v3(t1), qv[:, :, 0:32], cosb, ALU.mult, r=["qn", "cs16"], w=["t1"])
                        TT("dve", v3(t2), qv[:, :, 32:64], sinb, ALU.mult, r=["qn", "cs16"], w=["t2"])
                        TT("dve", qrv[:, :, 0:32], v3(t1), v3(t2), ALU.subtract, r=["t1", "t2"], w=[qrk])
                        TT("pool", v3(t3), qv[:, :, 32:64], cosb, ALU.mult, r=["qn", "cs16"], w=["t3"])
                        TT("pool", v3(t4), qv[:, :, 0:32], sinb, ALU.mult, r=["qn", "cs16"], w=["t4"])
                        TT("pool", qrv[:, :, 32:64], v3(t3), v3(t4), ALU.add, r=["t3", "t4"], w=[qrk])

                        def tr_stage(qr=qr, qrk=qrk, t=t, qkT=qkTs[u % 2], qkk=("qkT", u % 2)):
                            for j in range(8):
                                TR(pTT[:, j, :], qr[:, j * 128:(j + 1) * 128], ident[:, :], r=[qrk, "ident"], w=["pTT"])
                            CP("act", qkT[:, :, 128 * t:128 * (t + 1)], pTT[:, :, :], r=[], w=["pTT", qkk])
                        deferred.append(tr_stage)
                    while deferred:
                        deferred.pop(0)()
                    qkT = qkTs[u % 2]
                    DMA("sp", V_d[KPAD + t0:KPAD + t0 + 512, :].rearrange("(t p) c -> p t c", p=128), vb[:], r=["vb"], w=[("V_d", u)])
                    DMA("sp", G_d[t0:t0 + 512, :].rearrange("(t p) c -> p t c", p=128), gs[:], r=["gs"], w=[("G_d", u)])
                    DMA("sp", GB_d[t0:t0 + 512, :].rearrange("(t p) c -> p t c", p=128), gbt[:], r=["gbt"], w=[("GB_d", u)])
                    DMA("sp", QT_d[:, :, KPAD + t0:KPAD + t0 + 512].rearrange("j p w -> p j w"), qkT[:, 0:4, :], r=[("qkT", u % 2)], w=[("QT_d", u)])
                    DMA("sp", KT_d[:, :, KPAD + t0:KPAD + t0 + 512].rearrange("j p w -> p j w"), qkT[:, 4:8, :], r=[("qkT", u % 2)], w=[("KT_d", u)])
            P.barrier()

        def pass2():
            with contextlib.ExitStack() as es:
                def sb(name, shape, dt=F32):
                    return es.enter_context(nc.sbuf_tensor("p2_" + name, list(shape), dt))

                def ps(name, shape, dt=F32):
                    return es.enter_context(nc.psum_tensor("p2_" + name, list(shape), dt))
                lv = sb("lv", [128, 2 * NSEG])
                am_f = sb("am_f", [128, 256])
                am = sb("am", [128, 256], BF16)
                amL = sb("amL", [128, 256], BF16); amR = sb("amR", [128, 256], BF16); amLR = sb("amLR", [128, 256], BF16)
                ones_b = sb("ones_b", [128, 128])
                DMA("sp", lv[:], lv_d[:, :], w=["lv"])
                DMA("sp", am_f[:], amask_d[:, :], w=["am_f"])
                CP("dve", am[:], am_f[:], r=["am_f"], w=["am"])
                MSET("dve", ones_b[:], 1.0, w=["ones_b"])
                QTs = [sb(f"QTs{i}", [128, 4, SEG], BF16) for i in range(1)]
                KTw = [sb(f"KTw{i}", [128, 4, 2 * SEG], BF16) for i in range(1)]
                acc = [sb(f"acc{e}", [128, 4, SEG]) for e in range(2)]
                attT = sb("attT", [128, 4, SEG], BF16)
                NVB = 8
                vraw = [sb(f"vraw{i}", [128, 512], BF16) for i in range(NVB)]
                vt2 = [sb(f"vt2_{i}", [128, 4, 2, 128], BF16) for i in range(NVB)]
                NPT = 4
                pt = [sb(f"pt{i}", [128, 256], BF16) for i in range(NPT)]
                rden = sb("rden", [128, SEG])
                pS = [ps(f"pS{i}", [128, 512]) for i in range(3)]
                pA = [ps(f"pA{i}", [128, 512]) for i in range(3)]
                pBc = [ps(f"pBc{i}", [128, 512]) for i in range(2)]
                for i in range(NVB):
                    MSET("pool", vt2[i][:], 0.0, w=[("vt2", i)])
                    MSET("pool", vt2[i][:, :, 0, 64:65], 1.0, w=[("vt2", i)])
                    MSET("pool", vt2[i][:, :, 1, 32:33], 1.0, w=[("vt2", i)])
                cnt = {"v": 0, "s": 0, "a": 0, "pt": 0, "m": 0, "bc": 0}
                import collections
                pending = collections.deque()
                SKEW = 2
                for s in range(NSEG):
                    b = 0
                    base = KPAD + s * SEG
                    DMA("sp", QTs[b][:], QT_d[:, :, base:base + SEG].rearrange("j p w -> p j w"),
                        r=[("QT_d", u) for u in range(4 * s, 4 * s + 4)], w=[("QTs", b)])
                    ulo = max(0, 4 * s - 2); uhi = min(NU, 4 * s + 6)
                    DMA("sp", KTw[b][:], KT_d[:, :, base - 1024:base + SEG + 1024].rearrange("j p w -> p j w"),
                        r=[("KT_d", u) for u in range(ulo, uhi)] + [("KT_d", "padL"), ("KT_d", "padR")], w=[("KTw", b)])
                    TS("pool", amL[:, 0:128], am[:, 0:128], lv[:, 2 * s:2 * s + 1], None, ALU.mult, None, r=["am", "lv"], w=["amL"])
                    CP("pool", amL[:, 128:256], am[:, 128:256], r=["am"], w=["amL"])
                    CP("pool", amR[:, 0:128], am[:, 0:128], r=["am"], w=["amR"])
                    TS("pool", amR[:, 128:256], am[:, 128:256], lv[:, 2 * s + 1:2 * s + 2], None, ALU.mult, None, r=["am", "lv"], w=["amR"])
                    CP("pool", amLR[:, 0:128], amL[:, 0:128], r=["amL"], w=["amLR"])
                    CP("pool", amLR[:, 128:256], amR[:, 128:256], r=["amR"], w=["amLR"])
                    vkeys = [("V_d", u) for u in range(ulo, uhi)] + [("V_d", "padL"), ("V_d", "padR")]
                    groups = []
                    for (d, first) in ((1, True), (4, False), (16, False)):
                        nqb = SEG // d // 128
                        for c in range(d):
                            for qb in range(nqb):
                                groups.append((d, first, c, qb, nqb))
                    needed = []
                    seen = set()
                    for (d, first, c, qb, nqb) in groups:
                        for k in (qb, qb + 1):
                            if (d, c, k) not in seen:
                                seen.add((d, c, k))
                                needed.append((d, c, k))
                    vslot = {}
                    nl = [0]

                    def load_v(d, c, k):
                        i = cnt["v"] % NVB
                        cnt["v"] += 1
                        row0 = base + c + d * (128 * k - 64)
                        DMA("sp", vraw[i][:], V_d[_sl(row0, 128, d), :], r=vkeys, w=[("vraw", i)])
                        vr = vraw[i][:].rearrange("p (j e x) -> p j e x", e=2, x=64)
                        CP("pool", vt2[i][:, :, 0, 0:64], vr[:, :, 0, :], r=[("vraw", i)], w=[("vt2", i)])
                        CP("pool", vt2[i][:, :, 1, 64:128], vr[:, :, 1, :], r=[("vraw", i)], w=[("vt2", i)])
                        vslot[(d, c, k)] = i

                    def ensure_loaded(gi):
                        if gi >= len(groups):
                            return
                        d, first, c, qb, nqb = groups[gi]
                        while not ((d, c, qb) in vslot and (d, c, qb + 1) in vslot):
                            load_v(*needed[nl[0]])
                            nl[0] += 1
                    PF = 2
                    for gi, (d, first, c, qb, nqb) in enumerate(groups):
                        for g2 in range(gi, gi + PF + 1):
                            ensure_loaded(g2)
                        if nqb == 1:
                            msk, mkey = amLR, "amLR"
                        elif qb == 0:
                            msk, mkey = amL, "amL"
                        elif qb == nqb - 1:
                            msk, mkey = amR, "amR"
                        else:
                            msk, mkey = am, "am"
                        qcol0 = c + d * 128 * qb
                        qsl = _sl(qcol0, 128, d)
                        for hp in range(4):
                            for e in range(2):
                                pb = 64 * e
                                iS = cnt["s"] % 3; cnt["s"] += 1
                                for half in range(2):
                                    k = qb + half
                                    kc0 = 1024 + c + d * (128 * k - 64)
                                    MM(pS[iS][:, 128 * half:128 * (half + 1)], KTw[b][pb:pb + 64, hp, _sl(kc0, 128, d)],
                                       QTs[b][pb:pb + 64, hp, qsl], True, True, r=[("KTw", b), ("QTs", b)], w=[("pS", iS)])
                                ip = cnt["pt"] % NPT; cnt["pt"] += 1
                                ACTV(pt[ip][:], pS[iS][:, 0:256], AF.Exp, r=[], w=[("pS", iS), ("pt", ip)], scale=0.125)
                                meng = "pool" if (cnt["m"] % 3 == 0) else "dve"
                                cnt["m"] += 1
                                TT(meng, pt[ip][:], pt[ip][:], msk[:], ALU.mult, r=[("pt", ip), mkey], w=[("pt", ip)])

                                def pv_stage(ip=ip, e=e, hp=hp, qsl=qsl, first=first, v0=vslot[(d, c, qb)], v1=vslot[(d, c, qb + 1)]):
                                    iA = cnt["a"] % 3; cnt["a"] += 1
                                    M = 65 if e == 0 else 128
                                    for half, vi in ((0, v0), (1, v1)):
                                        MM(pA[iA][0:M, 0:128], vt2[vi][:, hp, e, 0:M], pt[ip][:, 128 * half:128 * (half + 1)],
                                           half == 0, half == 1, r=[("vt2", vi), ("pt", ip)], w=[("pA", iA)])
                                    lo, hi = (0, 65) if e == 0 else (0, 128)
                                    dst = acc[e][lo:hi, hp, qsl]
                                    if first:
                                        CP("dve", dst, pA[iA][lo:hi, 0:128], r=[], w=[("pA", iA), ("acc", e, hp)])
                                    else:
                                        TT("dve", dst, dst, pA[iA][lo:hi, 0:128], ALU.add, r=[], w=[("pA", iA), ("acc", e, hp)])
                                pending.append(pv_stage)
                                if len(pending) > SKEW:
                                    pending.popleft()()
                    while pending:
                        pending.popleft()()
                    for hp in range(4):
                        for e in range(2):
                            dp = 64 if e == 0 else 32
                            lo, hi = (0, 64) if e == 0 else (64, 128)
                            ACTV(rden[dp:dp + 1, :], acc[e][dp:dp + 1, hp, :], AF.Ln, r=[("acc", e, hp)], w=["rden"])
                            ACTV(rden[dp:dp + 1, :], rden[dp:dp + 1, :], AF.Exp, r=["rden"], w=["rden"], scale=-1.0)
                            for cc in range(SEG // 512):
                                csl = slice(512 * cc, 512 * (cc + 1))
                                ib = cnt["bc"] % 2; cnt["bc"] += 1
                                MM(pBc[ib][0:hi, :], ones_b[dp:dp + 1, 0:hi], rden[dp:dp + 1, csl], True, True,
                                   r=["ones_b", "rden"], w=[("pBc", ib)])
                                TT("dve", attT[lo:hi, hp, csl], acc[e][lo:hi, hp, csl], pBc[ib][lo:hi, :], ALU.mult,
                                   r=[("acc", e, hp)], w=[("pBc", ib), "attT"])
                    DMA("sp", AT_d[:, :, s * SEG:(s + 1) * SEG].rearrange("j p w -> p j w"), attT[:], r=["attT"], w=[("AT_d", s)])
            P.barrier()


        def pass_prep():
            with contextlib.ExitStack() as es:
                def sb(name, shape, dt=F32):
                    return es.enter_context(nc.sbuf_tensor("pp_" + name, list(shape), dt))

                def ps(name, shape, dt=F32):
                    return es.enter_context(nc.psum_tensor("pp_" + name, list(shape), dt))
                dncw = sb("dncw", [128, 5, 12])
                dnsc = sb("dnsc", [128, 4 * NT])
                lnq = sb("lnq", [128, 1])
                dg = sb("dg", [128, 60, 128], BF16)
                ones_b = sb("ones_b", [128, 128], BF16)
                DMA("sp", dncw[:], dncw_d[:, :, :], w=["dncw"])
                DMA("sp", dnsc[:], dnsc_d[:, :], w=["dnsc"])
                MSET("dve", lnq[:], float(-0.5 * np.log(128.0)), w=["lnq"])
                MSET("dve", ones_b[:], 1.0, w=["ones_b"])
                for j in range(5):
                    for fc in range(12):
                        TS("dve", dg[:, j * 12 + fc, :], ident[:, :], dncw[:, j, fc:fc + 1], None, ALU.mult, None,
                           r=["ident", "dncw"], w=["dg"])
                dqb = [sb(f"dqb{i}", [128, 12, 516], BF16) for i in range(2)]
                qk32 = sb("qk32", [128, 8, 512])
                sqb = sb("sqb", [128, 8, 512], BF16)
                rn = sb("rn", [128, 8, 512])
                oqkv = [sb(f"oqkv{i}", [128, 12, 512], BF16) for i in range(2)]
                pC = [ps(f"pC{i}", [128, 512]) for i in range(4)]
                pNn = [ps(f"pN{i}", [128, 512]) for i in range(4)]
                dqv = DQ_d.rearrange("(f p) w -> p f w", p=128)
                dnv = DN_d.rearrange("(f p) w -> p f w", p=128)

                def load(u):
                    t0 = u * 512
                    deps = [("DQ_d", uu) for uu in range(max(0, u - 1), min(NU, u + 2))] + [("DQ_d", "padL"), ("DQ_d", "padR")]
                    DMA("sp", dqb[u % 2][:], dqv[:, :, DQPAD + t0 - 2:DQPAD + t0 + 514], r=deps, w=[("dqb", u % 2)])
                load(0)
                for u in range(NU):
                    b = u % 2
                    t0 = u * 512
                    if u + 1 < NU:
                        load(u + 1)
                    ti = u * 4
                    if t0 % SEG == 0:
                        TS("dve", dqb[b][:, :, 0:2], dqb[b][:, :, 0:2], dnsc[:, 4 * ti:4 * ti + 1], None, ALU.mult, None,
                           r=[("dqb", b), "dnsc"], w=[("dqb", b)])
                    if (t0 + 512) % SEG == 0:
                        TS("dve", dqb[b][:, :, 514:516], dqb[b][:, :, 514:516], dnsc[:, 4 * (ti + 3) + 1:4 * (ti + 3) + 2], None,
                           ALU.mult, None, r=[("dqb", b), "dnsc"], w=[("dqb", b)])
                    for fc in range(12):
                        i = fc % 4
                        for j in range(5):
                            MM(pC[i][:, :], dg[:, j * 12 + fc, :], dqb[b][:, fc, j:j + 512], j == 0, j == 4, r=["dg", ("dqb", b)], w=[("ppC", i)])
                        if fc < 8:
                            ACTV(qk32[:, fc, :], pC[i][:, :], AF.Silu, r=[], w=[("ppC", i), ("qk32", fc)])
                        else:
                            ACTV(oqkv[b][:, fc, :], pC[i][:, :], AF.Silu, r=[], w=[("ppC", i), ("oqkv", b)])
                    for fc in range(8):
                        ACTV(sqb[:, fc, :], qk32[:, fc, :], AF.Square, r=[("qk32", fc)], w=[("sqb", fc)])
                    for fc in range(8):
                        i = fc % 4
                        MM(pNn[i][:, :], ones_b[:, :], sqb[:, fc, :], True, True, r=["ones_b", ("sqb", fc)], w=[("ppN", i)])
                        ACTV(rn[:, fc, :], pNn[i][:, :], AF.Ln, r=["eps"], w=[("ppN", i), ("rn", fc)], bias=epsT[:])
                    for fc in range(8):
                        if fc < 4:
                            ACTV(rn[:, fc, :], rn[:, fc, :], AF.Exp, r=[("rn", fc), "lnq"], w=[("rn", fc)], scale=-0.5, bias=lnq[:])
                        else:
                            ACTV(rn[:, fc, :], rn[:, fc, :], AF.Exp, r=[("rn", fc)], w=[("rn", fc)], scale=-0.5)
                        TT("dve", oqkv[b][:, fc, :], qk32[:, fc, :], rn[:, fc, :], ALU.mult, r=[("qk32", fc), ("rn", fc)], w=[("oqkv", b)])
                    DMA("sp", dnv[:, :, t0:t0 + 512], oqkv[b][:], r=[("oqkv", b)], w=[("DN_d", u)])
            P.barrier()

        def pass_dn_gen(dr, es):
            NEGC = 11
            if True:
                def sb(name, shape, dt=F32):
                    return es.enter_context(nc.sbuf_tensor(f"p3{dr}_" + name, list(shape), dt))

                def ps(name, shape, dt=F32):
                    return es.enter_context(nc.psum_tensor(f"p3{dr}_" + name, list(shape), dt))
                dnc = sb("dnc", [128, NEGC, 128])
                dnsc = sb("dnsc", [128, 4 * NT])
                DMA("sp", dnc[:], dnc_d[:, :, :], w=["dnc"])
                DMA("sp", dnsc[:], dnsc_d[:, :], w=["dnsc"])
                U = dnc[:, 0 + dr, :]
                ONES = dnc[:, 2, :]
                MLOW = dnc[:, 3 + dr, :]
                MQ = dnc[:, 5 + dr, :]
                IDF = dnc[:, 7, :]
                BD = dnc[:, 10, :]
                qkvb = [sb(f"qkvb{i}", [128, 12, 128], BF16) for i in range(2)]
                gb = [sb(f"gb{i}", [128, 16]) for i in range(2)]
                gst = sb("gst", [128, 16])
                egc = sb("egc", [128, 4]); be = sb("be", [128, 4]); edk = sb("edk", [128, 4]); egl = sb("egl", [128, 8])
                Ug = sb("Ug", [128, 4, 128])
                Nm = sb("Nm", [128, 4, 128])
                DL = sb("DL", [128, 4, 128]); DQm = sb("DQm", [128, 4, 128])
                egcr = sb("egcr", [128, 4, 128])
                Lb = [sb(f"Lb{i}", [128, 4, 128], BF16) for i in range(2)]
                Mb = [sb(f"Mb{i}", [128, 4, 128], BF16) for i in range(2)]
                Z = sb("Z", [128, 4, 128], BF16)
                kkb = sb("kkb", [128, 4, 128])
                qkTm = sb("qkTm", [128, 4, 128], BF16)
                kbe = sb("kbe", [128, 4, 128], BF16); kdec = sb("kdec", [128, 4, 128], BF16); vbt = sb("vbt", [128, 4, 128], BF16)
                um = sb("um", [128, 4, 128]); wTm = sb("wTm", [128, 4, 128], BF16); qdTm = sb("qdTm", [128, 4, 128], BF16)
                vn = sb("vn", [128, 4, 128], BF16)
                ot = [sb(f"ot{i}", [128, 4, 128]) for i in range(2)]
                S = sb("S", [128, 4, 128])
                Sb = sb("Sb", [128, 4, 128], BF16)
                g = [ps(f"g{i}", [128, 4, 128]) for i in range(3)]
                tbb = ps("tbb", [128, 8, 128], BF16)
                tb = [tbb[:, 0:4, :], tbb[:, 4:8, :]]
                gi = [0]
                tbi = [0]

                def nextg():
                    i = gi[0] % 3
                    gi[0] += 1
                    return g[i], ("p3g", i)

                def nexttb():
                    i = tbi[0] % 2
                    tbi[0] += 1
                    return tb[i], "p3tb"
                bc = lambda ap4: ap4.unsqueeze(2).to_broadcast([128, 4, 128])
                identb4 = ident[:, :].unsqueeze(1).to_broadcast([128, 4, 128])
                MSET("dve", S[:], 0.0, w=["S"])
                MSET("dve", Sb[:], 0.0, w=["Sb"])
                order = list(range(NT)) if dr == 0 else list(range(NT - 1, -1, -1))
                dnv = DN_d.rearrange("(f p) w -> p f w", p=128)

                def load(ti, b):
                    t0 = ti * 128
                    u = ti // 4
                    DMA("sp", qkvb[b][:], dnv[:, :, t0:t0 + 128], r=[("DN_d", u)], w=[("qkvb", b)])
                    DMA("sp", gb[b][:], GB_d[t0:t0 + 128, :], r=[("GB_d", u)], w=[("gb", b)])
                load(order[0], 0)
                for n_, ti in enumerate(order):
                    b = n_ % 2
                    t0 = ti * 128
                    if n_ + 1 < NT:
                        load(order[n_ + 1], 1 - b)
                    g4 = gb[b][:, 4 * dr:4 * dr + 4]
                    b4 = gb[b][:, 8 + 4 * dr:12 + 4 * dr]
                    qT = lambda h: qkvb[b][:, h, :]
                    kT = lambda h: qkvb[b][:, 4 + h, :]
                    vT = lambda h: qkvb[b][:, 8 + h, :]
                    pG3, kG = nextg()
                    pG = pG3[:, :, :].rearrange("p a b -> p (a b)")
                    MM(pG[:, 0:4], U, g4, True, True, r=["dnc", ("gb", b)], w=[kG])
                    MM(pG[:, 4:8], BD, g4, True, True, r=["dnc", ("gb", b)], w=[kG])
                    MM(pG[:, 8:12], dnc[:, 8, :], g4, True, True, r=["dnc", ("gb", b)], w=[kG])
                    MM(pG[:, 12:16], dnc[:, 9, :], g4, True, True, r=["dnc", ("gb", b)], w=[kG])
                    CP("dve", gst[:], pG[:, 0:16], r=[], w=[kG, "gst"])
                    ACTV(egc[:], gst[:, 0:4], AF.Exp, r=["gst"], w=["egc"])
                    TT("dve", be[:], egc[:], b4, ALU.mult, r=["egc", ("gb", b)], w=["be"])
                    TT("dve", edk[:], gst[:, 4:8], gst[:, 0:4], ALU.subtract, r=["gst"], w=["edk"])
                    ACTV(edk[:], edk[:], AF.Exp, r=["edk"], w=["edk"])
                    ACTV(egl[:], gst[:, 8:16], AF.Exp, r=["gst"], w=["egl"])
                    TT("dve", Ug[:], U.unsqueeze(1).to_broadcast([128, 4, 128]), bc(g4), ALU.mult, r=["dnc", ("gb", b)], w=["Ug"])
                    yield
                    pR, kR = nextg()
                    MM(pR[:, :, :], ONES, Ug[:, :, :], True, True, r=["dnc", "Ug"], w=[kR])
                    TT("dve", Nm[:], pR[:, :, :], bc(gst[:, 0:4]), ALU.subtract, r=["gst"], w=[kR, "Nm"])
                    ACTV(egcr[:], pR[:, :, :], AF.Exp, r=[], w=[kR, "egcr"])
                    mlb = MLOW.unsqueeze(1).to_broadcast([128, 4, 128])
                    mqb = MQ.unsqueeze(1).to_broadcast([128, 4, 128])
                    STT("dve", DL[:], Nm[:], -1.0, mlb, ALU.mult, ALU.add, r=["Nm", "dnc"], w=["DL"])
                    TT("pool", DQm[:], Nm[:], mqb, ALU.add, r=["Nm", "dnc"], w=["DQm"])
                    ACTV(DL[:], DL[:], AF.Exp, r=["DL"], w=["DL"])
                    ACTV(DQm[:], DQm[:], AF.Exp, r=["DQm"], w=["DQm"])
                    yield
                    pK_, kK = nextg()
                    for h in range(4):
                        MM(pK_[:, h, :], kT(h), kT(h), True, True, r=[("qkvb", b)], w=[kK])
                    TT("dve", kkb[:], pK_[:, :, :], bc(b4), ALU.mult, r=[("gb", b)], w=[kK, "kkb"])
                    TT("dve", Lb[0][:], kkb[:], DL[:], ALU.mult, r=["kkb", "DL"], w=[("Lb", 0)])
                    yield
                    pQ_, kQ = nextg()
                    for h in range(4):
                        MM(pQ_[:, h, :], kT(h), qT(h), True, True, r=[("qkvb", b)], w=[kQ])
                    TT("dve", qkTm[:], pQ_[:, :, :], DQm[:], ALU.mult, r=["DQm"], w=[kQ, "qkTm"])
                    pM_, kM = nexttb()
                    for h in range(4):
                        TR(pM_[:, h, :], Lb[0][:, h, :], ident[:, :], r=[("Lb", 0), "ident"], w=[kM])
                    CP("act", Mb[0][:], pM_[:, :, :], r=[], w=[kM, ("Mb", 0)])
                    yield
                    STT("dve", Z[:], Mb[0][:], -1.0, identb4, ALU.mult, ALU.add, r=[("Mb", 0), "ident"], w=["Z"])
                    cur = 0
                    for lvl in range(5):
                        nxt = 1 - cur
                        pP_, kP = nextg()
                        for h in range(4):
                            MM(pP_[:, h, :], Mb[cur][:, h, :], Lb[cur][:, h, :], True, True, r=[("Mb", cur), ("Lb", cur)], w=[kP])
                        CP("act", Lb[nxt][:], pP_[:, :, :], r=[], w=[kP, ("Lb", nxt)])
                        yield
                        if lvl < 4:
                            pM2, kM2 = nextg()
                            for h in range(4):
                                MM(pM2[:, h, :], Lb[cur][:, h, :], Mb[cur][:, h, :], True, True, r=[("Mb", cur), ("Lb", cur)], w=[kM2])
                            CP("dve", Mb[nxt][:], pM2[:, :, :], r=[], w=[kM2, ("Mb", nxt)])
                        pZ_, kZ = nextg()
                        for h in range(4):
                            MM(pZ_[:, h, :], Lb[nxt][:, h, :], Z[:, h, :], True, True, r=[("Lb", nxt), "Z"], w=[kZ])
                        TT("dve", Z[:], Z[:], pZ_[:, :, :], ALU.add, r=["Z"], w=[kZ, "Z"])
                        yield
                        cur = nxt
                    pT1, kT1 = nexttb()
                    for h in range(4):
                        TR(pT1[:, h, :], kT(h), ident[:, :], r=[("qkvb", b), "ident"], w=[kT1])
                    TT("dve", kbe[:], pT1[:, :, :], bc(be[:]), ALU.mult, r=["be"], w=[kT1, "kbe"])
                    TT("dve", kdec[:], pT1[:, :, :], bc(edk[:]), ALU.mult, r=["edk"], w=[kT1, "kdec"])
                    pT2, kT2 = nexttb()
                    for h in range(4):
                        TR(pT2[:, h, :], vT(h), ident[:, :], r=[("qkvb", b), "ident"], w=[kT2])
                    TT("dve", vbt[:], pT2[:, :, :], bc(b4), ALU.mult, r=[("gb", b)], w=[kT2, "vbt"])
                    yield
                    pU_, kU = nextg()
                    for h in range(4):
                        MM(pU_[:, h, :], Z[:, h, :], vbt[:, h, :], True, True, r=["Z", "vbt"], w=[kU])
                    CP("act", um[:], pU_[:, :, :], r=[], w=[kU, "um"])
                    pW_, kW = nextg()
                    for h in range(4):
                        MM(pW_[:, h, :], kbe[:, h, :], Z[:, h, :], True, True, r=["Z", "kbe"], w=[kW])
                    CP("act", wTm[:], pW_[:, :, :], r=[], w=[kW, "wTm"])
                    TT("pool", qdTm[:], qkvb[b][:, 0:4, :], egcr[:], ALU.mult, r=[("qkvb", b), "egcr"], w=["qdTm"])
                    yield
                    carry = None
                    if dr == 0 and ti % 16 == 0 and ti > 0:
                        carry = dnsc[:, 4 * ti + 2:4 * ti + 3]
                    if dr == 1 and ti % 16 == 15 and ti < NT - 1:
                        carry = dnsc[:, 4 * ti + 3:4 * ti + 4]
                    if carry is not None:
                        TS("dve", S[:], S[:], carry, None, ALU.mult, None, r=["S", "dnsc"], w=["S"])
                        CP("act", Sb[:], S[:], r=["S"], w=["Sb"])
                    io = n_ % 2
                    for c in ((0, 1) if dr == 0 else (1, 0)):
                        cs = slice(64 * c, 64 * c + 64)
                        pVN, kVN = nextg()
                        for h in range(4):
                            MM(pVN[cs, h, :], wTm[:, h, cs], Sb[:, h, :], True, True, r=["wTm", "Sb"], w=[kVN])
                        TT("dve", vn[cs, :, :], um[cs, :, :], pVN[cs, :, :], ALU.subtract, r=["um"], w=[kVN, "vn"])
                        yield
                        pO, kO = nextg()
                        for h in range(4):
                            MM(pO[cs, h, :], qdTm[:, h, cs], Sb[:, h, :], True, False, r=["qdTm", "Sb"], w=[kO])
                            MM(pO[cs, h, :], qkTm[cs, h, cs], vn[cs, h, :], False, True, r=["qkTm", "vn"], w=[kO])
                        pSn, kSn = nextg()
                        for h in range(4):
                            MM(pSn[:, h, :], kdec[cs, h, :], vn[cs, h, :], True, True, r=["kdec", "vn"], w=[kSn])
                        for h in range(4):
                            STT("dve", S[:, h, :], S[:, h, :], egl[:, 4 * c + h:4 * c + h + 1], pSn[:, h, :], ALU.mult, ALU.add,
                                r=["S", "egl"], w=[kSn, "S"])
                        CP("act", Sb[:], S[:], r=["S"], w=["Sb"])
                        CP("act", ot[io][cs, :, :], pO[cs, :, :], r=[], w=[kO, ("ot", io)])
                        yield
                    DMA("sp", OD_d[dr][t0:t0 + 128, :], ot[io][:].rearrange("p h d -> p (h d)"), r=[("ot", io)], w=[("OD_d", dr, ti)])
                    yield

        def pass_dn_both(dirs):
            P.shared = {"eps", "one", "ident", "GB_d", "DQ_d", "OD_d", "DN_d"}
            with contextlib.ExitStack() as es:
                gens = [(dr, pass_dn_gen(dr, es)) for dr in dirs]
                while gens:
                    for item in list(gens):
                        P.ns = ("dn", item[0])
                        try:
                            next(item[1])
                        except StopIteration:
                            gens.remove(item)
                P.ns = None
            P.barrier()

        def pass5():
            with contextlib.ExitStack() as es:
                def sb(name, shape, dt=F32):
                    return es.enter_context(nc.sbuf_tensor("p5_" + name, list(shape), dt))

                def ps(name, shape, dt=F32):
                    return es.enter_context(nc.psum_tensor("p5_" + name, list(shape), dt))
                wout = sb("wout", [128, 8, D], BF16)
                dnog = sb("dnog", [128, 512])
                DMA("sp", dnog[:], dnog_d[:, :], w=["dnog"])
                for kc in range(8):
                    DMA("pool", wout[:, kc, :], w_out_d[kc * 128:(kc + 1) * 128, :], w=[("wout", kc)])
                NB = 3
                NX = 4
                of_ = [sb(f"of{i}", [128, 512]) for i in range(NB)]
                ob_ = [sb(f"ob{i}", [128, 512]) for i in range(NB)]
                gt = [sb(f"gt{i}", [128, 512]) for i in range(NB)]
                xt = [sb(f"xt{i}", [128, D]) for i in range(NX)]
                at = [sb(f"at{i}", [128, 4, 128], BF16) for i in range(NX)]
                osum = sb("osum", [128, 512]); sq = sb("sq", [128, 512]); ss = sb("ss", [128, 4])
                gg = sb("gg", [128, 512]); on = sb("on", [128, 512])
                odn = [sb(f"odn{i}", [128, 512], BF16) for i in range(2)]
                eg = sb("eg", [128, 512])
                dnT = sb("dnT", [128, 4, 128], BF16)
                ht = [sb(f"ht{i}", [128, D]) for i in range(2)]
                pT = ps("pT", [128, 4, 128], BF16)
                pH = [ps(f"pH{i}", [128, 512]) for i in range(4)]

                def loads(ti):
                    b = ti % NB
                    bx = ti % NX
                    t0 = ti * 128
                    u = ti // 4
                    DMA("sp", of_[b][:], OD_d[0][t0:t0 + 128, :], r=[("OD_d", 0, ti)], w=[("of", b)])
                    DMA("sp", ob_[b][:], OD_d[1][t0:t0 + 128, :], r=[("OD_d", 1, ti)], w=[("ob", b)])
                    DMA("sp", gt[b][:], G_d[t0:t0 + 128, :], r=[("G_d", u)], w=[("gt", b)])
                    DMA("sp", xt[bx][:], xs[t0:t0 + 128, :], w=[("xt5", bx)])
                    DMA("sp", at[bx][:], AT_d[:, :, t0:t0 + 128].rearrange("j p w -> p j w"), r=[("AT_d", ti // 16)], w=[("at", bx)])

                def stage_a(ti):
                    b = ti % NB
                    TT("dve", osum[:], of_[b][:], ob_[b][:], ALU.add, r=[("of", b), ("ob", b)], w=["osum"])
                    ACTV(sq[:], osum[:], AF.Square, r=["osum"], w=["sq5"])
                    RED("dve", ss[:], sq[:].rearrange("p (h d) -> p h d", d=128), r=["sq5"], w=["ss5"])
                    ACTV(ss[:], ss[:], AF.Ln, r=["ss5", "eps"], w=["ss5"], scale=1.0 / 128, bias=epsT[:])
                    ACTV(ss[:], ss[:], AF.Exp, r=["ss5"], w=["ss5"], scale=-0.5)
                    ACTV(eg[:], gt[b][:], AF.Exp, r=[("gt", b)], w=["eg"], scale=-1.0)
                    ACTV(eg[:], eg[:], AF.Ln, r=["eg", "one"], w=["eg"], bias=oneT[:])
                    ACTV(eg[:], eg[:], AF.Exp, r=["eg"], w=["eg"], scale=-1.0)
                    TT("pool", gg[:], gt[b][:], dnog[:], ALU.mult, r=[("gt", b), "dnog"], w=["gg"])
                    TT("dve", gg[:], gg[:], eg[:], ALU.mult, r=["gg", "eg"], w=["gg"])
                    TT("dve", on[:].rearrange("p (h d) -> p h d", d=128), osum[:].rearrange("p (h d) -> p h d", d=128),
                       ss[:].unsqueeze(2).to_broadcast([128, 4, 128]), ALU.mult, r=["osum", "ss5"], w=["on"])
                    TT("dve", odn[ti % 2][:], on[:], gg[:], ALU.mult, r=["on", "gg"], w=[("odn", ti % 2)])

                def stage_b(ti):
                    bx = ti % NX
                    t0 = ti * 128
                    u = ti // 4
                    for j in range(4):
                        TR(pT[:, j, :], odn[ti % 2][:, j * 128:(j + 1) * 128], ident[:, :], r=[("odn", ti % 2), "ident"], w=["p5T"])
                    CP("act", dnT[:], pT[:, :, :], r=[], w=["p5T", "dnT"])
                    ih = ti % 2
                    for nh in range(2):
                        ip = (2 * ti + nh) % 4
                        csl = slice(512 * nh, 512 * (nh + 1))
                        for j in range(4):
                            MM(pH[ip][:, :], at[bx][:, j, :], wout[:, j, csl], j == 0, False, r=[("at", bx), ("wout", j)], w=[("p5H", ip)])
                        for j in range(4):
                            MM(pH[ip][:, :], dnT[:, j, :], wout[:, 4 + j, csl], False, j == 3, r=["dnT", ("wout", 4 + j)], w=[("p5H", ip)])
                        TT("dve", ht[ih][:, csl], pH[ip][:, :], xt[bx][:, csl], ALU.add, r=[("xt5", bx)], w=[("p5H", ip), ("ht5", ih)])
                    DMA("sp", H_d[t0:t0 + 128, :], ht[ih][:], r=[("ht5", ih)], w=[("H_d", u)])
                loads(0)
                if NT > 1:
                    loads(1)
                stage_a(0)
                for ti in range(NT):
                    if ti + 2 < NT:
                        loads(ti + 2)
                    if ti + 1 < NT:
                        stage_a(ti + 1)
                    stage_b(ti)
            P.barrier()

        def pass6(hbuf, hkeys_fn):
            with contextlib.ExitStack() as es:
                def sb(name, shape, dt=F32):
                    return es.enter_context(nc.sbuf_tensor("p6_" + name, list(shape), dt))

                def ps(name, shape, dt=F32):
                    return es.enter_context(nc.psum_tensor("p6_" + name, list(shape), dt))
                wup = sb("wup", [128, 8, 2 * FFN], BF16)
                wdn = sb("wdn", [128, NFC, D], BF16)
                g2 = sb("g2", [128, 8])
                cw = sb("cw", [128, 3, 44])
                cb = sb("cb", [128, 44])
                hsc = sb("hsc", [2, NU])
                DMA("sp", g2[:], norm2_d[:, :], w=["g2"])
                DMA("sp", cw[:], cw_d[:, :, :], w=["cw"])
                DMA("sp", cb[:], cb_d[:, :], w=["cb"])
                DMA("sp", hsc[:], halo_sc_d[:, :], w=["hsc"])
                for kc in range(8):
                    DMA("pool", wup[:, kc, :], w_up_d[kc * 128:(kc + 1) * 128, :], w=[("wup", kc)])
                    TS("dve", wup[:, kc, :], wup[:, kc, :], g2[:, kc:kc + 1], None, ALU.mult, None, r=["g2", ("wup", kc)], w=[("wup", kc)])
                for j in range(NFC):
                    DMA("pool", wdn[:, j, :], w_down_d[j * 128:(j + 1) * 128, :], w=[("wdn", j)])
                ht = sb("ht", [128, 4, D])
                hh = sb("hh", [2, D], BF16)
                hn = sb("hn", [128, D], BF16)
                hhn = sb("hhn", [2, D], BF16)
                ms = [sb(f"ms{i}", [128, 1]) for i in range(2)]
                hnTs = [sb(f"hnT{i}", [128, 8, 514], BF16) for i in range(2)]
                actTh = [sb(f"actT{i}", [128, NFC, 256], BF16) for i in range(2)]
                downq = []
                accg = [sb(f"accg{i}", [128, 256]) for i in range(2)]
                accu = [sb(f"accu{i}", [128, 256]) for i in range(2)]
                sg = [sb(f"sg{i}", [128, 256], BF16) for i in range(2)]
                yt = [sb(f"yt{i}", [128, D]) for i in range(2)]
                pT = [ps(f"pT{i}", [128, 8, 128], BF16) for i in range(2)]
                pU = [ps(f"pU{i}", [128, 512]) for i in range(4)]
                pD = [ps(f"pD{i}", [128, 512]) for i in range(2)]
                ucount = [0]
                dcount = [0]

                def load_h(u):
                    t0 = u * 512
                    DMA("sp", ht[:], hbuf[t0:t0 + 512, :].rearrange("(t p) d -> p t d", p=128), r=hkeys_fn(u), w=["ht"])
                    lo = max(t0 - 1, 0)
                    hi = min(t0 + 512, NTOK - 1)
                    DMA("pool", hh[0:1, :], hbuf[lo:lo + 1, :], r=hkeys_fn(max(u - 1, 0)), w=["hh"])
                    DMA("pool", hh[1:2, :], hbuf[hi:hi + 1, :], r=hkeys_fn(min(u + 1, NU - 1)), w=["hh"])

                ndef = []

                def norm_piece(u, piece, defer=None):
                    hnT = hnTs[u % 2]
                    hk = ("hnT", u % 2)
                    if piece == 0:
                        def dst_halo():
                            CP("dve", hnT[:, :, 0:514:513], pT[1][:, :, 0:2], r=[], w=["pT1", hk])
                        rmsnorm_T(None, hh[:, :], 2, ms[1], hhn, hhn, pT[1], "pT1", dst_halo, ["hh"], "p6h",
                                  scale_ap=hsc[:, u:u + 1], scale_key="hsc", junk_key="p6hhn", defer=defer)
                    else:
                        t = piece - 1

                        def dst_main(t=t):
                            CP("act", hnT[:, :, 1 + 128 * t:1 + 128 * (t + 1)], pT[0][:, :, :], r=[], w=["pT0", hk])
                        rmsnorm_T(None, ht[:, t, :], 128, ms[0], hn, hn, pT[0], "pT0", dst_main, ["ht"], "p6m", junk_key="p6mhn", defer=defer)
                load_h(0)
                for piece in range(5):
                    norm_piece(0, piece)
                for u in range(NU):
                    t0 = u * 512
                    hnT = hnTs[u % 2]
                    hk = ("hnT", u % 2)
                    if u + 1 < NU:
                        load_h(u + 1)
                    pair = 0
                    for wdw in range(2):
                        gw = 2 * u + wdw
                        actT = actTh[gw % 2]
                        ak = ("actT", gw % 2)
                        for j in range(NFC):
                            ig = (ucount[0] * 2) % 4
                            iu = (ucount[0] * 2 + 1) % 4
                            ia = ucount[0] % 2
                            ucount[0] += 1
                            for (ip, fc) in ((ig, j), (iu, NFC + j)):
                                for kc in range(8):
                                    MM(pU[ip][:, 0:258], wup[:, kc, fc * 128:(fc + 1) * 128], hnT[:, kc, 256 * wdw:256 * wdw + 258],
                                       kc == 0, kc == 7, r=[("wup", kc), hk], w=[("pU", ip)])
                            if u + 1 < NU and pair in (2, 10, 18, 26, 34):
                                norm_piece(u + 1, (pair - 2) // 8, defer=ndef)
                            if pair in (9, 17, 25, 33, 41):
                                while ndef:
                                    ndef.pop(0)()
                            if j in (3, 8, 13, 18) and downq:
                                downq.pop(0)()
                            pair += 1
                            for (ip, fc, acc_, ka) in ((ig, j, accg[ia], ("accg", ia)), (iu, NFC + j, accu[ia], ("accu", ia))):
                                ACTV(acc_[:], pU[ip][:, 1:257], AF.Identity, r=["cw", "cb"], w=[("pU", ip), ka],
                                     scale=cw[:, 1, fc:fc + 1], bias=cb[:, fc:fc + 1])
                                STT("dve", acc_[:], pU[ip][:, 0:256], cw[:, 0, fc:fc + 1], acc_[:], ALU.mult, ALU.add,
                                    r=["cw", ka], w=[("pU", ip), ka])
                                STT("dve", acc_[:], pU[ip][:, 2:258], cw[:, 2, fc:fc + 1], acc_[:], ALU.mult, ALU.add,
                                    r=["cw", ka], w=[("pU", ip), ka])
                            ACTV(sg[ia][:], accg[ia][:], AF.Silu, r=[("accg", ia)], w=[("sg", ia)])
                            TT("pool", actT[:, j, :], sg[ia][:], accu[ia][:], ALU.mult,
                               r=[("sg", ia), ("accu", ia)], w=[ak])
                        while downq:
                            downq.pop(0)()
                        for tt in range(2):
                            tok = t0 + 256 * wdw + 128 * tt
                            for nh in range(2):
                                def down_group(tt=tt, nh=nh, tok=tok, actT=actT, ak=ak, u=u, gw=gw):
                                    iy = (2 * gw + tt) % 2
                                    if nh == 0:
                                        DMA("sp", yt[iy][:], hbuf[tok:tok + 128, :], r=hkeys_fn(u), w=[("yt", iy)])
                                    ipd = dcount[0] % 2
                                    dcount[0] += 1
                                    for j in range(NFC):
                                        MM(pD[ipd][:, :], actT[:, j, 128 * tt:128 * (tt + 1)], wdn[:, j, 512 * nh:512 * (nh + 1)],
                                           j == 0, j == NFC - 1, r=[ak, ("wdn", j)], w=[("pD", ipd)])
                                    TT("dve", yt[iy][:, 512 * nh:512 * (nh + 1)], pD[ipd][:, :], yt[iy][:, 512 * nh:512 * (nh + 1)], ALU.add,
                                       r=[("yt", iy)], w=[("pD", ipd), ("yt", iy)])
                                    if nh == 1:
                                        DMA("sp", ys[tok:tok + 128, :], yt[iy][:], r=[("yt", iy)], w=[("ys", tok)])
                                downq.append(down_group)
                while downq:
                    downq.pop(0)()
            P.barrier()

        if "1" in passes:
            pass1()
        if "2" in passes:
            pass2()
        if "P" in passes:
            pass_prep()
        dirs = [dr for dr, nm in ((0, "3"), (1, "4")) if nm in passes]
        if dirs:
            pass_dn_both(dirs)
        if "5" in passes:
            pass5()
        if "6" in passes:
            if "5" in passes:
                pass6(H_d, lambda u: [("H_d", u)])
            else:
                pass6(xs, lambda u: [])
        P.emit()
    return nc


def _dn_consts():
    NEG = -30000.0
    p = np.arange(128)[:, None]
    f = np.arange(128)[None, :]
    same = (p // 64) == (f // 64)
    c = np.zeros((128, 11, 128), np.float32)
    c[:, 0] = same & (p <= f)
    c[:, 1] = same & (p >= f)
    c[:, 2] = 1.0
    c[:, 3] = np.where(same & (f < p), 0.0, NEG)
    c[:, 4] = np.where(same & (f > p), 0.0, NEG)
    c[:, 5] = np.where(same & (f >= p), 0.0, NEG)
    c[:, 6] = np.where(same & (f <= p), 0.0, NEG)
    c[:, 7] = np.eye(128)
    c[:, 8] = (p // 64 == 0) & (f >= 0)
    c[:, 9] = (p // 64 == 1) & (f >= 0)
    c[:, 10] = same
    return c


def _dn_scales(NSEG, link):
    NT = NSEG * SEG // 128
    sc = np.ones((128, 4 * NT), np.float32)
    for ti in range(NT):
        s = ti // 16
        if ti % 16 == 0:
            sc[:, 4 * ti + 0] = link[s]
            sc[:, 4 * ti + 2] = link[s]
        if ti % 16 == 15:
            sc[:, 4 * ti + 1] = link[s + 1]
            sc[:, 4 * ti + 3] = link[s + 1]
    return sc


def host_consts(NSEG, link, pos0):
    NTOK = NSEG * SEG
    NU = NTOK // 512
    NT = NTOK // 128
    hs = np.ones((2, NU), np.float32)
    for u in range(NU):
        t0 = u * 512
        if t0 % SEG == 0:
            hs[0, u] = link[t0 // SEG]
        if (t0 + 512) % SEG == 0:
            hs[1, u] = link[(t0 + 512) // SEG]
    lv = np.ones((128, 2 * NSEG), np.float32)
    for s in range(NSEG):
        lv[0:64, 2 * s] = link[s]
        lv[64:128, 2 * s + 1] = link[s + 1]
    r_ = np.arange(128)[:, None]
    q_ = np.arange(128)[None, :]
    amask = np.concatenate([(q_ <= r_), (q_ >= r_)], axis=1).astype(np.float32)
    half = 32
    inv_freq = (1.0 / (10000.0 ** (np.arange(half, dtype=np.float32) * 2.0 / 64))).astype(np.float32)
    invf = np.tile(np.concatenate([inv_freq, inv_freq])[None, :], (128, 1)).astype(np.float32)
    phase = np.tile(np.concatenate([np.zeros(32), np.full(32, np.pi / 2)])[None, :], (128, 1)).astype(np.float32)
    pos = np.zeros((128, NT), np.float32)
    for s in range(NSEG):
        for t in range(16):
            pos[:, s * 16 + t] = pos0[s] + t * 128 + np.arange(128)
    return {"ident": np.eye(128, dtype=np.float32), "halo_sc": hs, "lv": lv, "amask": amask, "invf": invf,
            "phase": phase, "pos": pos,
            "dnc": _dn_consts(), "dnsc": _dn_scales(NSEG, link)}


def weight_maps(norm1, w_in, att_q_norm, att_k_norm, dn_conv_w, dn_a_log, dn_dt_bias, dn_out_norm, w_out, norm2,
                w_up, ffn_conv_w, ffn_conv_b, w_down):
    f = lambda a: np.asarray(a, np.float32)[0]
    qg = np.concatenate([np.tile(f(att_q_norm), 8), np.tile(f(att_k_norm), 8)])
    return {
        "w_in": np.ascontiguousarray(f(w_in)),
        "w_out": np.ascontiguousarray(f(w_out)),
        "w_up": np.ascontiguousarray(f(w_up)),
        "w_down": np.ascontiguousarray(f(w_down)),
        "norm1": np.ascontiguousarray(f(norm1).reshape(8, 128).T),
        "norm2": np.ascontiguousarray(f(norm2).reshape(8, 128).T),
        "qkg": np.ascontiguousarray(np.tile(qg[None, :], (128, 1))),
        "alog": np.ascontiguousarray(np.tile(f(dn_a_log).reshape(1, 8), (128, 1))),
        "dtb": np.ascontiguousarray(np.tile(f(dn_dt_bias).reshape(1, 8), (128, 1))),
        "dncw": np.ascontiguousarray(f(dn_conv_w).reshape(5, 12, 128).transpose(2, 0, 1)),
        "dnog": np.ascontiguousarray(np.tile(f(dn_out_norm)[None, :], (128, 4))),
        "ffn_cw": np.ascontiguousarray(f(ffn_conv_w).reshape(3, 44, 128).transpose(2, 0, 1)),
        "ffn_cb": np.ascontiguousarray(f(ffn_conv_b).reshape(44, 128).T),
    }


SAMPLE_SLOTS = [6, 6, 5, 5, 5, 5]
_NC_CACHE = {}


def _core_layout():
    lay = []
    for b in range(2):
        lay.append([("p", b, s) for s in range(8)])
    nxt = 0
    for n in SAMPLE_SLOTS:
        row = []
        for i in range(8):
            if i < n:
                row.append(("s", nxt, 0))
                nxt += 1
            else:
                row.append(None)
        lay.append(row)
    return lay


def kernel(x_prompt, x_sample, norm1, w_in, att_q_norm, att_k_norm, dn_conv_w, dn_a_log, dn_dt_bias,
           dn_out_norm, w_out, norm2, w_up, ffn_conv_w, ffn_conv_b, w_down):
    NSEG = 8
    lay = _core_layout()
    if "nc" not in _NC_CACHE:
        _NC_CACHE["nc"] = build_program(NSEG=NSEG)
    nc = _NC_CACHE["nc"]
    x_prompt = np.asarray(x_prompt, np.float32)
    x_sample = np.asarray(x_sample, np.float32)
    common = weight_maps(norm1, w_in, att_q_norm, att_k_norm, dn_conv_w, dn_a_log, dn_dt_bias, dn_out_norm, w_out,
                         norm2, w_up, ffn_conv_w, ffn_conv_b, w_down)
    in_maps = []
    for c in range(NCORES):
        xs = np.zeros((NSEG * SEG, D), np.float32)
        link = np.zeros(NSEG + 1, np.float32)
        pos0 = np.zeros(NSEG, np.float32)
        for i, ent in enumerate(lay[c]):
            if ent is None:
                continue
            kind, b, s = ent
            if kind == "p":
                xs[i * SEG:(i + 1) * SEG] = x_prompt[b, s * SEG:(s + 1) * SEG]
                pos0[i] = s * SEG
                if s > 0:
                    link[i] = 1.0
            else:
                xs[i * SEG:(i + 1) * SEG] = x_sample[b]
        m = dict(common)
        m.update(host_consts(NSEG, link, pos0))
        m["xs"] = xs
        in_maps.append(m)
    res = run_bass_kernel_spmd(nc, in_maps, core_ids=list(range(NCORES)))
    y_prompt = np.zeros_like(x_prompt)
    y_sample = np.zeros_like(x_sample)
    for c in range(NCORES):
        ys = res.results[c]["ys"]
        for i, ent in enumerate(lay[c]):
            if ent is None:
                continue
            kind, b, s = ent
            if kind == "p":
                y_prompt[b, s * SEG:(s + 1) * SEG] = ys[i * SEG:(i + 1) * SEG]
            else:
                y_sample[b] = ys[i * SEG:(i + 1) * SEG]
    return (y_prompt, y_sample)
```

```python
import numpy as np
import concourse.bass as bass
import concourse.mybir as mybir
from concourse.bass_utils import run_bass_kernel_spmd

F32 = mybir.dt.float32
BF16 = mybir.dt.bfloat16
I32 = mybir.dt.int32
AF = mybir.ActivationFunctionType
ALU = mybir.AluOpType
AX = mybir.AxisListType

D = 1024
FFN = 2816
NFC = FFN // 128
SEG = 2048
NCORES = 8
EPS = 1e-6


class _Op:
    __slots__ = ("eng", "fn", "deps", "sig", "is_dma", "dsem", "dval", "need_sig", "idx")


class Prog:
    ENGS = ("sp", "act", "dve", "pool", "pe")

    def __init__(self, nc, n_dma_sems=8):
        self.nc = nc
        self.ops = []
        self.last_w = {}
        self.readers = {}
        self.n_dma_sems = n_dma_sems
        self.dma_rr = {"sp": 0, "pool": 0, "act": 0}
        self.dma_last = {}
        self.dma_cnt = {}
        self.extra = {}
        self.ns = None
        self.shared = set()
        self.last_op = {}

    def _k(self, k):
        if self.ns is None or k in self.shared or (isinstance(k, tuple) and k[0] in self.shared):
            return k
        return (self.ns, k)

    def op(self, eng, fn, r=(), w=(), dma=False):
        if self.ns is not None:
            r = [self._k(k) for k in r]
            w = [self._k(k) for k in w]
        o = _Op()
        o.eng = eng; o.fn = fn; o.is_dma = dma; o.need_sig = dma; o.sig = None
        o.idx = len(self.ops)
        deps = list(self.extra.pop(eng, []))
        for k in r:
            p = self.last_w.get(k)
            if p is not None:
                deps.append(p)
        for k in w:
            p = self.last_w.get(k)
            if p is not None:
                deps.append(p)
            for q in self.readers.get(k, ()):
                deps.append(q)
        if dma:
            j = self.dma_rr[eng]
            self.dma_rr[eng] = (j + 1) % self.n_dma_sems
            key = (eng, j)
            prev = self.dma_last.get(key)
            if prev is not None:
                deps.append(prev)
            self.dma_last[key] = o
            self.dma_cnt[key] = self.dma_cnt.get(key, 0) + 1
            o.dsem = key
            o.dval = 16 * self.dma_cnt[key]
        dd = []
        seen = set()
        for p in deps:
            if p is o or id(p) in seen:
                continue
            if eng == "pe" and p.eng == "pe" and not p.is_dma:
                continue
            seen.add(id(p))
            dd.append(p)
            p.need_sig = True
        o.deps = dd
        for k in w:
            self.last_w[k] = o
            self.readers[k] = []
        for k in r:
            self.readers.setdefault(k, []).append(o)
        self.ops.append(o)
        if not dma:
            self.last_op[eng] = o
        return o

    def barrier(self):
        markers = [o for o in self.last_op.values()] + [o for o in self.dma_last.values()]
        for e in self.ENGS:
            self.extra[e] = list(markers) + self.extra.get(e, [])

    def emit(self, final_wait_eng="sp"):
        nc = self.nc
        cnt = {e: 0 for e in self.ENGS}
        for o in self.ops:
            if o.is_dma:
                o.sig = (("dma",) + o.dsem, o.dval)
            elif o.need_sig:
                cnt[o.eng] += 1
                o.sig = (("eng", o.eng), cnt[o.eng])
        sem_keys = [("eng", e) for e in self.ENGS] + [("dma", q, j) for q in ("sp", "pool", "act") for j in range(self.n_dma_sems)]
        per_eng = {e: [o for o in self.ops if o.eng == e] for e in self.ENGS}
        finals = {}
        for o in self.ops:
            if o.is_dma:
                finals[o.sig[0]] = max(finals.get(o.sig[0], 0), o.sig[1])
        import contextlib
        with contextlib.ExitStack() as st:
            sems = {}
            for k in sem_keys:
                sems[k] = st.enter_context(nc.semaphore("s_" + "_".join(str(x) for x in k)))
            block = st.enter_context(nc.Block())

            def replay(e, eobj):
                known = {}
                for o in per_eng[e]:
                    need = {}
                    for p in o.deps:
                        sk, v = p.sig
                        if v > need.get(sk, 0):
                            need[sk] = v
                    for sk, v in need.items():
                        if known.get(sk, 0) < v:
                            eobj.wait_ge(sems[sk], v)
                            known[sk] = v
                    ins = o.fn(eobj)
                    if o.sig is not None:
                        ins.then_inc(sems[o.sig[0]], 16 if o.is_dma else 1)
                if e == final_wait_eng:
                    for sk, v in finals.items():
                        if known.get(sk, 0) < v:
                            eobj.wait_ge(sems[sk], v)

            @block.sync
            def _(e):
                replay("sp", e)

            @block.scalar
            def _(e):
                replay("act", e)

            @block.vector
            def _(e):
                replay("dve", e)

            @block.gpsimd
            def _(e):
                replay("pool", e)

            @block.tensor
            def _(e):
                replay("pe", e)


import contextlib

IN_COLS = 3600
KPAD = 1024
DQPAD = 2
TWO_PI = float(2 * np.pi)


def _sl(start, count, step):
    return slice(start, start + (count - 1) * step + 1, step)


def build_program(NSEG=8, passes=("1", "2", "P", "3", "4", "5", "6"), dbg=False):
    nc = bass.Bass("TRN2", target_bir_lowering=False)
    NTOK = NSEG * SEG
    NT = NTOK // 128
    NU = NTOK // 512
    P = Prog(nc)

    def din(name, shape, dt=F32):
        return nc.dram_tensor(name, list(shape), dt, kind="ExternalInput").ap()

    def dout(name, shape, dt=F32):
        return nc.dram_tensor(name, list(shape), dt, kind="ExternalOutput").ap()

    def dscr(name, shape, dt=F32):
        return nc.dram_tensor(name, list(shape), dt, kind=("ExternalOutput" if dbg else "Internal")).ap()

    xs = din("xs", [NTOK, D])
    ident_d = din("ident", [128, 128])
    w_in_d = din("w_in", [D, IN_COLS])
    w_out_d = din("w_out", [D, D])
    w_up_d = din("w_up", [D, 2 * FFN])
    w_down_d = din("w_down", [FFN, D])
    norm1_d = din("norm1", [128, 8])
    norm2_d = din("norm2", [128, 8])
    qkg_d = din("qkg", [128, 1024])
    invf_d = din("invf", [128, 64])
    phase_d = din("phase", [128, 64])
    pos_d = din("pos", [128, NT])
    alog_d = din("alog", [128, 8])
    dtb_d = din("dtb", [128, 8])
    dncw_d = din("dncw", [128, 5, 12])
    dnog_d = din("dnog", [128, 512])
    cw_d = din("ffn_cw", [128, 3, 44])
    cb_d = din("ffn_cb", [128, 44])
    halo_sc_d = din("halo_sc", [2, NU])
    lv_d = din("lv", [128, 2 * NSEG])
    amask_d = din("amask", [128, 256])
    dnc_d = din("dnc", [128, 11, 128])
    dnsc_d = din("dnsc", [128, 4 * NT])
    ys = dout("ys", [NTOK, D])
    KW = NTOK + 2 * KPAD
    QT_d = dscr("QT_s", [4, 128, KW], BF16)
    KT_d = dscr("KT_s", [4, 128, KW], BF16)
    V_d = dscr("V_s", [KW, 512], BF16)
    DQ_d = dscr("DQ_s", [1536, NTOK + 2 * DQPAD], BF16)
    DN_d = dscr("DN_s", [1536, NTOK], BF16)
    G_d = dscr("G_s", [NTOK, 512], F32)
    GB_d = dscr("GB_s", [NTOK, 16], F32)
    AT_d = dscr("AT_s", [4, 128, NTOK], BF16)
    OD_d = [dscr("OF_s", [NTOK, 512], F32), dscr("OB_s", [NTOK, 512], F32)]
    H_d = dscr("H_s", [NTOK, D], F32)

    def MM(out, lhsT, rhs, start, stop, r, w):
        P.op("pe", lambda e, o=out, l=lhsT, rr=rhs, s=start, t=stop: e.matmul(o, lhsT=l, rhs=rr, start=s, stop=t), r=r, w=w)

    def TR(out, in_, idn, r, w):
        P.op("pe", lambda e, o=out, i=in_, d=idn: e.transpose(out=o, in_=i, identity=d), r=r, w=w)

    def ACTV(out, in_, func, r, w, **kw):
        P.op("act", lambda e, o=out, i=in_, f=func, kw=kw: e.activation(out=o, in_=i, func=f, **kw), r=r, w=w)

    def TT(eng, out, in0, in1, op, r, w):
        P.op(eng, lambda e, o=out, a=in0, b=in1, p=op: e.tensor_tensor(out=o, in0=a, in1=b, op=p), r=r, w=w)

    def TS(eng, out, in0, s1, s2, op0, op1, r, w):
        if s2 is None:
            P.op(eng, lambda e, o=out, a=in0, x=s1, p0=op0: e.tensor_scalar(out=o, in0=a, scalar1=x, scalar2=None, op0=p0), r=r, w=w)
        else:
            P.op(eng, lambda e, o=out, a=in0, x=s1, y=s2, p0=op0, p1=op1: e.tensor_scalar(out=o, in0=a, scalar1=x, scalar2=y, op0=p0, op1=p1), r=r, w=w)

    def STT(eng, out, in0, scalar, in1, op0, op1, r, w):
        P.op(eng, lambda e, o=out, a=in0, sc=scalar, b=in1, p0=op0, p1=op1: e.scalar_tensor_tensor(out=o, in0=a, scalar=sc, in1=b, op0=p0, op1=p1), r=r, w=w)

    def CP(eng, out, in_, r, w):
        if eng == "act":
            P.op("act", lambda e, o=out, i=in_: e.copy(out=o, in_=i), r=r, w=w)
        else:
            P.op(eng, lambda e, o=out, i=in_: e.tensor_copy(out=o, in_=i), r=r, w=w)

    def MSET(eng, ap, val, w):
        P.op(eng, lambda e, a=ap, v=val: e.memset(a, v), w=w)

    def DMA(q, out, in_, r=(), w=()):
        P.op(q, lambda e, o=out, i=in_: e.dma_start(out=o, in_=i), r=r, w=w, dma=True)

    def RED(eng, out, in_, r, w):
        P.op(eng, lambda e, o=out, i=in_: e.tensor_reduce(out=o, in_=i, axis=AX.X, op=ALU.add), r=r, w=w)

    def RECIP(out, in_, r, w):
        P.op("dve", lambda e, o=out, i=in_: e.reciprocal(out=o, in_=i), r=r, w=w)

    with contextlib.ExitStack() as es0:
        def sb0(name, shape, dt=F32):
            return es0.enter_context(nc.sbuf_tensor(name, list(shape), dt))
        ident_f = sb0("ident_f", [128, 128])
        ident = sb0("ident_b", [128, 128], BF16)
        epsT = sb0("epsT", [128, 1])
        oneT = sb0("oneT", [128, 1])
        DMA("sp", ident_f[:], ident_d[:, :], w=["ident_f"])
        CP("dve", ident[:], ident_f[:], r=["ident_f"], w=["ident"])
        MSET("dve", epsT[:], EPS, w=["eps"])
        MSET("dve", oneT[:], 1.0, w=["one"])

        def rmsnorm_T(es_sb, src_ap, npart, msT, hnb, sq, pTt, pkey, dst_fn, rkeys, keyp, scale_ap=None, scale_key=None, junk_key="sq",
                      defer=None):
            ACTV(sq[0:npart, :], src_ap, AF.Square, r=rkeys, w=[junk_key, keyp + "ms"], scale=1.0 / 32, accum_out=msT[0:npart, :])
            ACTV(msT[0:npart, :], msT[0:npart, :], AF.Ln, r=[keyp + "ms", "eps"], w=[keyp + "ms"], bias=epsT[0:npart, :])
            ACTV(msT[0:npart, :], msT[0:npart, :], AF.Exp, r=[keyp + "ms"], w=[keyp + "ms"], scale=-0.5)
            if scale_ap is not None:
                TT("dve", msT[0:npart, :], msT[0:npart, :], scale_ap, ALU.mult, r=[keyp + "ms", scale_key], w=[keyp + "ms"])
            TS("dve", hnb[0:npart, :], src_ap, msT[0:npart, 0:1], None, ALU.mult, None, r=rkeys + [keyp + "ms"], w=[keyp + "hn"])
            def second():
                for kc in range(8):
                    TR(pTt[:, kc, 0:npart], hnb[0:npart, kc * 128:(kc + 1) * 128], ident[0:npart, 0:npart],
                       r=[keyp + "hn", "ident"], w=[pkey])
                dst_fn()
            if defer is None:
                second()
            else:
                defer.append(second)

        def pass1():
            with contextlib.ExitStack() as es:
                def sb(name, shape, dt=F32):
                    return es.enter_context(nc.sbuf_tensor("p1_" + name, list(shape), dt))

                def ps(name, shape, dt=F32):
                    return es.enter_context(nc.psum_tensor("p1_" + name, list(shape), dt))
                win = sb("win", [128, 8, IN_COLS], BF16)
                g1 = sb("g1", [128, 8])
                qkg = sb("qkg", [128, 1024])
                invf = sb("invf", [128, 64])
                phase = sb("phase", [128, 64])
                post = sb("post", [128, NT])
                nexpA = sb("nexpA", [128, 8])
                dtb = sb("dtb", [128, 8])
                zb = sb("zb", [128, 2, 512], BF16)
                zf = sb("zf", [128, 12, DQPAD], BF16)
                DMA("sp", g1[:], norm1_d[:, :], w=["g1"])
                DMA("sp", qkg[:], qkg_d[:, :], w=["qkg"])
                DMA("sp", invf[:], invf_d[:, :], w=["invf"])
                DMA("sp", phase[:], phase_d[:, :], w=["phase"])
                DMA("sp", post[:], pos_d[:, :], w=["post"])
                DMA("sp", nexpA[:], alog_d[:, :], w=["nexpA"])
                DMA("sp", dtb[:], dtb_d[:, :], w=["dtb"])
                ACTV(nexpA[:], nexpA[:], AF.Exp, r=["nexpA"], w=["nexpA"])
                TS("dve", nexpA[:], nexpA[:], -1.0, None, ALU.mult, None, r=["nexpA"], w=["nexpA"])
                MSET("pool", zb[:], 0.0, w=["zb"])
                MSET("pool", zf[:], 0.0, w=["zf"])
                for j in range(4):
                    DMA("sp", KT_d[j, :, 0:KPAD], zb[:, 0:2, :].rearrange("p a b -> p (a b)"), r=["zb"], w=[("KT_d", "padL")])
                    DMA("sp", KT_d[j, :, KPAD + NTOK:KW], zb[:, 0:2, :].rearrange("p a b -> p (a b)"), r=["zb"], w=[("KT_d", "padR")])
                for q4 in range(4):
                    DMA("sp", V_d[256 * q4:256 * (q4 + 1), :].rearrange("(t p) c -> p t c", p=128), zb[:], r=["zb"], w=[("V_d", "padL")])
                    DMA("sp", V_d[KPAD + NTOK + 256 * q4:KPAD + NTOK + 256 * (q4 + 1), :].rearrange("(t p) c -> p t c", p=128), zb[:],
                        r=["zb"], w=[("V_d", "padR")])
                dqv = DQ_d.rearrange("(f p) w -> p f w", p=128)
                DMA("sp", dqv[:, :, 0:DQPAD], zf[:], r=["zf"], w=[("DQ_d", "padL")])
                DMA("sp", dqv[:, :, DQPAD + NTOK:DQPAD + NTOK + DQPAD], zf[:], r=["zf"], w=[("DQ_d", "padR")])
                for kc in range(8):
                    DMA("pool", win[:, kc, :], w_in_d[kc * 128:(kc + 1) * 128, :], w=[("win", kc)])
                    TS("dve", win[:, kc, :], win[:, kc, :], g1[:, kc:kc + 1], None, ALU.mult, None, r=["g1", ("win", kc)], w=[("win", kc)])
                xt = [sb(f"xt{i}", [128, 4, D]) for i in range(2)]
                nTs = [sb(f"nT{i}", [128, 8, 512], BF16) for i in range(2)]
                hn = sb("hn", [128, D], BF16)
                sq = sb("sq", [128, D])
                ms = sb("ms", [128, 1])
                dqs = sb("dqs", [128, 12, 512], BF16)
                vb = sb("vb", [128, 4, 512], BF16)
                gs = sb("gs", [128, 4, 512])
                gbt = sb("gbt", [128, 4, 16])
                zt = sb("zt", [128, 8])
                bt8 = sb("bt8", [128, 8])
                qraw = sb("qraw", [128, 1024])
                qn = sb("qn", [128, 1024])
                t1 = sb("t1", [128, 512]); t2 = sb("t2", [128, 512]); t3 = sb("t3", [128, 512]); t4 = sb("t4", [128, 512])
                qrs = [sb(f"qr{i}", [128, 1024], BF16) for i in range(2)]
                qkTs = [sb(f"qkT{i}", [128, 8, 512], BF16) for i in range(2)]
                ssq = sb("ssq", [128, 16])
                ang16 = sb("ang16", [128, 16, 64]); kfi16 = sb("kfi16", [128, 16, 64], I32); kff16 = sb("kff16", [128, 16, 64])
                cs16 = sb("cs16", [128, 16, 64])
                deferred = []
                pTT = ps("pTT", [128, 8, 128], BF16)
                pFs = [ps(f"pF{i}", [128, 512]) for i in range(2)]
                pQ = ps("pQ", [128, 512]); pK = ps("pK", [128, 512]); pV = ps("pV", [128, 512])
                pG = ps("pG", [128, 512]); pB = ps("pB", [128, 512])
                fcnt = [0]

                def norm_piece(u, t):
                    b = u % 2

                    def dst(t=t, b=b):
                        CP("act", nTs[b][:, :, 128 * t:128 * (t + 1)], pTT[:, :, :], r=[], w=["pTT", ("nT", b)])
                    rmsnorm_T(None, xt[b][:, t, :], 128, ms, hn, sq, pTT, "pTT", dst, [("xt", b)], "p1")
                def load_x(u):
                    DMA("sp", xt[u % 2][:], xs[u * 512:u * 512 + 512, :].rearrange("(t p) d -> p t d", p=128), w=[("xt", u % 2)])
                load_x(0)
                for t in range(4):
                    norm_piece(0, t)
                for u in range(NU):
                    b = u % 2
                    if u + 1 < NU:
                        load_x(u + 1)
                    nT = nTs[b]
                    nTk = ("nT", b)
                    t0 = u * 512
                    for fc in range(12):
                        c0 = 1536 + fc * 128
                        pF = pFs[fcnt[0] % 2]; pFk = ("pF", fcnt[0] % 2); fcnt[0] += 1
                        for kc in range(8):
                            MM(pF[:, :], win[:, kc, c0:c0 + 128], nT[:, kc, :], kc == 0, kc == 7, r=[("win", kc), nTk], w=[pFk])
                        CP("act", dqs[:, fc, :], pF[:, :], r=[], w=[pFk, "dqs"])
                    DMA("sp", dqv[:, :, DQPAD + t0:DQPAD + t0 + 512], dqs[:], r=["dqs"], w=[("DQ_d", u)])
                    for t in range(4):
                        ti = u * 4 + t
                        for (pp, key, c0, n) in ((pQ, "pQ", 0, 512), (pK, "pK", 512, 512), (pV, "pV", 1024, 512),
                                                 (pG, "pG", 3072, 512), (pB, "pB", 3584, 16)):
                            for kc in range(8):
                                MM(pp[:, 0:n], nT[:, kc, 128 * t:128 * (t + 1)], win[:, kc, c0:c0 + n], kc == 0, kc == 7,
                                   r=[("win", kc), nTk], w=[key])
                        while deferred:
                            deferred.pop(0)()
                        if u + 1 < NU:
                            norm_piece(u + 1, t)
                        CP("act", qraw[:, 0:512], pQ[:, :], r=[], w=["pQ", "qraw"])
                        CP("dve", qraw[:, 512:1024], pK[:, :], r=[], w=["pK", "qraw"])
                        CP("act", vb[:, t, :], pV[:, :], r=[], w=["pV", "vb"])
                        CP("act", gs[:, t, :], pG[:, :], r=[], w=["pG", "gs"])
                        ACTV(bt8[:], pB[:, 0:8], AF.Exp, r=[], w=["pB", "bt8"], scale=-1.0)
                        TS("dve", bt8[:], bt8[:], 1.0, None, ALU.add, None, r=["bt8"], w=["bt8"])
                        RECIP(gbt[:, t, 8:16], bt8[:], r=["bt8"], w=["gbt"])
                        TT("dve", zt[:], pB[:, 8:16], dtb[:], ALU.add, r=["dtb"], w=["pB", "zt"])
                        ACTV(zt[:], zt[:], AF.Exp, r=["zt"], w=["zt"])
                        ACTV(zt[:], zt[:], AF.Ln, r=["zt", "one"], w=["zt"], bias=oneT[:])
                        TT("dve", gbt[:, t, 0:8], zt[:], nexpA[:], ALU.mult, r=["zt", "nexpA"], w=["gbt"])
                        ACTV(sq[:], qraw[:], AF.Square, r=["qraw"], w=["sq"])
                        RED("dve", ssq[:], sq[:].rearrange("p (h d) -> p h d", d=64), r=["sq"], w=["ssq"])
                        ACTV(ssq[:], ssq[:], AF.Ln, r=["ssq", "eps"], w=["ssq"], scale=1.0 / 64, bias=epsT[:])
                        ACTV(ssq[:], ssq[:], AF.Exp, r=["ssq"], w=["ssq"], scale=-0.5)
                        TT("dve", qn[:].rearrange("p (h d) -> p h d", d=64), qraw[:].rearrange("p (h d) -> p h d", d=64),
                           ssq[:].unsqueeze(2).to_broadcast([128, 16, 64]), ALU.mult, r=["ssq", "qraw"], w=["qn"])
                        TT("dve", qn[:], qn[:], qkg[:], ALU.mult, r=["qn", "qkg"], w=["qn"])
                        if ti % 16 == 0:
                            for tt in range(16):
                                STT("dve", ang16[:, tt, :], invf[:], post[:, ti + tt:ti + tt + 1], phase[:], ALU.mult, ALU.add,
                                    r=["invf", "post", "phase"], w=["ang16"])
                            TS("dve", kfi16[:], ang16[:], 1.0 / TWO_PI, None, ALU.mult, None, r=["ang16"], w=["kfi16"])
                            CP("dve", kff16[:], kfi16[:], r=["kfi16"], w=["kff16"])
                            STT("dve", ang16[:], kff16[:], -TWO_PI, ang16[:], ALU.mult, ALU.add, r=["kff16", "ang16"], w=["ang16"])
                            TS("dve", ang16[:], ang16[:], -float(np.pi), float(np.pi), ALU.max, ALU.min, r=["ang16"], w=["ang16"])
                            ACTV(cs16[:], ang16[:], AF.Sin, r=["ang16"], w=["cs16"])
                        qr = qrs[ti % 2]
                        qrk = ("qr", ti % 2)
                        cs = cs16[:, ti % 16, :]
                        qv = qn[:].rearrange("p (h d) -> p h d", d=64)
                        qrv = qr[:].rearrange("p (h d) -> p h d", d=64)
                        sinb = cs[:, 0:32].unsqueeze(1).to_broadcast([128, 16, 32])
                        cosb = cs[:, 32:64].unsqueeze(1).to_broadcast([128, 16, 32])
                        v3 = lambda tt: tt[:].rearrange("p (h d) -> p h d", d=32)
                        TT("dve", v3(t1), qv[:, :, 0:32], cosb, ALU.mult, r=["qn", "cs16"], w=["t1"])
                        TT("dve", v3(t2), qv[:, :, 32:64], sinb, ALU.mult, r=["qn", "cs16"], w=["t2"])
                        TT("dve", qrv[:, :, 0:32], v3(t1), v3(t2), ALU.subtract, r=["t1", "t2"], w=[qrk])
                        TT("pool", v3(t3), qv[:, :, 32:64], cosb, ALU.mult, r=["qn", "cs16"], w=["t3"])
                        TT("pool", v3(t4), qv[:, :, 0:32], sinb, ALU.mult, r=["qn", "cs16"], w=["t4"])
                        TT("pool", qrv[:, :, 32:64], v3(t3), v3(t4), ALU.add, r=["t3", "t4"], w=[qrk])

                        def tr_stage(qr=qr, qrk=qrk, t=t, qkT=qkTs[u % 2], qkk=("qkT", u % 2)):
                            for j in range(8):
                                TR(pTT[:, j, :], qr[:, j * 128:(j + 1) * 128], ident[:, :], r=[qrk, "ident"], w=["pTT"])
                            CP("act", qkT[:, :, 128 * t:128 * (t + 1)], pTT[:, :, :], r=[], w=["pTT", qkk])
                        deferred.append(tr_stage)
                    while deferred:
                        deferred.pop(0)()
                    qkT = qkTs[u % 2]
                    DMA("sp", V_d[KPAD + t0:KPAD + t0 + 512, :].rearrange("(t p) c -> p t c", p=128), vb[:], r=["vb"], w=[("V_d", u)])
                    DMA("sp", G_d[t0:t0 + 512, :].rearrange("(t p) c -> p t c", p=128), gs[:], r=["gs"], w=[("G_d", u)])
                    DMA("sp", GB_d[t0:t0 + 512, :].rearrange("(t p) c -> p t c", p=128), gbt[:], r=["gbt"], w=[("GB_d", u)])
                    DMA("sp", QT_d[:, :, KPAD + t0:KPAD + t0 + 512].rearrange("j p w -> p j w"), qkT[:, 0:4, :], r=[("qkT", u % 2)], w=[("QT_d", u)])
                    DMA("sp", KT_d[:, :, KPAD + t0:KPAD + t0 + 512].rearrange("j p w -> p j w"), qkT[:, 4:8, :], r=[("qkT", u % 2)], w=[("KT_d", u)])
            P.barrier()

        def pass2():
            with contextlib.ExitStack() as es:
                def sb(name, shape, dt=F32):
                    return es.enter_context(nc.sbuf_tensor("p2_" + name, list(shape), dt))

                def ps(name, shape, dt=F32):
                    return es.enter_context(nc.psum_tensor("p2_" + name, list(shape), dt))
                lv = sb("lv", [128, 2 * NSEG])
                am_f = sb("am_f", [128, 256])
                am = sb("am", [128, 256], BF16)
                amL = sb("amL", [128, 256], BF16); amR = sb("amR", [128, 256], BF16); amLR = sb("amLR", [128, 256], BF16)
                ones_b = sb("ones_b", [128, 128])
                DMA("sp", lv[:], lv_d[:, :], w=["lv"])
                DMA("sp", am_f[:], amask_d[:, :], w=["am_f"])
                CP("dve", am[:], am_f[:], r=["am_f"], w=["am"])
                MSET("dve", ones_b[:], 1.0, w=["ones_b"])
                QTs = [sb(f"QTs{i}", [128, 4, SEG], BF16) for i in range(1)]
                KTw = [sb(f"KTw{i}", [128, 4, 2 * SEG], BF16) for i in range(1)]
                acc = [sb(f"acc{e}", [128, 4, SEG]) for e in range(2)]
                attT = sb("attT", [128, 4, SEG], BF16)
                NVB = 8
                vraw = [sb(f"vraw{i}", [128, 512], BF16) for i in range(NVB)]
                vt2 = [sb(f"vt2_{i}", [128, 4, 2, 128], BF16) for i in range(NVB)]
                NPT = 6
                pt = [sb(f"pt{i}", [128, 256], BF16) for i in range(NPT)]
                rden = sb("rden", [128, SEG])
                pS = [ps(f"pS{i}", [128, 512]) for i in range(4)]
                pA = [ps(f"pA{i}", [128, 512]) for i in range(2)]
                pBc = [ps(f"pBc{i}", [128, 512]) for i in range(2)]
                for i in range(NVB):
                    MSET("pool", vt2[i][:], 0.0, w=[("vt2", i)])
                    MSET("pool", vt2[i][:, :, 0, 64:65], 1.0, w=[("vt2", i)])
                    MSET("pool", vt2[i][:, :, 1, 32:33], 1.0, w=[("vt2", i)])
                cnt = {"v": 0, "s": 0, "a": 0, "pt": 0, "m": 0, "bc": 0}
                import collections
                pending = collections.deque()
                SKEW = 3
                for s in range(NSEG):
                    b = 0
                    base = KPAD + s * SEG
                    DMA("sp", QTs[b][:], QT_d[:, :, base:base + SEG].rearrange("j p w -> p j w"),
                        r=[("QT_d", u) for u in range(4 * s, 4 * s + 4)], w=[("QTs", b)])
                    ulo = max(0, 4 * s - 2); uhi = min(NU, 4 * s + 6)
                    DMA("sp", KTw[b][:], KT_d[:, :, base - 1024:base + SEG + 1024].rearrange("j p w -> p j w"),
                        r=[("KT_d", u) for u in range(ulo, uhi)] + [("KT_d", "padL"), ("KT_d", "padR")], w=[("KTw", b)])
                    TS("pool", amL[:, 0:128], am[:, 0:128], lv[:, 2 * s:2 * s + 1], None, ALU.mult, None, r=["am", "lv"], w=["amL"])
                    CP("pool", amL[:, 128:256], am[:, 128:256], r=["am"], w=["amL"])
                    CP("pool", amR[:, 0:128], am[:, 0:128], r=["am"], w=["amR"])
                    TS("pool", amR[:, 128:256], am[:, 128:256], lv[:, 2 * s + 1:2 * s + 2], None, ALU.mult, None, r=["am", "lv"], w=["amR"])
                    CP("pool", amLR[:, 0:128], amL[:, 0:128], r=["amL"], w=["amLR"])
                    CP("pool", amLR[:, 128:256], amR[:, 128:256], r=["amR"], w=["amLR"])
                    vkeys = [("V_d", u) for u in range(ulo, uhi)] + [("V_d", "padL"), ("V_d", "padR")]
                    groups = []
                    for (d, first) in ((1, True), (4, False), (16, False)):
                        nqb = SEG // d // 128
                        for c in range(d):
                            for qb in range(nqb):
                                groups.append((d, first, c, qb, nqb))
                    needed = []
                    seen = set()
                    for (d, first, c, qb, nqb) in groups:
                        for k in (qb, qb + 1):
                            if (d, c, k) not in seen:
                                seen.add((d, c, k))
                                needed.append((d, c, k))
                    vslot = {}
                    nl = [0]

                    def load_v(d, c, k):
                        i = cnt["v"] % NVB
                        cnt["v"] += 1
                        row0 = base + c + d * (128 * k - 64)
                        DMA("sp", vraw[i][:], V_d[_sl(row0, 128, d), :], r=vkeys, w=[("vraw", i)])
                        vr = vraw[i][:].rearrange("p (j e x) -> p j e x", e=2, x=64)
                        CP("pool", vt2[i][:, :, 0, 0:64], vr[:, :, 0, :], r=[("vraw", i)], w=[("vt2", i)])
                        CP("pool", vt2[i][:, :, 1, 64:128], vr[:, :, 1, :], r=[("vraw", i)], w=[("vt2", i)])
                        vslot[(d, c, k)] = i

                    def ensure_loaded(gi):
                        if gi >= len(groups):
                            return
                        d, first, c, qb, nqb = groups[gi]
                        while not ((d, c, qb) in vslot and (d, c, qb + 1) in vslot):
                            load_v(*needed[nl[0]])
                            nl[0] += 1
                    PF = 2
                    for gi, (d, first, c, qb, nqb) in enumerate(groups):
                        for g2 in range(gi, gi + PF + 1):
                            ensure_loaded(g2)
                        if nqb == 1:
                            msk, mkey = amLR, "amLR"
                        elif qb == 0:
                            msk, mkey = amL, "amL"
                        elif qb == nqb - 1:
                            msk, mkey = amR, "amR"
                        else:
                            msk, mkey = am, "am"
                        qcol0 = c + d * 128 * qb
                        qsl = _sl(qcol0, 128, d)
                        for hp in range(4):
                            iSs = []
                            for e in range(2):
                                iSs.append(cnt["s"] % 4); cnt["s"] += 1
                            for half in range(2):
                                k = qb + half
                                kc0 = 1024 + c + d * (128 * k - 64)
                                for e in range(2):
                                    pb = 64 * e
                                    MM(pS[iSs[e]][:, 128 * half:128 * (half + 1)], KTw[b][pb:pb + 64, hp, _sl(kc0, 128, d)],
                                       QTs[b][pb:pb + 64, hp, qsl], True, True, r=[("KTw", b), ("QTs", b)], w=[("pS", iSs[e])])
                            for e in range(2):
                                iS = iSs[e]
                                ip = cnt["pt"] % NPT; cnt["pt"] += 1
                                ACTV(pt[ip][:], pS[iS][:, 0:256], AF.Exp, r=[], w=[("pS", iS), ("pt", ip)], scale=0.125)
                                meng = "pool" if (cnt["m"] % 3 == 0) else "dve"
                                cnt["m"] += 1
                                TT(meng, pt[ip][:], pt[ip][:], msk[:], ALU.mult, r=[("pt", ip), mkey], w=[("pt", ip)])

                                def pv_stage(ip=ip, e=e, hp=hp, qsl=qsl, first=first, v0=vslot[(d, c, qb)], v1=vslot[(d, c, qb + 1)]):
                                    iA = cnt["a"] % 2; cnt["a"] += 1
                                    M = 65 if e == 0 else 128
                                    for half, vi in ((0, v0), (1, v1)):
                                        MM(pA[iA][0:M, 0:128], vt2[vi][:, hp, e, 0:M], pt[ip][:, 128 * half:128 * (half + 1)],
                                           half == 0, half == 1, r=[("vt2", vi), ("pt", ip)], w=[("pA", iA)])
                                    lo, hi = (0, 65) if e == 0 else (0, 128)
                                    dst = acc[e][lo:hi, hp, qsl]
                                    if first:
                                        CP("dve", dst, pA[iA][lo:hi, 0:128], r=[], w=[("pA", iA), ("acc", e, hp)])
                                    else:
                                        TT("dve", dst, dst, pA[iA][lo:hi, 0:128], ALU.add, r=[], w=[("pA", iA), ("acc", e, hp)])
                                pending.append(pv_stage)
                            while len(pending) > SKEW:
                                pending.popleft()()
                    while pending:
                        pending.popleft()()
                    for hp in range(4):
                        for e in range(2):
                            dp = 64 if e == 0 else 32
                            lo, hi = (0, 64) if e == 0 else (64, 128)
                            ACTV(rden[dp:dp + 1, :], acc[e][dp:dp + 1, hp, :], AF.Ln, r=[("acc", e, hp)], w=["rden"])
                            ACTV(rden[dp:dp + 1, :], rden[dp:dp + 1, :], AF.Exp, r=["rden"], w=["rden"], scale=-1.0)
                            for cc in range(SEG // 512):
                                csl = slice(512 * cc, 512 * (cc + 1))
                                ib = cnt["bc"] % 2; cnt["bc"] += 1
                                MM(pBc[ib][0:hi, :], ones_b[dp:dp + 1, 0:hi], rden[dp:dp + 1, csl], True, True,
                                   r=["ones_b", "rden"], w=[("pBc", ib)])
                                TT("dve", attT[lo:hi, hp, csl], acc[e][lo:hi, hp, csl], pBc[ib][lo:hi, :], ALU.mult,
                                   r=[("acc", e, hp)], w=[("pBc", ib), "attT"])
                    DMA("sp", AT_d[:, :, s * SEG:(s + 1) * SEG].rearrange("j p w -> p j w"), attT[:], r=["attT"], w=[("AT_d", s)])
            P.barrier()


        def pass_prep():
            with contextlib.ExitStack() as es:
                def sb(name, shape, dt=F32):
                    return es.enter_context(nc.sbuf_tensor("pp_" + name, list(shape), dt))

                def ps(name, shape, dt=F32):
                    return es.enter_context(nc.psum_tensor("pp_" + name, list(shape), dt))
                dncw = sb("dncw", [128, 5, 12])
                dnsc = sb("dnsc", [128, 4 * NT])
                lnq = sb("lnq", [128, 1])
                dg = sb("dg", [128, 60, 128], BF16)
                ones_b = sb("ones_b", [128, 128], BF16)
                DMA("sp", dncw[:], dncw_d[:, :, :], w=["dncw"])
                DMA("sp", dnsc[:], dnsc_d[:, :], w=["dnsc"])
                MSET("dve", lnq[:], float(-0.5 * np.log(128.0)), w=["lnq"])
                MSET("dve", ones_b[:], 1.0, w=["ones_b"])
                for j in range(5):
                    for fc in range(12):
                        TS("dve", dg[:, j * 12 + fc, :], ident[:, :], dncw[:, j, fc:fc + 1], None, ALU.mult, None,
                           r=["ident", "dncw"], w=["dg"])
                dqb = [sb(f"dqb{i}", [128, 12, 516], BF16) for i in range(2)]
                qk32 = sb("qk32", [128, 8, 512])
                sqb = sb("sqb", [128, 8, 512], BF16)
                rn = sb("rn", [128, 8, 512])
                oqkv = [sb(f"oqkv{i}", [128, 12, 512], BF16) for i in range(2)]
                pC = [ps(f"pC{i}", [128, 512]) for i in range(4)]
                pNn = [ps(f"pN{i}", [128, 512]) for i in range(4)]
                dqv = DQ_d.rearrange("(f p) w -> p f w", p=128)
                dnv = DN_d.rearrange("(f p) w -> p f w", p=128)

                def load(u):
                    t0 = u * 512
                    deps = [("DQ_d", uu) for uu in range(max(0, u - 1), min(NU, u + 2))] + [("DQ_d", "padL"), ("DQ_d", "padR")]
                    DMA("sp", dqb[u % 2][:], dqv[:, :, DQPAD + t0 - 2:DQPAD + t0 + 514], r=deps, w=[("dqb", u % 2)])
                load(0)
                for u in range(NU):
                    b = u % 2
                    t0 = u * 512
                    if u + 1 < NU:
                        load(u + 1)
                    ti = u * 4
                    if t0 % SEG == 0:
                        TS("dve", dqb[b][:, :, 0:2], dqb[b][:, :, 0:2], dnsc[:, 4 * ti:4 * ti + 1], None, ALU.mult, None,
                           r=[("dqb", b), "dnsc"], w=[("dqb", b)])
                    if (t0 + 512) % SEG == 0:
                        TS("dve", dqb[b][:, :, 514:516], dqb[b][:, :, 514:516], dnsc[:, 4 * (ti + 3) + 1:4 * (ti + 3) + 2], None,
                           ALU.mult, None, r=[("dqb", b), "dnsc"], w=[("dqb", b)])
                    for fc in range(12):
                        i = fc % 4
                        for j in range(5):
                            MM(pC[i][:, :], dg[:, j * 12 + fc, :], dqb[b][:, fc, j:j + 512], j == 0, j == 4, r=["dg", ("dqb", b)], w=[("ppC", i)])
                        if fc < 8:
                            ACTV(qk32[:, fc, :], pC[i][:, :], AF.Silu, r=[], w=[("ppC", i), ("qk32", fc)])
                        else:
                            ACTV(oqkv[b][:, fc, :], pC[i][:, :], AF.Silu, r=[], w=[("ppC", i), ("oqkv", b)])
                    for fc in range(8):
                        ACTV(sqb[:, fc, :], qk32[:, fc, :], AF.Square, r=[("qk32", fc)], w=[("sqb", fc)])
                    for fc in range(8):
                        i = fc % 4
                        MM(pNn[i][:, :], ones_b[:, :], sqb[:, fc, :], True, True, r=["ones_b", ("sqb", fc)], w=[("ppN", i)])
                        ACTV(rn[:, fc, :], pNn[i][:, :], AF.Ln, r=["eps"], w=[("ppN", i), ("rn", fc)], bias=epsT[:])
                    for fc in range(8):
                        if fc < 4:
                            ACTV(rn[:, fc, :], rn[:, fc, :], AF.Exp, r=[("rn", fc), "lnq"], w=[("rn", fc)], scale=-0.5, bias=lnq[:])
                        else:
                            ACTV(rn[:, fc, :], rn[:, fc, :], AF.Exp, r=[("rn", fc)], w=[("rn", fc)], scale=-0.5)
                        TT("dve", oqkv[b][:, fc, :], qk32[:, fc, :], rn[:, fc, :], ALU.mult, r=[("qk32", fc), ("rn", fc)], w=[("oqkv", b)])
                    DMA("sp", dnv[:, :, t0:t0 + 512], oqkv[b][:], r=[("oqkv", b)], w=[("DN_d", u)])
            P.barrier()

        def pass_dn_gen(dr, es):
            NEGC = 11
            if True:
                def sb(name, shape, dt=F32):
                    return es.enter_context(nc.sbuf_tensor(f"p3{dr}_" + name, list(shape), dt))

                def ps(name, shape, dt=F32):
                    return es.enter_context(nc.psum_tensor(f"p3{dr}_" + name, list(shape), dt))
                dnc = sb("dnc", [128, NEGC, 128])
                dnsc = sb("dnsc", [128, 4 * NT])
                DMA("sp", dnc[:], dnc_d[:, :, :], w=["dnc"])
                DMA("sp", dnsc[:], dnsc_d[:, :], w=["dnsc"])
                U = dnc[:, 0 + dr, :]
                ONES = dnc[:, 2, :]
                MLOW = dnc[:, 3 + dr, :]
                MQ = dnc[:, 5 + dr, :]
                IDF = dnc[:, 7, :]
                BD = dnc[:, 10, :]
                qkvb = [sb(f"qkvb{i}", [128, 12, 128], BF16) for i in range(2)]
                gb = [sb(f"gb{i}", [128, 16]) for i in range(2)]
                gst = sb("gst", [128, 16])
                egc = sb("egc", [128, 4]); be = sb("be", [128, 4]); edk = sb("edk", [128, 4]); egl = sb("egl", [128, 8])
                Ug = sb("Ug", [128, 4, 128])
                Nm = sb("Nm", [128, 4, 128])
                DL = sb("DL", [128, 4, 128]); DQm = sb("DQm", [128, 4, 128])
                egcr = sb("egcr", [128, 4, 128])
                Lb = [sb(f"Lb{i}", [128, 4, 128], BF16) for i in range(2)]
                Mb = [sb(f"Mb{i}", [128, 4, 128], BF16) for i in range(2)]
                Z = sb("Z", [128, 4, 128], BF16)
                kkb = sb("kkb", [128, 4, 128])
                qkTm = sb("qkTm", [128, 4, 128], BF16)
                kbe = sb("kbe", [128, 4, 128], BF16); kdec = sb("kdec", [128, 4, 128], BF16); vbt = sb("vbt", [128, 4, 128], BF16)
                um = sb("um", [128, 4, 128]); wTm = sb("wTm", [128, 4, 128], BF16); qdTm = sb("qdTm", [128, 4, 128], BF16)
                vn = sb("vn", [128, 4, 128], BF16)
                ot = [sb(f"ot{i}", [128, 4, 128]) for i in range(2)]
                S = sb("S", [128, 4, 128])
                Sb = sb("Sb", [128, 4, 128], BF16)
                g = [ps(f"g{i}", [128, 4, 128]) for i in range(3)]
                tbb = ps("tbb", [128, 8, 128], BF16)
                tb = [tbb[:, 0:4, :], tbb[:, 4:8, :]]
                gi = [0]
                tbi = [0]

                def nextg():
                    i = gi[0] % 3
                    gi[0] += 1
                    return g[i], ("p3g", i)

                def nexttb():
                    i = tbi[0] % 2
                    tbi[0] += 1
                    return tb[i], "p3tb"
                bc = lambda ap4: ap4.unsqueeze(2).to_broadcast([128, 4, 128])
                identb4 = ident[:, :].unsqueeze(1).to_broadcast([128, 4, 128])
                MSET("dve", S[:], 0.0, w=["S"])
                MSET("dve", Sb[:], 0.0, w=["Sb"])
                order = list(range(NT)) if dr == 0 else list(range(NT - 1, -1, -1))
                dnv = DN_d.rearrange("(f p) w -> p f w", p=128)

                def load(ti, b):
                    t0 = ti * 128
                    u = ti // 4
                    DMA("sp", qkvb[b][:], dnv[:, :, t0:t0 + 128], r=[("DN_d", u)], w=[("qkvb", b)])
                    DMA("sp", gb[b][:], GB_d[t0:t0 + 128, :], r=[("GB_d", u)], w=[("gb", b)])
                load(order[0], 0)
                for n_, ti in enumerate(order):
                    b = n_ % 2
                    t0 = ti * 128
                    if n_ + 1 < NT:
                        load(order[n_ + 1], 1 - b)
                    g4 = gb[b][:, 4 * dr:4 * dr + 4]
                    b4 = gb[b][:, 8 + 4 * dr:12 + 4 * dr]
                    qT = lambda h: qkvb[b][:, h, :]
                    kT = lambda h: qkvb[b][:, 4 + h, :]
                    vT = lambda h: qkvb[b][:, 8 + h, :]
                    pG3, kG = nextg()
                    pG = pG3[:, :, :].rearrange("p a b -> p (a b)")
                    MM(pG[:, 0:4], U, g4, True, True, r=["dnc", ("gb", b)], w=[kG])
                    MM(pG[:, 4:8], BD, g4, True, True, r=["dnc", ("gb", b)], w=[kG])
                    MM(pG[:, 8:12], dnc[:, 8, :], g4, True, True, r=["dnc", ("gb", b)], w=[kG])
                    MM(pG[:, 12:16], dnc[:, 9, :], g4, True, True, r=["dnc", ("gb", b)], w=[kG])
                    CP("dve", gst[:], pG[:, 0:16], r=[], w=[kG, "gst"])
                    yield
                    ACTV(egc[:], gst[:, 0:4], AF.Exp, r=["gst"], w=["egc"])
                    TT("dve", be[:], egc[:], b4, ALU.mult, r=["egc", ("gb", b)], w=["be"])
                    TT("dve", edk[:], gst[:, 4:8], gst[:, 0:4], ALU.subtract, r=["gst"], w=["edk"])
                    ACTV(edk[:], edk[:], AF.Exp, r=["edk"], w=["edk"])
                    ACTV(egl[:], gst[:, 8:16], AF.Exp, r=["gst"], w=["egl"])
                    TT("dve", Ug[:], U.unsqueeze(1).to_broadcast([128, 4, 128]), bc(g4), ALU.mult, r=["dnc", ("gb", b)], w=["Ug"])
                    yield
                    pR, kR = nextg()
                    MM(pR[:, :, :], ONES, Ug[:, :, :], True, True, r=["dnc", "Ug"], w=[kR])
                    TT("dve", Nm[:], pR[:, :, :], bc(gst[:, 0:4]), ALU.subtract, r=["gst"], w=[kR, "Nm"])
                    ACTV(egcr[:], pR[:, :, :], AF.Exp, r=[], w=[kR, "egcr"])
                    yield
                    mlb = MLOW.unsqueeze(1).to_broadcast([128, 4, 128])
                    mqb = MQ.unsqueeze(1).to_broadcast([128, 4, 128])
                    STT("dve", DL[:], Nm[:], -1.0, mlb, ALU.mult, ALU.add, r=["Nm", "dnc"], w=["DL"])
                    TT("pool", DQm[:], Nm[:], mqb, ALU.add, r=["Nm", "dnc"], w=["DQm"])
                    ACTV(DL[:], DL[:], AF.Exp, r=["DL"], w=["DL"])
                    ACTV(DQm[:], DQm[:], AF.Exp, r=["DQm"], w=["DQm"])
                    yield
                    pK_, kK = nextg()
                    for h in range(4):
                        MM(pK_[:, h, :], kT(h), kT(h), True, True, r=[("qkvb", b)], w=[kK])
                    TT("dve", kkb[:], pK_[:, :, :], bc(b4), ALU.mult, r=[("gb", b)], w=[kK, "kkb"])
                    TT("dve", Lb[0][:], kkb[:], DL[:], ALU.mult, r=["kkb", "DL"], w=[("Lb", 0)])
                    yield
                    pQ_, kQ = nextg()
                    for h in range(4):
                        MM(pQ_[:, h, :], kT(h), qT(h), True, True, r=[("qkvb", b)], w=[kQ])
                    TT("dve", qkTm[:], pQ_[:, :, :], DQm[:], ALU.mult, r=["DQm"], w=[kQ, "qkTm"])
                    yield
                    pM_, kM = nexttb()
                    for h in range(4):
                        TR(pM_[:, h, :], Lb[0][:, h, :], ident[:, :], r=[("Lb", 0), "ident"], w=[kM])
                    CP("act", Mb[0][:], pM_[:, :, :], r=[], w=[kM, ("Mb", 0)])
                    yield
                    STT("dve", Z[:], Mb[0][:], -1.0, identb4, ALU.mult, ALU.add, r=[("Mb", 0), "ident"], w=["Z"])
                    cur = 0
                    for lvl in range(5):
                        nxt = 1 - cur
                        pP_, kP = nextg()
                        for h in range(4):
                            MM(pP_[:, h, :], Mb[cur][:, h, :], Lb[cur][:, h, :], True, True, r=[("Mb", cur), ("Lb", cur)], w=[kP])
                        CP("act", Lb[nxt][:], pP_[:, :, :], r=[], w=[kP, ("Lb", nxt)])
                        yield
                        if lvl < 4:
                            pM2, kM2 = nextg()
                            for h in range(4):
                                MM(pM2[:, h, :], Lb[cur][:, h, :], Mb[cur][:, h, :], True, True, r=[("Mb", cur), ("Lb", cur)], w=[kM2])
                            CP("dve", Mb[nxt][:], pM2[:, :, :], r=[], w=[kM2, ("Mb", nxt)])
                            yield
                        pZ_, kZ = nextg()
                        for h in range(4):
                            MM(pZ_[:, h, :], Lb[nxt][:, h, :], Z[:, h, :], True, True, r=[("Lb", nxt), "Z"], w=[kZ])
                        TT("dve", Z[:], Z[:], pZ_[:, :, :], ALU.add, r=["Z"], w=[kZ, "Z"])
                        yield
                        cur = nxt
                    pT1, kT1 = nexttb()
                    for h in range(4):
                        TR(pT1[:, h, :], kT(h), ident[:, :], r=[("qkvb", b), "ident"], w=[kT1])
                    TT("dve", kbe[:], pT1[:, :, :], bc(be[:]), ALU.mult, r=["be"], w=[kT1, "kbe"])
                    TT("dve", kdec[:], pT1[:, :, :], bc(edk[:]), ALU.mult, r=["edk"], w=[kT1, "kdec"])
                    yield
                    pT2, kT2 = nexttb()
                    for h in range(4):
                        TR(pT2[:, h, :], vT(h), ident[:, :], r=[("qkvb", b), "ident"], w=[kT2])
                    TT("dve", vbt[:], pT2[:, :, :], bc(b4), ALU.mult, r=[("gb", b)], w=[kT2, "vbt"])
                    yield
                    pU_, kU = nextg()
                    for h in range(4):
                        MM(pU_[:, h, :], Z[:, h, :], vbt[:, h, :], True, True, r=["Z", "vbt"], w=[kU])
                    CP("act", um[:], pU_[:, :, :], r=[], w=[kU, "um"])
                    yield
                    pW_, kW = nextg()
                    for h in range(4):
                        MM(pW_[:, h, :], kbe[:, h, :], Z[:, h, :], True, True, r=["Z", "kbe"], w=[kW])
                    CP("act", wTm[:], pW_[:, :, :], r=[], w=[kW, "wTm"])
                    TT("pool", qdTm[:], qkvb[b][:, 0:4, :], egcr[:], ALU.mult, r=[("qkvb", b), "egcr"], w=["qdTm"])
                    yield
                    carry = None
                    if dr == 0 and ti % 16 == 0 and ti > 0:
                        carry = dnsc[:, 4 * ti + 2:4 * ti + 3]
                    if dr == 1 and ti % 16 == 15 and ti < NT - 1:
                        carry = dnsc[:, 4 * ti + 3:4 * ti + 4]
                    if carry is not None:
                        TS("dve", S[:], S[:], carry, None, ALU.mult, None, r=["S", "dnsc"], w=["S"])
                        CP("act", Sb[:], S[:], r=["S"], w=["Sb"])
                    io = n_ % 2
                    for c in ((0, 1) if dr == 0 else (1, 0)):
                        cs = slice(64 * c, 64 * c + 64)
                        pVN, kVN = nextg()
                        for h in range(4):
                            MM(pVN[cs, h, :], wTm[:, h, cs], Sb[:, h, :], True, True, r=["wTm", "Sb"], w=[kVN])
                        TT("dve", vn[cs, :, :], um[cs, :, :], pVN[cs, :, :], ALU.subtract, r=["um"], w=[kVN, "vn"])
                        yield
                        pO, kO = nextg()
                        for h in range(4):
                            MM(pO[cs, h, :], qdTm[:, h, cs], Sb[:, h, :], True, False, r=["qdTm", "Sb"], w=[kO])
                            MM(pO[cs, h, :], qkTm[cs, h, cs], vn[cs, h, :], False, True, r=["qkTm", "vn"], w=[kO])
                        pSn, kSn = nextg()
                        for h in range(4):
                            MM(pSn[:, h, :], kdec[cs, h, :], vn[cs, h, :], True, True, r=["kdec", "vn"], w=[kSn])
                        for h in range(4):
                            STT("dve", S[:, h, :], S[:, h, :], egl[:, 4 * c + h:4 * c + h + 1], pSn[:, h, :], ALU.mult, ALU.add,
                                r=["S", "egl"], w=[kSn, "S"])
                        CP("act", Sb[:], S[:], r=["S"], w=["Sb"])
                        CP("act", ot[io][cs, :, :], pO[cs, :, :], r=[], w=[kO, ("ot", io)])
                        yield
                    DMA("sp", OD_d[dr][t0:t0 + 128, :], ot[io][:].rearrange("p h d -> p (h d)"), r=[("ot", io)], w=[("OD_d", dr, ti)])
                    yield

        def pass_dn_both(dirs):
            P.shared = {"eps", "one", "ident", "GB_d", "DQ_d", "OD_d", "DN_d"}
            with contextlib.ExitStack() as es:
                gens = [(dr, pass_dn_gen(dr, es)) for dr in dirs]
                while gens:
                    for item in list(gens):
                        P.ns = ("dn", item[0])
                        try:
                            next(item[1])
                        except StopIteration:
                            gens.remove(item)
                P.ns = None
            P.barrier()

        def pass5():
            with contextlib.ExitStack() as es:
                def sb(name, shape, dt=F32):
                    return es.enter_context(nc.sbuf_tensor("p5_" + name, list(shape), dt))

                def ps(name, shape, dt=F32):
                    return es.enter_context(nc.psum_tensor("p5_" + name, list(shape), dt))
                wout = sb("wout", [128, 8, D], BF16)
                dnog = sb("dnog", [128, 512])
                DMA("sp", dnog[:], dnog_d[:, :], w=["dnog"])
                for kc in range(8):
                    DMA("pool", wout[:, kc, :], w_out_d[kc * 128:(kc + 1) * 128, :], w=[("wout", kc)])
                NB = 3
                NX = 4
                of_ = [sb(f"of{i}", [128, 512]) for i in range(NB)]
                ob_ = [sb(f"ob{i}", [128, 512]) for i in range(NB)]
                gt = [sb(f"gt{i}", [128, 512]) for i in range(NB)]
                xt = [sb(f"xt{i}", [128, D]) for i in range(NX)]
                at = [sb(f"at{i}", [128, 4, 128], BF16) for i in range(NX)]
                osum = sb("osum", [128, 512]); sq = sb("sq", [128, 512]); ss = sb("ss", [128, 4])
                gg = sb("gg", [128, 512]); on = sb("on", [128, 512])
                odn = [sb(f"odn{i}", [128, 512], BF16) for i in range(2)]
                eg = sb("eg", [128, 512])
                dnT = sb("dnT", [128, 4, 128], BF16)
                ht = [sb(f"ht{i}", [128, D]) for i in range(2)]
                pT = ps("pT", [128, 4, 128], BF16)
                pH = [ps(f"pH{i}", [128, 512]) for i in range(4)]

                def loads(ti):
                    b = ti % NB
                    bx = ti % NX
                    t0 = ti * 128
                    u = ti // 4
                    DMA("sp", of_[b][:], OD_d[0][t0:t0 + 128, :], r=[("OD_d", 0, ti)], w=[("of", b)])
                    DMA("sp", ob_[b][:], OD_d[1][t0:t0 + 128, :], r=[("OD_d", 1, ti)], w=[("ob", b)])
                    DMA("sp", gt[b][:], G_d[t0:t0 + 128, :], r=[("G_d", u)], w=[("gt", b)])
                    DMA("sp", xt[bx][:], xs[t0:t0 + 128, :], w=[("xt5", bx)])
                    DMA("sp", at[bx][:], AT_d[:, :, t0:t0 + 128].rearrange("j p w -> p j w"), r=[("AT_d", ti // 16)], w=[("at", bx)])

                def stage_a(ti):
                    b = ti % NB
                    TT("dve", osum[:], of_[b][:], ob_[b][:], ALU.add, r=[("of", b), ("ob", b)], w=["osum"])
                    ACTV(sq[:], osum[:], AF.Square, r=["osum"], w=["sq5"])
                    RED("dve", ss[:], sq[:].rearrange("p (h d) -> p h d", d=128), r=["sq5"], w=["ss5"])
                    ACTV(ss[:], ss[:], AF.Ln, r=["ss5", "eps"], w=["ss5"], scale=1.0 / 128, bias=epsT[:])
                    ACTV(ss[:], ss[:], AF.Exp, r=["ss5"], w=["ss5"], scale=-0.5)
                    ACTV(eg[:], gt[b][:], AF.Exp, r=[("gt", b)], w=["eg"], scale=-1.0)
                    ACTV(eg[:], eg[:], AF.Ln, r=["eg", "one"], w=["eg"], bias=oneT[:])
                    ACTV(eg[:], eg[:], AF.Exp, r=["eg"], w=["eg"], scale=-1.0)
                    TT("pool", gg[:], gt[b][:], dnog[:], ALU.mult, r=[("gt", b), "dnog"], w=["gg"])
                    TT("dve", gg[:], gg[:], eg[:], ALU.mult, r=["gg", "eg"], w=["gg"])
                    TT("dve", on[:].rearrange("p (h d) -> p h d", d=128), osum[:].rearrange("p (h d) -> p h d", d=128),
                       ss[:].unsqueeze(2).to_broadcast([128, 4, 128]), ALU.mult, r=["osum", "ss5"], w=["on"])
                    TT("dve", odn[ti % 2][:], on[:], gg[:], ALU.mult, r=["on", "gg"], w=[("odn", ti % 2)])

                def stage_b(ti):
                    bx = ti % NX
                    t0 = ti * 128
                    u = ti // 4
                    for j in range(4):
                        TR(pT[:, j, :], odn[ti % 2][:, j * 128:(j + 1) * 128], ident[:, :], r=[("odn", ti % 2), "ident"], w=["p5T"])
                    CP("act", dnT[:], pT[:, :, :], r=[], w=["p5T", "dnT"])
                    ih = ti % 2
                    for nh in range(2):
                        ip = (2 * ti + nh) % 4
                        csl = slice(512 * nh, 512 * (nh + 1))
                        for j in range(4):
                            MM(pH[ip][:, :], at[bx][:, j, :], wout[:, j, csl], j == 0, False, r=[("at", bx), ("wout", j)], w=[("p5H", ip)])
                        for j in range(4):
                            MM(pH[ip][:, :], dnT[:, j, :], wout[:, 4 + j, csl], False, j == 3, r=["dnT", ("wout", 4 + j)], w=[("p5H", ip)])
                        TT("dve", ht[ih][:, csl], pH[ip][:, :], xt[bx][:, csl], ALU.add, r=[("xt5", bx)], w=[("p5H", ip), ("ht5", ih)])
                    DMA("sp", H_d[t0:t0 + 128, :], ht[ih][:], r=[("ht5", ih)], w=[("H_d", u)])
                loads(0)
                if NT > 1:
                    loads(1)
                stage_a(0)
                for ti in range(NT):
                    if ti + 2 < NT:
                        loads(ti + 2)
                    if ti + 1 < NT:
                        stage_a(ti + 1)
                    stage_b(ti)
            P.barrier()

        def pass6(hbuf, hkeys_fn):
            with contextlib.ExitStack() as es:
                def sb(name, shape, dt=F32):
                    return es.enter_context(nc.sbuf_tensor("p6_" + name, list(shape), dt))

                def ps(name, shape, dt=F32):
                    return es.enter_context(nc.psum_tensor("p6_" + name, list(shape), dt))
                wup = sb("wup", [128, 8, 2 * FFN], BF16)
                wdn = sb("wdn", [128, NFC, D], BF16)
                g2 = sb("g2", [128, 8])
                cw = sb("cw", [128, 3, 44])
                cb = sb("cb", [128, 44])
                hsc = sb("hsc", [2, NU])
                DMA("sp", g2[:], norm2_d[:, :], w=["g2"])
                DMA("sp", cw[:], cw_d[:, :, :], w=["cw"])
                DMA("sp", cb[:], cb_d[:, :], w=["cb"])
                DMA("sp", hsc[:], halo_sc_d[:, :], w=["hsc"])
                for kc in range(8):
                    DMA("pool", wup[:, kc, :], w_up_d[kc * 128:(kc + 1) * 128, :], w=[("wup", kc)])
                    TS("dve", wup[:, kc, :], wup[:, kc, :], g2[:, kc:kc + 1], None, ALU.mult, None, r=["g2", ("wup", kc)], w=[("wup", kc)])
                for j in range(NFC):
                    DMA("pool", wdn[:, j, :], w_down_d[j * 128:(j + 1) * 128, :], w=[("wdn", j)])
                ht = sb("ht", [128, 4, D])
                hh = sb("hh", [2, D], BF16)
                hn = sb("hn", [128, D], BF16)
                hhn = sb("hhn", [2, D], BF16)
                ms = [sb(f"ms{i}", [128, 1]) for i in range(2)]
                hnTs = [sb(f"hnT{i}", [128, 8, 514], BF16) for i in range(2)]
                actTh = [sb(f"actT{i}", [128, NFC, 256], BF16) for i in range(2)]
                downq = []
                accg = [sb(f"accg{i}", [128, 256]) for i in range(2)]
                accu = [sb(f"accu{i}", [128, 256]) for i in range(2)]
                sg = [sb(f"sg{i}", [128, 256], BF16) for i in range(2)]
                yt = [sb(f"yt{i}", [128, D]) for i in range(2)]
                pT = [ps(f"pT{i}", [128, 8, 128], BF16) for i in range(2)]
                pU = [ps(f"pU{i}", [128, 512]) for i in range(4)]
                pD = [ps(f"pD{i}", [128, 512]) for i in range(2)]
                ucount = [0]
                dcount = [0]

                def load_h(u):
                    t0 = u * 512
                    DMA("sp", ht[:], hbuf[t0:t0 + 512, :].rearrange("(t p) d -> p t d", p=128), r=hkeys_fn(u), w=["ht"])
                    lo = max(t0 - 1, 0)
                    hi = min(t0 + 512, NTOK - 1)
                    DMA("pool", hh[0:1, :], hbuf[lo:lo + 1, :], r=hkeys_fn(max(u - 1, 0)), w=["hh"])
                    DMA("pool", hh[1:2, :], hbuf[hi:hi + 1, :], r=hkeys_fn(min(u + 1, NU - 1)), w=["hh"])

                ndef = []

                def norm_piece(u, piece, defer=None):
                    hnT = hnTs[u % 2]
                    hk = ("hnT", u % 2)
                    if piece == 0:
                        def dst_halo():
                            CP("dve", hnT[:, :, 0:514:513], pT[1][:, :, 0:2], r=[], w=["pT1", hk])
                        rmsnorm_T(None, hh[:, :], 2, ms[1], hhn, hhn, pT[1], "pT1", dst_halo, ["hh"], "p6h",
                                  scale_ap=hsc[:, u:u + 1], scale_key="hsc", junk_key="p6hhn", defer=defer)
                    else:
                        t = piece - 1

                        def dst_main(t=t):
                            CP("act", hnT[:, :, 1 + 128 * t:1 + 128 * (t + 1)], pT[0][:, :, :], r=[], w=["pT0", hk])
                        rmsnorm_T(None, ht[:, t, :], 128, ms[0], hn, hn, pT[0], "pT0", dst_main, ["ht"], "p6m", junk_key="p6mhn", defer=defer)
                load_h(0)
                for piece in range(5):
                    norm_piece(0, piece)
                for u in range(NU):
                    t0 = u * 512
                    hnT = hnTs[u % 2]
                    hk = ("hnT", u % 2)
                    if u + 1 < NU:
                        load_h(u + 1)
                    pair = 0
                    for wdw in range(2):
                        gw = 2 * u + wdw
                        actT = actTh[gw % 2]
                        ak = ("actT", gw % 2)
                        for j in range(NFC):
                            ig = (ucount[0] * 2) % 4
                            iu = (ucount[0] * 2 + 1) % 4
                            ia = ucount[0] % 2
                            ucount[0] += 1
                            for (ip, fc) in ((ig, j), (iu, NFC + j)):
                                for kc in range(8):
                                    MM(pU[ip][:, 0:258], wup[:, kc, fc * 128:(fc + 1) * 128], hnT[:, kc, 256 * wdw:256 * wdw + 258],
                                       kc == 0, kc == 7, r=[("wup", kc), hk], w=[("pU", ip)])
                            if u + 1 < NU and pair in (2, 10, 18, 26, 34):
                                norm_piece(u + 1, (pair - 2) // 8, defer=ndef)
                            if pair in (9, 17, 25, 33, 41):
                                while ndef:
                                    ndef.pop(0)()
                            if j in (3, 8, 13, 18) and downq:
                                downq.pop(0)()
                            pair += 1
                            for (ip, fc, acc_, ka) in ((ig, j, accg[ia], ("accg", ia)), (iu, NFC + j, accu[ia], ("accu", ia))):
                                ACTV(acc_[:], pU[ip][:, 1:257], AF.Identity, r=["cw", "cb"], w=[("pU", ip), ka],
                                     scale=cw[:, 1, fc:fc + 1], bias=cb[:, fc:fc + 1])
                                STT("dve", acc_[:], pU[ip][:, 0:256], cw[:, 0, fc:fc + 1], acc_[:], ALU.mult, ALU.add,
                                    r=["cw", ka], w=[("pU", ip), ka])
                                STT("dve", acc_[:], pU[ip][:, 2:258], cw[:, 2, fc:fc + 1], acc_[:], ALU.mult, ALU.add,
                                    r=["cw", ka], w=[("pU", ip), ka])
                            ACTV(sg[ia][:], accg[ia][:], AF.Silu, r=[("accg", ia)], w=[("sg", ia)])
                            TT("pool", actT[:, j, :], sg[ia][:], accu[ia][:], ALU.mult,
                               r=[("sg", ia), ("accu", ia)], w=[ak])
                        while downq:
                            downq.pop(0)()
                        for tt in range(2):
                            tok = t0 + 256 * wdw + 128 * tt
                            for nh in range(2):
                                def down_group(tt=tt, nh=nh, tok=tok, actT=actT, ak=ak, u=u, gw=gw):
                                    iy = (2 * gw + tt) % 2
                                    if nh == 0:
                                        DMA("sp", yt[iy][:], hbuf[tok:tok + 128, :], r=hkeys_fn(u), w=[("yt", iy)])
                                    ipd = dcount[0] % 2
                                    dcount[0] += 1
                                    for j in range(NFC):
                                        MM(pD[ipd][:, :], actT[:, j, 128 * tt:128 * (tt + 1)], wdn[:, j, 512 * nh:512 * (nh + 1)],
                                           j == 0, j == NFC - 1, r=[ak, ("wdn", j)], w=[("pD", ipd)])
                                    TT("dve", yt[iy][:, 512 * nh:512 * (nh + 1)], pD[ipd][:, :], yt[iy][:, 512 * nh:512 * (nh + 1)], ALU.add,
                                       r=[("yt", iy)], w=[("pD", ipd), ("yt", iy)])
                                    if nh == 1:
                                        DMA("sp", ys[tok:tok + 128, :], yt[iy][:], r=[("yt", iy)], w=[("ys", tok)])
                                downq.append(down_group)
                while downq:
                    downq.pop(0)()
            P.barrier()

        if "1" in passes:
            pass1()
        if "2" in passes:
            pass2()
        if "P" in passes:
            pass_prep()
        dirs = [dr for dr, nm in ((0, "3"), (1, "4")) if nm in passes]
        if dirs:
            pass_dn_both(dirs)
        if "5" in passes:
            pass5()
        if "6" in passes:
            if "5" in passes:
                pass6(H_d, lambda u: [("H_d", u)])
            else:
                pass6(xs, lambda u: [])
        P.emit()
    return nc


def _dn_consts():
    NEG = -30000.0
    p = np.arange(128)[:, None]
    f = np.arange(128)[None, :]
    same = (p // 64) == (f // 64)
    c = np.zeros((128, 11, 128), np.float32)
    c[:, 0] = same & (p <= f)
    c[:, 1] = same & (p >= f)
    c[:, 2] = 1.0
    c[:, 3] = np.where(same & (f < p), 0.0, NEG)
    c[:, 4] = np.where(same & (f > p), 0.0, NEG)
    c[:, 5] = np.where(same & (f >= p), 0.0, NEG)
    c[:, 6] = np.where(same & (f <= p), 0.0, NEG)
    c[:, 7] = np.eye(128)
    c[:, 8] = (p // 64 == 0) & (f >= 0)
    c[:, 9] = (p // 64 == 1) & (f >= 0)
    c[:, 10] = same
    return c


def _dn_scales(NSEG, link):
    NT = NSEG * SEG // 128
    sc = np.ones((128, 4 * NT), np.float32)
    for ti in range(NT):
        s = ti // 16
        if ti % 16 == 0:
            sc[:, 4 * ti + 0] = link[s]
            sc[:, 4 * ti + 2] = link[s]
        if ti % 16 == 15:
            sc[:, 4 * ti + 1] = link[s + 1]
            sc[:, 4 * ti + 3] = link[s + 1]
    return sc


def host_consts(NSEG, link, pos0):
    NTOK = NSEG * SEG
    NU = NTOK // 512
    NT = NTOK // 128
    hs = np.ones((2, NU), np.float32)
    for u in range(NU):
        t0 = u * 512
        if t0 % SEG == 0:
            hs[0, u] = link[t0 // SEG]
        if (t0 + 512) % SEG == 0:
            hs[1, u] = link[(t0 + 512) // SEG]
    lv = np.ones((128, 2 * NSEG), np.float32)
    for s in range(NSEG):
        lv[0:64, 2 * s] = link[s]
        lv[64:128, 2 * s + 1] = link[s + 1]
    r_ = np.arange(128)[:, None]
    q_ = np.arange(128)[None, :]
    amask = np.concatenate([(q_ <= r_), (q_ >= r_)], axis=1).astype(np.float32)
    half = 32
    inv_freq = (1.0 / (10000.0 ** (np.arange(half, dtype=np.float32) * 2.0 / 64))).astype(np.float32)
    invf = np.tile(np.concatenate([inv_freq, inv_freq])[None, :], (128, 1)).astype(np.float32)
    phase = np.tile(np.concatenate([np.zeros(32), np.full(32, np.pi / 2)])[None, :], (128, 1)).astype(np.float32)
    pos = np.zeros((128, NT), np.float32)
    for s in range(NSEG):
        for t in range(16):
            pos[:, s * 16 + t] = pos0[s] + t * 128 + np.arange(128)
    return {"ident": np.eye(128, dtype=np.float32), "halo_sc": hs, "lv": lv, "amask": amask, "invf": invf,
            "phase": phase, "pos": pos,
            "dnc": _dn_consts(), "dnsc": _dn_scales(NSEG, link)}


def weight_maps(norm1, w_in, att_q_norm, att_k_norm, dn_conv_w, dn_a_log, dn_dt_bias, dn_out_norm, w_out, norm2,
                w_up, ffn_conv_w, ffn_conv_b, w_down):
    f = lambda a: np.asarray(a, np.float32)[0]
    qg = np.concatenate([np.tile(f(att_q_norm), 8), np.tile(f(att_k_norm), 8)])
    return {
        "w_in": np.ascontiguousarray(f(w_in)),
        "w_out": np.ascontiguousarray(f(w_out)),
        "w_up": np.ascontiguousarray(f(w_up)),
        "w_down": np.ascontiguousarray(f(w_down)),
        "norm1": np.ascontiguousarray(f(norm1).reshape(8, 128).T),
        "norm2": np.ascontiguousarray(f(norm2).reshape(8, 128).T),
        "qkg": np.ascontiguousarray(np.tile(qg[None, :], (128, 1))),
        "alog": np.ascontiguousarray(np.tile(f(dn_a_log).reshape(1, 8), (128, 1))),
        "dtb": np.ascontiguousarray(np.tile(f(dn_dt_bias).reshape(1, 8), (128, 1))),
        "dncw": np.ascontiguousarray(f(dn_conv_w).reshape(5, 12, 128).transpose(2, 0, 1)),
        "dnog": np.ascontiguousarray(np.tile(f(dn_out_norm)[None, :], (128, 4))),
        "ffn_cw": np.ascontiguousarray(f(ffn_conv_w).reshape(3, 44, 128).transpose(2, 0, 1)),
        "ffn_cb": np.ascontiguousarray(f(ffn_conv_b).reshape(44, 128).T),
    }


SAMPLE_SLOTS = [6, 6, 5, 5, 5, 5]
_NC_CACHE = {}


def _core_layout():
    lay = []
    for b in range(2):
        lay.append([("p", b, s) for s in range(8)])
    nxt = 0
    for n in SAMPLE_SLOTS:
        row = []
        for i in range(8):
            if i < n:
                row.append(("s", nxt, 0))
                nxt += 1
            else:
                row.append(None)
        lay.append(row)
    return lay


def kernel(x_prompt, x_sample, norm1, w_in, att_q_norm, att_k_norm, dn_conv_w, dn_a_log, dn_dt_bias,
           dn_out_norm, w_out, norm2, w_up, ffn_conv_w, ffn_conv_b, w_down):
    NSEG = 8
    lay = _core_layout()
    if "nc" not in _NC_CACHE:
        _NC_CACHE["nc"] = build_program(NSEG=NSEG)
    nc = _NC_CACHE["nc"]
    x_prompt = np.asarray(x_prompt, np.float32)
    x_sample = np.asarray(x_sample, np.float32)
    common = weight_maps(norm1, w_in, att_q_norm, att_k_norm, dn_conv_w, dn_a_log, dn_dt_bias, dn_out_norm, w_out,
                         norm2, w_up, ffn_conv_w, ffn_conv_b, w_down)
    in_maps = []
    for c in range(NCORES):
        xs = np.zeros((NSEG * SEG, D), np.float32)
        link = np.zeros(NSEG + 1, np.float32)
        pos0 = np.zeros(NSEG, np.float32)
        for i, ent in enumerate(lay[c]):
            if ent is None:
                continue
            kind, b, s = ent
            if kind == "p":
                xs[i * SEG:(i + 1) * SEG] = x_prompt[b, s * SEG:(s + 1) * SEG]
                pos0[i] = s * SEG
                if s > 0:
                    link[i] = 1.0
            else:
                xs[i * SEG:(i + 1) * SEG] = x_sample[b]
        m = dict(common)
        m.update(host_consts(NSEG, link, pos0))
        m["xs"] = xs
        in_maps.append(m)
    res = run_bass_kernel_spmd(nc, in_maps, core_ids=list(range(NCORES)))
    y_prompt = np.zeros_like(x_prompt)
    y_sample = np.zeros_like(x_sample)
    for c in range(NCORES):
        ys = res.results[c]["ys"]
        for i, ent in enumerate(lay[c]):
            if ent is None:
                continue
            kind, b, s = ent
            if kind == "p":
                y_prompt[b, s * SEG:(s + 1) * SEG] = ys[i * SEG:(i + 1) * SEG]
            else:
                y_sample[b] = ys[i * SEG:(i + 1) * SEG]
    return (y_prompt, y_sample)
```

```python
import numpy as np
import concourse.bass as bass
import concourse.mybir as mybir
from concourse.bass_utils import run_bass_kernel_spmd

F32 = mybir.dt.float32
BF16 = mybir.dt.bfloat16
I32 = mybir.dt.int32
AF = mybir.ActivationFunctionType
ALU = mybir.AluOpType
AX = mybir.AxisListType

D = 1024
FFN = 2816
NFC = FFN // 128
SEG = 2048
NCORES = 8
EPS = 1e-6


class _Op:
    __slots__ = ("eng", "fn", "deps", "sig", "is_dma", "dsem", "dval", "need_sig", "idx")


class Prog:
    ENGS = ("sp", "act", "dve", "pool", "pe")

    def __init__(self, nc, n_dma_sems=8):
        self.nc = nc
        self.ops = []
        self.last_w = {}
        self.readers = {}
        self.n_dma_sems = n_dma_sems
        self.dma_rr = {"sp": 0, "pool": 0, "act": 0}
        self.dma_last = {}
        self.dma_cnt = {}
        self.extra = {}
        self.ns = None
        self.shared = set()
        self.last_op = {}

    def _k(self, k):
        if self.ns is None or k in self.shared or (isinstance(k, tuple) and k[0] in self.shared):
            return k
        return (self.ns, k)

    def op(self, eng, fn, r=(), w=(), dma=False):
        if self.ns is not None:
            r = [self._k(k) for k in r]
            w = [self._k(k) for k in w]
        o = _Op()
        o.eng = eng; o.fn = fn; o.is_dma = dma; o.need_sig = dma; o.sig = None
        o.idx = len(self.ops)
        deps = list(self.extra.pop(eng, []))
        for k in r:
            p = self.last_w.get(k)
            if p is not None:
                deps.append(p)
        for k in w:
            p = self.last_w.get(k)
            if p is not None:
                deps.append(p)
            for q in self.readers.get(k, ()):
                deps.append(q)
        if dma:
            j = self.dma_rr[eng]
            self.dma_rr[eng] = (j + 1) % self.n_dma_sems
            key = (eng, j)
            prev = self.dma_last.get(key)
            if prev is not None:
                deps.append(prev)
            self.dma_last[key] = o
            self.dma_cnt[key] = self.dma_cnt.get(key, 0) + 1
            o.dsem = key
            o.dval = 16 * self.dma_cnt[key]
        dd = []
        seen = set()
        for p in deps:
            if p is o or id(p) in seen:
                continue
            if eng == "pe" and p.eng == "pe" and not p.is_dma:
                continue
            seen.add(id(p))
            dd.append(p)
            p.need_sig = True
        o.deps = dd
        for k in w:
            self.last_w[k] = o
            self.readers[k] = []
        for k in r:
            self.readers.setdefault(k, []).append(o)
        self.ops.append(o)
        if not dma:
            self.last_op[eng] = o
        return o

    def barrier(self):
        markers = [o for o in self.last_op.values()] + [o for o in self.dma_last.values()]
        for e in self.ENGS:
            self.extra[e] = list(markers) + self.extra.get(e, [])

    def emit(self, final_wait_eng="sp"):
        nc = self.nc
        cnt = {e: 0 for e in self.ENGS}
        for o in self.ops:
            if o.is_dma:
                o.sig = (("dma",) + o.dsem, o.dval)
            elif o.need_sig:
                cnt[o.eng] += 1
                o.sig = (("eng", o.eng), cnt[o.eng])
        sem_keys = [("eng", e) for e in self.ENGS] + [("dma", q, j) for q in ("sp", "pool", "act") for j in range(self.n_dma_sems)]
        per_eng = {e: [o for o in self.ops if o.eng == e] for e in self.ENGS}
        finals = {}
        for o in self.ops:
            if o.is_dma:
                finals[o.sig[0]] = max(finals.get(o.sig[0], 0), o.sig[1])
        import contextlib
        with contextlib.ExitStack() as st:
            sems = {}
            for k in sem_keys:
                sems[k] = st.enter_context(nc.semaphore("s_" + "_".join(str(x) for x in k)))
            block = st.enter_context(nc.Block())

            def replay(e, eobj):
                known = {}
                for o in per_eng[e]:
                    need = {}
                    for p in o.deps:
                        sk, v = p.sig
                        if v > need.get(sk, 0):
                            need[sk] = v
                    for sk, v in need.items():
                        if known.get(sk, 0) < v:
                            eobj.wait_ge(sems[sk], v)
                            known[sk] = v
                    ins = o.fn(eobj)
                    if o.sig is not None:
                        ins.then_inc(sems[o.sig[0]], 16 if o.is_dma else 1)
                if e == final_wait_eng:
                    for sk, v in finals.items():
                        if known.get(sk, 0) < v:
                            eobj.wait_ge(sems[sk], v)

            @block.sync
            def _(e):
                replay("sp", e)

            @block.scalar
            def _(e):
                replay("act", e)

            @block.vector
            def _(e):
                replay("dve", e)

            @block.gpsimd
            def _(e):
                replay("pool", e)

            @block.tensor
            def _(e):
                replay("pe", e)


import contextlib

IN_COLS = 3600
KPAD = 1024
DQPAD = 2
TWO_PI = float(2 * np.pi)


def _sl(start, count, step):
    return slice(start, start + (count - 1) * step + 1, step)


def build_program(NSEG=8, passes=("1", "2", "P", "3", "4", "5", "6"), dbg=False):
    nc = bass.Bass("TRN2", target_bir_lowering=False)
    NTOK = NSEG * SEG
    NT = NTOK // 128
    NU = NTOK // 512
    P = Prog(nc)

    def din(name, shape, dt=F32):
        return nc.dram_tensor(name, list(shape), dt, kind="ExternalInput").ap()

    def dout(name, shape, dt=F32):
        return nc.dram_tensor(name, list(shape), dt, kind="ExternalOutput").ap()

    def dscr(name, shape, dt=F32):
        return nc.dram_tensor(name, list(shape), dt, kind=("ExternalOutput" if dbg else "Internal")).ap()

    xs = din("xs", [NTOK, D])
    ident_d = din("ident", [128, 128])
    w_in_d = din("w_in", [D, IN_COLS])
    w_out_d = din("w_out", [D, D])
    w_up_d = din("w_up", [D, 2 * FFN])
    w_down_d = din("w_down", [FFN, D])
    norm1_d = din("norm1", [128, 8])
    norm2_d = din("norm2", [128, 8])
    qkg_d = din("qkg", [128, 1024])
    invf_d = din("invf", [128, 64])
    phase_d = din("phase", [128, 64])
    pos_d = din("pos", [128, NT])
    alog_d = din("alog", [128, 8])
    dtb_d = din("dtb", [128, 8])
    dncw_d = din("dncw", [128, 5, 12])
    dnog_d = din("dnog", [128, 512])
    cw_d = din("ffn_cw", [128, 3, 44])
    cb_d = din("ffn_cb", [128, 44])
    halo_sc_d = din("halo_sc", [2, NU])
    lv_d = din("lv", [128, 2 * NSEG])
    amask_d = din("amask", [128, 256])
    dnc_d = din("dnc", [128, 11, 128])
    dnsc_d = din("dnsc", [128, 4 * NT])
    ys = dout("ys", [NTOK, D])
    KW = NTOK + 2 * KPAD
    QT_d = dscr("QT_s", [4, 128, KW], BF16)
    KT_d = dscr("KT_s", [4, 128, KW], BF16)
    V_d = dscr("V_s", [KW, 512], BF16)
    DQ_d = dscr("DQ_s", [1536, NTOK + 2 * DQPAD], BF16)
    DN_d = dscr("DN_s", [1536, NTOK], BF16)
    G_d = dscr("G_s", [NTOK, 512], F32)
    GB_d = dscr("GB_s", [NTOK, 16], F32)
    AT_d = dscr("AT_s", [4, 128, NTOK], BF16)
    OD_d = [dscr("OF_s", [NTOK, 512], F32), dscr("OB_s", [NTOK, 512], F32)]
    H_d = dscr("H_s", [NTOK, D], F32)

    def MM(out, lhsT, rhs, start, stop, r, w):
        P.op("pe", lambda e, o=out, l=lhsT, rr=rhs, s=start, t=stop: e.matmul(o, lhsT=l, rhs=rr, start=s, stop=t), r=r, w=w)

    def TR(out, in_, idn, r, w):
        P.op("pe", lambda e, o=out, i=in_, d=idn: e.transpose(out=o, in_=i, identity=d), r=r, w=w)

    def ACTV(out, in_, func, r, w, **kw):
        P.op("act", lambda e, o=out, i=in_, f=func, kw=kw: e.activation(out=o, in_=i, func=f, **kw), r=r, w=w)

    def TT(eng, out, in0, in1, op, r, w):
        P.op(eng, lambda e, o=out, a=in0, b=in1, p=op: e.tensor_tensor(out=o, in0=a, in1=b, op=p), r=r, w=w)

    def TS(eng, out, in0, s1, s2, op0, op1, r, w):
        if s2 is None:
            P.op(eng, lambda e, o=out, a=in0, x=s1, p0=op0: e.tensor_scalar(out=o, in0=a, scalar1=x, scalar2=None, op0=p0), r=r, w=w)
        else:
            P.op(eng, lambda e, o=out, a=in0, x=s1, y=s2, p0=op0, p1=op1: e.tensor_scalar(out=o, in0=a, scalar1=x, scalar2=y, op0=p0, op1=p1), r=r, w=w)

    def STT(eng, out, in0, scalar, in1, op0, op1, r, w):
        P.op(eng, lambda e, o=out, a=in0, sc=scalar, b=in1, p0=op0, p1=op1: e.scalar_tensor_tensor(out=o, in0=a, scalar=sc, in1=b, op0=p0, op1=p1), r=r, w=w)

    def CP(eng, out, in_, r, w):
        if eng == "act":
            P.op("act", lambda e, o=out, i=in_: e.copy(out=o, in_=i), r=r, w=w)
        else:
            P.op(eng, lambda e, o=out, i=in_: e.tensor_copy(out=o, in_=i), r=r, w=w)

    def MSET(eng, ap, val, w):
        P.op(eng, lambda e, a=ap, v=val: e.memset(a, v), w=w)

    def DMA(q, out, in_, r=(), w=()):
        P.op(q, lambda e, o=out, i=in_: e.dma_start(out=o, in_=i), r=r, w=w, dma=True)

    def RED(eng, out, in_, r, w):
        P.op(eng, lambda e, o=out, i=in_: e.tensor_reduce(out=o, in_=i, axis=AX.X, op=ALU.add), r=r, w=w)

    def RECIP(out, in_, r, w):
        P.op("dve", lambda e, o=out, i=in_: e.reciprocal(out=o, in_=i), r=r, w=w)

    with contextlib.ExitStack() as es0:
        def sb0(name, shape, dt=F32):
            return es0.enter_context(nc.sbuf_tensor(name, list(shape), dt))
        ident_f = sb0("ident_f", [128, 128])
        ident = sb0("ident_b", [128, 128], BF16)
        epsT = sb0("epsT", [128, 1])
        oneT = sb0("oneT", [128, 1])
        DMA("sp", ident_f[:], ident_d[:, :], w=["ident_f"])
        CP("dve", ident[:], ident_f[:], r=["ident_f"], w=["ident"])
        MSET("dve", epsT[:], EPS, w=["eps"])
        MSET("dve", oneT[:], 1.0, w=["one"])

        def rmsnorm_T(es_sb, src_ap, npart, msT, hnb, sq, pTt, pkey, dst_fn, rkeys, keyp, scale_ap=None, scale_key=None, junk_key="sq",
                      defer=None):
            ACTV(sq[0:npart, :], src_ap, AF.Square, r=rkeys, w=[junk_key, keyp + "ms"], scale=1.0 / 32, accum_out=msT[0:npart, :])
            ACTV(msT[0:npart, :], msT[0:npart, :], AF.Ln, r=[keyp + "ms", "eps"], w=[keyp + "ms"], bias=epsT[0:npart, :])
            ACTV(msT[0:npart, :], msT[0:npart, :], AF.Exp, r=[keyp + "ms"], w=[keyp + "ms"], scale=-0.5)
            if scale_ap is not None:
                TT("dve", msT[0:npart, :], msT[0:npart, :], scale_ap, ALU.mult, r=[keyp + "ms", scale_key], w=[keyp + "ms"])
            TS("dve", hnb[0:npart, :], src_ap, msT[0:npart, 0:1], None, ALU.mult, None, r=rkeys + [keyp + "ms"], w=[keyp + "hn"])
            def second():
                for kc in range(8):
                    TR(pTt[:, kc, 0:npart], hnb[0:npart, kc * 128:(kc + 1) * 128], ident[0:npart, 0:npart],
                       r=[keyp + "hn", "ident"], w=[pkey])
                dst_fn()
            if defer is None:
                second()
            else:
                defer.append(second)

        def pass1():
            with contextlib.ExitStack() as es:
                def sb(name, shape, dt=F32):
                    return es.enter_context(nc.sbuf_tensor("p1_" + name, list(shape), dt))

                def ps(name, shape, dt=F32):
                    return es.enter_context(nc.psum_tensor("p1_" + name, list(shape), dt))
                win = sb("win", [128, 8, IN_COLS], BF16)
                g1 = sb("g1", [128, 8])
                qkg = sb("qkg", [128, 1024])
                invf = sb("invf", [128, 64])
                phase = sb("phase", [128, 64])
                post = sb("post", [128, NT])
                nexpA = sb("nexpA", [128, 8])
                dtb = sb("dtb", [128, 8])
                zb = sb("zb", [128, 2, 512], BF16)
                zf = sb("zf", [128, 12, DQPAD], BF16)
                DMA("sp", g1[:], norm1_d[:, :], w=["g1"])
                DMA("sp", qkg[:], qkg_d[:, :], w=["qkg"])
                DMA("sp", invf[:], invf_d[:, :], w=["invf"])
                DMA("sp", phase[:], phase_d[:, :], w=["phase"])
                DMA("sp", post[:], pos_d[:, :], w=["post"])
                DMA("sp", nexpA[:], alog_d[:, :], w=["nexpA"])
                DMA("sp", dtb[:], dtb_d[:, :], w=["dtb"])
                ACTV(nexpA[:], nexpA[:], AF.Exp, r=["nexpA"], w=["nexpA"])
                TS("dve", nexpA[:], nexpA[:], -1.0, None, ALU.mult, None, r=["nexpA"], w=["nexpA"])
                MSET("pool", zb[:], 0.0, w=["zb"])
                MSET("pool", zf[:], 0.0, w=["zf"])
                for j in range(4):
                    DMA("sp", KT_d[j, :, 0:KPAD], zb[:, 0:2, :].rearrange("p a b -> p (a b)"), r=["zb"], w=[("KT_d", "padL")])
                    DMA("sp", KT_d[j, :, KPAD + NTOK:KW], zb[:, 0:2, :].rearrange("p a b -> p (a b)"), r=["zb"], w=[("KT_d", "padR")])
                for q4 in range(4):
                    DMA("sp", V_d[256 * q4:256 * (q4 + 1), :].rearrange("(t p) c -> p t c", p=128), zb[:], r=["zb"], w=[("V_d", "padL")])
                    DMA("sp", V_d[KPAD + NTOK + 256 * q4:KPAD + NTOK + 256 * (q4 + 1), :].rearrange("(t p) c -> p t c", p=128), zb[:],
                        r=["zb"], w=[("V_d", "padR")])
                dqv = DQ_d.rearrange("(f p) w -> p f w", p=128)
                DMA("sp", dqv[:, :, 0:DQPAD], zf[:], r=["zf"], w=[("DQ_d", "padL")])
                DMA("sp", dqv[:, :, DQPAD + NTOK:DQPAD + NTOK + DQPAD], zf[:], r=["zf"], w=[("DQ_d", "padR")])
                for kc in range(8):
                    DMA("pool", win[:, kc, :], w_in_d[kc * 128:(kc + 1) * 128, :], w=[("win", kc)])
                    TS("dve", win[:, kc, :], win[:, kc, :], g1[:, kc:kc + 1], None, ALU.mult, None, r=["g1", ("win", kc)], w=[("win", kc)])
                xt = [sb(f"xt{i}", [128, 4, D]) for i in range(2)]
                nTs = [sb(f"nT{i}", [128, 8, 512], BF16) for i in range(2)]
                hn = sb("hn", [128, D], BF16)
                sq = sb("sq", [128, D])
                ms = sb("ms", [128, 1])
                dqs = sb("dqs", [128, 12, 512], BF16)
                vb = sb("vb", [128, 4, 512], BF16)
                gs = sb("gs", [128, 4, 512])
                gbt = sb("gbt", [128, 4, 16])
                zt = sb("zt", [128, 8])
                bt8 = sb("bt8", [128, 8])
                qraw = sb("qraw", [128, 1024])
                qn = sb("qn", [128, 1024])
                t1 = sb("t1", [128, 512]); t2 = sb("t2", [128, 512]); t3 = sb("t3", [128, 512]); t4 = sb("t4", [128, 512])
                qrs = [sb(f"qr{i}", [128, 1024], BF16) for i in range(2)]
                qkTs = [sb(f"qkT{i}", [128, 8, 512], BF16) for i in range(2)]
                ssq = sb("ssq", [128, 16])
                ang16 = sb("ang16", [128, 16, 64]); kfi16 = sb("kfi16", [128, 16, 64], I32); kff16 = sb("kff16", [128, 16, 64])
                cs16 = sb("cs16", [128, 16, 64])
                deferred = []
                pTT = ps("pTT", [128, 8, 128], BF16)
                pFs = [ps(f"pF{i}", [128, 512]) for i in range(2)]
                pQ = ps("pQ", [128, 512]); pK = ps("pK", [128, 512]); pV = ps("pV", [128, 512])
                pG = ps("pG", [128, 512]); pB = ps("pB", [128, 512])
                fcnt = [0]

                def norm_piece(u, t):
                    b = u % 2

                    def dst(t=t, b=b):
                        CP("act", nTs[b][:, :, 128 * t:128 * (t + 1)], pTT[:, :, :], r=[], w=["pTT", ("nT", b)])
                    rmsnorm_T(None, xt[b][:, t, :], 128, ms, hn, sq, pTT, "pTT", dst, [("xt", b)], "p1")
                def load_x(u):
                    DMA("sp", xt[u % 2][:], xs[u * 512:u * 512 + 512, :].rearrange("(t p) d -> p t d", p=128), w=[("xt", u % 2)])
                load_x(0)
                for t in range(4):
                    norm_piece(0, t)
                for u in range(NU):
                    b = u % 2
                    if u + 1 < NU:
                        load_x(u + 1)
                    nT = nTs[b]
                    nTk = ("nT", b)
                    t0 = u * 512
                    for fc in range(12):
                        c0 = 1536 + fc * 128
                        pF = pFs[fcnt[0] % 2]; pFk = ("pF", fcnt[0] % 2); fcnt[0] += 1
                        for kc in range(8):
                            MM(pF[:, :], win[:, kc, c0:c0 + 128], nT[:, kc, :], kc == 0, kc == 7, r=[("win", kc), nTk], w=[pFk])
                        CP("act", dqs[:, fc, :], pF[:, :], r=[], w=[pFk, "dqs"])
                    DMA("sp", dqv[:, :, DQPAD + t0:DQPAD + t0 + 512], dqs[:], r=["dqs"], w=[("DQ_d", u)])
                    for t in range(4):
                        ti = u * 4 + t
                        for (pp, key, c0, n) in ((pQ, "pQ", 0, 512), (pK, "pK", 512, 512), (pV, "pV", 1024, 512),
                                                 (pG, "pG", 3072, 512), (pB, "pB", 3584, 16)):
                            for kc in range(8):
                                MM(pp[:, 0:n], nT[:, kc, 128 * t:128 * (t + 1)], win[:, kc, c0:c0 + n], kc == 0, kc == 7,
                                   r=[("win", kc), nTk], w=[key])
                        while deferred:
                            deferred.pop(0)()
                        if u + 1 < NU:
                            norm_piece(u + 1, t)
                        CP("act", qraw[:, 0:512], pQ[:, :], r=[], w=["pQ", "qraw"])
                        CP("dve", qraw[:, 512:1024], pK[:, :], r=[], w=["pK", "qraw"])
                        ACTV(sq[:], qraw[:], AF.Square, r=["qraw"], w=["sq"])
                        RED("dve", ssq[:], sq[:].rearrange("p (h d) -> p h d", d=64), r=["sq"], w=["ssq"])
                        CP("act", vb[:, t, :], pV[:, :], r=[], w=["pV", "vb"])
                        ACTV(ssq[:], ssq[:], AF.Ln, r=["ssq", "eps"], w=["ssq"], scale=1.0 / 64, bias=epsT[:])
                        ACTV(ssq[:], ssq[:], AF.Exp, r=["ssq"], w=["ssq"], scale=-0.5)
                        TT("dve", qn[:].rearrange("p (h d) -> p h d", d=64), qraw[:].rearrange("p (h d) -> p h d", d=64),
                           ssq[:].unsqueeze(2).to_broadcast([128, 16, 64]), ALU.mult, r=["ssq", "qraw"], w=["qn"])
                        TT("dve", qn[:], qn[:], qkg[:], ALU.mult, r=["qn", "qkg"], w=["qn"])
                        CP("act", gs[:, t, :], pG[:, :], r=[], w=["pG", "gs"])
                        ACTV(bt8[:], pB[:, 0:8], AF.Exp, r=[], w=["pB", "bt8"], scale=-1.0)
                        TT("pool", zt[:], pB[:, 8:16], dtb[:], ALU.add, r=["dtb"], w=["pB", "zt"]) if False else TT("dve", zt[:], pB[:, 8:16], dtb[:], ALU.add, r=["dtb"], w=["pB", "zt"])
                        ACTV(zt[:], zt[:], AF.Exp, r=["zt"], w=["zt"])
                        ACTV(zt[:], zt[:], AF.Ln, r=["zt", "one"], w=["zt"], bias=oneT[:])
                        if ti % 16 == 0:
                            for tt in range(16):
                                STT("dve", ang16[:, tt, :], invf[:], post[:, ti + tt:ti + tt + 1], phase[:], ALU.mult, ALU.add,
                                    r=["invf", "post", "phase"], w=["ang16"])
                            TS("dve", kfi16[:], ang16[:], 1.0 / TWO_PI, None, ALU.mult, None, r=["ang16"], w=["kfi16"])
                            CP("dve", kff16[:], kfi16[:], r=["kfi16"], w=["kff16"])
                            STT("dve", ang16[:], kff16[:], -TWO_PI, ang16[:], ALU.mult, ALU.add, r=["kff16", "ang16"], w=["ang16"])
                            TS("dve", ang16[:], ang16[:], -float(np.pi), float(np.pi), ALU.max, ALU.min, r=["ang16"], w=["ang16"])
                            ACTV(cs16[:], ang16[:], AF.Sin, r=["ang16"], w=["cs16"])
                        qr = qrs[ti % 2]
                        qrk = ("qr", ti % 2)
                        cs = cs16[:, ti % 16, :]
                        qv = qn[:].rearrange("p (h d) -> p h d", d=64)
                        qrv = qr[:].rearrange("p (h d) -> p h d", d=64)
                        sinb = cs[:, 0:32].unsqueeze(1).to_broadcast([128, 16, 32])
                        cosb = cs[:, 32:64].unsqueeze(1).to_broadcast([128, 16, 32])
                        v3 = lambda tt: tt[:].rearrange("p (h d) -> p h d", d=32)
                        TT("dve", v3(t1), qv[:, :, 0:32], cosb, ALU.mult, r=["qn", "cs16"], w=["t1"])
                        TT("dve", v3(t2), qv[:, :, 32:64], sinb, ALU.mult, r=["qn", "cs16"], w=["t2"])
                        TT("dve", qrv[:, :, 0:32], v3(t1), v3(t2), ALU.subtract, r=["t1", "t2"], w=[qrk])
                        TS("dve", bt8[:], bt8[:], 1.0, None, ALU.add, None, r=["bt8"], w=["bt8"])
                        RECIP(gbt[:, t, 8:16], bt8[:], r=["bt8"], w=["gbt"])
                        TT("dve", gbt[:, t, 0:8], zt[:], nexpA[:], ALU.mult, r=["zt", "nexpA"], w=["gbt"])
                        TT("pool", v3(t3), qv[:, :, 32:64], cosb, ALU.mult, r=["qn", "cs16"], w=["t3"])
                        TT("pool", v3(t4), qv[:, :, 0:32], sinb, ALU.mult, r=["qn", "cs16"], w=["t4"])
                        TT("pool", qrv[:, :, 32:64], v3(t3), v3(t4), ALU.add, r=["t3", "t4"], w=[qrk])

                        def tr_stage(qr=qr, qrk=qrk, t=t, qkT=qkTs[u % 2], qkk=("qkT", u % 2)):
                            for j in range(8):
                                TR(pTT[:, j, :], qr[:, j * 128:(j + 1) * 128], ident[:, :], r=[qrk, "ident"], w=["pTT"])
                            CP("act", qkT[:, :, 128 * t:128 * (t + 1)], pTT[:, :, :], r=[], w=["pTT", qkk])
                        deferred.append(tr_stage)
                    while deferred:
                        deferred.pop(0)()
                    qkT = qkTs[u % 2]
                    DMA("sp", V_d[KPAD + t0:KPAD + t0 + 512, :].rearrange("(t p) c -> p t c", p=128), vb[:], r=["vb"], w=[("V_d", u)])
                    DMA("sp", G_d[t0:t0 + 512, :].rearrange("(t p) c -> p t c", p=128), gs[:], r=["gs"], w=[("G_d", u)])
                    DMA("sp", GB_d[t0:t0 + 512, :].rearrange("(t p) c -> p t c", p=128), gbt[:], r=["gbt"], w=[("GB_d", u)])
                    DMA("sp", QT_d[:, :, KPAD + t0:KPAD + t0 + 512].rearrange("j p w -> p j w"), qkT[:, 0:4, :], r=[("qkT", u % 2)], w=[("QT_d", u)])
                    DMA("sp", KT_d[:, :, KPAD + t0:KPAD + t0 + 512].rearrange("j p w -> p j w"), qkT[:, 4:8, :], r=[("qkT", u % 2)], w=[("KT_d", u)])
            P.barrier()

        def pass2():
            with contextlib.ExitStack() as es:
                def sb(name, shape, dt=F32):
                    return es.enter_context(nc.sbuf_tensor("p2_" + name, list(shape), dt))

                def ps(name, shape, dt=F32):
                    return es.enter_context(nc.psum_tensor("p2_" + name, list(shape), dt))
                lv = sb("lv", [128, 2 * NSEG])
                am_f = sb("am_f", [128, 256])
                am = sb("am", [128, 256], BF16)
                amL = sb("amL", [128, 256], BF16); amR = sb("amR", [128, 256], BF16); amLR = sb("amLR", [128, 256], BF16)
                ones_b = sb("ones_b", [128, 128])
                DMA("sp", lv[:], lv_d[:, :], w=["lv"])
                DMA("sp", am_f[:], amask_d[:, :], w=["am_f"])
                CP("dve", am[:], am_f[:], r=["am_f"], w=["am"])
                MSET("dve", ones_b[:], 1.0, w=["ones_b"])
                QTs = [sb(f"QTs{i}", [128, 4, SEG], BF16) for i in range(1)]
                KTw = [sb(f"KTw{i}", [128, 4, 2 * SEG], BF16) for i in range(1)]
                acc = [sb(f"acc{e}", [128, 4, SEG]) for e in range(2)]
                attT = sb("attT", [128, 4, SEG], BF16)
                NVB = 8
                vraw = [sb(f"vraw{i}", [128, 512], BF16) for i in range(NVB)]
                vt2 = [sb(f"vt2_{i}", [128, 4, 2, 128], BF16) for i in range(NVB)]
                NPT = 6
                pt = [sb(f"pt{i}", [128, 256], BF16) for i in range(NPT)]
                rden = sb("rden", [128, SEG])
                pS = [ps(f"pS{i}", [128, 512]) for i in range(4)]
                pA = [ps(f"pA{i}", [128, 512]) for i in range(2)]
                pBc = [ps(f"pBc{i}", [128, 512]) for i in range(2)]
                for i in range(NVB):
                    MSET("pool", vt2[i][:], 0.0, w=[("vt2", i)])
                    MSET("pool", vt2[i][:, :, 0, 64:65], 1.0, w=[("vt2", i)])
                    MSET("pool", vt2[i][:, :, 1, 32:33], 1.0, w=[("vt2", i)])
                cnt = {"v": 0, "s": 0, "a": 0, "pt": 0, "m": 0, "bc": 0}
                import collections
                pending = collections.deque()
                SKEW = 3
                for s in range(NSEG):
                    b = 0
                    base = KPAD + s * SEG
                    DMA("sp", QTs[b][:], QT_d[:, :, base:base + SEG].rearrange("j p w -> p j w"),
                        r=[("QT_d", u) for u in range(4 * s, 4 * s + 4)], w=[("QTs", b)])
                    ulo = max(0, 4 * s - 2); uhi = min(NU, 4 * s + 6)
                    DMA("sp", KTw[b][:], KT_d[:, :, base - 1024:base + SEG + 1024].rearrange("j p w -> p j w"),
                        r=[("KT_d", u) for u in range(ulo, uhi)] + [("KT_d", "padL"), ("KT_d", "padR")], w=[("KTw", b)])
                    TS("pool", amL[:, 0:128], am[:, 0:128], lv[:, 2 * s:2 * s + 1], None, ALU.mult, None, r=["am", "lv"], w=["amL"])
                    CP("pool", amL[:, 128:256], am[:, 128:256], r=["am"], w=["amL"])
                    CP("pool", amR[:, 0:128], am[:, 0:128], r=["am"], w=["amR"])
                    TS("pool", amR[:, 128:256], am[:, 128:256], lv[:, 2 * s + 1:2 * s + 2], None, ALU.mult, None, r=["am", "lv"], w=["amR"])
                    CP("pool", amLR[:, 0:128], amL[:, 0:128], r=["amL"], w=["amLR"])
                    CP("pool", amLR[:, 128:256], amR[:, 128:256], r=["amR"], w=["amLR"])
                    vkeys = [("V_d", u) for u in range(ulo, uhi)] + [("V_d", "padL"), ("V_d", "padR")]
                    groups = []
                    for (d, first) in ((1, True), (4, False), (16, False)):
                        nqb = SEG // d // 128
                        for c in range(d):
                            for qb in range(nqb):
                                groups.append((d, first, c, qb, nqb))
                    needed = []
                    seen = set()
                    for (d, first, c, qb, nqb) in groups:
                        for k in (qb, qb + 1):
                            if (d, c, k) not in seen:
                                seen.add((d, c, k))
                                needed.append((d, c, k))
                    vslot = {}
                    nl = [0]

                    def load_v(d, c, k):
                        i = cnt["v"] % NVB
                        cnt["v"] += 1
                        row0 = base + c + d * (128 * k - 64)
                        DMA("sp", vraw[i][:], V_d[_sl(row0, 128, d), :], r=vkeys, w=[("vraw", i)])
                        vr = vraw[i][:].rearrange("p (j e x) -> p j e x", e=2, x=64)
                        CP("pool", vt2[i][:, :, 0, 0:64], vr[:, :, 0, :], r=[("vraw", i)], w=[("vt2", i)])
                        CP("pool", vt2[i][:, :, 1, 64:128], vr[:, :, 1, :], r=[("vraw", i)], w=[("vt2", i)])
                        vslot[(d, c, k)] = i

                    def ensure_loaded(gi):
                        if gi >= len(groups):
                            return
                        d, first, c, qb, nqb = groups[gi]
                        while not ((d, c, qb) in vslot and (d, c, qb + 1) in vslot):
                            load_v(*needed[nl[0]])
                            nl[0] += 1
                    PF = 2
                    for gi, (d, first, c, qb, nqb) in enumerate(groups):
                        for g2 in range(gi, gi + PF + 1):
                            ensure_loaded(g2)
                        if nqb == 1:
                            msk, mkey = amLR, "amLR"
                        elif qb == 0:
                            msk, mkey = amL, "amL"
                        elif qb == nqb - 1:
                            msk, mkey = amR, "amR"
                        else:
                            msk, mkey = am, "am"
                        qcol0 = c + d * 128 * qb
                        qsl = _sl(qcol0, 128, d)
                        for hp in range(4):
                            iSs = []
                            for e in range(2):
                                iSs.append(cnt["s"] % 4); cnt["s"] += 1
                            for half in range(2):
                                k = qb + half
                                kc0 = 1024 + c + d * (128 * k - 64)
                                for e in range(2):
                                    pb = 64 * e
                                    MM(pS[iSs[e]][:, 128 * half:128 * (half + 1)], KTw[b][pb:pb + 64, hp, _sl(kc0, 128, d)],
                                       QTs[b][pb:pb + 64, hp, qsl], True, True, r=[("KTw", b), ("QTs", b)], w=[("pS", iSs[e])])
                            for e in range(2):
                                iS = iSs[e]
                                ip = cnt["pt"] % NPT; cnt["pt"] += 1
                                ACTV(pt[ip][:], pS[iS][:, 0:256], AF.Exp, r=[], w=[("pS", iS), ("pt", ip)], scale=0.125)
                                meng = "pool" if (cnt["m"] % 3 == 0) else "dve"
                                cnt["m"] += 1
                                TT(meng, pt[ip][:], pt[ip][:], msk[:], ALU.mult, r=[("pt", ip), mkey], w=[("pt", ip)])

                                def pv_stage(ip=ip, e=e, hp=hp, qsl=qsl, first=first, v0=vslot[(d, c, qb)], v1=vslot[(d, c, qb + 1)]):
                                    iA = cnt["a"] % 2; cnt["a"] += 1
                                    M = 65 if e == 0 else 128
                                    for half, vi in ((0, v0), (1, v1)):
                                        MM(pA[iA][0:M, 0:128], vt2[vi][:, hp, e, 0:M], pt[ip][:, 128 * half:128 * (half + 1)],
                                           half == 0, half == 1, r=[("vt2", vi), ("pt", ip)], w=[("pA", iA)])
                                    lo, hi = (0, 65) if e == 0 else (0, 128)
                                    dst = acc[e][lo:hi, hp, qsl]
                                    if first:
                                        CP("dve", dst, pA[iA][lo:hi, 0:128], r=[], w=[("pA", iA), ("acc", e, hp)])
                                    else:
                                        TT("dve", dst, dst, pA[iA][lo:hi, 0:128], ALU.add, r=[], w=[("pA", iA), ("acc", e, hp)])
                                pending.append(pv_stage)
                            while len(pending) > SKEW:
                                pending.popleft()()
                    while pending:
                        pending.popleft()()
                    for hp in range(4):
                        for e in range(2):
                            dp = 64 if e == 0 else 32
                            lo, hi = (0, 64) if e == 0 else (64, 128)
                            ACTV(rden[dp:dp + 1, :], acc[e][dp:dp + 1, hp, :], AF.Ln, r=[("acc", e, hp)], w=["rden"])
                            ACTV(rden[dp:dp + 1, :], rden[dp:dp + 1, :], AF.Exp, r=["rden"], w=["rden"], scale=-1.0)
                            for cc in range(SEG // 512):
                                csl = slice(512 * cc, 512 * (cc + 1))
                                ib = cnt["bc"] % 2; cnt["bc"] += 1
                                MM(pBc[ib][0:hi, :], ones_b[dp:dp + 1, 0:hi], rden[dp:dp + 1, csl], True, True,
                                   r=["ones_b", "rden"], w=[("pBc", ib)])
                                TT("dve", attT[lo:hi, hp, csl], acc[e][lo:hi, hp, csl], pBc[ib][lo:hi, :], ALU.mult,
                                   r=[("acc", e, hp)], w=[("pBc", ib), "attT"])
                    DMA("sp", AT_d[:, :, s * SEG:(s + 1) * SEG].rearrange("j p w -> p j w"), attT[:], r=["attT"], w=[("AT_d", s)])
            P.barrier()


        def pass_prep():
            with contextlib.ExitStack() as es:
                def sb(name, shape, dt=F32):
                    return es.enter_context(nc.sbuf_tensor("pp_" + name, list(shape), dt))

                def ps(name, shape, dt=F32):
                    return es.enter_context(nc.psum_tensor("pp_" + name, list(shape), dt))
                dncw = sb("dncw", [128, 5, 12])
                dnsc = sb("dnsc", [128, 4 * NT])
                lnq = sb("lnq", [128, 1])
                dg = sb("dg", [128, 60, 128], BF16)
                ones_b = sb("ones_b", [128, 128], BF16)
                DMA("sp", dncw[:], dncw_d[:, :, :], w=["dncw"])
                DMA("sp", dnsc[:], dnsc_d[:, :], w=["dnsc"])
                MSET("dve", lnq[:], float(-0.5 * np.log(128.0)), w=["lnq"])
                MSET("dve", ones_b[:], 1.0, w=["ones_b"])
                for j in range(5):
                    for fc in range(12):
                        TS("dve", dg[:, j * 12 + fc, :], ident[:, :], dncw[:, j, fc:fc + 1], None, ALU.mult, None,
                           r=["ident", "dncw"], w=["dg"])
                dqb = [sb(f"dqb{i}", [128, 12, 516], BF16) for i in range(2)]
                qk32 = sb("qk32", [128, 8, 512])
                sqb = sb("sqb", [128, 8, 512], BF16)
                rn = sb("rn", [128, 8, 512])
                oqkv = [sb(f"oqkv{i}", [128, 12, 512], BF16) for i in range(2)]
                pC = [ps(f"pC{i}", [128, 512]) for i in range(4)]
                pNn = [ps(f"pN{i}", [128, 512]) for i in range(4)]
                dqv = DQ_d.rearrange("(f p) w -> p f w", p=128)
                dnv = DN_d.rearrange("(f p) w -> p f w", p=128)

                def load(u):
                    t0 = u * 512
                    deps = [("DQ_d", uu) for uu in range(max(0, u - 1), min(NU, u + 2))] + [("DQ_d", "padL"), ("DQ_d", "padR")]
                    DMA("sp", dqb[u % 2][:], dqv[:, :, DQPAD + t0 - 2:DQPAD + t0 + 514], r=deps, w=[("dqb", u % 2)])
                load(0)
                for u in range(NU):
                    b = u % 2
                    t0 = u * 512
                    if u + 1 < NU:
                        load(u + 1)
                    ti = u * 4
                    if t0 % SEG == 0:
                        TS("dve", dqb[b][:, :, 0:2], dqb[b][:, :, 0:2], dnsc[:, 4 * ti:4 * ti + 1], None, ALU.mult, None,
                           r=[("dqb", b), "dnsc"], w=[("dqb", b)])
                    if (t0 + 512) % SEG == 0:
                        TS("dve", dqb[b][:, :, 514:516], dqb[b][:, :, 514:516], dnsc[:, 4 * (ti + 3) + 1:4 * (ti + 3) + 2], None,
                           ALU.mult, None, r=[("dqb", b), "dnsc"], w=[("dqb", b)])
                    for fc in range(12):
                        i = fc % 4
                        for j in range(5):
                            MM(pC[i][:, :], dg[:, j * 12 + fc, :], dqb[b][:, fc, j:j + 512], j == 0, j == 4, r=["dg", ("dqb", b)], w=[("ppC", i)])
                        if fc < 8:
                            ACTV(qk32[:, fc, :], pC[i][:, :], AF.Silu, r=[], w=[("ppC", i), ("qk32", fc)])
                        else:
                            ACTV(oqkv[b][:, fc, :], pC[i][:, :], AF.Silu, r=[], w=[("ppC", i), ("oqkv", b)])
                    for fc in range(8):
                        ACTV(sqb[:, fc, :], qk32[:, fc, :], AF.Square, r=[("qk32", fc)], w=[("sqb", fc)])
                    for fc in range(8):
                        i = fc % 4
                        MM(pNn[i][:, :], ones_b[:, :], sqb[:, fc, :], True, True, r=["ones_b", ("sqb", fc)], w=[("ppN", i)])
                        ACTV(rn[:, fc, :], pNn[i][:, :], AF.Ln, r=["eps"], w=[("ppN", i), ("rn", fc)], bias=epsT[:])
                    for fc in range(8):
                        if fc < 4:
                            ACTV(rn[:, fc, :], rn[:, fc, :], AF.Exp, r=[("rn", fc), "lnq"], w=[("rn", fc)], scale=-0.5, bias=lnq[:])
                        else:
                            ACTV(rn[:, fc, :], rn[:, fc, :], AF.Exp, r=[("rn", fc)], w=[("rn", fc)], scale=-0.5)
                        TT("dve", oqkv[b][:, fc, :], qk32[:, fc, :], rn[:, fc, :], ALU.mult, r=[("qk32", fc), ("rn", fc)], w=[("oqkv", b)])
                    DMA("sp", dnv[:, :, t0:t0 + 512], oqkv[b][:], r=[("oqkv", b)], w=[("DN_d", u)])
            P.barrier()

        def pass_dn_gen(dr, es):
            NEGC = 11
            if True:
                def sb(name, shape, dt=F32):
                    return es.enter_context(nc.sbuf_tensor(f"p3{dr}_" + name, list(shape), dt))

                def ps(name, shape, dt=F32):
                    return es.enter_context(nc.psum_tensor(f"p3{dr}_" + name, list(shape), dt))
                dnc = sb("dnc", [128, NEGC, 128])
                dnsc = sb("dnsc", [128, 4 * NT])
                DMA("sp", dnc[:], dnc_d[:, :, :], w=["dnc"])
                DMA("sp", dnsc[:], dnsc_d[:, :], w=["dnsc"])
                U = dnc[:, 0 + dr, :]
                ONES = dnc[:, 2, :]
                MLOW = dnc[:, 3 + dr, :]
                MQ = dnc[:, 5 + dr, :]
                IDF = dnc[:, 7, :]
                BD = dnc[:, 10, :]
                qkvb = [sb(f"qkvb{i}", [128, 12, 128], BF16) for i in range(2)]
                gb = [sb(f"gb{i}", [128, 16]) for i in range(2)]
                gst = sb("gst", [128, 16])
                egc = sb("egc", [128, 4]); be = sb("be", [128, 4]); edk = sb("edk", [128, 4]); egl = sb("egl", [128, 8])
                Ug = sb("Ug", [128, 4, 128])
                Nm = sb("Nm", [128, 4, 128])
                DL = sb("DL", [128, 4, 128]); DQm = sb("DQm", [128, 4, 128])
                egcr = sb("egcr", [128, 4, 128])
                Lb = [sb(f"Lb{i}", [128, 4, 128], BF16) for i in range(2)]
                Mb = [sb(f"Mb{i}", [128, 4, 128], BF16) for i in range(2)]
                Z = sb("Z", [128, 4, 128], BF16)
                kkb = sb("kkb", [128, 4, 128])
                qkTm = sb("qkTm", [128, 4, 128], BF16)
                kbe = sb("kbe", [128, 4, 128], BF16); kdec = sb("kdec", [128, 4, 128], BF16); vbt = sb("vbt", [128, 4, 128], BF16)
                um = sb("um", [128, 4, 128]); wTm = sb("wTm", [128, 4, 128], BF16); qdTm = sb("qdTm", [128, 4, 128], BF16)
                vn = sb("vn", [128, 4, 128], BF16)
                ot = [sb(f"ot{i}", [128, 4, 128]) for i in range(2)]
                S = sb("S", [128, 4, 128])
                Sb = sb("Sb", [128, 4, 128], BF16)
                g = [ps(f"g{i}", [128, 4, 128]) for i in range(3)]
                tbb = ps("tbb", [128, 8, 128], BF16)
                tb = [tbb[:, 0:4, :], tbb[:, 4:8, :]]
                gi = [0]
                tbi = [0]

                def nextg():
                    i = gi[0] % 3
                    gi[0] += 1
                    return g[i], ("p3g", i)

                def nexttb():
                    i = tbi[0] % 2
                    tbi[0] += 1
                    return tb[i], "p3tb"
                bc = lambda ap4: ap4.unsqueeze(2).to_broadcast([128, 4, 128])
                identb4 = ident[:, :].unsqueeze(1).to_broadcast([128, 4, 128])
                MSET("dve", S[:], 0.0, w=["S"])
                MSET("dve", Sb[:], 0.0, w=["Sb"])
                order = list(range(NT)) if dr == 0 else list(range(NT - 1, -1, -1))
                dnv = DN_d.rearrange("(f p) w -> p f w", p=128)

                def load(ti, b):
                    t0 = ti * 128
                    u = ti // 4
                    DMA("sp", qkvb[b][:], dnv[:, :, t0:t0 + 128], r=[("DN_d", u)], w=[("qkvb", b)])
                    DMA("sp", gb[b][:], GB_d[t0:t0 + 128, :], r=[("GB_d", u)], w=[("gb", b)])
                load(order[0], 0)
                for n_, ti in enumerate(order):
                    b = n_ % 2
                    t0 = ti * 128
                    if n_ + 1 < NT:
                        load(order[n_ + 1], 1 - b)
                    g4 = gb[b][:, 4 * dr:4 * dr + 4]
                    b4 = gb[b][:, 8 + 4 * dr:12 + 4 * dr]
                    qT = lambda h: qkvb[b][:, h, :]
                    kT = lambda h: qkvb[b][:, 4 + h, :]
                    vT = lambda h: qkvb[b][:, 8 + h, :]
                    pG3, kG = nextg()
                    pG = pG3[:, :, :].rearrange("p a b -> p (a b)")
                    MM(pG[:, 0:4], U, g4, True, True, r=["dnc", ("gb", b)], w=[kG])
                    MM(pG[:, 4:8], BD, g4, True, True, r=["dnc", ("gb", b)], w=[kG])
                    MM(pG[:, 8:12], dnc[:, 8, :], g4, True, True, r=["dnc", ("gb", b)], w=[kG])
                    MM(pG[:, 12:16], dnc[:, 9, :], g4, True, True, r=["dnc", ("gb", b)], w=[kG])
                    CP("dve", gst[:], pG[:, 0:16], r=[], w=[kG, "gst"])
                    yield
                    ACTV(egc[:], gst[:, 0:4], AF.Exp, r=["gst"], w=["egc"])
                    TT("dve", be[:], egc[:], b4, ALU.mult, r=["egc", ("gb", b)], w=["be"])
                    TT("dve", edk[:], gst[:, 4:8], gst[:, 0:4], ALU.subtract, r=["gst"], w=["edk"])
                    ACTV(edk[:], edk[:], AF.Exp, r=["edk"], w=["edk"])
                    ACTV(egl[:], gst[:, 8:16], AF.Exp, r=["gst"], w=["egl"])
                    TT("dve", Ug[:], U.unsqueeze(1).to_broadcast([128, 4, 128]), bc(g4), ALU.mult, r=["dnc", ("gb", b)], w=["Ug"])
                    yield
                    pR, kR = nextg()
                    MM(pR[:, :, :], ONES, Ug[:, :, :], True, True, r=["dnc", "Ug"], w=[kR])
                    TT("dve", Nm[:], pR[:, :, :], bc(gst[:, 0:4]), ALU.subtract, r=["gst"], w=[kR, "Nm"])
                    ACTV(egcr[:], pR[:, :, :], AF.Exp, r=[], w=[kR, "egcr"])
                    yield
                    mlb = MLOW.unsqueeze(1).to_broadcast([128, 4, 128])
                    mqb = MQ.unsqueeze(1).to_broadcast([128, 4, 128])
                    STT("dve", DL[:], Nm[:], -1.0, mlb, ALU.mult, ALU.add, r=["Nm", "dnc"], w=["DL"])
                    TT("pool", DQm[:], Nm[:], mqb, ALU.add, r=["Nm", "dnc"], w=["DQm"])
                    ACTV(DL[:], DL[:], AF.Exp, r=["DL"], w=["DL"])
                    ACTV(DQm[:], DQm[:], AF.Exp, r=["DQm"], w=["DQm"])
                    yield
                    pK_, kK = nextg()
                    for h in range(4):
                        MM(pK_[:, h, :], kT(h), kT(h), True, True, r=[("qkvb", b)], w=[kK])
                    TT("dve", kkb[:], pK_[:, :, :], bc(b4), ALU.mult, r=[("gb", b)], w=[kK, "kkb"])
                    TT("dve", Lb[0][:], kkb[:], DL[:], ALU.mult, r=["kkb", "DL"], w=[("Lb", 0)])
                    yield
                    pQ_, kQ = nextg()
                    for h in range(4):
                        MM(pQ_[:, h, :], kT(h), qT(h), True, True, r=[("qkvb", b)], w=[kQ])
                    TT("dve", qkTm[:], pQ_[:, :, :], DQm[:], ALU.mult, r=["DQm"], w=[kQ, "qkTm"])
                    yield
                    pM_, kM = nexttb()
                    for h in range(4):
                        TR(pM_[:, h, :], Lb[0][:, h, :], ident[:, :], r=[("Lb", 0), "ident"], w=[kM])
                    CP("act", Mb[0][:], pM_[:, :, :], r=[], w=[kM, ("Mb", 0)])
                    yield
                    STT("dve", Z[:], Mb[0][:], -1.0, identb4, ALU.mult, ALU.add, r=[("Mb", 0), "ident"], w=["Z"])
                    cur = 0
                    for lvl in range(5):
                        nxt = 1 - cur
                        pP_, kP = nextg()
                        for h in range(4):
                            MM(pP_[:, h, :], Mb[cur][:, h, :], Lb[cur][:, h, :], True, True, r=[("Mb", cur), ("Lb", cur)], w=[kP])
                        CP("act", Lb[nxt][:], pP_[:, :, :], r=[], w=[kP, ("Lb", nxt)])
                        yield
                        if lvl < 4:
                            pM2, kM2 = nextg()
                            for h in range(4):
                                MM(pM2[:, h, :], Lb[cur][:, h, :], Mb[cur][:, h, :], True, True, r=[("Mb", cur), ("Lb", cur)], w=[kM2])
                            CP("dve", Mb[nxt][:], pM2[:, :, :], r=[], w=[kM2, ("Mb", nxt)])
                            yield
                        pZ_, kZ = nextg()
                        for h in range(4):
                            MM(pZ_[:, h, :], Lb[nxt][:, h, :], Z[:, h, :], True, True, r=[("Lb", nxt), "Z"], w=[kZ])
                        TT("dve", Z[:], Z[:], pZ_[:, :, :], ALU.add, r=["Z"], w=[kZ, "Z"])
                        yield
                        cur = nxt
                    pT1, kT1 = nexttb()
                    for h in range(4):
                        TR(pT1[:, h, :], kT(h), ident[:, :], r=[("qkvb", b), "ident"], w=[kT1])
                    TT("dve", kbe[:], pT1[:, :, :], bc(be[:]), ALU.mult, r=["be"], w=[kT1, "kbe"])
                    TT("dve", kdec[:], pT1[:, :, :], bc(edk[:]), ALU.mult, r=["edk"], w=[kT1, "kdec"])
                    yield
                    pT2, kT2 = nexttb()
                    for h in range(4):
                        TR(pT2[:, h, :], vT(h), ident[:, :], r=[("qkvb", b), "ident"], w=[kT2])
                    TT("dve", vbt[:], pT2[:, :, :], bc(b4), ALU.mult, r=[("gb", b)], w=[kT2, "vbt"])
                    yield
                    pU_, kU = nextg()
                    for h in range(4):
                        MM(pU_[:, h, :], Z[:, h, :], vbt[:, h, :], True, True, r=["Z", "vbt"], w=[kU])
                    CP("act", um[:], pU_[:, :, :], r=[], w=[kU, "um"])
                    yield
                    pW_, kW = nextg()
                    for h in range(4):
                        MM(pW_[:, h, :], kbe[:, h, :], Z[:, h, :], True, True, r=["Z", "kbe"], w=[kW])
                    CP("act", wTm[:], pW_[:, :, :], r=[], w=[kW, "wTm"])
                    TT("pool", qdTm[:], qkvb[b][:, 0:4, :], egcr[:], ALU.mult, r=[("qkvb", b), "egcr"], w=["qdTm"])
                    yield
                    carry = None
                    if dr == 0 and ti % 16 == 0 and ti > 0:
                        carry = dnsc[:, 4 * ti + 2:4 * ti + 3]
                    if dr == 1 and ti % 16 == 15 and ti < NT - 1:
                        carry = dnsc[:, 4 * ti + 3:4 * ti + 4]
                    if carry is not None:
                        TS("dve", S[:], S[:], carry, None, ALU.mult, None, r=["S", "dnsc"], w=["S"])
                        CP("act", Sb[:], S[:], r=["S"], w=["Sb"])
                    io = n_ % 2
                    for c in ((0, 1) if dr == 0 else (1, 0)):
                        cs = slice(64 * c, 64 * c + 64)
                        pVN, kVN = nextg()
                        for h in range(4):
                            MM(pVN[cs, h, :], wTm[:, h, cs], Sb[:, h, :], True, True, r=["wTm", "Sb"], w=[kVN])
                        TT("dve", vn[cs, :, :], um[cs, :, :], pVN[cs, :, :], ALU.subtract, r=["um"], w=[kVN, "vn"])
                        yield
                        pO, kO = nextg()
                        for h in range(4):
                            MM(pO[cs, h, :], qdTm[:, h, cs], Sb[:, h, :], True, False, r=["qdTm", "Sb"], w=[kO])
                            MM(pO[cs, h, :], qkTm[cs, h, cs], vn[cs, h, :], False, True, r=["qkTm", "vn"], w=[kO])
                        pSn, kSn = nextg()
                        for h in range(4):
                            MM(pSn[:, h, :], kdec[cs, h, :], vn[cs, h, :], True, True, r=["kdec", "vn"], w=[kSn])
                        for h in range(4):
                            STT("dve", S[:, h, :], S[:, h, :], egl[:, 4 * c + h:4 * c + h + 1], pSn[:, h, :], ALU.mult, ALU.add,
                                r=["S", "egl"], w=[kSn, "S"])
                        CP("act", Sb[:], S[:], r=["S"], w=["Sb"])
                        CP("act", ot[io][cs, :, :], pO[cs, :, :], r=[], w=[kO, ("ot", io)])
                        yield
                    DMA("sp", OD_d[dr][t0:t0 + 128, :], ot[io][:].rearrange("p h d -> p (h d)"), r=[("ot", io)], w=[("OD_d", dr, ti)])
                    yield

        def pass_dn_both(dirs):
            P.shared = {"eps", "one", "ident", "GB_d", "DQ_d", "OD_d", "DN_d"}
            with contextlib.ExitStack() as es:
                gens = [(dr, pass_dn_gen(dr, es)) for dr in dirs]
                while gens:
                    for item in list(gens):
                        P.ns = ("dn", item[0])
                        try:
                            next(item[1])
                        except StopIteration:
                            gens.remove(item)
                P.ns = None
            P.barrier()

        def pass5():
            with contextlib.ExitStack() as es:
                def sb(name, shape, dt=F32):
                    return es.enter_context(nc.sbuf_tensor("p5_" + name, list(shape), dt))

                def ps(name, shape, dt=F32):
                    return es.enter_context(nc.psum_tensor("p5_" + name, list(shape), dt))
                wout = sb("wout", [128, 8, D], BF16)
                dnog = sb("dnog", [128, 512])
                DMA("sp", dnog[:], dnog_d[:, :], w=["dnog"])
                for kc in range(8):
                    DMA("pool", wout[:, kc, :], w_out_d[kc * 128:(kc + 1) * 128, :], w=[("wout", kc)])
                NB = 3
                NX = 4
                of_ = [sb(f"of{i}", [128, 512]) for i in range(NB)]
                ob_ = [sb(f"ob{i}", [128, 512]) for i in range(NB)]
                gt = [sb(f"gt{i}", [128, 512]) for i in range(NB)]
                xt = [sb(f"xt{i}", [128, D]) for i in range(NX)]
                at = [sb(f"at{i}", [128, 4, 128], BF16) for i in range(NX)]
                osum = sb("osum", [128, 512]); sq = sb("sq", [128, 512]); ss = sb("ss", [128, 4])
                gg = sb("gg", [128, 512]); on = sb("on", [128, 512])
                odn = [sb(f"odn{i}", [128, 512], BF16) for i in range(2)]
                eg = sb("eg", [128, 512])
                dnT = sb("dnT", [128, 4, 128], BF16)
                ht = [sb(f"ht{i}", [128, D]) for i in range(2)]
                pT = ps("pT", [128, 4, 128], BF16)
                pH = [ps(f"pH{i}", [128, 512]) for i in range(4)]

                def loads(ti):
                    b = ti % NB
                    bx = ti % NX
                    t0 = ti * 128
                    u = ti // 4
                    DMA("sp", of_[b][:], OD_d[0][t0:t0 + 128, :], r=[("OD_d", 0, ti)], w=[("of", b)])
                    DMA("sp", ob_[b][:], OD_d[1][t0:t0 + 128, :], r=[("OD_d", 1, ti)], w=[("ob", b)])
                    DMA("sp", gt[b][:], G_d[t0:t0 + 128, :], r=[("G_d", u)], w=[("gt", b)])
                    DMA("sp", xt[bx][:], xs[t0:t0 + 128, :], w=[("xt5", bx)])
                    DMA("sp", at[bx][:], AT_d[:, :, t0:t0 + 128].rearrange("j p w -> p j w"), r=[("AT_d", ti // 16)], w=[("at", bx)])

                def stage_a(ti):
                    b = ti % NB
                    TT("dve", osum[:], of_[b][:], ob_[b][:], ALU.add, r=[("of", b), ("ob", b)], w=["osum"])
                    ACTV(sq[:], osum[:], AF.Square, r=["osum"], w=["sq5"])
                    RED("dve", ss[:], sq[:].rearrange("p (h d) -> p h d", d=128), r=["sq5"], w=["ss5"])
                    ACTV(ss[:], ss[:], AF.Ln, r=["ss5", "eps"], w=["ss5"], scale=1.0 / 128, bias=epsT[:])
                    ACTV(ss[:], ss[:], AF.Exp, r=["ss5"], w=["ss5"], scale=-0.5)
                    ACTV(eg[:], gt[b][:], AF.Exp, r=[("gt", b)], w=["eg"], scale=-1.0)
                    ACTV(eg[:], eg[:], AF.Ln, r=["eg", "one"], w=["eg"], bias=oneT[:])
                    ACTV(eg[:], eg[:], AF.Exp, r=["eg"], w=["eg"], scale=-1.0)
                    TT("pool", gg[:], gt[b][:], dnog[:], ALU.mult, r=[("gt", b), "dnog"], w=["gg"])
                    TT("dve", gg[:], gg[:], eg[:], ALU.mult, r=["gg", "eg"], w=["gg"])
                    TT("dve", on[:].rearrange("p (h d) -> p h d", d=128), osum[:].rearrange("p (h d) -> p h d", d=128),
                       ss[:].unsqueeze(2).to_broadcast([128, 4, 128]), ALU.mult, r=["osum", "ss5"], w=["on"])
                    TT("dve", odn[ti % 2][:], on[:], gg[:], ALU.mult, r=["on", "gg"], w=[("odn", ti % 2)])

                def stage_b(ti):
                    bx = ti % NX
                    t0 = ti * 128
                    u = ti // 4
                    for j in range(4):
                        TR(pT[:, j, :], odn[ti % 2][:, j * 128:(j + 1) * 128], ident[:, :], r=[("odn", ti % 2), "ident"], w=["p5T"])
                    CP("act", dnT[:], pT[:, :, :], r=[], w=["p5T", "dnT"])
                    ih = ti % 2
                    for nh in range(2):
                        ip = (2 * ti + nh) % 4
                        csl = slice(512 * nh, 512 * (nh + 1))
                        for j in range(4):
                            MM(pH[ip][:, :], at[bx][:, j, :], wout[:, j, csl], j == 0, False, r=[("at", bx), ("wout", j)], w=[("p5H", ip)])
                        for j in range(4):
                            MM(pH[ip][:, :], dnT[:, j, :], wout[:, 4 + j, csl], False, j == 3, r=["dnT", ("wout", 4 + j)], w=[("p5H", ip)])
                        TT("dve", ht[ih][:, csl], pH[ip][:, :], xt[bx][:, csl], ALU.add, r=[("xt5", bx)], w=[("p5H", ip), ("ht5", ih)])
                    DMA("sp", H_d[t0:t0 + 128, :], ht[ih][:], r=[("ht5", ih)], w=[("H_d", u)])
                loads(0)
                if NT > 1:
                    loads(1)
                stage_a(0)
                for ti in range(NT):
                    if ti + 2 < NT:
                        loads(ti + 2)
                    if ti + 1 < NT:
                        stage_a(ti + 1)
                    stage_b(ti)
            P.barrier()

        def pass6(hbuf, hkeys_fn):
            with contextlib.ExitStack() as es:
                def sb(name, shape, dt=F32):
                    return es.enter_context(nc.sbuf_tensor("p6_" + name, list(shape), dt))

                def ps(name, shape, dt=F32):
                    return es.enter_context(nc.psum_tensor("p6_" + name, list(shape), dt))
                wup = sb("wup", [128, 8, 2 * FFN], BF16)
                wdn = sb("wdn", [128, NFC, D], BF16)
                g2 = sb("g2", [128, 8])
                cw = sb("cw", [128, 3, 44])
                cb = sb("cb", [128, 44])
                hsc = sb("hsc", [2, NU])
                DMA("sp", g2[:], norm2_d[:, :], w=["g2"])
                DMA("sp", cw[:], cw_d[:, :, :], w=["cw"])
                DMA("sp", cb[:], cb_d[:, :], w=["cb"])
                DMA("sp", hsc[:], halo_sc_d[:, :], w=["hsc"])
                for kc in range(8):
                    DMA("pool", wup[:, kc, :], w_up_d[kc * 128:(kc + 1) * 128, :], w=[("wup", kc)])
                    TS("dve", wup[:, kc, :], wup[:, kc, :], g2[:, kc:kc + 1], None, ALU.mult, None, r=["g2", ("wup", kc)], w=[("wup", kc)])
                for j in range(NFC):
                    DMA("pool", wdn[:, j, :], w_down_d[j * 128:(j + 1) * 128, :], w=[("wdn", j)])
                ht = sb("ht", [128, 4, D])
                hh = sb("hh", [2, D], BF16)
                hn = sb("hn", [128, D], BF16)
                hhn = sb("hhn", [2, D], BF16)
                ms = [sb(f"ms{i}", [128, 1]) for i in range(2)]
                hnTs = [sb(f"hnT{i}", [128, 8, 514], BF16) for i in range(2)]
                actTh = [sb(f"actT{i}", [128, NFC, 256], BF16) for i in range(2)]
                downq = []
                accg = [sb(f"accg{i}", [128, 256]) for i in range(2)]
                accu = [sb(f"accu{i}", [128, 256]) for i in range(2)]
                sg = [sb(f"sg{i}", [128, 256], BF16) for i in range(2)]
                yt = [sb(f"yt{i}", [128, D]) for i in range(2)]
                pT = [ps(f"pT{i}", [128, 8, 128], BF16) for i in range(2)]
                pU = [ps(f"pU{i}", [128, 512]) for i in range(4)]
                pD = [ps(f"pD{i}", [128, 512]) for i in range(2)]
                ucount = [0]
                dcount = [0]

                def load_h(u):
                    t0 = u * 512
                    DMA("sp", ht[:], hbuf[t0:t0 + 512, :].rearrange("(t p) d -> p t d", p=128), r=hkeys_fn(u), w=["ht"])
                    lo = max(t0 - 1, 0)
                    hi = min(t0 + 512, NTOK - 1)
                    DMA("pool", hh[0:1, :], hbuf[lo:lo + 1, :], r=hkeys_fn(max(u - 1, 0)), w=["hh"])
                    DMA("pool", hh[1:2, :], hbuf[hi:hi + 1, :], r=hkeys_fn(min(u + 1, NU - 1)), w=["hh"])

                ndef = []

                def norm_piece(u, piece, defer=None):
                    hnT = hnTs[u % 2]
                    hk = ("hnT", u % 2)
                    if piece == 0:
                        def dst_halo():
                            CP("dve", hnT[:, :, 0:514:513], pT[1][:, :, 0:2], r=[], w=["pT1", hk])
                        rmsnorm_T(None, hh[:, :], 2, ms[1], hhn, hhn, pT[1], "pT1", dst_halo, ["hh"], "p6h",
                                  scale_ap=hsc[:, u:u + 1], scale_key="hsc", junk_key="p6hhn", defer=defer)
                    else:
                        t = piece - 1

                        def dst_main(t=t):
                            CP("act", hnT[:, :, 1 + 128 * t:1 + 128 * (t + 1)], pT[0][:, :, :], r=[], w=["pT0", hk])
                        rmsnorm_T(None, ht[:, t, :], 128, ms[0], hn, hn, pT[0], "pT0", dst_main, ["ht"], "p6m", junk_key="p6mhn", defer=defer)
                load_h(0)
                for piece in range(5):
                    norm_piece(0, piece)
                for u in range(NU):
                    t0 = u * 512
                    hnT = hnTs[u % 2]
                    hk = ("hnT", u % 2)
                    if u + 1 < NU:
                        load_h(u + 1)
                    pair = 0
                    for wdw in range(2):
                        gw = 2 * u + wdw
                        actT = actTh[gw % 2]
                        ak = ("actT", gw % 2)
                        for j in range(NFC):
                            ig = (ucount[0] * 2) % 4
                            iu = (ucount[0] * 2 + 1) % 4
                            ia = ucount[0] % 2
                            ucount[0] += 1
                            for (ip, fc) in ((ig, j), (iu, NFC + j)):
                                for kc in range(8):
                                    MM(pU[ip][:, 0:258], wup[:, kc, fc * 128:(fc + 1) * 128], hnT[:, kc, 256 * wdw:256 * wdw + 258],
                                       kc == 0, kc == 7, r=[("wup", kc), hk], w=[("pU", ip)])
                            if u + 1 < NU and pair in (2, 10, 18, 26, 34):
                                norm_piece(u + 1, (pair - 2) // 8, defer=ndef)
                            if pair in (9, 17, 25, 33, 41):
                                while ndef:
                                    ndef.pop(0)()
                            if j in (3, 8, 13, 18) and downq:
                                downq.pop(0)()
                            pair += 1
                            for (ip, fc, acc_, ka) in ((ig, j, accg[ia], ("accg", ia)), (iu, NFC + j, accu[ia], ("accu", ia))):
                                ACTV(acc_[:], pU[ip][:, 1:257], AF.Identity, r=["cw", "cb"], w=[("pU", ip), ka],
                                     scale=cw[:, 1, fc:fc + 1], bias=cb[:, fc:fc + 1])
                                STT("dve", acc_[:], pU[ip][:, 0:256], cw[:, 0, fc:fc + 1], acc_[:], ALU.mult, ALU.add,
                                    r=["cw", ka], w=[("pU", ip), ka])
                                STT("dve", acc_[:], pU[ip][:, 2:258], cw[:, 2, fc:fc + 1], acc_[:], ALU.mult, ALU.add,
                                    r=["cw", ka], w=[("pU", ip), ka])
                            ACTV(sg[ia][:], accg[ia][:], AF.Silu, r=[("accg", ia)], w=[("sg", ia)])
                            TT("pool", actT[:, j, :], sg[ia][:], accu[ia][:], ALU.mult,
                               r=[("sg", ia), ("accu", ia)], w=[ak])
                        while downq:
                            downq.pop(0)()
                        for tt in range(2):
                            tok = t0 + 256 * wdw + 128 * tt
                            for nh in range(2):
                                def down_group(tt=tt, nh=nh, tok=tok, actT=actT, ak=ak, u=u, gw=gw):
                                    iy = (2 * gw + tt) % 2
                                    if nh == 0:
                                        DMA("sp", yt[iy][:], hbuf[tok:tok + 128, :], r=hkeys_fn(u), w=[("yt", iy)])
                                    ipd = dcount[0] % 2
                                    dcount[0] += 1
                                    for j in range(NFC):
                                        MM(pD[ipd][:, :], actT[:, j, 128 * tt:128 * (tt + 1)], wdn[:, j, 512 * nh:512 * (nh + 1)],
                                           j == 0, j == NFC - 1, r=[ak, ("wdn", j)], w=[("pD", ipd)])
                                    TT("dve", yt[iy][:, 512 * nh:512 * (nh + 1)], pD[ipd][:, :], yt[iy][:, 512 * nh:512 * (nh + 1)], ALU.add,
                                       r=[("yt", iy)], w=[("pD", ipd), ("yt", iy)])
                                    if nh == 1:
                                        DMA("sp", ys[tok:tok + 128, :], yt[iy][:], r=[("yt", iy)], w=[("ys", tok)])
                                downq.append(down_group)
                while downq:
                    downq.pop(0)()
            P.barrier()

        if "1" in passes:
            pass1()
        if "2" in passes:
            pass2()
        if "P" in passes:
            pass_prep()
        dirs = [dr for dr, nm in ((0, "3"), (1, "4")) if nm in passes]
        if dirs:
            pass_dn_both(dirs)
        if "5" in passes:
            pass5()
        if "6" in passes:
            if "5" in passes:
                pass6(H_d, lambda u: [("H_d", u)])
            else:
                pass6(xs, lambda u: [])
        P.emit()
    return nc


def _dn_consts():
    NEG = -30000.0
    p = np.arange(128)[:, None]
    f = np.arange(128)[None, :]
    same = (p // 64) == (f // 64)
    c = np.zeros((128, 11, 128), np.float32)
    c[:, 0] = same & (p <= f)
    c[:, 1] = same & (p >= f)
    c[:, 2] = 1.0
    c[:, 3] = np.where(same & (f < p), 0.0, NEG)
    c[:, 4] = np.where(same & (f > p), 0.0, NEG)
    c[:, 5] = np.where(same & (f >= p), 0.0, NEG)
    c[:, 6] = np.where(same & (f <= p), 0.0, NEG)
    c[:, 7] = np.eye(128)
    c[:, 8] = (p // 64 == 0) & (f >= 0)
    c[:, 9] = (p // 64 == 1) & (f >= 0)
    c[:, 10] = same
    return c


def _dn_scales(NSEG, link):
    NT = NSEG * SEG // 128
    sc = np.ones((128, 4 * NT), np.float32)
    for ti in range(NT):
        s = ti // 16
        if ti % 16 == 0:
            sc[:, 4 * ti + 0] = link[s]
            sc[:, 4 * ti + 2] = link[s]
        if ti % 16 == 15:
            sc[:, 4 * ti + 1] = link[s + 1]
            sc[:, 4 * ti + 3] = link[s + 1]
    return sc


def host_consts(NSEG, link, pos0):
    NTOK = NSEG * SEG
    NU = NTOK // 512
    NT = NTOK // 128
    hs = np.ones((2, NU), np.float32)
    for u in range(NU):
        t0 = u * 512
        if t0 % SEG == 0:
            hs[0, u] = link[t0 // SEG]
        if (t0 + 512) % SEG == 0:
            hs[1, u] = link[(t0 + 512) // SEG]
    lv = np.ones((128, 2 * NSEG), np.float32)
    for s in range(NSEG):
        lv[0:64, 2 * s] = link[s]
        lv[64:128, 2 * s + 1] = link[s + 1]
    r_ = np.arange(128)[:, None]
    q_ = np.arange(128)[None, :]
    amask = np.concatenate([(q_ <= r_), (q_ >= r_)], axis=1).astype(np.float32)
    half = 32
    inv_freq = (1.0 / (10000.0 ** (np.arange(half, dtype=np.float32) * 2.0 / 64))).astype(np.float32)
    invf = np.tile(np.concatenate([inv_freq, inv_freq])[None, :], (128, 1)).astype(np.float32)
    phase = np.tile(np.concatenate([np.zeros(32), np.full(32, np.pi / 2)])[None, :], (128, 1)).astype(np.float32)
    pos = np.zeros((128, NT), np.float32)
    for s in range(NSEG):
        for t in range(16):
            pos[:, s * 16 + t] = pos0[s] + t * 128 + np.arange(128)
    return {"ident": np.eye(128, dtype=np.float32), "halo_sc": hs, "lv": lv, "amask": amask, "invf": invf,
            "phase": phase, "pos": pos,
            "dnc": _dn_consts(), "dnsc": _dn_scales(NSEG, link)}


def weight_maps(norm1, w_in, att_q_norm, att_k_norm, dn_conv_w, dn_a_log, dn_dt_bias, dn_out_norm, w_out, norm2,
                w_up, ffn_conv_w, ffn_conv_b, w_down):
    f = lambda a: np.asarray(a, np.float32)[0]
    qg = np.concatenate([np.tile(f(att_q_norm), 8), np.tile(f(att_k_norm), 8)])
    return {
        "w_in": np.ascontiguousarray(f(w_in)),
        "w_out": np.ascontiguousarray(f(w_out)),
        "w_up": np.ascontiguousarray(f(w_up)),
        "w_down": np.ascontiguousarray(f(w_down)),
        "norm1": np.ascontiguousarray(f(norm1).reshape(8, 128).T),
        "norm2": np.ascontiguousarray(f(norm2).reshape(8, 128).T),
        "qkg": np.ascontiguousarray(np.tile(qg[None, :], (128, 1))),
        "alog": np.ascontiguousarray(np.tile(f(dn_a_log).reshape(1, 8), (128, 1))),
        "dtb": np.ascontiguousarray(np.tile(f(dn_dt_bias).reshape(1, 8), (128, 1))),
        "dncw": np.ascontiguousarray(f(dn_conv_w).reshape(5, 12, 128).transpose(2, 0, 1)),
        "dnog": np.ascontiguousarray(np.tile(f(dn_out_norm)[None, :], (128, 4))),
        "ffn_cw": np.ascontiguousarray(f(ffn_conv_w).reshape(3, 44, 128).transpose(2, 0, 1)),
        "ffn_cb": np.ascontiguousarray(f(ffn_conv_b).reshape(44, 128).T),
    }


SAMPLE_SLOTS = [6, 6, 5, 5, 5, 5]
_NC_CACHE = {}


def _core_layout():
    lay = []
    for b in range(2):
        lay.append([("p", b, s) for s in range(8)])
    nxt = 0
    for n in SAMPLE_SLOTS:
        row = []
        for i in range(8):
            if i < n:
                row.append(("s", nxt, 0))
                nxt += 1
            else:
                row.append(None)
        lay.append(row)
    return lay


def kernel(x_prompt, x_sample, norm1, w_in, att_q_norm, att_k_norm, dn_conv_w, dn_a_log, dn_dt_bias,
           dn_out_norm, w_out, norm2, w_up, ffn_conv_w, ffn_conv_b, w_down):
    NSEG = 8
    lay = _core_layout()
    if "nc" not in _NC_CACHE:
        _NC_CACHE["nc"] = build_program(NSEG=NSEG)
    nc = _NC_CACHE["nc"]
    x_prompt = np.asarray(x_prompt, np.float32)
    x_sample = np.asarray(x_sample, np.float32)
    common = weight_maps(norm1, w_in, att_q_norm, att_k_norm, dn_conv_w, dn_a_log, dn_dt_bias, dn_out_norm, w_out,
                         norm2, w_up, ffn_conv_w, ffn_conv_b, w_down)
    in_maps = []
    for c in range(NCORES):
        xs = np.zeros((NSEG * SEG, D), np.float32)
        link = np.zeros(NSEG + 1, np.float32)
        pos0 = np.zeros(NSEG, np.float32)
        for i, ent in enumerate(lay[c]):
            if ent is None:
                continue
            kind, b, s = ent
            if kind == "p":
                xs[i * SEG:(i + 1) * SEG] = x_prompt[b, s * SEG:(s + 1) * SEG]
                pos0[i] = s * SEG
                if s > 0:
                    link[i] = 1.0
            else:
                xs[i * SEG:(i + 1) * SEG] = x_sample[b]
        m = dict(common)
        m.update(host_consts(NSEG, link, pos0))
        m["xs"] = xs
        in_maps.append(m)
    res = run_bass_kernel_spmd(nc, in_maps, core_ids=list(range(NCORES)))
    y_prompt = np.zeros_like(x_prompt)
    y_sample = np.zeros_like(x_sample)
    for c in range(NCORES):
        ys = res.results[c]["ys"]
        for i, ent in enumerate(lay[c]):
            if ent is None:
                continue
            kind, b, s = ent
            if kind == "p":
                y_prompt[b, s * SEG:(s + 1) * SEG] = ys[i * SEG:(i + 1) * SEG]
            else:
                y_sample[b] = ys[i * SEG:(i + 1) * SEG]
    return (y_prompt, y_sample)
```

```python
import numpy as np
import concourse.bass as bass
import concourse.mybir as mybir
from concourse.bass_utils import run_bass_kernel_spmd

F32 = mybir.dt.float32
BF16 = mybir.dt.bfloat16
I32 = mybir.dt.int32
AF = mybir.ActivationFunctionType
ALU = mybir.AluOpType
AX = mybir.AxisListType

D = 1024
FFN = 2816
NFC = FFN // 128
SEG = 2048
NCORES = 8
EPS = 1e-6


class _Op:
    __slots__ = ("eng", "fn", "deps", "sig", "is_dma", "dsem", "dval", "need_sig", "idx")


class Prog:
    ENGS = ("sp", "act", "dve", "pool", "pe")

    def __init__(self, nc, n_dma_sems=8):
        self.nc = nc
        self.ops = []
        self.last_w = {}
        self.readers = {}
        self.n_dma_sems = n_dma_sems
        self.dma_rr = {"sp": 0, "pool": 0, "act": 0}
        self.dma_last = {}
        self.dma_cnt = {}
        self.extra = {}
        self.ns = None
        self.shared = set()
        self.last_op = {}

    def _k(self, k):
        if self.ns is None or k in self.shared or (isinstance(k, tuple) and k[0] in self.shared):
            return k
        return (self.ns, k)

    def op(self, eng, fn, r=(), w=(), dma=False):
        if self.ns is not None:
            r = [self._k(k) for k in r]
            w = [self._k(k) for k in w]
        o = _Op()
        o.eng = eng; o.fn = fn; o.is_dma = dma; o.need_sig = dma; o.sig = None
        o.idx = len(self.ops)
        deps = list(self.extra.pop(eng, []))
        for k in r:
            p = self.last_w.get(k)
            if p is not None:
                deps.append(p)
        for k in w:
            p = self.last_w.get(k)
            if p is not None:
                deps.append(p)
            for q in self.readers.get(k, ()):
                deps.append(q)
        if dma:
            j = self.dma_rr[eng]
            self.dma_rr[eng] = (j + 1) % self.n_dma_sems
            key = (eng, j)
            prev = self.dma_last.get(key)
            if prev is not None:
                deps.append(prev)
            self.dma_last[key] = o
            self.dma_cnt[key] = self.dma_cnt.get(key, 0) + 1
            o.dsem = key
            o.dval = 16 * self.dma_cnt[key]
        dd = []
        seen = set()
        for p in deps:
            if p is o or id(p) in seen:
                continue
            if eng == "pe" and p.eng == "pe" and not p.is_dma:
                continue
            seen.add(id(p))
            dd.append(p)
            p.need_sig = True
        o.deps = dd
        for k in w:
            self.last_w[k] = o
            self.readers[k] = []
        for k in r:
            self.readers.setdefault(k, []).append(o)
        self.ops.append(o)
        if not dma:
            self.last_op[eng] = o
        return o

    def barrier(self):
        markers = [o for o in self.last_op.values()] + [o for o in self.dma_last.values()]
        for e in self.ENGS:
            self.extra[e] = list(markers) + self.extra.get(e, [])

    def emit(self, final_wait_eng="sp"):
        nc = self.nc
        cnt = {e: 0 for e in self.ENGS}
        for o in self.ops:
            if o.is_dma:
                o.sig = (("dma",) + o.dsem, o.dval)
            elif o.need_sig:
                cnt[o.eng] += 1
                o.sig = (("eng", o.eng), cnt[o.eng])
        sem_keys = [("eng", e) for e in self.ENGS] + [("dma", q, j) for q in ("sp", "pool", "act") for j in range(self.n_dma_sems)]
        per_eng = {e: [o for o in self.ops if o.eng == e] for e in self.ENGS}
        finals = {}
        for o in self.ops:
            if o.is_dma:
                finals[o.sig[0]] = max(finals.get(o.sig[0], 0), o.sig[1])
        import contextlib
        with contextlib.ExitStack() as st:
            sems = {}
            for k in sem_keys:
                sems[k] = st.enter_context(nc.semaphore("s_" + "_".join(str(x) for x in k)))
            block = st.enter_context(nc.Block())

            def replay(e, eobj):
                known = {}
                for o in per_eng[e]:
                    need = {}
                    for p in o.deps:
                        sk, v = p.sig
                        if v > need.get(sk, 0):
                            need[sk] = v
                    for sk, v in need.items():
                        if known.get(sk, 0) < v:
                            eobj.wait_ge(sems[sk], v)
                            known[sk] = v
                    ins = o.fn(eobj)
                    if o.sig is not None:
                        ins.then_inc(sems[o.sig[0]], 16 if o.is_dma else 1)
                if e == final_wait_eng:
                    for sk, v in finals.items():
                        if known.get(sk, 0) < v:
                            eobj.wait_ge(sems[sk], v)

            @block.sync
            def _(e):
                replay("sp", e)

            @block.scalar
            def _(e):
                replay("act", e)

            @block.vector
            def _(e):
                replay("dve", e)

            @block.gpsimd
            def _(e):
                replay("pool", e)

            @block.tensor
            def _(e):
                replay("pe", e)


import contextlib

IN_COLS = 3600
KPAD = 1024
DQPAD = 2
TWO_PI = float(2 * np.pi)


def _sl(start, count, step):
    return slice(start, start + (count - 1) * step + 1, step)


def build_program(NSEG=8, passes=("1", "2", "P", "3", "4", "5", "6"), dbg=False):
    nc = bass.Bass("TRN2", target_bir_lowering=False)
    NTOK = NSEG * SEG
    NT = NTOK // 128
    NU = NTOK // 512
    P = Prog(nc)

    def din(name, shape, dt=F32):
        return nc.dram_tensor(name, list(shape), dt, kind="ExternalInput").ap()

    def dout(name, shape, dt=F32):
        return nc.dram_tensor(name, list(shape), dt, kind="ExternalOutput").ap()

    def dscr(name, shape, dt=F32):
        return nc.dram_tensor(name, list(shape), dt, kind=("ExternalOutput" if dbg else "Internal")).ap()

    xs = din("xs", [NTOK, D])
    ident_d = din("ident", [128, 128])
    w_in_d = din("w_in", [D, IN_COLS])
    w_out_d = din("w_out", [D, D])
    w_up_d = din("w_up", [D, 2 * FFN])
    w_down_d = din("w_down", [FFN, D])
    norm1_d = din("norm1", [128, 8])
    norm2_d = din("norm2", [128, 8])
    qkg_d = din("qkg", [128, 1024])
    invf_d = din("invf", [128, 64])
    phase_d = din("phase", [128, 64])
    pos_d = din("pos", [128, NT])
    alog_d = din("alog", [128, 8])
    dtb_d = din("dtb", [128, 8])
    dncw_d = din("dncw", [128, 5, 12])
    dnog_d = din("dnog", [128, 512])
    cw_d = din("ffn_cw", [128, 3, 44])
    cb_d = din("ffn_cb", [128, 44])
    halo_sc_d = din("halo_sc", [2, NU])
    lv_d = din("lv", [128, 2 * NSEG])
    amask_d = din("amask", [128, 256])
    dnc_d = din("dnc", [128, 11, 128])
    dnsc_d = din("dnsc", [128, 4 * NT])
    ys = dout("ys", [NTOK, D])
    KW = NTOK + 2 * KPAD
    QT_d = dscr("QT_s", [4, 128, KW], BF16)
    KT_d = dscr("KT_s", [4, 128, KW], BF16)
    V_d = dscr("V_s", [KW, 512], BF16)
    DQ_d = dscr("DQ_s", [1536, NTOK + 2 * DQPAD], BF16)
    DN_d = dscr("DN_s", [1536, NTOK], BF16)
    G_d = dscr("G_s", [NTOK, 512], F32)
    GB_d = dscr("GB_s", [NTOK, 16], F32)
    AT_d = dscr("AT_s", [4, 128, NTOK], BF16)
    OD_d = [dscr("OF_s", [NTOK, 512], F32), dscr("OB_s", [NTOK, 512], F32)]
    H_d = dscr("H_s", [NTOK, D], F32)

    def MM(out, lhsT, rhs, start, stop, r, w):
        P.op("pe", lambda e, o=out, l=lhsT, rr=rhs, s=start, t=stop: e.matmul(o, lhsT=l, rhs=rr, start=s, stop=t), r=r, w=w)

    def TR(out, in_, idn, r, w):
        P.op("pe", lambda e, o=out, i=in_, d=idn: e.transpose(out=o, in_=i, identity=d), r=r, w=w)

    def ACTV(out, in_, func, r, w, **kw):
        P.op("act", lambda e, o=out, i=in_, f=func, kw=kw: e.activation(out=o, in_=i, func=f, **kw), r=r, w=w)

    def TT(eng, out, in0, in1, op, r, w):
        P.op(eng, lambda e, o=out, a=in0, b=in1, p=op: e.tensor_tensor(out=o, in0=a, in1=b, op=p), r=r, w=w)

    def TS(eng, out, in0, s1, s2, op0, op1, r, w):
        if s2 is None:
            P.op(eng, lambda e, o=out, a=in0, x=s1, p0=op0: e.tensor_scalar(out=o, in0=a, scalar1=x, scalar2=None, op0=p0), r=r, w=w)
        else:
            P.op(eng, lambda e, o=out, a=in0, x=s1, y=s2, p0=op0, p1=op1: e.tensor_scalar(out=o, in0=a, scalar1=x, scalar2=y, op0=p0, op1=p1), r=r, w=w)

    def STT(eng, out, in0, scalar, in1, op0, op1, r, w):
        P.op(eng, lambda e, o=out, a=in0, sc=scalar, b=in1, p0=op0, p1=op1: e.scalar_tensor_tensor(out=o, in0=a, scalar=sc, in1=b, op0=p0, op1=p1), r=r, w=w)

    def CP(eng, out, in_, r, w):
        if eng == "act":
            P.op("act", lambda e, o=out, i=in_: e.copy(out=o, in_=i), r=r, w=w)
        else:
            P.op(eng, lambda e, o=out, i=in_: e.tensor_copy(out=o, in_=i), r=r, w=w)

    def MSET(eng, ap, val, w):
        P.op(eng, lambda e, a=ap, v=val: e.memset(a, v), w=w)

    def DMA(q, out, in_, r=(), w=()):
        P.op(q, lambda e, o=out, i=in_: e.dma_start(out=o, in_=i), r=r, w=w, dma=True)

    def RED(eng, out, in_, r, w):
        P.op(eng, lambda e, o=out, i=in_: e.tensor_reduce(out=o, in_=i, axis=AX.X, op=ALU.add), r=r, w=w)

    def RECIP(out, in_, r, w):
        P.op("dve", lambda e, o=out, i=in_: e.reciprocal(out=o, in_=i), r=r, w=w)

    with contextlib.ExitStack() as es0:
        def sb0(name, shape, dt=F32):
            return es0.enter_context(nc.sbuf_tensor(name, list(shape), dt))
        ident_f = sb0("ident_f", [128, 128])
        ident = sb0("ident_b", [128, 128], BF16)
        epsT = sb0("epsT", [128, 1])
        oneT = sb0("oneT", [128, 1])
        DMA("sp", ident_f[:], ident_d[:, :], w=["ident_f"])
        CP("dve", ident[:], ident_f[:], r=["ident_f"], w=["ident"])
        MSET("dve", epsT[:], EPS, w=["eps"])
        MSET("dve", oneT[:], 1.0, w=["one"])

        def rmsnorm_T(es_sb, src_ap, npart, msT, hnb, sq, pTt, pkey, dst_fn, rkeys, keyp, scale_ap=None, scale_key=None, junk_key="sq",
                      defer=None):
            ACTV(sq[0:npart, :], src_ap, AF.Square, r=rkeys, w=[junk_key, keyp + "ms"], scale=1.0 / 32, accum_out=msT[0:npart, :])
            ACTV(msT[0:npart, :], msT[0:npart, :], AF.Ln, r=[keyp + "ms", "eps"], w=[keyp + "ms"], bias=epsT[0:npart, :])
            ACTV(msT[0:npart, :], msT[0:npart, :], AF.Exp, r=[keyp + "ms"], w=[keyp + "ms"], scale=-0.5)
            if scale_ap is not None:
                TT("dve", msT[0:npart, :], msT[0:npart, :], scale_ap, ALU.mult, r=[keyp + "ms", scale_key], w=[keyp + "ms"])
            TS("dve", hnb[0:npart, :], src_ap, msT[0:npart, 0:1], None, ALU.mult, None, r=rkeys + [keyp + "ms"], w=[keyp + "hn"])
            def second():
                for kc in range(8):
                    TR(pTt[:, kc, 0:npart], hnb[0:npart, kc * 128:(kc + 1) * 128], ident[0:npart, 0:npart],
                       r=[keyp + "hn", "ident"], w=[pkey])
                dst_fn()
            if defer is None:
                second()
            else:
                defer.append(second)

        def pass1():
            with contextlib.ExitStack() as es:
                def sb(name, shape, dt=F32):
                    return es.enter_context(nc.sbuf_tensor("p1_" + name, list(shape), dt))

                def ps(name, shape, dt=F32):
                    return es.enter_context(nc.psum_tensor("p1_" + name, list(shape), dt))
                win = sb("win", [128, 8, IN_COLS], BF16)
                g1 = sb("g1", [128, 8])
                qkg = sb("qkg", [128, 1024])
                invf = sb("invf", [128, 64])
                phase = sb("phase", [128, 64])
                post = sb("post", [128, NT])
                nexpA = sb("nexpA", [128, 8])
                dtb = sb("dtb", [128, 8])
                zb = sb("zb", [128, 2, 512], BF16)
                zf = sb("zf", [128, 12, DQPAD], BF16)
                DMA("sp", g1[:], norm1_d[:, :], w=["g1"])
                DMA("sp", qkg[:], qkg_d[:, :], w=["qkg"])
                DMA("sp", invf[:], invf_d[:, :], w=["invf"])
                DMA("sp", phase[:], phase_d[:, :], w=["phase"])
                DMA("sp", post[:], pos_d[:, :], w=["post"])
                DMA("sp", nexpA[:], alog_d[:, :], w=["nexpA"])
                DMA("sp", dtb[:], dtb_d[:, :], w=["dtb"])
                ACTV(nexpA[:], nexpA[:], AF.Exp, r=["nexpA"], w=["nexpA"])
                TS("dve", nexpA[:], nexpA[:], -1.0, None, ALU.mult, None, r=["nexpA"], w=["nexpA"])
                MSET("pool", zb[:], 0.0, w=["zb"])
                MSET("pool", zf[:], 0.0, w=["zf"])
                for j in range(4):
                    DMA("sp", KT_d[j, :, 0:KPAD], zb[:, 0:2, :].rearrange("p a b -> p (a b)"), r=["zb"], w=[("KT_d", "padL")])
                    DMA("sp", KT_d[j, :, KPAD + NTOK:KW], zb[:, 0:2, :].rearrange("p a b -> p (a b)"), r=["zb"], w=[("KT_d", "padR")])
                for q4 in range(4):
                    DMA("sp", V_d[256 * q4:256 * (q4 + 1), :].rearrange("(t p) c -> p t c", p=128), zb[:], r=["zb"], w=[("V_d", "padL")])
                    DMA("sp", V_d[KPAD + NTOK + 256 * q4:KPAD + NTOK + 256 * (q4 + 1), :].rearrange("(t p) c -> p t c", p=128), zb[:],
                        r=["zb"], w=[("V_d", "padR")])
                dqv = DQ_d.rearrange("(f p) w -> p f w", p=128)
                DMA("sp", dqv[:, :, 0:DQPAD], zf[:], r=["zf"], w=[("DQ_d", "padL")])
                DMA("sp", dqv[:, :, DQPAD + NTOK:DQPAD + NTOK + DQPAD], zf[:], r=["zf"], w=[("DQ_d", "padR")])
                for kc in range(8):
                    DMA("pool", win[:, kc, :], w_in_d[kc * 128:(kc + 1) * 128, :], w=[("win", kc)])
                    TS("dve", win[:, kc, :], win[:, kc, :], g1[:, kc:kc + 1], None, ALU.mult, None, r=["g1", ("win", kc)], w=[("win", kc)])
                xt = [sb(f"xt{i}", [128, 4, D]) for i in range(2)]
                nTs = [sb(f"nT{i}", [128, 8, 512], BF16) for i in range(2)]
                hn = sb("hn", [128, D], BF16)
                sq = sb("sq", [128, D])
                ms = sb("ms", [128, 1])
                dqs = sb("dqs", [128, 12, 512], BF16)
                vb = sb("vb", [128, 4, 512], BF16)
                gs = sb("gs", [128, 4, 512])
                gbt = sb("gbt", [128, 4, 16])
                zt = sb("zt", [128, 8])
                bt8 = sb("bt8", [128, 8])
                qraw = sb("qraw", [128, 1024])
                qn = sb("qn", [128, 1024])
                t1 = sb("t1", [128, 512]); t2 = sb("t2", [128, 512]); t3 = sb("t3", [128, 512]); t4 = sb("t4", [128, 512])
                qrs = [sb(f"qr{i}", [128, 1024], BF16) for i in range(2)]
                qkTs = [sb(f"qkT{i}", [128, 8, 512], BF16) for i in range(2)]
                ssq = sb("ssq", [128, 16])
                ang16 = sb("ang16", [128, 16, 64]); kfi16 = sb("kfi16", [128, 16, 64], I32); kff16 = sb("kff16", [128, 16, 64])
                cs16 = sb("cs16", [128, 16, 64])
                deferred = []
                pTT = ps("pTT", [128, 8, 128], BF16)
                pFs = [ps(f"pF{i}", [128, 512]) for i in range(2)]
                pQ = ps("pQ", [128, 512]); pK = ps("pK", [128, 512]); pV = ps("pV", [128, 512])
                pG = ps("pG", [128, 512]); pB = ps("pB", [128, 512])
                fcnt = [0]

                def norm_piece(u, t):
                    b = u % 2

                    def dst(t=t, b=b):
                        CP("act", nTs[b][:, :, 128 * t:128 * (t + 1)], pTT[:, :, :], r=[], w=["pTT", ("nT", b)])
                    rmsnorm_T(None, xt[b][:, t, :], 128, ms, hn, sq, pTT, "pTT", dst, [("xt", b)], "p1")
                def load_x(u):
                    DMA("sp", xt[u % 2][:], xs[u * 512:u * 512 + 512, :].rearrange("(t p) d -> p t d", p=128), w=[("xt", u % 2)])
                load_x(0)
                for t in range(4):
                    norm_piece(0, t)
                for u in range(NU):
                    b = u % 2
                    if u + 1 < NU:
                        load_x(u + 1)
                    nT = nTs[b]
                    nTk = ("nT", b)
                    t0 = u * 512
                    for fc in range(12):
                        c0 = 1536 + fc * 128
                        pF = pFs[fcnt[0] % 2]; pFk = ("pF", fcnt[0] % 2); fcnt[0] += 1
                        for kc in range(8):
                            MM(pF[:, :], win[:, kc, c0:c0 + 128], nT[:, kc, :], kc == 0, kc == 7, r=[("win", kc), nTk], w=[pFk])
                        CP("act", dqs[:, fc, :], pF[:, :], r=[], w=[pFk, "dqs"])
                    DMA("sp", dqv[:, :, DQPAD + t0:DQPAD + t0 + 512], dqs[:], r=["dqs"], w=[("DQ_d", u)])
                    for t in range(4):
                        ti = u * 4 + t
                        for (pp, key, c0, n) in ((pQ, "pQ", 0, 512), (pK, "pK", 512, 512), (pV, "pV", 1024, 512),
                                                 (pG, "pG", 3072, 512), (pB, "pB", 3584, 16)):
                            for kc in range(8):
                                MM(pp[:, 0:n], nT[:, kc, 128 * t:128 * (t + 1)], win[:, kc, c0:c0 + n], kc == 0, kc == 7,
                                   r=[("win", kc), nTk], w=[key])
                        while deferred:
                            deferred.pop(0)()
                        if u + 1 < NU:
                            norm_piece(u + 1, t)
                        CP("act", qraw[:, 0:512], pQ[:, :], r=[], w=["pQ", "qraw"])
                        CP("dve", qraw[:, 512:1024], pK[:, :], r=[], w=["pK", "qraw"])
                        ACTV(sq[:], qraw[:], AF.Square, r=["qraw"], w=["sq"])
                        RED("dve", ssq[:], sq[:].rearrange("p (h d) -> p h d", d=64), r=["sq"], w=["ssq"])
                        CP("act", vb[:, t, :], pV[:, :], r=[], w=["pV", "vb"])
                        ACTV(ssq[:], ssq[:], AF.Ln, r=["ssq", "eps"], w=["ssq"], scale=1.0 / 64, bias=epsT[:])
                        ACTV(ssq[:], ssq[:], AF.Exp, r=["ssq"], w=["ssq"], scale=-0.5)
                        TT("dve", qn[:].rearrange("p (h d) -> p h d", d=64), qraw[:].rearrange("p (h d) -> p h d", d=64),
                           ssq[:].unsqueeze(2).to_broadcast([128, 16, 64]), ALU.mult, r=["ssq", "qraw"], w=["qn"])
                        TT("dve", qn[:], qn[:], qkg[:], ALU.mult, r=["qn", "qkg"], w=["qn"])
                        CP("act", gs[:, t, :], pG[:, :], r=[], w=["pG", "gs"])
                        ACTV(bt8[:], pB[:, 0:8], AF.Exp, r=[], w=["pB", "bt8"], scale=-1.0)
                        TT("pool", zt[:], pB[:, 8:16], dtb[:], ALU.add, r=["dtb"], w=["pB", "zt"]) if False else TT("dve", zt[:], pB[:, 8:16], dtb[:], ALU.add, r=["dtb"], w=["pB", "zt"])
                        ACTV(zt[:], zt[:], AF.Exp, r=["zt"], w=["zt"])
                        ACTV(zt[:], zt[:], AF.Ln, r=["zt", "one"], w=["zt"], bias=oneT[:])
                        if ti % 16 == 0:
                            for tt in range(16):
                                STT("dve", ang16[:, tt, :], invf[:], post[:, ti + tt:ti + tt + 1], phase[:], ALU.mult, ALU.add,
                                    r=["invf", "post", "phase"], w=["ang16"])
                            TS("dve", kfi16[:], ang16[:], 1.0 / TWO_PI, None, ALU.mult, None, r=["ang16"], w=["kfi16"])
                            CP("dve", kff16[:], kfi16[:], r=["kfi16"], w=["kff16"])
                            STT("dve", ang16[:], kff16[:], -TWO_PI, ang16[:], ALU.mult, ALU.add, r=["kff16", "ang16"], w=["ang16"])
                            TS("dve", ang16[:], ang16[:], -float(np.pi), float(np.pi), ALU.max, ALU.min, r=["ang16"], w=["ang16"])
                            ACTV(cs16[:], ang16[:], AF.Sin, r=["ang16"], w=["cs16"])
                        qr = qrs[ti % 2]
                        qrk = ("qr", ti % 2)
                        cs = cs16[:, ti % 16, :]
                        qv = qn[:].rearrange("p (h d) -> p h d", d=64)
                        qrv = qr[:].rearrange("p (h d) -> p h d", d=64)
                        sinb = cs[:, 0:32].unsqueeze(1).to_broadcast([128, 16, 32])
                        cosb = cs[:, 32:64].unsqueeze(1).to_broadcast([128, 16, 32])
                        v3 = lambda tt: tt[:].rearrange("p (h d) -> p h d", d=32)
                        TT("dve", v3(t1), qv[:, :, 0:32], cosb, ALU.mult, r=["qn", "cs16"], w=["t1"])
                        TT("dve", v3(t2), qv[:, :, 32:64], sinb, ALU.mult, r=["qn", "cs16"], w=["t2"])
                        TT("dve", qrv[:, :, 0:32], v3(t1), v3(t2), ALU.subtract, r=["t1", "t2"], w=[qrk])
                        TS("dve", bt8[:], bt8[:], 1.0, None, ALU.add, None, r=["bt8"], w=["bt8"])
                        RECIP(gbt[:, t, 8:16], bt8[:], r=["bt8"], w=["gbt"])
                        TT("dve", gbt[:, t, 0:8], zt[:], nexpA[:], ALU.mult, r=["zt", "nexpA"], w=["gbt"])
                        TT("pool", v3(t3), qv[:, :, 32:64], cosb, ALU.mult, r=["qn", "cs16"], w=["t3"])
                        TT("pool", v3(t4), qv[:, :, 0:32], sinb, ALU.mult, r=["qn", "cs16"], w=["t4"])
                        TT("pool", qrv[:, :, 32:64], v3(t3), v3(t4), ALU.add, r=["t3", "t4"], w=[qrk])

                        def tr_stage(qr=qr, qrk=qrk, t=t, qkT=qkTs[u % 2], qkk=("qkT", u % 2)):
                            for j in range(8):
                                TR(pTT[:, j, :], qr[:, j * 128:(j + 1) * 128], ident[:, :], r=[qrk, "ident"], w=["pTT"])
                            CP("act", qkT[:, :, 128 * t:128 * (t + 1)], pTT[:, :, :], r=[], w=["pTT", qkk])
                        deferred.append(tr_stage)
                    while deferred:
                        deferred.pop(0)()
                    qkT = qkTs[u % 2]
                    DMA("sp", V_d[KPAD + t0:KPAD + t0 + 512, :].rearrange("(t p) c -> p t c", p=128), vb[:], r=["vb"], w=[("V_d", u)])
                    DMA("sp", G_d[t0:t0 + 512, :].rearrange("(t p) c -> p t c", p=128), gs[:], r=["gs"], w=[("G_d", u)])
                    DMA("sp", GB_d[t0:t0 + 512, :].rearrange("(t p) c -> p t c", p=128), gbt[:], r=["gbt"], w=[("GB_d", u)])
                    DMA("sp", QT_d[:, :, KPAD + t0:KPAD + t0 + 512].rearrange("j p w -> p j w"), qkT[:, 0:4, :], r=[("qkT", u % 2)], w=[("QT_d", u)])
                    DMA("sp", KT_d[:, :, KPAD + t0:KPAD + t0 + 512].rearrange("j p w -> p j w"), qkT[:, 4:8, :], r=[("qkT", u % 2)], w=[("KT_d", u)])
            P.barrier()

        def pass2():
            with contextlib.ExitStack() as es:
                def sb(name, shape, dt=F32):
                    return es.enter_context(nc.sbuf_tensor("p2_" + name, list(shape), dt))

                def ps(name, shape, dt=F32):
                    return es.enter_context(nc.psum_tensor("p2_" + name, list(shape), dt))
                lv = sb("lv", [128, 2 * NSEG])
                am_f = sb("am_f", [128, 256])
                am = sb("am", [128, 256], BF16)
                amL = sb("amL", [128, 256], BF16); amR = sb("amR", [128, 256], BF16); amLR = sb("amLR", [128, 256], BF16)
                ones_b = sb("ones_b", [128, 128])
                DMA("sp", lv[:], lv_d[:, :], w=["lv"])
                DMA("sp", am_f[:], amask_d[:, :], w=["am_f"])
                CP("dve", am[:], am_f[:], r=["am_f"], w=["am"])
                MSET("dve", ones_b[:], 1.0, w=["ones_b"])
                QTs = [sb(f"QTs{i}", [128, 4, SEG], BF16) for i in range(1)]
                KTw = [sb(f"KTw{i}", [128, 4, 2 * SEG], BF16) for i in range(1)]
                acc = [sb(f"acc{e}", [128, 4, SEG]) for e in range(2)]
                attT = sb("attT", [128, 4, SEG], BF16)
                NVB = 8
                vraw = [sb(f"vraw{i}", [128, 512], BF16) for i in range(NVB)]
                vt2 = [sb(f"vt2_{i}", [128, 4, 2, 128], BF16) for i in range(NVB)]
                NPT = 6
                pt = [sb(f"pt{i}", [128, 256], BF16) for i in range(NPT)]
                rden = sb("rden", [128, SEG])
                pS = [ps(f"pS{i}", [128, 512]) for i in range(4)]
                pA = [ps(f"pA{i}", [128, 512]) for i in range(2)]
                pBc = [ps(f"pBc{i}", [128, 512]) for i in range(2)]
                for i in range(NVB):
                    MSET("pool", vt2[i][:], 0.0, w=[("vt2", i)])
                    MSET("pool", vt2[i][:, :, 0, 64:65], 1.0, w=[("vt2", i)])
                    MSET("pool", vt2[i][:, :, 1, 32:33], 1.0, w=[("vt2", i)])
                cnt = {"v": 0, "s": 0, "a": 0, "pt": 0, "m": 0, "bc": 0}
                import collections
                pending = collections.deque()
                SKEW = 3
                for s in range(NSEG):
                    b = 0
                    base = KPAD + s * SEG
                    DMA("sp", QTs[b][:], QT_d[:, :, base:base + SEG].rearrange("j p w -> p j w"),
                        r=[("QT_d", u) for u in range(4 * s, 4 * s + 4)], w=[("QTs", b)])
                    ulo = max(0, 4 * s - 2); uhi = min(NU, 4 * s + 6)
                    DMA("sp", KTw[b][:], KT_d[:, :, base - 1024:base + SEG + 1024].rearrange("j p w -> p j w"),
                        r=[("KT_d", u) for u in range(ulo, uhi)] + [("KT_d", "padL"), ("KT_d", "padR")], w=[("KTw", b)])
                    TS("pool", amL[:, 0:128], am[:, 0:128], lv[:, 2 * s:2 * s + 1], None, ALU.mult, None, r=["am", "lv"], w=["amL"])
                    CP("pool", amL[:, 128:256], am[:, 128:256], r=["am"], w=["amL"])
                    CP("pool", amR[:, 0:128], am[:, 0:128], r=["am"], w=["amR"])
                    TS("pool", amR[:, 128:256], am[:, 128:256], lv[:, 2 * s + 1:2 * s + 2], None, ALU.mult, None, r=["am", "lv"], w=["amR"])
                    CP("pool", amLR[:, 0:128], amL[:, 0:128], r=["amL"], w=["amLR"])
                    CP("pool", amLR[:, 128:256], amR[:, 128:256], r=["amR"], w=["amLR"])
                    vkeys = [("V_d", u) for u in range(ulo, uhi)] + [("V_d", "padL"), ("V_d", "padR")]
                    groups = []
                    for (d, first) in ((1, True), (4, False), (16, False)):
                        nqb = SEG // d // 128
                        for c in range(d):
                            for qb in range(nqb):
                                groups.append((d, first, c, qb, nqb))
                    needed = []
                    seen = set()
                    for (d, first, c, qb, nqb) in groups:
                        for k in (qb, qb + 1):
                            if (d, c, k) not in seen:
                                seen.add((d, c, k))
                                needed.append((d, c, k))
                    vslot = {}
                    nl = [0]

                    def load_v(d, c, k):
                        i = cnt["v"] % NVB
                        cnt["v"] += 1
                        row0 = base + c + d * (128 * k - 64)
                        DMA("sp", vraw[i][:], V_d[_sl(row0, 128, d), :], r=vkeys, w=[("vraw", i)])
                        vr = vraw[i][:].rearrange("p (j e x) -> p j e x", e=2, x=64)
                        CP("pool", vt2[i][:, :, 0, 0:64], vr[:, :, 0, :], r=[("vraw", i)], w=[("vt2", i)])
                        CP("pool", vt2[i][:, :, 1, 64:128], vr[:, :, 1, :], r=[("vraw", i)], w=[("vt2", i)])
                        vslot[(d, c, k)] = i

                    def ensure_loaded(gi):
                        if gi >= len(groups):
                            return
                        d, first, c, qb, nqb = groups[gi]
                        while not ((d, c, qb) in vslot and (d, c, qb + 1) in vslot):
                            load_v(*needed[nl[0]])
                            nl[0] += 1
                    PF = 2
                    for gi, (d, first, c, qb, nqb) in enumerate(groups):
                        for g2 in range(gi, gi + PF + 1):
                            ensure_loaded(g2)
                        if nqb == 1:
                            msk, mkey = amLR, "amLR"
                        elif qb == 0:
                            msk, mkey = amL, "amL"
                        elif qb == nqb - 1:
                            msk, mkey = amR, "amR"
                        else:
                            msk, mkey = am, "am"
                        qcol0 = c + d * 128 * qb
                        qsl = _sl(qcol0, 128, d)
                        for hp in range(4):
                            iSs = []
                            for e in range(2):
                                iSs.append(cnt["s"] % 4); cnt["s"] += 1
                            for half in range(2):
                                k = qb + half
                                kc0 = 1024 + c + d * (128 * k - 64)
                                for e in range(2):
                                    pb = 64 * e
                                    MM(pS[iSs[e]][:, 128 * half:128 * (half + 1)], KTw[b][pb:pb + 64, hp, _sl(kc0, 128, d)],
                                       QTs[b][pb:pb + 64, hp, qsl], True, True, r=[("KTw", b), ("QTs", b)], w=[("pS", iSs[e])])
                            for e in range(2):
                                iS = iSs[e]
                                ip = cnt["pt"] % NPT; cnt["pt"] += 1
                                ACTV(pt[ip][:], pS[iS][:, 0:256], AF.Exp, r=[], w=[("pS", iS), ("pt", ip)], scale=0.125)
                                meng = "pool" if (cnt["m"] % 3 == 0) else "dve"
                                cnt["m"] += 1
                                TT(meng, pt[ip][:], pt[ip][:], msk[:], ALU.mult, r=[("pt", ip), mkey], w=[("pt", ip)])

                                def pv_stage(ip=ip, e=e, hp=hp, qsl=qsl, first=first, v0=vslot[(d, c, qb)], v1=vslot[(d, c, qb + 1)]):
                                    iA = cnt["a"] % 2; cnt["a"] += 1
                                    M = 65 if e == 0 else 128
                                    for half, vi in ((0, v0), (1, v1)):
                                        MM(pA[iA][0:M, 0:128], vt2[vi][:, hp, e, 0:M], pt[ip][:, 128 * half:128 * (half + 1)],
                                           half == 0, half == 1, r=[("vt2", vi), ("pt", ip)], w=[("pA", iA)])
                                    lo, hi = (0, 65) if e == 0 else (0, 128)
                                    dst = acc[e][lo:hi, hp, qsl]
                                    if first:
                                        CP("dve", dst, pA[iA][lo:hi, 0:128], r=[], w=[("pA", iA), ("acc", e, hp)])
                                    else:
                                        TT("dve", dst, dst, pA[iA][lo:hi, 0:128], ALU.add, r=[], w=[("pA", iA), ("acc", e, hp)])
                                pending.append(pv_stage)
                            while len(pending) > SKEW:
                                pending.popleft()()
                    while pending:
                        pending.popleft()()
                    for hp in range(4):
                        for e in range(2):
                            dp = 64 if e == 0 else 32
                            lo, hi = (0, 64) if e == 0 else (64, 128)
                            ACTV(rden[dp:dp + 1, :], acc[e][dp:dp + 1, hp, :], AF.Ln, r=[("acc", e, hp)], w=["rden"])
                            ACTV(rden[dp:dp + 1, :], rden[dp:dp + 1, :], AF.Exp, r=["rden"], w=["rden"], scale=-1.0)
                            for cc in range(SEG // 512):
                                csl = slice(512 * cc, 512 * (cc + 1))
                                ib = cnt["bc"] % 2; cnt["bc"] += 1
                                MM(pBc[ib][0:hi, :], ones_b[dp:dp + 1, 0:hi], rden[dp:dp + 1, csl], True, True,
                                   r=["ones_b", "rden"], w=[("pBc", ib)])
                                TT("dve", attT[lo:hi, hp, csl], acc[e][lo:hi, hp, csl], pBc[ib][lo:hi, :], ALU.mult,
                                   r=[("acc", e, hp)], w=[("pBc", ib), "attT"])
                    DMA("sp", AT_d[:, :, s * SEG:(s + 1) * SEG].rearrange("j p w -> p j w"), attT[:], r=["attT"], w=[("AT_d", s)])
            P.barrier()


        def pass_prep():
            with contextlib.ExitStack() as es:
                def sb(name, shape, dt=F32):
                    return es.enter_context(nc.sbuf_tensor("pp_" + name, list(shape), dt))

                def ps(name, shape, dt=F32):
                    return es.enter_context(nc.psum_tensor("pp_" + name, list(shape), dt))
                dncw = sb("dncw", [128, 5, 12])
                dnsc = sb("dnsc", [128, 4 * NT])
                lnq = sb("lnq", [128, 1])
                dg = sb("dg", [128, 60, 128], BF16)
                ones_b = sb("ones_b", [128, 128], BF16)
                DMA("sp", dncw[:], dncw_d[:, :, :], w=["dncw"])
                DMA("sp", dnsc[:], dnsc_d[:, :], w=["dnsc"])
                MSET("dve", lnq[:], float(-0.5 * np.log(128.0)), w=["lnq"])
                MSET("dve", ones_b[:], 1.0, w=["ones_b"])
                for j in range(5):
                    for fc in range(12):
                        TS("dve", dg[:, j * 12 + fc, :], ident[:, :], dncw[:, j, fc:fc + 1], None, ALU.mult, None,
                           r=["ident", "dncw"], w=["dg"])
                dqb = [sb(f"dqb{i}", [128, 12, 516], BF16) for i in range(2)]
                qk32 = sb("qk32", [128, 8, 512])
                sqb = sb("sqb", [128, 8, 512], BF16)
                rn = sb("rn", [128, 8, 512])
                oqkv = [sb(f"oqkv{i}", [128, 12, 512], BF16) for i in range(2)]
                pC = [ps(f"pC{i}", [128, 512]) for i in range(4)]
                pNn = [ps(f"pN{i}", [128, 512]) for i in range(4)]
                dqv = DQ_d.rearrange("(f p) w -> p f w", p=128)
                dnv = DN_d.rearrange("(f p) w -> p f w", p=128)

                def load(u):
                    t0 = u * 512
                    deps = [("DQ_d", uu) for uu in range(max(0, u - 1), min(NU, u + 2))] + [("DQ_d", "padL"), ("DQ_d", "padR")]
                    DMA("sp", dqb[u % 2][:], dqv[:, :, DQPAD + t0 - 2:DQPAD + t0 + 514], r=deps, w=[("dqb", u % 2)])
                load(0)
                for u in range(NU):
                    b = u % 2
                    t0 = u * 512
                    if u + 1 < NU:
                        load(u + 1)
                    ti = u * 4
                    if t0 % SEG == 0:
                        TS("dve", dqb[b][:, :, 0:2], dqb[b][:, :, 0:2], dnsc[:, 4 * ti:4 * ti + 1], None, ALU.mult, None,
                           r=[("dqb", b), "dnsc"], w=[("dqb", b)])
                    if (t0 + 512) % SEG == 0:
                        TS("dve", dqb[b][:, :, 514:516], dqb[b][:, :, 514:516], dnsc[:, 4 * (ti + 3) + 1:4 * (ti + 3) + 2], None,
                           ALU.mult, None, r=[("dqb", b), "dnsc"], w=[("dqb", b)])
                    for fc in range(12):
                        i = fc % 4
                        for j in range(5):
                            MM(pC[i][:, :], dg[:, j * 12 + fc, :], dqb[b][:, fc, j:j + 512], j == 0, j == 4, r=["dg", ("dqb", b)], w=[("ppC", i)])
                        if fc < 8:
                            ACTV(qk32[:, fc, :], pC[i][:, :], AF.Silu, r=[], w=[("ppC", i), ("qk32", fc)])
                        else:
                            ACTV(oqkv[b][:, fc, :], pC[i][:, :], AF.Silu, r=[], w=[("ppC", i), ("oqkv", b)])
                    for fc in range(8):
                        TT("dve", sqb[:, fc, :], qk32[:, fc, :], qk32[:, fc, :], ALU.mult, r=[("qk32", fc)], w=[("sqb", fc)])
                    for fc in range(8):
                        i = fc % 4
                        MM(pNn[i][:, :], ones_b[:, :], sqb[:, fc, :], True, True, r=["ones_b", ("sqb", fc)], w=[("ppN", i)])
                        ACTV(rn[:, fc, :], pNn[i][:, :], AF.Ln, r=["eps"], w=[("ppN", i), ("rn", fc)], bias=epsT[:])
                    for fc in range(8):
                        if fc < 4:
                            ACTV(rn[:, fc, :], rn[:, fc, :], AF.Exp, r=[("rn", fc), "lnq"], w=[("rn", fc)], scale=-0.5, bias=lnq[:])
                        else:
                            ACTV(rn[:, fc, :], rn[:, fc, :], AF.Exp, r=[("rn", fc)], w=[("rn", fc)], scale=-0.5)
                        TT("dve", oqkv[b][:, fc, :], qk32[:, fc, :], rn[:, fc, :], ALU.mult, r=[("qk32", fc), ("rn", fc)], w=[("oqkv", b)])
                    DMA("sp", dnv[:, :, t0:t0 + 512], oqkv[b][:], r=[("oqkv", b)], w=[("DN_d", u)])
            P.barrier()

        def pass_dn_gen(dr, es):
            NEGC = 11
            if True:
                def sb(name, shape, dt=F32):
                    return es.enter_context(nc.sbuf_tensor(f"p3{dr}_" + name, list(shape), dt))

                def ps(name, shape, dt=F32):
                    return es.enter_context(nc.psum_tensor(f"p3{dr}_" + name, list(shape), dt))
                dnc = sb("dnc", [128, NEGC, 128])
                dnsc = sb("dnsc", [128, 4 * NT])
                DMA("sp", dnc[:], dnc_d[:, :, :], w=["dnc"])
                DMA("sp", dnsc[:], dnsc_d[:, :], w=["dnsc"])
                U = dnc[:, 0 + dr, :]
                ONES = dnc[:, 2, :]
                MLOW = dnc[:, 3 + dr, :]
                MQ = dnc[:, 5 + dr, :]
                IDF = dnc[:, 7, :]
                BD = dnc[:, 10, :]
                qkvb = [sb(f"qkvb{i}", [128, 12, 128], BF16) for i in range(2)]
                gb = [sb(f"gb{i}", [128, 16]) for i in range(2)]
                gst = sb("gst", [128, 16])
                egc = sb("egc", [128, 4]); be = sb("be", [128, 4]); edk = sb("edk", [128, 4]); egl = sb("egl", [128, 8])
                Ug = sb("Ug", [128, 4, 128])
                Nm = sb("Nm", [128, 4, 128])
                DL = sb("DL", [128, 4, 128]); DQm = sb("DQm", [128, 4, 128])
                egcr = sb("egcr", [128, 4, 128])
                Lb = [sb(f"Lb{i}", [128, 4, 128], BF16) for i in range(2)]
                Mb = [sb(f"Mb{i}", [128, 4, 128], BF16) for i in range(2)]
                Z = sb("Z", [128, 4, 128], BF16)
                kkb = sb("kkb", [128, 4, 128])
                qkTm = sb("qkTm", [128, 4, 128], BF16)
                kbe = sb("kbe", [128, 4, 128], BF16); kdec = sb("kdec", [128, 4, 128], BF16); vbt = sb("vbt", [128, 4, 128], BF16)
                um = sb("um", [128, 4, 128]); wTm = sb("wTm", [128, 4, 128], BF16); qdTm = sb("qdTm", [128, 4, 128], BF16)
                vn = sb("vn", [128, 4, 128], BF16)
                ot = [sb(f"ot{i}", [128, 4, 128]) for i in range(2)]
                S = sb("S", [128, 4, 128])
                Sb = sb("Sb", [128, 4, 128], BF16)
                g = [ps(f"g{i}", [128, 4, 128]) for i in range(3)]
                tbb = ps("tbb", [128, 8, 128], BF16)
                tb = [tbb[:, 0:4, :], tbb[:, 4:8, :]]
                gi = [0]
                tbi = [0]

                def nextg():
                    i = gi[0] % 3
                    gi[0] += 1
                    return g[i], ("p3g", i)

                def nexttb():
                    i = tbi[0] % 2
                    tbi[0] += 1
                    return tb[i], "p3tb"
                bc = lambda ap4: ap4.unsqueeze(2).to_broadcast([128, 4, 128])
                identb4 = ident[:, :].unsqueeze(1).to_broadcast([128, 4, 128])
                MSET("dve", S[:], 0.0, w=["S"])
                MSET("dve", Sb[:], 0.0, w=["Sb"])
                order = list(range(NT)) if dr == 0 else list(range(NT - 1, -1, -1))
                dnv = DN_d.rearrange("(f p) w -> p f w", p=128)

                def load(ti, b):
                    t0 = ti * 128
                    u = ti // 4
                    DMA("sp", qkvb[b][:], dnv[:, :, t0:t0 + 128], r=[("DN_d", u)], w=[("qkvb", b)])
                    DMA("sp", gb[b][:], GB_d[t0:t0 + 128, :], r=[("GB_d", u)], w=[("gb", b)])
                load(order[0], 0)
                for n_, ti in enumerate(order):
                    b = n_ % 2
                    t0 = ti * 128
                    if n_ + 1 < NT:
                        load(order[n_ + 1], 1 - b)
                    g4 = gb[b][:, 4 * dr:4 * dr + 4]
                    b4 = gb[b][:, 8 + 4 * dr:12 + 4 * dr]
                    qT = lambda h: qkvb[b][:, h, :]
                    kT = lambda h: qkvb[b][:, 4 + h, :]
                    vT = lambda h: qkvb[b][:, 8 + h, :]
                    pG3, kG = nextg()
                    pG = pG3[:, :, :].rearrange("p a b -> p (a b)")
                    MM(pG[:, 0:4], U, g4, True, True, r=["dnc", ("gb", b)], w=[kG])
                    MM(pG[:, 4:8], BD, g4, True, True, r=["dnc", ("gb", b)], w=[kG])
                    MM(pG[:, 8:12], dnc[:, 8, :], g4, True, True, r=["dnc", ("gb", b)], w=[kG])
                    MM(pG[:, 12:16], dnc[:, 9, :], g4, True, True, r=["dnc", ("gb", b)], w=[kG])
                    CP("dve", gst[:], pG[:, 0:16], r=[], w=[kG, "gst"])
                    yield
                    ACTV(egc[:], gst[:, 0:4], AF.Exp, r=["gst"], w=["egc"])
                    TT("dve", be[:], egc[:], b4, ALU.mult, r=["egc", ("gb", b)], w=["be"])
                    TT("dve", edk[:], gst[:, 4:8], gst[:, 0:4], ALU.subtract, r=["gst"], w=["edk"])
                    ACTV(edk[:], edk[:], AF.Exp, r=["edk"], w=["edk"])
                    ACTV(egl[:], gst[:, 8:16], AF.Exp, r=["gst"], w=["egl"])
                    TT("dve", Ug[:], U.unsqueeze(1).to_broadcast([128, 4, 128]), bc(g4), ALU.mult, r=["dnc", ("gb", b)], w=["Ug"])
                    yield
                    pR, kR = nextg()
                    MM(pR[:, :, :], ONES, Ug[:, :, :], True, True, r=["dnc", "Ug"], w=[kR])
                    TT("dve", Nm[:], pR[:, :, :], bc(gst[:, 0:4]), ALU.subtract, r=["gst"], w=[kR, "Nm"])
                    ACTV(egcr[:], pR[:, :, :], AF.Exp, r=[], w=[kR, "egcr"])
                    yield
                    mlb = MLOW.unsqueeze(1).to_broadcast([128, 4, 128])
                    mqb = MQ.unsqueeze(1).to_broadcast([128, 4, 128])
                    STT("dve", DL[:], Nm[:], -1.0, mlb, ALU.mult, ALU.add, r=["Nm", "dnc"], w=["DL"])
                    TT("pool", DQm[:], Nm[:], mqb, ALU.add, r=["Nm", "dnc"], w=["DQm"])
                    ACTV(DL[:], DL[:], AF.Exp, r=["DL"], w=["DL"])
                    ACTV(DQm[:], DQm[:], AF.Exp, r=["DQm"], w=["DQm"])
                    yield
                    pK_, kK = nextg()
                    for h in range(4):
                        MM(pK_[:, h, :], kT(h), kT(h), True, True, r=[("qkvb", b)], w=[kK])
                    TT("dve", kkb[:], pK_[:, :, :], bc(b4), ALU.mult, r=[("gb", b)], w=[kK, "kkb"])
                    TT("dve", Lb[0][:], kkb[:], DL[:], ALU.mult, r=["kkb", "DL"], w=[("Lb", 0)])
                    yield
                    pQ_, kQ = nextg()
                    for h in range(4):
                        MM(pQ_[:, h, :], kT(h), qT(h), True, True, r=[("qkvb", b)], w=[kQ])
                    TT("dve", qkTm[:], pQ_[:, :, :], DQm[:], ALU.mult, r=["DQm"], w=[kQ, "qkTm"])
                    yield
                    pM_, kM = nexttb()
                    for h in range(4):
                        TR(pM_[:, h, :], Lb[0][:, h, :], ident[:, :], r=[("Lb", 0), "ident"], w=[kM])
                    CP("act", Mb[0][:], pM_[:, :, :], r=[], w=[kM, ("Mb", 0)])
                    yield
                    STT("dve", Z[:], Mb[0][:], -1.0, identb4, ALU.mult, ALU.add, r=[("Mb", 0), "ident"], w=["Z"])
                    cur = 0
                    for lvl in range(5):
                        nxt = 1 - cur
                        pP_, kP = nextg()
                        for h in range(4):
                            MM(pP_[:, h, :], Mb[cur][:, h, :], Lb[cur][:, h, :], True, True, r=[("Mb", cur), ("Lb", cur)], w=[kP])
                        CP("act", Lb[nxt][:], pP_[:, :, :], r=[], w=[kP, ("Lb", nxt)])
                        yield
                        if lvl < 4:
                            pM2, kM2 = nextg()
                            for h in range(4):
                                MM(pM2[:, h, :], Lb[cur][:, h, :], Mb[cur][:, h, :], True, True, r=[("Mb", cur), ("Lb", cur)], w=[kM2])
                            CP("dve", Mb[nxt][:], pM2[:, :, :], r=[], w=[kM2, ("Mb", nxt)])
                            yield
                        pZ_, kZ = nextg()
                        for h in range(4):
                            MM(pZ_[:, h, :], Lb[nxt][:, h, :], Z[:, h, :], True, True, r=[("Lb", nxt), "Z"], w=[kZ])
                        TT("dve", Z[:], Z[:], pZ_[:, :, :], ALU.add, r=["Z"], w=[kZ, "Z"])
                        yield
                        cur = nxt
                    pT1, kT1 = nexttb()
                    for h in range(4):
                        TR(pT1[:, h, :], kT(h), ident[:, :], r=[("qkvb", b), "ident"], w=[kT1])
                    TT("dve", kbe[:], pT1[:, :, :], bc(be[:]), ALU.mult, r=["be"], w=[kT1, "kbe"])
                    TT("dve", kdec[:], pT1[:, :, :], bc(edk[:]), ALU.mult, r=["edk"], w=[kT1, "kdec"])
                    yield
                    pT2, kT2 = nexttb()
                    for h in range(4):
                        TR(pT2[:, h, :], vT(h), ident[:, :], r=[("qkvb", b), "ident"], w=[kT2])
                    TT("dve", vbt[:], pT2[:, :, :], bc(b4), ALU.mult, r=[("gb", b)], w=[kT2, "vbt"])
                    yield
                    pU_, kU = nextg()
                    for h in range(4):
                        MM(pU_[:, h, :], Z[:, h, :], vbt[:, h, :], True, True, r=["Z", "vbt"], w=[kU])
                    CP("act", um[:], pU_[:, :, :], r=[], w=[kU, "um"])
                    yield
                    pW_, kW = nextg()
                    for h in range(4):
                        MM(pW_[:, h, :], kbe[:, h, :], Z[:, h, :], True, True, r=["Z", "kbe"], w=[kW])
                    CP("act", wTm[:], pW_[:, :, :], r=[], w=[kW, "wTm"])
                    TT("pool", qdTm[:], qkvb[b][:, 0:4, :], egcr[:], ALU.mult, r=[("qkvb", b), "egcr"], w=["qdTm"])
                    yield
                    carry = None
                    if dr == 0 and ti % 16 == 0 and ti > 0:
                        carry = dnsc[:, 4 * ti + 2:4 * ti + 3]
                    if dr == 1 and ti % 16 == 15 and ti < NT - 1:
                        carry = dnsc[:, 4 * ti + 3:4 * ti + 4]
                    if carry is not None:
                        TS("dve", S[:], S[:], carry, None, ALU.mult, None, r=["S", "dnsc"], w=["S"])
                        CP("act", Sb[:], S[:], r=["S"], w=["Sb"])
                    io = n_ % 2
                    for c in ((0, 1) if dr == 0 else (1, 0)):
                        cs = slice(64 * c, 64 * c + 64)
                        pVN, kVN = nextg()
                        for h in range(4):
                            MM(pVN[cs, h, :], wTm[:, h, cs], Sb[:, h, :], True, True, r=["wTm", "Sb"], w=[kVN])
                        TT("dve", vn[cs, :, :], um[cs, :, :], pVN[cs, :, :], ALU.subtract, r=["um"], w=[kVN, "vn"])
                        yield
                        pO, kO = nextg()
                        for h in range(4):
                            MM(pO[cs, h, :], qdTm[:, h, cs], Sb[:, h, :], True, False, r=["qdTm", "Sb"], w=[kO])
                            MM(pO[cs, h, :], qkTm[cs, h, cs], vn[cs, h, :], False, True, r=["qkTm", "vn"], w=[kO])
                        pSn, kSn = nextg()
                        for h in range(4):
                            MM(pSn[:, h, :], kdec[cs, h, :], vn[cs, h, :], True, True, r=["kdec", "vn"], w=[kSn])
                        for h in range(4):
                            STT("dve", S[:, h, :], S[:, h, :], egl[:, 4 * c + h:4 * c + h + 1], pSn[:, h, :], ALU.mult, ALU.add,
                                r=["S", "egl"], w=[kSn, "S"])
                        CP("act", Sb[:], S[:], r=["S"], w=["Sb"])
                        CP("act", ot[io][cs, :, :], pO[cs, :, :], r=[], w=[kO, ("ot", io)])
                        yield
                    DMA("sp", OD_d[dr][t0:t0 + 128, :], ot[io][:].rearrange("p h d -> p (h d)"), r=[("ot", io)], w=[("OD_d", dr, ti)])
                    yield

        def pass_dn_both(dirs):
            P.shared = {"eps", "one", "ident", "GB_d", "DQ_d", "OD_d", "DN_d"}
            with contextlib.ExitStack() as es:
                gens = [(dr, pass_dn_gen(dr, es)) for dr in dirs]
                while gens:
                    for item in list(gens):
                        P.ns = ("dn", item[0])
                        try:
                            next(item[1])
                        except StopIteration:
                            gens.remove(item)
                P.ns = None
            P.barrier()

        def pass5():
            with contextlib.ExitStack() as es:
                def sb(name, shape, dt=F32):
                    return es.enter_context(nc.sbuf_tensor("p5_" + name, list(shape), dt))

                def ps(name, shape, dt=F32):
                    return es.enter_context(nc.psum_tensor("p5_" + name, list(shape), dt))
                wout = sb("wout", [128, 8, D], BF16)
                dnog = sb("dnog", [128, 512])
                DMA("sp", dnog[:], dnog_d[:, :], w=["dnog"])
                for kc in range(8):
                    DMA("pool", wout[:, kc, :], w_out_d[kc * 128:(kc + 1) * 128, :], w=[("wout", kc)])
                NB = 3
                NX = 4
                of_ = [sb(f"of{i}", [128, 512]) for i in range(NB)]
                ob_ = [sb(f"ob{i}", [128, 512]) for i in range(NB)]
                gt = [sb(f"gt{i}", [128, 512]) for i in range(NB)]
                xt = [sb(f"xt{i}", [128, D]) for i in range(NX)]
                at = [sb(f"at{i}", [128, 4, 128], BF16) for i in range(NX)]
                osum = sb("osum", [128, 512]); sq = sb("sq", [128, 512]); ss = sb("ss", [128, 4])
                gg = sb("gg", [128, 512]); on = sb("on", [128, 512])
                odn = [sb(f"odn{i}", [128, 512], BF16) for i in range(2)]
                eg = sb("eg", [128, 512])
                dnT = sb("dnT", [128, 4, 128], BF16)
                ht = [sb(f"ht{i}", [128, D]) for i in range(2)]
                pT = ps("pT", [128, 4, 128], BF16)
                pH = [ps(f"pH{i}", [128, 512]) for i in range(4)]

                def loads(ti):
                    b = ti % NB
                    bx = ti % NX
                    t0 = ti * 128
                    u = ti // 4
                    DMA("sp", of_[b][:], OD_d[0][t0:t0 + 128, :], r=[("OD_d", 0, ti)], w=[("of", b)])
                    DMA("sp", ob_[b][:], OD_d[1][t0:t0 + 128, :], r=[("OD_d", 1, ti)], w=[("ob", b)])
                    DMA("sp", gt[b][:], G_d[t0:t0 + 128, :], r=[("G_d", u)], w=[("gt", b)])
                    DMA("sp", xt[bx][:], xs[t0:t0 + 128, :], w=[("xt5", bx)])
                    DMA("sp", at[bx][:], AT_d[:, :, t0:t0 + 128].rearrange("j p w -> p j w"), r=[("AT_d", ti // 16)], w=[("at", bx)])

                def stage_a(ti):
                    b = ti % NB
                    TT("dve", osum[:], of_[b][:], ob_[b][:], ALU.add, r=[("of", b), ("ob", b)], w=["osum"])
                    ACTV(eg[:], gt[b][:], AF.Exp, r=[("gt", b)], w=["eg"], scale=-1.0)
                    TT("pool", gg[:], gt[b][:], dnog[:], ALU.mult, r=[("gt", b), "dnog"], w=["gg"])
                    ACTV(sq[:], osum[:], AF.Square, r=["osum"], w=["sq5"])
                    RED("dve", ss[:], sq[:].rearrange("p (h d) -> p h d", d=128), r=["sq5"], w=["ss5"])
                    ACTV(eg[:], eg[:], AF.Ln, r=["eg", "one"], w=["eg"], bias=oneT[:])
                    ACTV(eg[:], eg[:], AF.Exp, r=["eg"], w=["eg"], scale=-1.0)
                    ACTV(ss[:], ss[:], AF.Ln, r=["ss5", "eps"], w=["ss5"], scale=1.0 / 128, bias=epsT[:])
                    ACTV(ss[:], ss[:], AF.Exp, r=["ss5"], w=["ss5"], scale=-0.5)
                    TT("dve", gg[:], gg[:], eg[:], ALU.mult, r=["gg", "eg"], w=["gg"])
                    TT("dve", on[:].rearrange("p (h d) -> p h d", d=128), osum[:].rearrange("p (h d) -> p h d", d=128),
                       ss[:].unsqueeze(2).to_broadcast([128, 4, 128]), ALU.mult, r=["osum", "ss5"], w=["on"])
                    TT("dve", odn[ti % 2][:], on[:], gg[:], ALU.mult, r=["on", "gg"], w=[("odn", ti % 2)])

                def stage_b(ti):
                    bx = ti % NX
                    t0 = ti * 128
                    u = ti // 4
                    for j in range(4):
                        TR(pT[:, j, :], odn[ti % 2][:, j * 128:(j + 1) * 128], ident[:, :], r=[("odn", ti % 2), "ident"], w=["p5T"])
                    CP("act", dnT[:], pT[:, :, :], r=[], w=["p5T", "dnT"])
                    ih = ti % 2
                    for nh in range(2):
                        ip = (2 * ti + nh) % 4
                        csl = slice(512 * nh, 512 * (nh + 1))
                        for j in range(4):
                            MM(pH[ip][:, :], at[bx][:, j, :], wout[:, j, csl], j == 0, False, r=[("at", bx), ("wout", j)], w=[("p5H", ip)])
                        for j in range(4):
                            MM(pH[ip][:, :], dnT[:, j, :], wout[:, 4 + j, csl], False, j == 3, r=["dnT", ("wout", 4 + j)], w=[("p5H", ip)])
                        TT("dve", ht[ih][:, csl], pH[ip][:, :], xt[bx][:, csl], ALU.add, r=[("xt5", bx)], w=[("p5H", ip), ("ht5", ih)])
                    DMA("sp", H_d[t0:t0 + 128, :], ht[ih][:], r=[("ht5", ih)], w=[("H_d", u)])
                loads(0)
                if NT > 1:
                    loads(1)
                stage_a(0)
                for ti in range(NT):
                    if ti + 2 < NT:
                        loads(ti + 2)
                    if ti + 1 < NT:
                        stage_a(ti + 1)
                    stage_b(ti)
            P.barrier()

        def pass6(hbuf, hkeys_fn):
            with contextlib.ExitStack() as es:
                def sb(name, shape, dt=F32):
                    return es.enter_context(nc.sbuf_tensor("p6_" + name, list(shape), dt))

                def ps(name, shape, dt=F32):
                    return es.enter_context(nc.psum_tensor("p6_" + name, list(shape), dt))
                wup = sb("wup", [128, 8, 2 * FFN], BF16)
                wdn = sb("wdn", [128, NFC, D], BF16)
                g2 = sb("g2", [128, 8])
                cw = sb("cw", [128, 3, 44])
                cb = sb("cb", [128, 44])
                hsc = sb("hsc", [2, NU])
                DMA("sp", g2[:], norm2_d[:, :], w=["g2"])
                DMA("sp", cw[:], cw_d[:, :, :], w=["cw"])
                DMA("sp", cb[:], cb_d[:, :], w=["cb"])
                DMA("sp", hsc[:], halo_sc_d[:, :], w=["hsc"])
                for kc in range(8):
                    DMA("pool", wup[:, kc, :], w_up_d[kc * 128:(kc + 1) * 128, :], w=[("wup", kc)])
                    TS("dve", wup[:, kc, :], wup[:, kc, :], g2[:, kc:kc + 1], None, ALU.mult, None, r=["g2", ("wup", kc)], w=[("wup", kc)])
                for j in range(NFC):
                    DMA("pool", wdn[:, j, :], w_down_d[j * 128:(j + 1) * 128, :], w=[("wdn", j)])
                ht = sb("ht", [128, 4, D])
                hh = sb("hh", [2, D], BF16)
                hn = sb("hn", [128, D], BF16)
                hhn = sb("hhn", [2, D], BF16)
                ms = [sb(f"ms{i}", [128, 1]) for i in range(2)]
                hnTs = [sb(f"hnT{i}", [128, 8, 514], BF16) for i in range(2)]
                actTh = [sb(f"actT{i}", [128, NFC, 256], BF16) for i in range(2)]
                downq = []
                accg = [sb(f"accg{i}", [128, 256]) for i in range(2)]
                accu = [sb(f"accu{i}", [128, 256]) for i in range(2)]
                sg = [sb(f"sg{i}", [128, 256], BF16) for i in range(2)]
                yt = [sb(f"yt{i}", [128, D]) for i in range(2)]
                pT = [ps(f"pT{i}", [128, 8, 128], BF16) for i in range(2)]
                pU = [ps(f"pU{i}", [128, 512]) for i in range(4)]
                pD = [ps(f"pD{i}", [128, 512]) for i in range(2)]
                ucount = [0]
                dcount = [0]

                def load_h(u):
                    t0 = u * 512
                    DMA("sp", ht[:], hbuf[t0:t0 + 512, :].rearrange("(t p) d -> p t d", p=128), r=hkeys_fn(u), w=["ht"])
                    lo = max(t0 - 1, 0)
                    hi = min(t0 + 512, NTOK - 1)
                    DMA("pool", hh[0:1, :], hbuf[lo:lo + 1, :], r=hkeys_fn(max(u - 1, 0)), w=["hh"])
                    DMA("pool", hh[1:2, :], hbuf[hi:hi + 1, :], r=hkeys_fn(min(u + 1, NU - 1)), w=["hh"])

                ndef = []

                def norm_piece(u, piece, defer=None):
                    hnT = hnTs[u % 2]
                    hk = ("hnT", u % 2)
                    if piece == 0:
                        def dst_halo():
                            CP("dve", hnT[:, :, 0:514:513], pT[1][:, :, 0:2], r=[], w=["pT1", hk])
                        rmsnorm_T(None, hh[:, :], 2, ms[1], hhn, hhn, pT[1], "pT1", dst_halo, ["hh"], "p6h",
                                  scale_ap=hsc[:, u:u + 1], scale_key="hsc", junk_key="p6hhn", defer=defer)
                    else:
                        t = piece - 1

                        def dst_main(t=t):
                            CP("act", hnT[:, :, 1 + 128 * t:1 + 128 * (t + 1)], pT[0][:, :, :], r=[], w=["pT0", hk])
                        rmsnorm_T(None, ht[:, t, :], 128, ms[0], hn, hn, pT[0], "pT0", dst_main, ["ht"], "p6m", junk_key="p6mhn", defer=defer)
                load_h(0)
                for piece in range(5):
                    norm_piece(0, piece)
                for u in range(NU):
                    t0 = u * 512
                    hnT = hnTs[u % 2]
                    hk = ("hnT", u % 2)
                    if u + 1 < NU:
                        load_h(u + 1)
                    pair = 0
                    for wdw in range(2):
                        gw = 2 * u + wdw
                        actT = actTh[gw % 2]
                        ak = ("actT", gw % 2)
                        for j in range(NFC):
                            ig = (ucount[0] * 2) % 4
                            iu = (ucount[0] * 2 + 1) % 4
                            ia = ucount[0] % 2
                            ucount[0] += 1
                            for (ip, fc) in ((ig, j), (iu, NFC + j)):
                                for kc in range(8):
                                    MM(pU[ip][:, 0:258], wup[:, kc, fc * 128:(fc + 1) * 128], hnT[:, kc, 256 * wdw:256 * wdw + 258],
                                       kc == 0, kc == 7, r=[("wup", kc), hk], w=[("pU", ip)])
                            if u + 1 < NU and pair in (2, 10, 18, 26, 34):
                                norm_piece(u + 1, (pair - 2) // 8, defer=ndef)
                            if pair in (9, 17, 25, 33, 41):
                                while ndef:
                                    ndef.pop(0)()
                            if j in (3, 8, 13, 18) and downq:
                                downq.pop(0)()
                            pair += 1
                            for (ip, fc, acc_, ka) in ((ig, j, accg[ia], ("accg", ia)), (iu, NFC + j, accu[ia], ("accu", ia))):
                                ACTV(acc_[:], pU[ip][:, 1:257], AF.Identity, r=["cw", "cb"], w=[("pU", ip), ka],
                                     scale=cw[:, 1, fc:fc + 1], bias=cb[:, fc:fc + 1])
                                STT("dve", acc_[:], pU[ip][:, 0:256], cw[:, 0, fc:fc + 1], acc_[:], ALU.mult, ALU.add,
                                    r=["cw", ka], w=[("pU", ip), ka])
                                STT("dve", acc_[:], pU[ip][:, 2:258], cw[:, 2, fc:fc + 1], acc_[:], ALU.mult, ALU.add,
                                    r=["cw", ka], w=[("pU", ip), ka])
                            ACTV(sg[ia][:], accg[ia][:], AF.Silu, r=[("accg", ia)], w=[("sg", ia)])
                            TT("pool", actT[:, j, :], sg[ia][:], accu[ia][:], ALU.mult,
                               r=[("sg", ia), ("accu", ia)], w=[ak])
                        while downq:
                            downq.pop(0)()
                        for tt in range(2):
                            tok = t0 + 256 * wdw + 128 * tt
                            for nh in range(2):
                                def down_group(tt=tt, nh=nh, tok=tok, actT=actT, ak=ak, u=u, gw=gw):
                                    iy = (2 * gw + tt) % 2
                                    if nh == 0:
                                        DMA("sp", yt[iy][:], hbuf[tok:tok + 128, :], r=hkeys_fn(u), w=[("yt", iy)])
                                    ipd = dcount[0] % 2
                                    dcount[0] += 1
                                    for j in range(NFC):
                                        MM(pD[ipd][:, :], actT[:, j, 128 * tt:128 * (tt + 1)], wdn[:, j, 512 * nh:512 * (nh + 1)],
                                           j == 0, j == NFC - 1, r=[ak, ("wdn", j)], w=[("pD", ipd)])
                                    TT("dve", yt[iy][:, 512 * nh:512 * (nh + 1)], pD[ipd][:, :], yt[iy][:, 512 * nh:512 * (nh + 1)], ALU.add,
                                       r=[("yt", iy)], w=[("pD", ipd), ("yt", iy)])
                                    if nh == 1:
                                        DMA("sp", ys[tok:tok + 128, :], yt[iy][:], r=[("yt", iy)], w=[("ys", tok)])
                                downq.append(down_group)
                while downq:
                    downq.pop(0)()
            P.barrier()

        if "1" in passes:
            pass1()
        if "2" in passes:
            pass2()
        if "P" in passes:
            pass_prep()
        dirs = [dr for dr, nm in ((0, "3"), (1, "4")) if nm in passes]
        if dirs:
            pass_dn_both(dirs)
        if "5" in passes:
            pass5()
        if "6" in passes:
            if "5" in passes:
                pass6(H_d, lambda u: [("H_d", u)])
            else:
                pass6(xs, lambda u: [])
        P.emit()
    return nc


def _dn_consts():
    NEG = -30000.0
    p = np.arange(128)[:, None]
    f = np.arange(128)[None, :]
    same = (p // 64) == (f // 64)
    c = np.zeros((128, 11, 128), np.float32)
    c[:, 0] = same & (p <= f)
    c[:, 1] = same & (p >= f)
    c[:, 2] = 1.0
    c[:, 3] = np.where(same & (f < p), 0.0, NEG)
    c[:, 4] = np.where(same & (f > p), 0.0, NEG)
    c[:, 5] = np.where(same & (f >= p), 0.0, NEG)
    c[:, 6] = np.where(same & (f <= p), 0.0, NEG)
    c[:, 7] = np.eye(128)
    c[:, 8] = (p // 64 == 0) & (f >= 0)
    c[:, 9] = (p // 64 == 1) & (f >= 0)
    c[:, 10] = same
    return c


def _dn_scales(NSEG, link):
    NT = NSEG * SEG // 128
    sc = np.ones((128, 4 * NT), np.float32)
    for ti in range(NT):
        s = ti // 16
        if ti % 16 == 0:
            sc[:, 4 * ti + 0] = link[s]
            sc[:, 4 * ti + 2] = link[s]
        if ti % 16 == 15:
            sc[:, 4 * ti + 1] = link[s + 1]
            sc[:, 4 * ti + 3] = link[s + 1]
    return sc


def host_consts(NSEG, link, pos0):
    NTOK = NSEG * SEG
    NU = NTOK // 512
    NT = NTOK // 128
    hs = np.ones((2, NU), np.float32)
    for u in range(NU):
        t0 = u * 512
        if t0 % SEG == 0:
            hs[0, u] = link[t0 // SEG]
        if (t0 + 512) % SEG == 0:
            hs[1, u] = link[(t0 + 512) // SEG]
    lv = np.ones((128, 2 * NSEG), np.float32)
    for s in range(NSEG):
        lv[0:64, 2 * s] = link[s]
        lv[64:128, 2 * s + 1] = link[s + 1]
    r_ = np.arange(128)[:, None]
    q_ = np.arange(128)[None, :]
    amask = np.concatenate([(q_ <= r_), (q_ >= r_)], axis=1).astype(np.float32)
    half = 32
    inv_freq = (1.0 / (10000.0 ** (np.arange(half, dtype=np.float32) * 2.0 / 64))).astype(np.float32)
    invf = np.tile(np.concatenate([inv_freq, inv_freq])[None, :], (128, 1)).astype(np.float32)
    phase = np.tile(np.concatenate([np.zeros(32), np.full(32, np.pi / 2)])[None, :], (128, 1)).astype(np.float32)
    pos = np.zeros((128, NT), np.float32)
    for s in range(NSEG):
        for t in range(16):
            pos[:, s * 16 + t] = pos0[s] + t * 128 + np.arange(128)
    return {"ident": np.eye(128, dtype=np.float32), "halo_sc": hs, "lv": lv, "amask": amask, "invf": invf,
            "phase": phase, "pos": pos,
            "dnc": _dn_consts(), "dnsc": _dn_scales(NSEG, link)}


def weight_maps(norm1, w_in, att_q_norm, att_k_norm, dn_conv_w, dn_a_log, dn_dt_bias, dn_out_norm, w_out, norm2,
                w_up, ffn_conv_w, ffn_conv_b, w_down):
    f = lambda a: np.asarray(a, np.float32)[0]
    qg = np.concatenate([np.tile(f(att_q_norm), 8), np.tile(f(att_k_norm), 8)])
    return {
        "w_in": np.ascontiguousarray(f(w_in)),
        "w_out": np.ascontiguousarray(f(w_out)),
        "w_up": np.ascontiguousarray(f(w_up)),
        "w_down": np.ascontiguousarray(f(w_down)),
        "norm1": np.ascontiguousarray(f(norm1).reshape(8, 128).T),
        "norm2": np.ascontiguousarray(f(norm2).reshape(8, 128).T),
        "qkg": np.ascontiguousarray(np.tile(qg[None, :], (128, 1))),
        "alog": np.ascontiguousarray(np.tile(f(dn_a_log).reshape(1, 8), (128, 1))),
        "dtb": np.ascontiguousarray(np.tile(f(dn_dt_bias).reshape(1, 8), (128, 1))),
        "dncw": np.ascontiguousarray(f(dn_conv_w).reshape(5, 12, 128).transpose(2, 0, 1)),
        "dnog": np.ascontiguousarray(np.tile(f(dn_out_norm)[None, :], (128, 4))),
        "ffn_cw": np.ascontiguousarray(f(ffn_conv_w).reshape(3, 44, 128).transpose(2, 0, 1)),
        "ffn_cb": np.ascontiguousarray(f(ffn_conv_b).reshape(44, 128).T),
    }


SAMPLE_SLOTS = [6, 6, 5, 5, 5, 5]
_NC_CACHE = {}


def _core_layout():
    lay = []
    for b in range(2):
        lay.append([("p", b, s) for s in range(8)])
    nxt = 0
    for n in SAMPLE_SLOTS:
        row = []
        for i in range(8):
            if i < n:
                row.append(("s", nxt, 0))
                nxt += 1
            else:
                row.append(None)
        lay.append(row)
    return lay


def kernel(x_prompt, x_sample, norm1, w_in, att_q_norm, att_k_norm, dn_conv_w, dn_a_log, dn_dt_bias,
           dn_out_norm, w_out, norm2, w_up, ffn_conv_w, ffn_conv_b, w_down):
    NSEG = 8
    lay = _core_layout()
    if "nc" not in _NC_CACHE:
        _NC_CACHE["nc"] = build_program(NSEG=NSEG)
    nc = _NC_CACHE["nc"]
    x_prompt = np.asarray(x_prompt, np.float32)
    x_sample = np.asarray(x_sample, np.float32)
    common = weight_maps(norm1, w_in, att_q_norm, att_k_norm, dn_conv_w, dn_a_log, dn_dt_bias, dn_out_norm, w_out,
                         norm2, w_up, ffn_conv_w, ffn_conv_b, w_down)
    in_maps = []
    for c in range(NCORES):
        xs = np.zeros((NSEG * SEG, D), np.float32)
        link = np.zeros(NSEG + 1, np.float32)
        pos0 = np.zeros(NSEG, np.float32)
        for i, ent in enumerate(lay[c]):
            if ent is None:
                continue
            kind, b, s = ent
            if kind == "p":
                xs[i * SEG:(i + 1) * SEG] = x_prompt[b, s * SEG:(s + 1) * SEG]
                pos0[i] = s * SEG
                if s > 0:
                    link[i] = 1.0
            else:
                xs[i * SEG:(i + 1) * SEG] = x_sample[b]
        m = dict(common)
        m.update(host_consts(NSEG, link, pos0))
        m["xs"] = xs
        in_maps.append(m)
    res = run_bass_kernel_spmd(nc, in_maps, core_ids=list(range(NCORES)))
    y_prompt = np.zeros_like(x_prompt)
    y_sample = np.zeros_like(x_sample)
    for c in range(NCORES):
        ys = res.results[c]["ys"]
        for i, ent in enumerate(lay[c]):
            if ent is None:
                continue
            kind, b, s = ent
            if kind == "p":
                y_prompt[b, s * SEG:(s + 1) * SEG] = ys[i * SEG:(i + 1) * SEG]
            else:
                y_sample[b] = ys[i * SEG:(i + 1) * SEG]
    return (y_prompt, y_sample)
```
